# Optimizing a Trainium2 kernel written in Bass

```python
import jax, jax.numpy as jnp
from jax import lax
import numpy as np

D_MODEL = 1024
BATCH = 8
SEQ = 4096
DEPTH = 1

MIX_WIDTH = D_MODEL
CONV_CH = MIX_WIDTH // 2
RWKV_CH = MIX_WIDTH - CONV_CH
RWKV_HEAD = 64
RWKV_HEADS = RWKV_CH // RWKV_HEAD
CONV_WIDTH = 31
LORA_W = 64
LORA_A = 64
LORA_G = 128
RWKV_PROJ = 3 * RWKV_CH + LORA_W + LORA_A + LORA_G
IN_PROJ = 2 * CONV_CH + RWKV_PROJ
RWKV_SPLITS = [RWKV_CH, 2 * RWKV_CH, 3 * RWKV_CH, 3 * RWKV_CH + LORA_W, 3 * RWKV_CH + LORA_W + LORA_A]
PEER_HEADS = 8
PEER_NKEYS = 128
PEER_EXPERTS = PEER_NKEYS * PEER_NKEYS
PEER_DQ = 256
PEER_TOPK = 16
PEER_CHUNK = 128
RMS_EPS = 1e-6
LN_EPS = 1e-5
GN_EPS = 64e-5

kernel_name = 'hymba_conv_rwkv7_peer_adaln_block'

F32 = jnp.float32


def rmsnorm(x, g):
    xf = x.astype(F32)
    y = xf * lax.rsqrt(jnp.mean(xf * xf, axis=-1, keepdims=True) + RMS_EPS)
    return (y * g.astype(F32)).astype(x.dtype)


def modulate(x, shift, scale):
    return x * (1 + scale[:, None, :]) + shift[:, None, :]


def conformer_conv(p_conv, w_dw, b_dw, ln_w, ln_b):
    a, b = jnp.split(p_conv, 2, axis=-1)
    y = a * jax.nn.sigmoid(b)
    y = lax.conv_general_dilated(
        y, w_dw[:, None, :], window_strides=(1,), padding=[(CONV_WIDTH - 1, 0)],
        dimension_numbers=('NWC', 'WIO', 'NWC'), feature_group_count=CONV_CH) + b_dw
    yf = y.astype(F32)
    mu = jnp.mean(yf, axis=-1, keepdims=True)
    var = jnp.mean(jnp.square(yf - mu), axis=-1, keepdims=True)
    yf = (yf - mu) * lax.rsqrt(var + LN_EPS) * ln_w.astype(F32) + ln_b.astype(F32)
    return jax.nn.silu(yf).astype(p_conv.dtype)


def token_shift(p, mu):
    prev = jnp.pad(p, ((0, 0), (1, 0), (0, 0)))[:, :-1]
    return p + mu * (prev - p)


def rwkv7_mix(p_rwkv, mu, w0, w2, a0, a2, g2, k_k, k_a, r_k, gn_w, gn_b):
    B, S, _ = p_rwkv.shape
    xs = token_shift(p_rwkv, mu)
    r, k, v, wl, al, gl = jnp.split(xs, RWKV_SPLITS, axis=-1)
    w = -jax.nn.softplus(-(w0 + jnp.tanh(wl) @ w2)) - 0.5
    decay = jnp.exp(-jnp.exp(w.astype(F32)))
    a = jax.nn.sigmoid(a0 + al @ a2)
    g = jax.nn.sigmoid(gl) @ g2
    kk = k * k_k
    k = k * (1 + (a - 1) * k_a)

    def heads(t):
        return t.reshape(B, S, RWKV_HEADS, RWKV_HEAD).astype(F32)

    kk_h = heads(kk)
    kk_h = kk_h / jnp.maximum(jnp.sqrt(jnp.sum(kk_h * kk_h, axis=-1, keepdims=True)), 1e-12)
    r_h, k_h, v_h, w_h, a_h = heads(r), heads(k), heads(v), decay.reshape(B, S, RWKV_HEADS, RWKV_HEAD), heads(a)

    def step(state, inp):
        r_t, k_t, v_t, w_t, kk_t, a_t = inp
        sa = jnp.einsum('bhij,bhj->bhi', state, -kk_t)
        state = (state * w_t[:, :, None, :]
                 + sa[..., None] * (kk_t * a_t)[:, :, None, :]
                 + v_t[..., None] * k_t[:, :, None, :])
        y_t = jnp.einsum('bhij,bhj->bhi', state, r_t)
        return state, y_t

    def seq_first(t):
        return jnp.moveaxis(t, 1, 0)

    state0 = jnp.zeros((B, RWKV_HEADS, RWKV_HEAD, RWKV_HEAD), F32)
    _, y = lax.scan(step, state0, (seq_first(r_h), seq_first(k_h), seq_first(v_h),
                                   seq_first(w_h), seq_first(kk_h), seq_first(a_h)))
    y = jnp.moveaxis(y, 0, 1)
    mean = jnp.mean(y, axis=-1, keepdims=True)
    var = jnp.mean(jnp.square(y - mean), axis=-1, keepdims=True)
    y = ((y - mean) * lax.rsqrt(var + GN_EPS)).reshape(B, S, RWKV_CH) * gn_w.astype(F32) + gn_b.astype(F32)
    bonus = jnp.sum(r_h * k_h * r_k.astype(F32), axis=-1, keepdims=True) * v_h
    y = y + bonus.reshape(B, S, RWKV_CH)
    return (y * g.astype(F32)).astype(p_rwkv.dtype)


def peer_ffn(u, w_q, sub_keys, expert_u, expert_v):
    B, S, D = u.shape
    tokens = u.reshape(-1, PEER_CHUNK, D)

    def chunk(xc):
        tc = xc.shape[0]
        q = (xc @ w_q).reshape(tc, PEER_HEADS, 2, PEER_DQ // 2)
        s = jnp.einsum('thpd,hpkd->thpk', q, sub_keys).astype(F32)
        s_top, i_top = lax.top_k(s, PEER_TOPK)
        cand = s_top[:, :, 0, :, None] + s_top[:, :, 1, None, :]
        cand_idx = i_top[:, :, 0, :, None] * PEER_NKEYS + i_top[:, :, 1, None, :]
        best, pos = lax.top_k(cand.reshape(tc, PEER_HEADS, PEER_TOPK * PEER_TOPK), PEER_TOPK)
        expert = jnp.take_along_axis(cand_idx.reshape(tc, PEER_HEADS, PEER_TOPK * PEER_TOPK), pos, axis=-1)
        gate = jax.nn.softmax(best, axis=-1)
        ue = jnp.take(expert_u, expert, axis=0)
        ve = jnp.take(expert_v, expert, axis=0)
        h = jax.nn.gelu(jnp.einsum('thkd,td->thk', ue, xc).astype(F32), approximate=False)
        return jnp.einsum('thk,thkd->td', (gate * h).astype(xc.dtype), ve)

    out = lax.map(chunk, tokens)
    return out.reshape(B, S, D)


def setup_inputs(seed: int = 0) -> dict:
    key = jax.random.key(seed)
    ks = iter(jax.random.split(key, 32))

    def nrm(shape, scale):
        return scale * jax.random.normal(next(ks), shape, jnp.float32)

    def gain(shape):
        return 1.0 + nrm(shape, 0.02)

    L = DEPTH
    return {
        'x': nrm((BATCH, SEQ, D_MODEL), 1.0),
        'c': nrm((BATCH, D_MODEL), 1.0),
        'ada_w': nrm((L, D_MODEL, 6 * D_MODEL), D_MODEL ** -0.5),
        'ada_b': nrm((L, 6 * D_MODEL), 0.02),
        'norm_mix_g': gain((L, D_MODEL)),
        'w_in': nrm((L, D_MODEL, IN_PROJ), D_MODEL ** -0.5),
        'conv_dw_w': nrm((L, CONV_WIDTH, CONV_CH), CONV_WIDTH ** -0.5),
        'conv_dw_b': nrm((L, CONV_CH), 0.02),
        'conv_ln_w': gain((L, CONV_CH)),
        'conv_ln_b': nrm((L, CONV_CH), 0.02),
        'rwkv_mu': jax.random.uniform(next(ks), (L, RWKV_PROJ), jnp.float32),
        'rwkv_w0': jax.random.uniform(next(ks), (L, RWKV_CH), jnp.float32, -6.0, 1.0),
        'rwkv_w2': nrm((L, LORA_W, RWKV_CH), 0.5 * LORA_W ** -0.5),
        'rwkv_a0': nrm((L, RWKV_CH), 0.1),
        'rwkv_a2': nrm((L, LORA_A, RWKV_CH), 0.5 * LORA_A ** -0.5),
        'rwkv_g2': nrm((L, LORA_G, RWKV_CH), LORA_G ** -0.5),
        'rwkv_k_k': 0.85 + nrm((L, RWKV_CH), 0.02),
        'rwkv_k_a': gain((L, RWKV_CH)),
        'rwkv_r_k': nrm((L, RWKV_HEADS, RWKV_HEAD), 0.1),
        'rwkv_gn_w': gain((L, RWKV_CH)),
        'rwkv_gn_b': nrm((L, RWKV_CH), 0.02),
        'w_out': nrm((L, MIX_WIDTH, D_MODEL), MIX_WIDTH ** -0.5),
        'norm_ffn_g': gain((L, D_MODEL)),
        'peer_w_q': nrm((L, D_MODEL, PEER_HEADS * PEER_DQ), D_MODEL ** -0.5),
        'peer_sub_keys': nrm((L, PEER_HEADS, 2, PEER_NKEYS, PEER_DQ // 2), (PEER_DQ // 2) ** -0.5),
        'peer_u': nrm((L, PEER_EXPERTS, D_MODEL), D_MODEL ** -0.5),
        'peer_v': nrm((L, PEER_EXPERTS, D_MODEL), (PEER_HEADS * PEER_TOPK) ** -0.5),
        'final_g': gain((D_MODEL,)),
    }


def reference(x, c, ada_w, ada_b, norm_mix_g, w_in, conv_dw_w, conv_dw_b, conv_ln_w, conv_ln_b,
              rwkv_mu, rwkv_w0, rwkv_w2, rwkv_a0, rwkv_a2, rwkv_g2, rwkv_k_k, rwkv_k_a, rwkv_r_k,
              rwkv_gn_w, rwkv_gn_b, w_out, norm_ffn_g, peer_w_q, peer_sub_keys, peer_u, peer_v, final_g):
    for l in range(DEPTH):
        mod = jax.nn.silu(c) @ ada_w[l] + ada_b[l]
        sh_mix, sc_mix, gt_mix, sh_ffn, sc_ffn, gt_ffn = jnp.split(mod, 6, axis=-1)

        u = modulate(rmsnorm(x, norm_mix_g[l]), sh_mix, sc_mix)
        p = u @ w_in[l]
        y_conv = conformer_conv(p[..., :2 * CONV_CH], conv_dw_w[l], conv_dw_b[l], conv_ln_w[l], conv_ln_b[l])
        y_rwkv = rwkv7_mix(p[..., 2 * CONV_CH:], rwkv_mu[l], rwkv_w0[l], rwkv_w2[l], rwkv_a0[l], rwkv_a2[l],
                           rwkv_g2[l], rwkv_k_k[l], rwkv_k_a[l], rwkv_r_k[l], rwkv_gn_w[l], rwkv_gn_b[l])
        mix = jnp.concatenate([y_conv, y_rwkv], axis=-1) @ w_out[l]
        x = x + gt_mix[:, None, :] * mix

        u = modulate(rmsnorm(x, norm_ffn_g[l]), sh_ffn, sc_ffn)
        x = x + gt_ffn[:, None, :] * peer_ffn(u, peer_w_q[l], peer_sub_keys[l], peer_u[l], peer_v[l])
    return rmsnorm(x, final_g)
```

```python
import numpy as np
from contextlib import ExitStack
import concourse.bass as bass
import concourse.mybir as mybir
from concourse.bass_utils import run_bass_kernel_spmd

F32 = mybir.dt.float32
BF16 = mybir.dt.bfloat16
U16 = mybir.dt.uint16
I32 = mybir.dt.int32
AF = mybir.ActivationFunctionType
ALU = mybir.AluOpType
AX = mybir.AxisListType

T = 4096
D = 1024
NCORES = 8
C0 = 0.6065306597126334

G1, G2, CW, CB, LW, LB, MU, W0, A0, KK, KA, RK, GW, GB, NCOL = (
    0, 8, 16, 140, 144, 148, 152, 166, 170, 174, 178, 182, 186, 190, 194)


class Sync:
    NDMA = 12

    def __init__(self, nc, stack):
        self.nc = nc
        self.eng = {"pe": nc.tensor, "act": nc.scalar, "dve": nc.vector,
                    "pool": nc.gpsimd, "sp": nc.sync}
        self.sem, self.cnt = {}, {}
        self.semobj = {}
        for e in self.eng:
            self.sem[e] = stack.enter_context(nc.semaphore("s_" + e))
            self.semobj[self.sem[e].name] = self.sem[e]
            self.cnt[e] = 0
        self.dsem, self.dnext = {}, {}
        self.eng["cast"] = nc.gpsimd
        for q in ("sp", "pool", "act", "cast"):
            self.dsem[q] = [[stack.enter_context(nc.semaphore(f"d_{q}{i}")), 0]
                            for i in range(32 if q == "cast" else self.NDMA)]
            for s, _ in self.dsem[q]:
                self.semobj[s.name] = s
            self.dnext[q] = 0
        self.seen = {e: {} for e in self.eng}
        self.snap = {}
        self.lastw = {}
        self.readers = {}
        self.nwaits = 0
        self.ninst = 0

    def _need(self, e, ev, waits):
        if ev is None:
            return
        name, val = ev
        if self.seen[e].get(name, 0) >= val:
            return
        if e == "pe" and name == self.sem["pe"].name:
            return
        if waits.get(name, 0) < val:
            waits[name] = val

    def _do_waits(self, e, waits):
        for name, val in waits.items():
            if self.seen[e].get(name, 0) >= val:
                continue
            self.eng[e].wait_ge(self.semobj[name], val)
            self.nwaits += 1
            sn = self.snap.get((name, val))
            se = self.seen[e]
            if sn:
                for k, v in sn.items():
                    if se.get(k, 0) < v:
                        se[k] = v
            if se.get(name, 0) < val:
                se[name] = val

    def _deps(self, e, reads, writes):
        waits = {}
        for k in reads:
            self._need(e, self.lastw.get(k), waits)
        for k in writes:
            self._need(e, self.lastw.get(k), waits)
            for ev in self.readers.get(k, ()):
                self._need(e, ev, waits)
        self._do_waits(e, waits)

    def _record(self, e, ev, reads, writes):
        for k in reads:
            self.readers.setdefault(k, []).append(ev)
        for k in writes:
            self.lastw[k] = ev
            self.readers[k] = []
        self.snap[ev] = dict(self.seen[e])

    EXCL = {"psmod", "psb", "pT", "psA", "psB", "psC", "ps1", "ps2", "rpsA0", "rpsA1", "rpsM0", "rpsM1",
            "rW", "pO", "pT3", "pW", "pS", "pTr", "pAcc", "pSx", "pG"}

    def op(self, e, fn, reads=(), writes=()):
        ex = [k for k in reads if (k if isinstance(k, str) else k[0]) in self.EXCL]
        if ex:
            writes = list(writes) + ex
        self._deps(e, reads, writes)
        ins = fn(self.eng[e])
        self.cnt[e] += 1
        s = self.sem[e]
        ins.then_inc(s, 1)
        ev = (s.name, self.cnt[e])
        self._record(e, ev, reads, writes)
        self.ninst += 1
        return ev

    def dma(self, q, out, in_, reads=(), writes=(), **kw):
        pool = self.dsem[q]
        slot = pool[self.dnext[q] % len(pool)]
        self.dnext[q] += 1
        s = slot[0]
        waits = {}
        if slot[1] > 0:
            self._need(q, (s.name, slot[1]), waits)
        self._do_waits(q, waits)
        self._deps(q, reads, writes)
        ins = self.eng[q].dma_start(out=out, in_=in_, **kw)
        slot[1] += 16
        ins.then_inc(s, 16)
        ev = (s.name, slot[1])
        self._record(q, ev, reads, writes)
        self.ninst += 1
        return ev

    def barrier(self):
        evs = []
        for e in self.cnt:
            if self.cnt[e]:
                evs.append((self.sem[e].name, self.cnt[e]))
        for q in self.dsem:
            if q == "cast":
                continue
            for s, c in self.dsem[q]:
                if c:
                    evs.append((s.name, c))
        for e in self.eng:
            if e == "cast":
                continue
            waits = {}
            for ev in evs:
                self._need(e, ev, waits)
            self._do_waits(e, waits)
        keep = lambda k: isinstance(k, tuple) and k[0] in ("Ub", "Vb")
        self.lastw = {k: v for k, v in self.lastw.items() if keep(k)}
        self.readers = {k: v for k, v in self.readers.items() if keep(k)}

    def finish(self, keys):
        waits = {}
        for k in keys:
            self._need("sp", self.lastw.get(k), waits)
        self._do_waits("sp", waits)


def build_nc(debug=(), stop_after=None):
    nc = bass.Bass("TRN2", target_bir_lowering=False)

    def din(name, shape, dt=F32):
        return nc.dram_tensor(name, list(shape), dt, kind="ExternalInput").ap()

    def dscr(name, shape, dt=F32):
        return nc.dram_tensor(name, list(shape), dt, kind="Internal").ap()

    x_d = din("x", [T, D])
    ccol_d = din("c_col", [128, 8])
    adaw_d = din("ada_w", [D, 6 * D])
    adab_d = din("ada_b_col", [128, 48])
    cols_d = din("cols", [128, NCOL])
    fing_d = din("final_g_bc", [128, D])
    win_d = din("w_in", [D, 2816])
    wa2_d = din("wa2", [128, 512])
    g2_d = din("g2", [128, 512])
    wout_d = din("w_out", [D, D])
    wqT_d = din("wqT", [2048, D])
    KT_d = din("KT", [128, 16 * 128])
    UT_d = din("UTt", [128 * 128, 1024])
    V_d = din("V", [16384, D])
    out_d = nc.dram_tensor("out", [T, D], F32, kind="ExternalOutput").ap()

    yc_s = dscr("yc_s", [512, T])
    ycat_s = dscr("ycat_s", [1024, T], BF16)
    x1_s = dscr("x1_s", [T, D])
    u2T_s = dscr("u2T_s", [128, 8 * T], BF16)
    gT_s = dscr("gT_s", [128, T])
    iT_s = dscr("iT_s", [128, T])
    jT_s = dscr("jT_s", [128, T])
    Ub_s = dscr("Ub_s", [128 * 128, 1024], BF16)
    Vb_s = dscr("Vb_s", [16384, D], BF16)

    dbg = {}

    def dbg_out(name, shape, dt=F32, force=False):
        if name in debug or force:
            dbg[name] = nc.dram_tensor("dbg_" + name, list(shape), dt, kind="ExternalOutput").ap()
            return dbg[name]
        return None

    with ExitStack() as top:
        S = Sync(nc, top)

        def sbt(st, name, shape, dt=F32):
            return st.enter_context(nc.sbuf_tensor("s_" + name, list(shape), dt))

        def pst(st, name, shape=(128, 512), dt=F32):
            return st.enter_context(nc.psum_tensor("p_" + name, list(shape), dt))

        def aff(e, out, in_, scale, bias, reads, writes, func=None):
            if e == "act":
                f = func or AF.Identity
                return S.op("act", lambda en: en.activation(out=out, in_=in_, func=f, bias=bias, scale=scale),
                            reads=reads, writes=writes)
            assert func is None
            return S.op(e, lambda en: en.tensor_scalar(out=out, in0=in_, scalar1=scale, scalar2=bias,
                                                       op0=ALU.mult, op1=ALU.add), reads=reads, writes=writes)

        def build_gate_bc(st, dst, mof, pb, pbkey):
            dg = sbt(st, "dg_" + dst.name[2:], [128, 128])
            for cidx in range(8):
                S.op("dve", lambda e: e.tensor_scalar_mul(out=dg[:], in0=ident[:], scalar1=modT[:, mof + cidx:mof + cidx + 1]),
                     reads=["dg"], writes=["dg"])
                S.op("pe", lambda e: e.matmul(pb[:, 0:128], lhsT=ones_f[:], rhs=dg[:], start=True, stop=True),
                     reads=["dg"], writes=[pbkey])
                S.op("act", lambda e: e.copy(out=dst[:, cidx * 128:(cidx + 1) * 128], in_=pb[:, 0:128]),
                     reads=[pbkey], writes=[dst.name[2:]])

        cols = sbt(top, "cols", [128, NCOL])
        dcol = sbt(top, "dcol", [128, 48])
        OMM, OMKA, GSC1, GSC2 = 0, 14, 18, 26
        modT = sbt(top, "modT", [128, 48])
        ident = sbt(top, "ident", [128, 128])
        iota_f = sbt(top, "iota_f", [128, 128])
        ones_f = sbt(top, "ones_f", [128, 128])
        blk1 = sbt(top, "blk1", [128, 128])
        mask4 = sbt(top, "mask4", [128, 4, 128])
        maskL = sbt(top, "maskL", [128, 128])

        NCAST = 16
        cast_jobs = []
        for i in range(NCAST):
            r0, r1 = i * (16384 // NCAST), (i + 1) * (16384 // NCAST)
            cast_jobs.append((Ub_s[r0:r1, :], UT_d[r0:r1, :], ("Ub", i)))
            cast_jobs.append((Vb_s[r0:r1, :], V_d[r0:r1, :], ("Vb", i)))

        def issue_casts(n):
            for _ in range(n):
                if cast_jobs:
                    o_, i_, k_ = cast_jobs.pop(0)
                    S.dma("cast", o_, i_, writes=[k_])

        with ExitStack() as ph:
            io_i = sbt(ph, "io_i", [128, 128], I32)
            io_f = sbt(ph, "io_f", [128, 128])
            ccol = sbt(ph, "ccol", [128, 8])
            sc = sbt(ph, "silu_c", [128, 8])
            adab = sbt(ph, "adab", [128, 48])
            abuf = [sbt(ph, f"abuf{i}", [128, 2048]) for i in range(2)]
            dg = sbt(ph, "dg", [128, 128])
            psmod_t = pst(ph, "psmod", [128, 512])
            psmod = psmod_t[:, 0:384]
            psb = [pst(ph, f"psb{i}", [128, 512]) for i in range(2)]

            S.dma("sp", cols[:], cols_d, writes=["cols"])
            S.dma("sp", ccol[:], ccol_d, writes=["ccol"])
            S.dma("sp", adab[:], adab_d, writes=["adab"])
            S.op("pool", lambda e: e.iota(io_i[:], pattern=[[1, 128]], base=0, channel_multiplier=-1), writes=["io_i"])
            S.op("dve", lambda e: e.tensor_copy(out=io_f[:], in_=io_i[:]), reads=["io_i"], writes=["io_f"])
            S.op("dve", lambda e: e.tensor_single_scalar(out=ident[:], in_=io_f[:], scalar=0.0, op=ALU.is_equal),
                 reads=["io_f"], writes=["ident"])
            S.op("pool", lambda e: e.iota(io_i[:], pattern=[[1, 128]], base=0, channel_multiplier=0),
                 reads=["io_i"], writes=["io_i"])
            S.op("dve", lambda e: e.tensor_copy(out=iota_f[:], in_=io_i[:]), reads=["io_i"], writes=["iota_f"])
            S.op("pool", lambda e: e.memset(ones_f[:], 1.0), writes=["ones_f"])
            S.op("pool", lambda e: e.memset(blk1[:], 0.0), writes=["blk1"])
            S.op("pool", lambda e: e.memset(blk1[0:64, 0:64], 1.0), writes=["blk1"])
            S.op("pool", lambda e: e.memset(blk1[64:128, 64:128], 1.0), writes=["blk1"])
            S.op("dve", lambda e: e.scalar_tensor_tensor(out=mask4[:, 0, :], in0=io_f[:], scalar=0.0, in1=blk1[:],
                                                        op0=ALU.is_gt, op1=ALU.mult), reads=["io_f", "blk1"], writes=["mask4"])
            S.op("dve", lambda e: e.tensor_copy(out=mask4[:, 1, :], in_=mask4[:, 0, :]), reads=["mask4"], writes=["mask4"])
            S.op("dve", lambda e: e.scalar_tensor_tensor(out=mask4[:, 2, :], in0=io_f[:], scalar=0.0, in1=blk1[:],
                                                        op0=ALU.is_ge, op1=ALU.mult), reads=["io_f", "blk1", "mask4"], writes=["mask4"])
            S.op("dve", lambda e: e.tensor_copy(out=mask4[:, 3, :], in_=mask4[:, 2, :]), reads=["mask4"], writes=["mask4"])
            S.op("dve", lambda e: e.scalar_tensor_tensor(out=maskL[:], in0=io_f[:], scalar=0.0, in1=blk1[:],
                                                        op0=ALU.is_lt, op1=ALU.mult), reads=["io_f", "blk1"], writes=["maskL"])
            S.op("dve", lambda e: e.tensor_scalar(out=dcol[:, OMM:OMM + 14], in0=cols[:, MU:MU + 14], scalar1=-1.0, scalar2=1.0,
                                                 op0=ALU.mult, op1=ALU.add), reads=["cols"], writes=["dcol"])
            S.op("dve", lambda e: e.tensor_scalar(out=dcol[:, OMKA:OMKA + 4], in0=cols[:, KA:KA + 4], scalar1=-1.0, scalar2=1.0,
                                                 op0=ALU.mult, op1=ALU.add), reads=["cols", "dcol"], writes=["dcol"])
            S.op("act", lambda e: e.activation(out=sc[:], in_=ccol[:], func=AF.Silu), reads=["ccol"], writes=["silu_c"])
            n = 0
            for k in range(8):
                for pc in range(3):
                    ab = abuf[n % 2]
                    S.dma("sp", ab[:], adaw_d[k * 128:(k + 1) * 128, pc * 2048:(pc + 1) * 2048], writes=[("abuf", n % 2)])
                    for fl in range(16):
                        f = pc * 16 + fl
                        S.op("pe", lambda e: e.matmul(psmod[:, f * 8 + k:f * 8 + k + 1], lhsT=ab[:, fl * 128:(fl + 1) * 128],
                                                      rhs=sc[:, k:k + 1], start=True, stop=True),
                             reads=[("abuf", n % 2), "silu_c"], writes=["psmod"])
                    n += 1
            S.op("dve", lambda e: e.tensor_reduce(out=modT[:], in_=psmod.rearrange("p (f k) -> p f k", k=8),
                                                 axis=AX.X, op=ALU.add), reads=["psmod"], writes=["modT"])
            S.op("dve", lambda e: e.tensor_add(out=modT[:], in0=modT[:], in1=adab[:]), reads=["modT", "adab"], writes=["modT"])
            for (dst, gof, sof) in ((GSC1, G1, 8), (GSC2, G2, 32)):
                S.op("dve", lambda e: e.scalar_tensor_tensor(out=dcol[:, dst:dst + 8], in0=modT[:, sof:sof + 8], scalar=1.0,
                                                            in1=cols[:, gof:gof + 8], op0=ALU.add, op1=ALU.mult),
                     reads=["modT", "cols", "dcol"], writes=["dcol"])
            if "modT" in debug:
                S.dma("sp", dbg_out("modT", [128, 48]), modT[:], reads=["modT"], writes=["dbg_modT"])
            S.barrier()

        done = [False]

        def fin():
            S.barrier()
            S.finish([])
            done[0] = True

        if stop_after == 0:
            fin()

        if not done[0]:
          with ExitStack() as mx:
            uT = sbt(mx, "uT", [128, 8, T + 1], BF16)

            with ExitStack() as ph:
                xb = [sbt(ph, f"xb{i}", [128, D]) for i in range(2)]
                junk = sbt(ph, "junk", [128, D])
                xn = [sbt(ph, f"xn{i}", [128, D]) for i in range(2)]
                ss = sbt(ph, "ss", [128, 32])
                rs = sbt(ph, "rs", [128, 32])
                pT = [pst(ph, f"pT{i}", [128, 1024]) for i in range(2)]
                S.op("pool", lambda e: e.memset(uT[:, :, 0:1], 0.0), writes=["uT0"])
                def p1_front(tt):
                    b = tt % 2
                    S.dma("sp", xb[b][:], x_d[tt * 128:(tt + 1) * 128, :], writes=[("xb", b)])
                    S.op("act", lambda e: e.activation(out=junk[:], in_=xb[b][:], func=AF.Square),
                         reads=[("xb", b)], writes=["junk"])
                    S.op("dve", lambda e: e.reduce_sum(out=ss[:, tt:tt + 1], in_=junk[:], axis=AX.X),
                         reads=["junk"], writes=[("ss", tt)])
                    S.op("act", lambda e: e.activation(out=rs[:, tt:tt + 1], in_=ss[:, tt:tt + 1], func=AF.Sqrt,
                                                       bias=1e-6, scale=1.0 / D), reads=[("ss", tt)], writes=[("rs", tt)])
                    S.op("dve", lambda e: e.reciprocal(out=rs[:, tt:tt + 1], in_=rs[:, tt:tt + 1]),
                         reads=[("rs", tt)], writes=[("rs", tt)])
                    S.op("dve", lambda e: e.tensor_scalar_mul(out=xn[b][:], in0=xb[b][:], scalar1=rs[:, tt:tt + 1]),
                         reads=[("xb", b), ("rs", tt)], writes=[("xn", b)])

                p1_front(0)
                for tt in range(32):
                    b = tt % 2
                    if tt + 1 < 32:
                        p1_front(tt + 1)
                    for k in range(8):
                        S.op("pe", lambda e: e.transpose(pT[b][:, k * 128:(k + 1) * 128], xn[b][:, k * 128:(k + 1) * 128], ident[:]),
                             reads=[("xn", b)], writes=[("pT", b, k // 4)])
                    for k in range(8):
                        aff("act" if k < 4 else "dve", uT[:, k, 1 + tt * 128:1 + (tt + 1) * 128],
                            pT[b][:, k * 128:(k + 1) * 128], dcol[:, GSC1 + k:GSC1 + k + 1], modT[:, k:k + 1],
                            reads=[("pT", b, k // 4)], writes=[("uT", tt)])
                S.barrier()
            if "uT" in debug:
                du = dbg_out("uT", [128, 8 * T], BF16).rearrange("p (k t) -> p k t", k=8)
                for k in range(8):
                    S.dma("sp", du[:, k, :], uT[:, k, 1:T + 1], writes=[("dbg_uT", k)])
                S.barrier()
            if stop_after == 1:
                fin()

            def load_w(st_f32, dst_bf, f, key):
                S.dma("sp", st_f32[:], win_d[:, f * 128:(f + 1) * 128].rearrange("(k p) f -> p k f", p=128),
                      writes=[("wst", st_f32.name[2:])])
                S.op("pool", lambda e: e.tensor_copy(out=dst_bf[:], in_=st_f32[:]), reads=[("wst", st_f32.name[2:])], writes=[("wbf", key)])

            if not done[0]:
              with ExitStack() as ph:
                wst = [sbt(ph, f"wst{i}", [128, 8, 128]) for i in range(2)]
                wab = [sbt(ph, f"wab{i}", [128, 8, 128], BF16) for i in range(2)]
                diag = sbt(ph, "diag", [128, 31, 128], BF16)
                ypad = sbt(ph, "ypad", [128, 30 + T], BF16)
                sig = [sbt(ph, f"sig{i}", [128, 512]) for i in range(2)]
                ycb = [sbt(ph, f"ycb{i}", [128, 512]) for i in range(2)]
                psA = [pst(ph, f"psA{i}") for i in range(2)]
                psB = [pst(ph, f"psB{i}") for i in range(2)]
                psC = [pst(ph, f"psC{i}") for i in range(2)]
                S.op("pool", lambda e: e.memset(ypad[:, 0:30], 0.0), writes=["ypad0"])
                for ct in range(4):
                    issue_casts(2)
                    load_w(wst[0], wab[0], ct, 0)
                    load_w(wst[1], wab[1], 4 + ct, 1)
                    for k in range(31):
                        S.op("dve", lambda e: e.tensor_scalar_mul(out=diag[:, k, :], in0=ident[:],
                                                                  scalar1=cols[:, CW + ct * 31 + k:CW + ct * 31 + k + 1]),
                             writes=["diag"])
                    for tb in range(8):
                        b = tb % 2
                        t0 = tb * 512
                        for k in range(8):
                            S.op("pe", lambda e: e.matmul(psA[b][:], lhsT=wab[0][:, k, :], rhs=uT[:, k, 1 + t0:1 + t0 + 512],
                                                          start=(k == 0), stop=(k == 7)), reads=[("wbf", 0)], writes=[("psA", b)])
                        for k in range(8):
                            S.op("pe", lambda e: e.matmul(psB[b][:], lhsT=wab[1][:, k, :], rhs=uT[:, k, 1 + t0:1 + t0 + 512],
                                                          start=(k == 0), stop=(k == 7)), reads=[("wbf", 1)], writes=[("psB", b)])
                        S.op("act", lambda e: e.activation(out=sig[b][:], in_=psB[b][:], func=AF.Sigmoid),
                             reads=[("psB", b)], writes=[("sig", b)])
                        S.op("dve", lambda e: e.tensor_tensor(out=ypad[:, 30 + t0:30 + t0 + 512], in0=psA[b][:], in1=sig[b][:],
                                                             op=ALU.mult), reads=[("psA", b), ("sig", b)], writes=[("ypad", tb)])
                    for tb in range(8):
                        b = tb % 2
                        t0 = tb * 512
                        for k in range(31):
                            S.op("pe", lambda e: e.matmul(psC[b][:], lhsT=diag[:, k, :], rhs=ypad[:, t0 + k:t0 + k + 512],
                                                          start=(k == 0), stop=(k == 30)),
                                 reads=["diag", "ypad0", ("ypad", tb), ("ypad", max(tb - 1, 0))], writes=[("psC", b)])
                        aff("act", ycb[b][:], psC[b][:], 1.0, cols[:, CB + ct:CB + ct + 1], reads=[("psC", b)], writes=[("ycb", b)])
                        S.dma("sp", yc_s[ct * 128:(ct + 1) * 128, t0:t0 + 512], ycb[b][:], reads=[("ycb", b)],
                              writes=[("yc_s", ct, tb)])
                S.barrier()
            if "yc" in debug and not done[0]:
                S.dma("sp", dbg_out("yc", [512, T]), yc_s, writes=["dbg_yc"])
                S.barrier()
            if stop_after == 2 and not done[0]:
                fin()
            done_2a = True

            if not done[0]:
              with ExitStack() as ph:
                yb = [sbt(ph, f"yb{i}", [128, 4, 512]) for i in range(2)]
                sq = sbt(ph, "sq", [128, 4, 512])
                mean = sbt(ph, "mean", [128, 512])
                msq = sbt(ph, "msq", [128, 512])
                var = sbt(ph, "var", [128, 512])
                dd = [sbt(ph, f"dd{i}", [128, 512]) for i in range(2)]
                yo = [sbt(ph, f"yo{i}", [128, 512], BF16) for i in range(2)]
                ps1 = pst(ph, "ps1")
                ps2 = pst(ph, "ps2")
                for tb in range(8):
                    b = tb % 2
                    t0 = tb * 512
                    S.dma("sp", yb[b][:], yc_s[:, t0:t0 + 512].rearrange("(c p) t -> p c t", p=128), writes=[("yb", b)])
                    S.op("act", lambda e: e.activation(out=sq[:], in_=yb[b][:], func=AF.Square), reads=[("yb", b)], writes=["sq"])
                    for ct in range(4):
                        S.op("pe", lambda e: e.matmul(ps1[:], lhsT=ones_f[:], rhs=yb[b][:, ct, :], start=(ct == 0), stop=(ct == 3)),
                             reads=[("yb", b)], writes=["ps1"])
                    for ct in range(4):
                        S.op("pe", lambda e: e.matmul(ps2[:], lhsT=ones_f[:], rhs=sq[:, ct, :], start=(ct == 0), stop=(ct == 3)),
                             reads=["sq"], writes=["ps2"])
                    S.op("act", lambda e: e.mul(out=mean[:], in_=ps1[:], mul=1.0 / 512), reads=["ps1"], writes=["mean"])
                    S.op("pool", lambda e: e.tensor_mul(out=msq[:], in0=mean[:], in1=mean[:]), reads=["mean"], writes=["msq"])
                    S.op("dve", lambda e: e.scalar_tensor_tensor(out=var[:], in0=ps2[:], scalar=1.0 / 512, in1=msq[:],
                                                                op0=ALU.mult, op1=ALU.subtract), reads=["ps2", "msq"], writes=["var"])
                    S.op("act", lambda e: e.activation(out=var[:], in_=var[:], func=AF.Sqrt, bias=1e-5, scale=1.0),
                         reads=["var"], writes=["var"])
                    S.op("dve", lambda e: e.reciprocal(out=var[:], in_=var[:]), reads=["var"], writes=["var"])
                    for ct in range(4):
                        d2 = ct % 2
                        S.op("dve", lambda e: e.tensor_sub(out=dd[d2][:], in0=yb[b][:, ct, :], in1=mean[:]),
                             reads=[("yb", b), "mean"], writes=[("dd", d2)])
                        S.op("pool", lambda e: e.tensor_mul(out=dd[d2][:], in0=dd[d2][:], in1=var[:]),
                             reads=[("dd", d2), "var"], writes=[("dd", d2)])
                        aff("act", yo[d2][:], dd[d2][:], cols[:, LW + ct:LW + ct + 1], cols[:, LB + ct:LB + ct + 1],
                            reads=[("dd", d2)], writes=[("yo", d2)], func=AF.Silu)
                        S.dma("sp", ycat_s[ct * 128:(ct + 1) * 128, t0:t0 + 512], yo[d2][:], reads=[("yo", d2)],
                              writes=[("ycat", ct, tb)])
                S.barrier()
            if "ycat" in debug and stop_after == 3 and not done[0]:
                S.dma("sp", dbg_out("ycat", [1024, T], BF16), ycat_s, writes=["dbg_ycat"])
                S.barrier()
            if stop_after == 3 and not done[0]:
                fin()

            if not done[0]:
              with ExitStack() as ph:
                NB = 256
                NCH = NB // 64
                wst_ = sbt(ph, "rwst", [128, 8, 128]); wst = [wst_, wst_]
                wl_bf = sbt(ph, "wl_bf", [128, 8, 128], BF16)
                wr_bf = [sbt(ph, f"wr_bf{i}", [128, 8, 128], BF16) for i in range(3)]
                twal = sbt(ph, "twal", [128, T], BF16)
                sg = sbt(ph, "sg", [128, T], BF16)
                wa2b = sbt(ph, "wa2b", [128, 512], BF16)
                g2b = sbt(ph, "g2b", [128, 512], BF16)
                cmask = sbt(ph, "cmask", [128, NB])
                tmpx = [sbt(ph, f"tmpx{i}", [128, 256]) for i in range(2)]
                xsl = sbt(ph, "xsl", [128, 256])
                psA = [pst(ph, f"rpsA{i}") for i in range(2)]
                psM = [pst(ph, f"rpsM{i}") for i in range(2)]
                Wk = [pst(ph, f"rW{i}") for i in range(4)]

                S.op("pool", lambda e: e.memset(cmask[:], 1.0), writes=["cmask"])
                S.op("pool", lambda e: e.memset(cmask[:].rearrange("p (c k) -> p c k", k=64)[:, :, 0:1], 0.0),
                     reads=["cmask"], writes=["cmask"])

                halos = sbt(ph, "halos", [128, 4])
                npx = [0]

                def proj_xs1(wbf, wkey, fcol, t0, n, out_ap, okey, j3, first):
                    bi = npx[0] % 2
                    npx[0] += 1
                    pa = psA[bi]
                    pk_ = "rpsA%d" % bi
                    tx = tmpx[bi]
                    mu_c = cols[:, MU + fcol:MU + fcol + 1]
                    hk = ("halo", j3)
                    for k in range(8):
                        S.op("pe", lambda e: e.matmul(pa[:, 0:n], lhsT=wbf[:, k, :], rhs=uT[:, k, 1 + t0:1 + t0 + n],
                                                      start=(k == 0), stop=(k == 7)), reads=[wkey], writes=[pk_])
                    if first:
                        S.op("act", lambda e: e.memzero(halos[:, j3:j3 + 1]), writes=[hk])
                    S.op("act", lambda e: e.activation(out=tx[:, 0:1], in_=halos[:, j3:j3 + 1], func=AF.Copy, scale=mu_c),
                         reads=[hk], writes=[("tmpx", bi)])
                    S.op("act", lambda e: e.activation(out=tx[:, 1:n], in_=pa[:, 0:n - 1], func=AF.Copy, scale=mu_c),
                         reads=[pk_, ("tmpx", bi)], writes=[("tmpx", bi)])
                    S.op("act", lambda e: e.copy(out=halos[:, j3:j3 + 1], in_=pa[:, n - 1:n]), reads=[pk_, hk], writes=[hk])
                    S.op("dve", lambda e: e.scalar_tensor_tensor(out=out_ap, in0=pa[:, 0:n],
                                                                scalar=dcol[:, OMM + fcol:OMM + fcol + 1], in1=tx[:, 0:n],
                                                                op0=ALU.mult, op1=ALU.add),
                         reads=[pk_, ("tmpx", bi)], writes=[okey])

                def proj_xs(wbf, wkey, fcol, t0, n, out_ap, okey, slot):
                    pa, pb = psA[0], psA[1]
                    for k in range(8):
                        S.op("pe", lambda e: e.matmul(pa[:, 0:n], lhsT=wbf[:, k, :], rhs=uT[:, k, 1 + t0:1 + t0 + n],
                                                      start=(k == 0), stop=(k == 7)), reads=[wkey], writes=["rpsA0"])
                    for k in range(8):
                        S.op("pe", lambda e: e.matmul(pb[:, 0:n], lhsT=wbf[:, k, :], rhs=uT[:, k, t0:t0 + n],
                                                      start=(k == 0), stop=(k == 7)), reads=[wkey], writes=["rpsA1"])
                    tx = tmpx[slot % 2]
                    S.op("act", lambda e: e.activation(out=tx[:, 0:n], in_=pb[:, 0:n], func=AF.Copy,
                                                       scale=cols[:, MU + fcol:MU + fcol + 1]),
                         reads=["rpsA1"], writes=[("tmpx", slot % 2)])
                    S.op("dve", lambda e: e.scalar_tensor_tensor(out=out_ap, in0=pa[:, 0:n],
                                                                scalar=dcol[:, OMM + fcol:OMM + fcol + 1], in1=tx[:, 0:n],
                                                                op0=ALU.mult, op1=ALU.add),
                         reads=["rpsA0", ("tmpx", slot % 2)], writes=[okey])

                load_w(wst[0], wl_bf, 8 + 12, "l")
                for tb in range(16):
                    t0 = tb * 256
                    proj_xs(wl_bf, ("wbf", "l"), 12, t0, 256, xsl[:], "xsl", tb)
                    S.op("act", lambda e: e.activation(out=twal[0:64, t0:t0 + 256], in_=xsl[0:64, :], func=AF.Tanh),
                         reads=["xsl"], writes=[("twal", tb)])
                    S.op("pool", lambda e: e.tensor_copy(out=twal[64:128, t0:t0 + 256], in_=xsl[64:128, :]),
                         reads=["xsl"], writes=[("twal2", tb)])
                load_w(wst[0], wl_bf, 8 + 13, "l")
                for tb in range(16):
                    t0 = tb * 256
                    proj_xs(wl_bf, ("wbf", "l"), 13, t0, 256, xsl[:], "xsl", tb)
                    S.op("act", lambda e: e.activation(out=sg[:, t0:t0 + 256], in_=xsl[:], func=AF.Sigmoid),
                         reads=["xsl"], writes=[("sg", tb)])

                def wk(name, shape=(128, NB), dt=F32):
                    return sbt(ph, name, list(shape), dt)
                R2 = [wk("r_0"), wk("r_1")]; k0 = wk("k0"); V2 = [wk("v_0"), wk("v_1")]
                sgm = wk("sgm"); a_ = wk("a_"); GG2 = [wk("g_0"), wk("g_1")]
                cum = wk("cum"); cex = wk("cex"); dend = cex
                EIN2 = [wk("Ein0"), wk("Ein1")]; Eex = wk("Eex"); Einv = wk("Einv"); Eend = wk("Eend")
                kk = wk("kk"); rinv = wk("rinv"); kkn = wk("kkn")
                t1 = wk("t1"); kk2 = t1; K22 = [wk("k2_0"), wk("k2_1")]; bb = wk("bb"); rkr = wk("rkr")
                ysc = wk("ysc"); ysq = wk("ysq"); gmean = wk("gmean"); gvar = wk("gvar"); gtmp = wk("gtmp")
                yob = [sbt(ph, f"ryo{i}", [128, NB], BF16) for i in range(2)]
                PADS = [[wk(f"{nm}{pb}", (128, NCH, 128)) for nm in ("AP_", "BP_", "KP_", "RP_", "BC_", "KC_", "VP_")] for pb in range(2)]
                for pb_ in range(2):
                    for arr in PADS[pb_]:
                        S.op("pool", lambda e: e.memset(arr[:], 0.0), writes=[("padinit", arr.name)])
                tok = [wk(f"tok{c}", (128, 4, 128)) for c in range(NCH)]
                gram = [wk(f"gram{c}", (128, 4, 128)) for c in range(NCH)]
                NTm = [wk(f"NTm{c}", (128, 128)) for c in range(NCH)]
                Pm = [[wk(f"Pm{c}_{i}", (128, 2, 128)) for i in range(2)] for c in range(NCH)]
                Xm = [[wk(f"Xm{c}_{i}", (128, 128)) for i in range(2)] for c in range(NCH)]
                AW = [wk(f"AW{c}", (128, 2, 128)) for c in range(NCH)]
                AU = [wk(f"AU{c}", (128, 2, 128)) for c in range(NCH)]
                PTs = [wk(f"PTs{c}", (128, 128)) for c in range(NCH)]
                Qm = [wk(f"Qm{c}", (128, 128)) for c in range(NCH)]
                RH = [wk(f"RH{c}", (128, 128)) for c in range(NCH)]
                nwb = [0]
                wa2f = tok[0][:].rearrange("p a b -> p (a b)")
                g2f = gram[0][:].rearrange("p a b -> p (a b)")
                S.dma("sp", wa2f, wa2_d, writes=[("tok", 0)])
                S.dma("sp", g2f, g2_d, writes=[("gram", 0)])
                S.op("dve", lambda e: e.tensor_copy(out=wa2b[:], in_=wa2f), reads=[("tok", 0)], writes=["wa2b"])
                S.op("dve", lambda e: e.tensor_copy(out=g2b[:], in_=g2f), reads=[("gram", 0)], writes=["g2b"])
                STs = [wk(f"ST{i}", (128, 128)) for i in range(2)]

                for hp in range(4):
                    for j3, f in enumerate((8 + hp, 12 + hp, 16 + hp)):
                        load_w(wst[j3 % 2], wr_bf[j3], 8 + (f - 8), ("r", j3))
                    S.op("pool", lambda e: e.memset(STs[0][:], 0.0), writes=[("ST", 0)])
                    cs = slice(hp * 128, (hp + 1) * 128)
                    nch = [0]
                    def rw_prep(blk):
                        issue_casts(1)
                        pb = blk % 2
                        t0 = blk * NB
                        r_, k2, v_, g_, Ein = R2[pb], K22[pb], V2[pb], GG2[pb], EIN2[pb]
                        AP_, BP_, KP_, RP_, BC_, KC_, VP_ = PADS[pb]
                        padkeys = [(nm, pb, hh) for nm in ("AP_", "BP_", "KP_", "RP_", "BC_", "KC_", "VP_") for hh in range(2)]
                        pm, pm1 = psM[0], psM[1]
                        proj_xs1(wr_bf[0], ("wbf", ("r", 0)), hp, t0, NB, r_[:], ("r_", pb), 0, blk == 0)
                        proj_xs1(wr_bf[1], ("wbf", ("r", 1)), 4 + hp, t0, NB, k0[:], "k0", 1, blk == 0)
                        proj_xs1(wr_bf[2], ("wbf", ("r", 2)), 8 + hp, t0, NB, v_[:], ("v_", pb), 2, blk == 0)
                        S.op("pe", lambda e: e.matmul(pm[:, 0:NB], lhsT=wa2b[0:64, cs], rhs=twal[0:64, t0:t0 + NB], start=True, stop=True),
                             reads=["wa2b"], writes=["rpsM0"])
                        S.op("act", lambda e: e.activation(out=sgm[:], in_=pm[:, 0:NB], func=AF.Sigmoid, bias=cols[:, W0 + hp:W0 + hp + 1]),
                             reads=["rpsM0"], writes=["sgm"])
                        S.op("pe", lambda e: e.matmul(pm1[:, 0:NB], lhsT=wa2b[64:128, cs], rhs=twal[64:128, t0:t0 + NB], start=True, stop=True),
                             reads=["wa2b"], writes=["rpsM1"])
                        S.op("act", lambda e: e.activation(out=a_[:], in_=pm1[:, 0:NB], func=AF.Sigmoid, bias=cols[:, A0 + hp:A0 + hp + 1]),
                             reads=["rpsM1"], writes=["a_"])
                        S.op("pe", lambda e: e.matmul(pm[:, 0:NB], lhsT=g2b[:, cs], rhs=sg[:, t0:t0 + NB], start=True, stop=True),
                             reads=["g2b"], writes=["rpsM0"])
                        S.op("act", lambda e: e.copy(out=g_[:], in_=pm[:, 0:NB]), reads=["rpsM0"], writes=[("g_", pb)])
                        S.op("dve", lambda e: e.tensor_tensor_scan(out=cum[:], data0=cmask[:], data1=sgm[:], initial=0.0,
                                                                  op0=ALU.mult, op1=ALU.add), reads=["sgm", "cmask"], writes=["cum"])
                        S.op("pool", lambda e: e.tensor_sub(out=cex[:], in0=cum[:], in1=sgm[:]), reads=["cum", "sgm"], writes=["cex"])
                        S.op("act", lambda e: e.activation(out=Eex[:], in_=cex[:], func=AF.Exp, scale=-C0), reads=["cex"], writes=["Eex"])
                        S.op("act", lambda e: e.activation(out=Ein[:], in_=cum[:], func=AF.Exp, scale=-C0), reads=["cum"], writes=[("Ein", pb)])
                        S.op("act", lambda e: e.activation(out=Einv[:], in_=cum[:], func=AF.Exp, scale=C0), reads=["cum"], writes=["Einv"])
                        cum3 = cum[:].rearrange("p (c k) -> p c k", k=64)
                        S.op("dve", lambda e: e.tensor_sub(out=dend[:].rearrange("p (c k) -> p c k", k=64),
                                                          in0=cum3[:, :, 63:64].to_broadcast([128, NCH, 64]), in1=cum3),
                             reads=["cum"], writes=["cex"])
                        S.op("act", lambda e: e.activation(out=Eend[:], in_=dend[:], func=AF.Exp, scale=-C0), reads=["cex"], writes=["Eend"])
                        S.op("act", lambda e: e.activation(out=kk[:], in_=k0[:], func=AF.Copy, scale=cols[:, KK + hp:KK + hp + 1]),
                             reads=["k0"], writes=["kk"])
                        S.op("pool", lambda e: e.tensor_mul(out=kk2[:], in0=kk[:], in1=kk[:]), reads=["kk"], writes=["t1"])
                        S.op("pe", lambda e: e.matmul(pm1[:, 0:NB], lhsT=blk1[:], rhs=kk2[:], start=True, stop=True),
                             reads=["t1"], writes=["rpsM1"])
                        S.op("act", lambda e: e.activation(out=rinv[:], in_=pm1[:, 0:NB], func=AF.Sqrt), reads=["rpsM1"], writes=["rinv"])
                        S.op("dve", lambda e: e.tensor_scalar_max(out=rinv[:], in0=rinv[:], scalar1=1e-12), reads=["rinv"], writes=["rinv"])
                        S.op("dve", lambda e: e.reciprocal(out=rinv[:], in_=rinv[:]), reads=["rinv"], writes=["rinv"])
                        S.op("dve", lambda e: e.tensor_mul(out=kkn[:], in0=kk[:], in1=rinv[:]), reads=["kk", "rinv"], writes=["kkn"])
                        S.op("pool", lambda e: e.tensor_scalar(out=t1[:], in0=a_[:], scalar1=cols[:, KA + hp:KA + hp + 1],
                                                              scalar2=dcol[:, OMKA + hp:OMKA + hp + 1], op0=ALU.mult, op1=ALU.add),
                             reads=["a_"], writes=["t1"])
                        S.op("pool", lambda e: e.tensor_mul(out=k2[:], in0=k0[:], in1=t1[:]), reads=["k0", "t1"], writes=[("k2", pb)])
                        S.op("dve", lambda e: e.tensor_mul(out=bb[:], in0=kkn[:], in1=a_[:]), reads=["kkn", "a_"], writes=["bb"])
                        for hh in range(2):
                            ps_ = slice(hh * 64, hh * 64 + 64)

                            def v3(tile_):
                                return tile_[ps_, :].rearrange("p (c k) -> p c k", k=64)

                            def o3(arr):
                                return arr[ps_, :, hh * 64:hh * 64 + 64]
                            eA = "dve" if hh == 0 else "pool"
                            eB = "pool" if hh == 0 else "dve"
                            S.op("dve", lambda e: e.scalar_tensor_tensor(out=o3(AP_), in0=v3(kkn), scalar=-1.0, in1=v3(Eex),
                                                                      op0=ALU.mult, op1=ALU.mult), reads=["kkn", "Eex"], writes=[("AP_", pb, hh)])
                            S.op(eB, lambda e: e.tensor_mul(out=o3(BP_), in0=v3(bb), in1=v3(Einv)), reads=["bb", "Einv"], writes=[("BP_", pb, hh)])
                            S.op(eA, lambda e: e.tensor_mul(out=o3(KP_), in0=v3(k2), in1=v3(Einv)), reads=[("k2", pb), "Einv"], writes=[("KP_", pb, hh)])
                            S.op(eB, lambda e: e.tensor_mul(out=o3(RP_), in0=v3(r_), in1=v3(Ein)), reads=[("r_", pb), ("Ein", pb)], writes=[("RP_", pb, hh)])
                            S.op(eA, lambda e: e.tensor_mul(out=o3(BC_), in0=v3(bb), in1=v3(Eend)), reads=["bb", "Eend"], writes=[("BC_", pb, hh)])
                            S.op(eB, lambda e: e.tensor_mul(out=o3(KC_), in0=v3(k2), in1=v3(Eend)), reads=[("k2", pb), "Eend"], writes=[("KC_", pb, hh)])
                            S.op(eA, lambda e: e.tensor_copy(out=o3(VP_), in_=v3(v_)), reads=[("v_", pb)], writes=[("VP_", pb, hh)])
                        padkeys = [(nm, pb, hh) for nm in ("AP_", "BP_", "KP_", "RP_", "BC_", "KC_", "VP_") for hh in range(2)]

                    def rw_chunk(blk):
                        pb = blk % 2
                        t0 = blk * NB
                        r_, k2, v_, g_, Ein = R2[pb], K22[pb], V2[pb], GG2[pb], EIN2[pb]
                        AP_, BP_, KP_, RP_, BC_, KC_, VP_ = PADS[pb]
                        padkeys = [(nm, pb, hh) for nm in ("AP_", "BP_", "KP_", "RP_", "BC_", "KC_", "VP_") for hh in range(2)]
                        pm, pm1 = psM[0], psM[1]
                        NI = NCH
                        for CH in [range(c0, c0 + NI) for c0 in range(0, NCH, NI)]:
                          if True:

                            def wbank():
                                i_ = nwb[0] % 4
                                nwb[0] += 1
                                return Wk[i_], ("rW", i_)
                            for c in CH:
                                wb_, wk_ = wbank()
                                for q_, arr in enumerate((AP_, BC_, KC_, VP_)):
                                    S.op("pe", lambda e: e.transpose(wb_[:, q_ * 128:(q_ + 1) * 128], arr[:, c, :], ident[:]),
                                         reads=padkeys, writes=[wk_])
                                S.op("act", lambda e: e.copy(out=tok[c][:].rearrange("p a b -> p (a b)"), in_=wb_[:]), reads=[wk_], writes=[("tok", c)])
                            for c in CH:
                                wb_, wk_ = wbank()
                                for q_, (l_, r2) in enumerate(((BP_, AP_), (KP_, AP_), (BP_, RP_), (KP_, RP_))):
                                    S.op("pe", lambda e: e.matmul(wb_[:, q_ * 128:(q_ + 1) * 128], lhsT=l_[:, c, :], rhs=r2[:, c, :], start=True, stop=True),
                                         reads=padkeys, writes=[wk_])
                                S.op("dve", lambda e: e.tensor_tensor(out=gram[c][:].rearrange("p a b -> p (a b)"), in0=wb_[:],
                                                                     in1=mask4[:].rearrange("p a b -> p (a b)"), op=ALU.mult),
                                     reads=[wk_], writes=[("gram", c)])
                            for c in CH:
                                wb_, wk_ = wbank()
                                S.op("pe", lambda e: e.matmul(wb_[:, 0:128], lhsT=AP_[:, c, :], rhs=BP_[:, c, :], start=True, stop=True),
                                     reads=padkeys, writes=[wk_])
                                S.op("dve", lambda e: e.tensor_tensor(out=NTm[c][:], in0=wb_[:, 0:128], in1=maskL[:], op=ALU.mult),
                                     reads=[wk_], writes=[("NTm", c)])
                                S.op("pool", lambda e: e.tensor_add(out=Xm[c][0][:], in0=gram[c][:, 0, :], in1=ident[:]),
                                     reads=[("gram", c)], writes=[("Xm", c, 0)])
                            Pc = {c: gram[c][:, 0, :] for c in CH}
                            PTc = {c: NTm[c][:] for c in CH}
                            pk = {c: [("gram", c), ("NTm", c)] for c in CH}
                            xi = 0
                            for step in range(5):
                                for c in CH:
                                    wb_, wk_ = wbank()
                                    pp = Pm[c][step % 2]
                                    S.op("pe", lambda e: e.matmul(wb_[:, 0:128], lhsT=PTc[c], rhs=Pc[c], start=True, stop=True),
                                         reads=pk[c], writes=[wk_])
                                    S.op("pe", lambda e: e.matmul(wb_[:, 128:256], lhsT=Pc[c], rhs=PTc[c], start=True, stop=True),
                                         reads=pk[c], writes=[wk_])
                                    S.op("act", lambda e: e.copy(out=pp[:].rearrange("p a b -> p (a b)"), in_=wb_[:, 0:256]),
                                         reads=[wk_], writes=[("Pm", c, step % 2)])
                                    Pc[c], PTc[c] = pp[:, 0, :], pp[:, 1, :]
                                    pk[c] = [("Pm", c, step % 2)]
                                for c in CH:
                                    wb_, wk_ = wbank()
                                    S.op("pe", lambda e: e.matmul(wb_[:, 0:128], lhsT=PTc[c], rhs=Xm[c][xi][:], start=True, stop=True),
                                         reads=pk[c] + [("Xm", c, xi)], writes=[wk_])
                                    S.op("dve", lambda e: e.tensor_add(out=Xm[c][1 - xi][:], in0=wb_[:, 0:128], in1=Xm[c][xi][:]),
                                         reads=[wk_, ("Xm", c, xi)], writes=[("Xm", c, 1 - xi)])
                                xi = 1 - xi
                            for c in CH:
                                wb_, wk_ = wbank()
                                S.op("pe", lambda e: e.matmul(wb_[:, 0:128], lhsT=gram[c][:, 1, :], rhs=tok[c][:, 3, :], start=True, stop=True),
                                     reads=[("gram", c), ("tok", c)], writes=[wk_])
                                S.op("act", lambda e: e.copy(out=AW[c][:, 1, :], in_=wb_[:, 0:128]), reads=[wk_], writes=[("AW1", c)])
                                S.op("pool", lambda e: e.tensor_copy(out=AW[c][:, 0, :], in_=tok[c][:, 0, :]), reads=[("tok", c)], writes=[("AW0", c)])
                            for c in CH:
                                wb_, wk_ = wbank()
                                S.op("pe", lambda e: e.matmul(wb_[:, 0:256], lhsT=Xm[c][xi][:], rhs=AW[c][:].rearrange("p a b -> p (a b)"), start=True, stop=True),
                                     reads=[("Xm", c, xi), ("AW0", c), ("AW1", c)], writes=[wk_])
                                S.op("act", lambda e: e.copy(out=AU[c][:].rearrange("p a b -> p (a b)"), in_=wb_[:, 0:256]),
                                     reads=[wk_], writes=[("AU", c)])
                            for c in CH:
                                wb_, wk_ = wbank()
                                Ah, Uh = AU[c][:, 0, :], AU[c][:, 1, :]
                                BCt, KCt, Vt = tok[c][:, 1, :], tok[c][:, 2, :], tok[c][:, 3, :]
                                gcol = Ein[:, c * 64 + 63:c * 64 + 64]
                                S.op("pe", lambda e: e.matmul(wb_[:, 0:128], lhsT=Ah, rhs=BCt, start=True, stop=True),
                                     reads=[("AU", c), ("tok", c)], writes=[wk_])
                                S.op("pe", lambda e: e.matmul(wb_[:, 128:256], lhsT=BCt, rhs=Uh, start=True, stop=False),
                                     reads=[("AU", c), ("tok", c)], writes=[wk_])
                                S.op("pe", lambda e: e.matmul(wb_[:, 128:256], lhsT=KCt, rhs=Vt, start=False, stop=True),
                                     reads=[("tok", c)], writes=[wk_])
                                S.op("pe", lambda e: e.matmul(wb_[:, 256:384], lhsT=Ah, rhs=gram[c][:, 2, :], start=True, stop=True),
                                     reads=[("AU", c), ("gram", c)], writes=[wk_])
                                S.op("dve", lambda e: e.scalar_tensor_tensor(out=PTs[c][:], in0=ident[:], scalar=gcol, in1=wb_[:, 0:128],
                                                                            op0=ALU.mult, op1=ALU.add), reads=[wk_, ("Ein", pb)], writes=[("PTs", c)])
                                S.op("act", lambda e: e.copy(out=Qm[c][:], in_=wb_[:, 128:256]), reads=[wk_], writes=[("Qm", c)])
                                S.op("dve", lambda e: e.tensor_add(out=RH[c][:], in0=wb_[:, 256:384], in1=RP_[:, c, :]),
                                     reads=[wk_] + padkeys, writes=[("RH", c)])
                            for c in CH:
                                cur = STs[nch[0] % 2]
                                nxt = STs[(nch[0] + 1) % 2]
                                kcur, knxt = ("ST", nch[0] % 2), ("ST", (nch[0] + 1) % 2)
                                wb2_, wk2_ = wbank()
                                S.op("pe", lambda e: e.matmul(wb2_[:, 0:128], lhsT=PTs[c][:], rhs=cur[:], start=True, stop=True),
                                     reads=[("PTs", c), kcur], writes=[wk2_])
                                S.op("dve", lambda e: e.tensor_add(out=nxt[:], in0=wb2_[:, 0:128], in1=Qm[c][:]),
                                     reads=[wk2_, ("Qm", c)], writes=[knxt])
                                wb_, wk_ = wbank()
                                S.op("pe", lambda e: e.matmul(wb_[:, 0:128], lhsT=cur[:], rhs=RH[c][:], start=True, stop=False),
                                     reads=[kcur, ("RH", c)], writes=[wk_])
                                S.op("pe", lambda e: e.matmul(wb_[:, 0:128], lhsT=AU[c][:, 1, :], rhs=gram[c][:, 2, :], start=False, stop=False),
                                     reads=[("AU", c), ("gram", c)], writes=[wk_])
                                S.op("pe", lambda e: e.matmul(wb_[:, 0:128], lhsT=tok[c][:, 3, :], rhs=gram[c][:, 3, :], start=False, stop=True),
                                     reads=[("tok", c), ("gram", c)], writes=[wk_])
                                S.op("act", lambda e: e.copy(out=ysc[0:64, c * 64:c * 64 + 64], in_=wb_[0:64, 0:64]),
                                     reads=[wk_], writes=[("ysc", 0, c)])
                                S.op("act", lambda e: e.copy(out=ysc[64:128, c * 64:c * 64 + 64], in_=wb_[64:128, 64:128]),
                                     reads=[wk_], writes=[("ysc", 1, c)])
                                nch[0] += 1

                    def rw_gn(blk):
                        pb = blk % 2
                        t0 = blk * NB
                        r_, k2, v_, g_, Ein = R2[pb], K22[pb], V2[pb], GG2[pb], EIN2[pb]
                        AP_, BP_, KP_, RP_, BC_, KC_, VP_ = PADS[pb]
                        padkeys = [(nm, pb, hh) for nm in ("AP_", "BP_", "KP_", "RP_", "BC_", "KC_", "VP_") for hh in range(2)]
                        pm, pm1 = psM[0], psM[1]
                        ykeys = [("ysc", hh, c) for hh in range(2) for c in range(NCH)]
                        if "yscan" in debug:
                            S.dma("sp", dbg_out("yscan", [512, T])[hp * 128:(hp + 1) * 128, t0:t0 + NB] if "yscan" not in dbg else
                                  dbg["yscan"][hp * 128:(hp + 1) * 128, t0:t0 + NB], ysc[:], reads=ykeys, writes=[("dbg_yscan", hp, blk)])
                        S.op("act", lambda e: e.activation(out=ysq[:], in_=ysc[:], func=AF.Square), reads=ykeys, writes=["ysq"])
                        S.op("pe", lambda e: e.matmul(pm[:, 0:NB], lhsT=blk1[:], rhs=ysc[:], start=True, stop=True), reads=ykeys, writes=["rpsM0"])
                        S.op("pe", lambda e: e.matmul(pm1[:, 0:NB], lhsT=blk1[:], rhs=ysq[:], start=True, stop=True), reads=["ysq"], writes=["rpsM1"])
                        S.op("act", lambda e: e.mul(out=gmean[:], in_=pm[:, 0:NB], mul=1.0 / 64), reads=["rpsM0"], writes=["gmean"])
                        S.op("pool", lambda e: e.tensor_mul(out=gtmp[:], in0=gmean[:], in1=gmean[:]), reads=["gmean"], writes=["gtmp"])
                        S.op("dve", lambda e: e.scalar_tensor_tensor(out=gvar[:], in0=pm1[:, 0:NB], scalar=1.0 / 64, in1=gtmp[:],
                                                                    op0=ALU.mult, op1=ALU.subtract), reads=["rpsM1", "gtmp"], writes=["gvar"])
                        S.op("act", lambda e: e.activation(out=gvar[:], in_=gvar[:], func=AF.Sqrt, bias=64e-5, scale=1.0), reads=["gvar"], writes=["gvar"])
                        S.op("dve", lambda e: e.reciprocal(out=gvar[:], in_=gvar[:]), reads=["gvar"], writes=["gvar"])
                        S.op("dve", lambda e: e.tensor_sub(out=gtmp[:], in0=ysc[:], in1=gmean[:]), reads=ykeys + ["gmean", "gtmp"], writes=["gtmp"])
                        S.op("pool", lambda e: e.tensor_mul(out=gtmp[:], in0=gtmp[:], in1=gvar[:]), reads=["gtmp", "gvar"], writes=["gtmp"])
                        S.op("pool", lambda e: e.tensor_scalar(out=gtmp[:], in0=gtmp[:], scalar1=cols[:, GW + hp:GW + hp + 1],
                                                              scalar2=cols[:, GB + hp:GB + hp + 1], op0=ALU.mult, op1=ALU.add),
                             reads=["gtmp"], writes=["gtmp"])
                        S.op("dve", lambda e: e.scalar_tensor_tensor(out=rkr[:], in0=r_[:], scalar=cols[:, RK + hp:RK + hp + 1], in1=k2[:],
                                                                    op0=ALU.mult, op1=ALU.mult), reads=[("r_", pb), ("k2", pb)], writes=["rkr"])
                        S.op("pe", lambda e: e.matmul(pm[:, 0:NB], lhsT=blk1[:], rhs=rkr[:], start=True, stop=True), reads=["rkr"], writes=["rpsM0"])
                        S.op("dve", lambda e: e.tensor_tensor(out=rkr[:], in0=pm[:, 0:NB], in1=v_[:], op=ALU.mult),
                             reads=["rpsM0", ("v_", pb), "rkr"], writes=["rkr"])
                        S.op("pool", lambda e: e.tensor_add(out=gtmp[:], in0=gtmp[:], in1=rkr[:]), reads=["gtmp", "rkr"], writes=["gtmp"])
                        yo_ = yob[blk % 2]
                        S.op("dve", lambda e: e.tensor_mul(out=yo_[:], in0=gtmp[:], in1=g_[:]), reads=["gtmp", ("g_", pb)], writes=[("ryo", blk % 2)])
                        S.dma("sp", ycat_s[(4 + hp) * 128:(5 + hp) * 128, t0:t0 + NB], yo_[:], reads=[("ryo", blk % 2)],
                              writes=[("ycat", 4 + hp, blk)])
                    nblk_ = T // NB
                    rw_prep(0)
                    for blk in range(nblk_):
                        if blk + 1 < nblk_:
                            rw_prep(blk + 1)
                        rw_chunk(blk)
                        rw_gn(blk)
                S.barrier()
            if "ycat" in debug and stop_after == 4 and not done[0]:
                S.dma("sp", dbg_out("ycat", [1024, T], BF16), ycat_s, writes=["dbg_ycat"])
                S.barrier()
            if stop_after == 4 and not done[0]:
                fin()

        if not done[0]:
          issue_casts(len(cast_jobs))
          with ExitStack() as ph:
            wost = [sbt(ph, f"wost{i}", [128, D]) for i in range(2)]
            wob = sbt(ph, "wob", [128, 8, D], BF16)
            ycb = [sbt(ph, f"ycatb{i}", [128, 8, 128], BF16) for i in range(2)]
            xb = [sbt(ph, f"x3b{i}", [128, D]) for i in range(2)]
            x1b = [sbt(ph, f"x1b{i}", [128, D]) for i in range(2)]
            junk = sbt(ph, "junk3", [128, D])
            xn = [sbt(ph, f"xn3{i}", [128, D]) for i in range(2)]
            u2b = [sbt(ph, f"u2b{i}", [128, 8, 128], BF16) for i in range(2)]
            ss = sbt(ph, "ss3", [128, 32])
            rs = sbt(ph, "rs3", [128, 32])
            pO = [pst(ph, f"pO{i}", [128, 1024]) for i in range(2)]
            pT = [pst(ph, f"pT3{i}", [128, 1024]) for i in range(2)]
            gtm_bc = sbt(ph, "gtm_bc", [128, D])
            build_gate_bc(ph, gtm_bc, 16, pO[0], ("pO", 0, 0))
            for ct in range(8):
                S.dma("sp", wost[ct % 2][:], wout_d[ct * 128:(ct + 1) * 128, :], writes=[("wost", ct % 2)])
                S.op("pool", lambda e: e.tensor_copy(out=wob[:, ct, :], in_=wost[ct % 2][:]), reads=[("wost", ct % 2)], writes=["wob"])
            u2T3 = u2T_s.rearrange("p (k t) -> p k t", k=8)
            def p3_mm(tt):
                b = tt % 2
                ts_ = slice(tt * 128, (tt + 1) * 128)
                S.dma("sp", ycb[b][:], ycat_s[:, ts_].rearrange("(c p) t -> p c t", p=128), writes=[("ycatb", b)])
                S.dma("sp", xb[b][:], x_d[ts_, :], writes=[("x3b", b)])
                for dh in range(2):
                    for ct in range(8):
                        S.op("pe", lambda e: e.matmul(pO[b][:, dh * 512:(dh + 1) * 512], lhsT=ycb[b][:, ct, :],
                                                      rhs=wob[:, ct, dh * 512:(dh + 1) * 512], start=(ct == 0), stop=(ct == 7)),
                             reads=[("ycatb", b), "wob"], writes=[("pO", b, dh)])

            p3_mm(0)
            for tt in range(32):
                b = tt % 2
                ts_ = slice(tt * 128, (tt + 1) * 128)
                if tt + 1 < 32:
                    p3_mm(tt + 1)
                S.op("dve", lambda e: e.tensor_tensor(out=x1b[b][:], in0=pO[b][:], in1=gtm_bc[:], op=ALU.mult),
                     reads=[("pO", b, 0), ("pO", b, 1), "gtm_bc"], writes=[("x1b", b)])
                S.op("pool", lambda e: e.tensor_add(out=x1b[b][:], in0=x1b[b][:], in1=xb[b][:]), reads=[("x1b", b), ("x3b", b)], writes=[("x1b", b)])
                S.dma("sp", x1_s[ts_, :], x1b[b][:], reads=[("x1b", b)], writes=[("x1_s", tt)])
                S.op("act", lambda e: e.activation(out=junk[:], in_=x1b[b][:], func=AF.Square), reads=[("x1b", b)], writes=["junk3"])
                S.op("dve", lambda e: e.reduce_sum(out=ss[:, tt:tt + 1], in_=junk[:], axis=AX.X), reads=["junk3"], writes=[("ss3", tt)])
                S.op("act", lambda e: e.activation(out=rs[:, tt:tt + 1], in_=ss[:, tt:tt + 1], func=AF.Sqrt, bias=1e-6, scale=1.0 / D),
                     reads=[("ss3", tt)], writes=[("rs3", tt)])
                S.op("dve", lambda e: e.reciprocal(out=rs[:, tt:tt + 1], in_=rs[:, tt:tt + 1]), reads=[("rs3", tt)], writes=[("rs3", tt)])
                S.op("act", lambda e: e.activation(out=xn[b][:], in_=x1b[b][:], func=AF.Copy, scale=rs[:, tt:tt + 1]),
                     reads=[("x1b", b), ("rs3", tt)], writes=[("xn3", b)])
                for k in range(8):
                    S.op("pe", lambda e: e.transpose(pT[b][:, k * 128:(k + 1) * 128], xn[b][:, k * 128:(k + 1) * 128], ident[:]),
                         reads=[("xn3", b)], writes=[("pT3", b, k // 4)])
                for k in range(8):
                    aff("act" if k < 4 else "dve", u2b[b][:, k, :], pT[b][:, k * 128:(k + 1) * 128],
                        dcol[:, GSC2 + k:GSC2 + k + 1], modT[:, 24 + k:24 + k + 1], reads=[("pT3", b, k // 4)], writes=[("u2b", b)])
                S.dma("sp", u2T3[:, :, ts_], u2b[b][:], reads=[("u2b", b)], writes=[("u2T_s", tt)])
            S.barrier()
            if "x1" in debug:
                S.dma("sp", dbg_out("x1", [T, D]), x1_s, writes=["dbg_x1"])
                S.dma("sp", dbg_out("u2T", [128, 8 * T], BF16, force=True), u2T_s, writes=["dbg_u2T"])
                S.barrier()
        if stop_after == 5 and not done[0]:
            fin()

        if not done[0]:
          with ExitStack() as ph:
            Wc = sbt(ph, "Wc", [128, 8, 2048], BF16)
            KTs = sbt(ph, "KTs", [128, 16, 128])
            wqs = [sbt(ph, f"wqs{i}", [128, D]) for i in range(2)]
            u2t = [sbt(ph, f"u2t{i}", [128, 8, 128], BF16) for i in range(2)]
            sc2 = [sbt(ph, f"sc_{i}", [128, 16, 128]) for i in range(2)]
            wrk4 = [sbt(ph, f"wrk{i}", [128, 256]) for i in range(4)]
            vals = sbt(ph, "vals", [128, 16, 16])
            idxu = sbt(ph, "idxu", [128, 16, 16], U16)
            idxf = sbt(ph, "idxf", [128, 16, 16])
            cand = sbt(ph, "cand", [128, 8, 256])
            best = sbt(ph, "best", [128, 8, 16])
            posu = sbt(ph, "posu", [128, 8, 16], U16)
            posf = sbt(ph, "posf", [128, 8, 16])
            big = sbt(ph, "big", [128, 8, 16, 16])
            thr16 = sbt(ph, "thr16", [128, 16])
            io16 = sbt(ph, "io16", [128, 16])
            ak = sbt(ph, "ak", [128, 8, 16])
            bk = sbt(ph, "bk", [128, 8, 16])
            ik = sbt(ph, "ik", [128, 128])
            jk = sbt(ph, "jk", [128, 128])
            gk = sbt(ph, "gk", [128, 128])
            zs = sbt(ph, "zs", [128, 8])
            slT = [sbt(ph, f"slT{i}", [128, 3, 128]) for i in range(2)]
            pW = [pst(ph, f"pW{i}", [128, 512]) for i in range(2)]
            pS = [pst(ph, f"pS{i}", [128, 512]) for i in range(4)]
            pTr = pst(ph, "pTr", [128, 512])

            S.dma("sp", KTs[:].rearrange("p a b -> p (a b)"), KT_d, writes=["KTs"])
            S.op("dve", lambda e: e.tensor_scalar_mul(out=thr16[:], in0=iota_f[:, 0:16], scalar1=16.0), writes=["thr16"])
            S.op("dve", lambda e: e.tensor_copy(out=io16[:], in_=iota_f[:, 0:16]), writes=["io16"])
            for hp in range(16):
                S.dma("sp", wqs[hp % 2][:], wqT_d[hp * 128:(hp + 1) * 128, :], writes=[("wqs", hp % 2)])
                for dk in range(8):
                    S.op("pe", lambda e: e.matmul(pW[dk % 2][:, 0:128], lhsT=wqs[hp % 2][:, dk * 128:(dk + 1) * 128], rhs=KTs[:, hp, :],
                                                  start=True, stop=True), reads=[("wqs", hp % 2), "KTs"], writes=[("pW", dk % 2)])
                    S.op("act" if dk % 2 == 0 else "dve",
                         (lambda e: e.copy(out=Wc[:, dk, hp * 128:(hp + 1) * 128], in_=pW[dk % 2][:, 0:128])) if dk % 2 == 0 else
                         (lambda e: e.tensor_copy(out=Wc[:, dk, hp * 128:(hp + 1) * 128], in_=pW[dk % 2][:, 0:128])),
                         reads=[("pW", dk % 2)], writes=["Wc"])
            u2T3 = u2T_s.rearrange("p (k t) -> p k t", k=8)
            def p4_scores(tt):
                b = tt % 2
                ts_ = slice(tt * 128, (tt + 1) * 128)
                S.dma("sp", u2t[b][:], u2T3[:, :, ts_], writes=[("u2t", b)])
                sc_ = sc2[b]
                for q4 in range(4):
                    for dk in range(8):
                        S.op("pe", lambda e: e.matmul(pS[q4][:], lhsT=u2t[b][:, dk, :], rhs=Wc[:, dk, q4 * 512:(q4 + 1) * 512],
                                                      start=(dk == 0), stop=(dk == 7)), reads=[("u2t", b), "Wc"], writes=[("pS", q4)])
                    S.op("act", lambda e: e.copy(out=sc_[:, q4 * 4:(q4 + 1) * 4, :].rearrange("p a b -> p (a b)"), in_=pS[q4][:]),
                         reads=[("pS", q4)], writes=[("sc_", b, q4)])

            p4_scores(0)
            for tt in range(32):
                b = tt % 2
                ts_ = slice(tt * 128, (tt + 1) * 128)
                sc_ = sc2[b]
                if tt + 1 < 32:
                    p4_scores(tt + 1)
                if "scores" in debug and tt == 0:
                    S.dma("sp", dbg_out("scores", [128, 2048]), sc_[:].rearrange("p a b -> p (a b)"),
                          reads=[("sc_", b, q) for q in range(4)], writes=["dbg_scores"])
                NIT = 4
                ALLV = [("vals", g, hf) for g in range(16) for hf in range(2)]
                ALLI = [("idxu", g, hf) for g in range(16) for hf in range(2)]
                ALLB = [("best", h, hf) for h in range(8) for hf in range(2)]
                ALLP = [("posu", h, hf) for h in range(8) for hf in range(2)]
                for g0 in range(0, 16, NIT):
                    grp = range(g0, g0 + NIT)
                    for g16 in grp:
                        S.op("dve", lambda e: e.max(out=vals[:, g16, 0:8], in_=sc_[:, g16, :]), reads=[("sc_", b, g16 // 4)], writes=[("vals", g16, 0)])
                    for g16 in grp:
                        S.op("dve", lambda e: e.max_index(out=idxu[:, g16, 0:8], in_max=vals[:, g16, 0:8], in_values=sc_[:, g16, :]),
                             reads=[("sc_", b, g16 // 4), ("vals", g16, 0)], writes=[("idxu", g16, 0)])
                    for g16 in grp:
                        w_ = wrk4[g16 % NIT]
                        S.op("dve", lambda e: e.match_replace(out=w_[:, 0:128], in_to_replace=vals[:, g16, 0:8], in_values=sc_[:, g16, :],
                                                             imm_value=-1e30), reads=[("sc_", b, g16 // 4), ("vals", g16, 0)], writes=[("wrk", g16 % NIT)])
                    for g16 in grp:
                        w_ = wrk4[g16 % NIT]
                        S.op("dve", lambda e: e.max(out=vals[:, g16, 8:16], in_=w_[:, 0:128]), reads=[("wrk", g16 % NIT)], writes=[("vals", g16, 1)])
                    for g16 in grp:
                        w_ = wrk4[g16 % NIT]
                        S.op("dve", lambda e: e.max_index(out=idxu[:, g16, 8:16], in_max=vals[:, g16, 8:16], in_values=w_[:, 0:128]),
                             reads=[("wrk", g16 % NIT), ("vals", g16, 1)], writes=[("idxu", g16, 1)])
                S.op("pool", lambda e: e.tensor_copy(out=idxf[:], in_=idxu[:]), reads=ALLI, writes=["idxf"])
                v4 = vals[:].rearrange("p (h two) k -> p h two k", two=2)
                i4 = idxf[:].rearrange("p (h two) k -> p h two k", two=2)
                cand4 = cand[:].rearrange("p h (a b) -> p h a b", b=16)
                S.op("dve", lambda e: e.tensor_tensor(out=cand4, in0=v4[:, :, 0, :].unsqueeze(3).to_broadcast([128, 8, 16, 16]),
                                                     in1=v4[:, :, 1, :].unsqueeze(2).to_broadcast([128, 8, 16, 16]), op=ALU.add),
                     reads=ALLV, writes=["cand"])
                for h0 in range(0, 8, NIT):
                    grp = range(h0, h0 + NIT)
                    for h in grp:
                        S.op("dve", lambda e: e.max(out=best[:, h, 0:8], in_=cand[:, h, :]), reads=["cand"], writes=[("best", h, 0)])
                    for h in grp:
                        S.op("dve", lambda e: e.max_index(out=posu[:, h, 0:8], in_max=best[:, h, 0:8], in_values=cand[:, h, :]),
                             reads=["cand", ("best", h, 0)], writes=[("posu", h, 0)])
                    for h in grp:
                        w_ = wrk4[h % NIT]
                        S.op("dve", lambda e: e.match_replace(out=w_[:], in_to_replace=best[:, h, 0:8], in_values=cand[:, h, :],
                                                             imm_value=-1e30), reads=["cand", ("best", h, 0)], writes=[("wrk", h % NIT)])
                    for h in grp:
                        w_ = wrk4[h % NIT]
                        S.op("dve", lambda e: e.max(out=best[:, h, 8:16], in_=w_[:]), reads=[("wrk", h % NIT)], writes=[("best", h, 1)])
                    for h in grp:
                        w_ = wrk4[h % NIT]
                        S.op("dve", lambda e: e.max_index(out=posu[:, h, 8:16], in_max=best[:, h, 8:16], in_values=w_[:]),
                             reads=[("wrk", h % NIT), ("best", h, 1)], writes=[("posu", h, 1)])
                S.op("dve", lambda e: e.tensor_copy(out=posf[:], in_=posu[:]), reads=ALLP, writes=["posf"])
                gk3 = gk[:].rearrange("p (h k) -> p h k", k=16)
                S.op("pool", lambda e: e.tensor_sub(out=gk3, in0=best[:], in1=best[:, :, 0:1].to_broadcast([128, 8, 16])),
                     reads=ALLB, writes=["gk"])
                S.op("act", lambda e: e.activation(out=gk[:], in_=gk[:], func=AF.Exp), reads=["gk"], writes=["gk"])
                S.op("dve", lambda e: e.tensor_reduce(out=zs[:], in_=gk3, axis=AX.X, op=ALU.add), reads=["gk"], writes=["zs"])
                S.op("dve", lambda e: e.reciprocal(out=zs[:], in_=zs[:]), reads=["zs"], writes=["zs"])
                S.op("pool", lambda e: e.tensor_mul(out=gk3, in0=gk3, in1=zs[:].unsqueeze(2).to_broadcast([128, 8, 16])),
                     reads=["gk", "zs"], writes=["gk"])
                S.op("dve", lambda e: e.tensor_tensor(out=big[:], in0=posf[:].unsqueeze(3).to_broadcast([128, 8, 16, 16]),
                                                     in1=thr16[:].unsqueeze(1).unsqueeze(1).to_broadcast([128, 8, 16, 16]), op=ALU.is_ge),
                     reads=["posf", "thr16"], writes=["big"])
                S.op("dve", lambda e: e.tensor_reduce(out=ak[:], in_=big[:, :, :, 1:16], axis=AX.X, op=ALU.add), reads=["big"], writes=["ak"])
                S.op("dve", lambda e: e.scalar_tensor_tensor(out=bk[:], in0=ak[:], scalar=-16.0, in1=posf[:], op0=ALU.mult, op1=ALU.add),
                     reads=["ak", "posf"], writes=["bk"])
                for (sel, half, dst) in ((ak, 0, ik), (bk, 1, jk)):
                    S.op("dve", lambda e: e.tensor_tensor(out=big[:], in0=sel[:].unsqueeze(3).to_broadcast([128, 8, 16, 16]),
                                                         in1=io16[:].unsqueeze(1).unsqueeze(1).to_broadcast([128, 8, 16, 16]), op=ALU.is_equal),
                         reads=[sel.name[2:], "big"], writes=["big"])
                    S.op("dve", lambda e: e.tensor_mul(out=big[:], in0=big[:],
                                                      in1=i4[:, :, half, :].unsqueeze(2).to_broadcast([128, 8, 16, 16])),
                         reads=["big", "idxf"], writes=["big"])
                    S.op("dve", lambda e: e.tensor_reduce(out=dst[:].rearrange("p (h k) -> p h k", k=16), in_=big[:], axis=AX.X, op=ALU.add),
                         reads=["big"], writes=[dst.name[2:]])
                if "route" in debug and tt == 0:
                    S.dma("sp", dbg_out("r_g", [128, 128], force=True), gk[:], reads=["gk"], writes=["dbg_rg"])
                    S.dma("sp", dbg_out("r_i", [128, 128], force=True), ik[:], reads=["ik"], writes=["dbg_ri"])
                    S.dma("sp", dbg_out("r_j", [128, 128], force=True), jk[:], reads=["jk"], writes=["dbg_rj"])
                for q_, src in enumerate((gk, ik, jk)):
                    S.op("pe", lambda e: e.transpose(pTr[:, q_ * 128:(q_ + 1) * 128], src[:], ident[:]), reads=[src.name[2:]], writes=["pTr"])
                S.op("act", lambda e: e.copy(out=slT[b][:].rearrange("p a b -> p (a b)"), in_=pTr[:, 0:384]), reads=["pTr"], writes=[("slT", b)])
                S.dma("sp", gT_s[:, ts_], slT[b][:, 0, :], reads=[("slT", b)], writes=[("gT_s", tt)])
                S.dma("sp", iT_s[:, ts_], slT[b][:, 1, :], reads=[("slT", b)], writes=[("iT_s", tt)])
                S.dma("sp", jT_s[:, ts_], slT[b][:, 2, :], reads=[("slT", b)], writes=[("jT_s", tt)])
            S.barrier()
        if "slots" in debug and not done[0]:
            S.dma("sp", dbg_out("gT", [128, T], force=True), gT_s, writes=["dbg_gT"])
            S.dma("sp", dbg_out("iT", [128, T], force=True), iT_s, writes=["dbg_iT"])
            S.dma("sp", dbg_out("jT", [128, T], force=True), jT_s, writes=["dbg_jT"])
            S.barrier()
        if stop_after == 6 and not done[0]:
            fin()

        if not done[0]:
          with ExitStack() as ph:
            TB = 256
            NPF = 6
            NTK = 8
            Gs = [sbt(ph, f"G{i}", [128, TB, 128], BF16) for i in range(2)]
            u2k = [sbt(ph, f"u2k{i}", [128, 8, TB], BF16) for i in range(2)]
            slg = [sbt(ph, f"slg{i}", [128, TB]) for i in range(2)]
            sli = [sbt(ph, f"sli{i}", [128, TB]) for i in range(2)]
            slj = [sbt(ph, f"slj{i}", [128, TB]) for i in range(2)]
            Pb = [sbt(ph, f"Pb{i}", [128, NTK, 128], BF16) for i in range(2)]
            Qb = [sbt(ph, f"Qb{i}", [128, NTK, 128], BF16) for i in range(2)]
            Ub = [sbt(ph, f"Ub{i}", [128, 8, 128], BF16) for i in range(NPF)]
            Vb = [sbt(ph, f"Vb{i}", [128, D], BF16) for i in range(NPF)]
            hb = [sbt(ph, f"hb{i}", [128, TB], BF16) for i in range(2)]
            Wb = [sbt(ph, f"Wb{i}", [128, TB], BF16) for i in range(2)]
            x1t_ = sbt(ph, "x1t", [128, D]); x1t = [x1t_, x1t_]
            x2t_ = sbt(ph, "x2t", [128, D]); x2t = [x2t_, x2t_]
            fs = sbt(ph, "fs", [128, 2]); fr = sbt(ph, "fr", [128, 2])
            ot_ = sbt(ph, "ot", [128, D]); ot = [ot_, ot_]
            junk = ot_
            pAcc = [pst(ph, f"pAcc{i}", [128, 512]) for i in range(4)]
            pSx = [pst(ph, f"pSx{i}", [128, 512]) for i in range(2)]
            pG = [pst(ph, f"pG{i}", [128, 512]) for i in range(2)]
            gtf_bc = sbt(ph, "gtf_bc", [128, D])
            fing_bc = sbt(ph, "fing_bc", [128, D])
            S.dma("sp", fing_bc[:], fing_d, writes=["fing"])
            build_gate_bc(ph, gtf_bc, 40, pG[0], ("pG", 0))
            u2T3 = u2T_s.rearrange("p (k t) -> p k t", k=8)
            allU = [("Ub", i) for i in range(NCAST)]
            allV = [("Vb", i) for i in range(NCAST)]
            nload = [0]
            ngq = [0]
            nblk = T // TB

            def load_expert(i):
                s_ = nload[0] % NPF
                nload[0] += 1
                S.dma("sp", Ub[s_][:].rearrange("p k j -> p (k j)"), Ub_s[i * 128:(i + 1) * 128, :], reads=allU, writes=[("Ubt", s_)])
                S.dma("sp", Vb[s_][:], Vb_s[i * 128:(i + 1) * 128, :], reads=allV, writes=[("Vbt", s_)])

            def load_block(blk):
                gb = blk % 2
                t0 = blk * TB
                S.dma("sp", u2k[gb][:], u2T3[:, :, t0:t0 + TB], writes=[("u2k", gb)])
                S.dma("sp", slg[gb][:], gT_s[:, t0:t0 + TB], writes=[("slg", gb)])
                S.dma("sp", sli[gb][:], iT_s[:, t0:t0 + TB], writes=[("sli", gb)])
                S.dma("sp", slj[gb][:], jT_s[:, t0:t0 + TB], writes=[("slj", gb)])

            def gbuild_dve(blk, tb):
                gb = blk % 2
                pbuf = (blk * (TB // NTK) + tb) % 2
                ts_ = slice(tb * NTK, (tb + 1) * NTK)
                io_bc = iota_f[:].unsqueeze(1).to_broadcast([128, NTK, 128])
                S.op("dve", lambda e: e.tensor_tensor(out=Pb[pbuf][:], in0=io_bc,
                                                     in1=sli[gb][:, ts_].unsqueeze(2).to_broadcast([128, NTK, 128]), op=ALU.is_equal),
                     reads=[("sli", gb)], writes=[("Pb", pbuf)])
                S.op("pool", lambda e: e.tensor_tensor(out=Pb[pbuf][:], in0=Pb[pbuf][:],
                                                      in1=slg[gb][:, ts_].unsqueeze(2).to_broadcast([128, NTK, 128]), op=ALU.mult),
                     reads=[("slg", gb), ("Pb", pbuf)], writes=[("Pb", pbuf)])
                S.op("dve", lambda e: e.tensor_tensor(out=Qb[pbuf][:], in0=io_bc,
                                                     in1=slj[gb][:, ts_].unsqueeze(2).to_broadcast([128, NTK, 128]), op=ALU.is_equal),
                     reads=[("slj", gb)], writes=[("Qb", pbuf)])

            def gbuild_pe(blk, tb):
                gb = blk % 2
                pbuf = (blk * (TB // NTK) + tb) % 2
                for tq in range(NTK // 4):
                    pgi = ngq[0] % 2
                    ngq[0] += 1
                    pg = pG[pgi]
                    for t4 in range(4):
                        tl = tq * 4 + t4
                        S.op("pe", lambda e: e.matmul(pg[:, t4 * 128:(t4 + 1) * 128], lhsT=Qb[pbuf][:, tl, :], rhs=Pb[pbuf][:, tl, :],
                                                      start=True, stop=True), reads=[("Pb", pbuf), ("Qb", pbuf)], writes=[("pG", pgi)])
                    tg = tb * NTK + tq * 4
                    S.op("act", lambda e: e.copy(out=Gs[gb][:, tg:tg + 4, :].rearrange("p a b -> p (a b)"), in_=pg[:]),
                         reads=[("pG", pgi)], writes=[("G", gb, tb)])

            def gbuild(blk, tb):
                gbuild_dve(blk, tb)
                gbuild_pe(blk, tb)

            def s_stage(blk, i):
                gb = blk % 2
                s_ = (blk * 128 + i) % NPF
                b2 = i % 2
                for dk in range(8):
                    S.op("pe", lambda e: e.matmul(pSx[b2][:, 0:TB], lhsT=Ub[s_][:, dk, :], rhs=u2k[gb][:, dk, :], start=(dk == 0), stop=(dk == 7)),
                         reads=[("Ubt", s_), ("u2k", gb)], writes=[("pSx", b2)])
                S.op("act", lambda e: e.activation(out=hb[b2][:], in_=pSx[b2][:, 0:TB], func=AF.Gelu), reads=[("pSx", b2)], writes=[("hb", b2)])
                S.op("dve", lambda e: e.tensor_tensor(out=Wb[b2][:], in0=hb[b2][:], in1=Gs[gb][:, :, i], op=ALU.mult),
                     reads=[("hb", b2)] + ([("G", gb, tb) for tb in range(TB // NTK)] if i < 2 else []), writes=[("Wb", b2)])

            def v_stage(blk, i):
                s_ = (blk * 128 + i) % NPF
                b2 = i % 2
                for tt2 in range(TB // 128):
                    if i == 0:
                        for dh in range(2):
                            S.op("pe", lambda e: e.matmul(pAcc[tt2 * 2 + dh][:], lhsT=Wb[b2][:, tt2 * 128:(tt2 + 1) * 128],
                                                          rhs=Vb[s_][:, dh * 512:(dh + 1) * 512], start=True, stop=False),
                                 reads=[("Wb", b2), ("Vbt", s_)], writes=[("pAcc", tt2 * 2 + dh)])
                        continue
                    for dq in range(4):
                        dh, hf = dq // 2, dq % 2
                        S.op("pe", lambda e: e.matmul(pAcc[tt2 * 2 + dh][:, hf * 256:(hf + 1) * 256], lhsT=Wb[b2][:, tt2 * 128:(tt2 + 1) * 128],
                                                      rhs=Vb[s_][:, dq * 256:(dq + 1) * 256], start=False, stop=(i == 127)),
                             reads=[("Wb", b2), ("Vbt", s_)], writes=[("pAcc", tt2 * 2 + dh)])

            def epilogue(blk):
                t0 = blk * TB
                for tt2 in range(TB // 128):
                    tsl = slice(t0 + tt2 * 128, t0 + (tt2 + 1) * 128)
                    S.dma("sp", x1t[tt2][:], x1_s[tsl, :], writes=["x1t"])
                    for dh in range(2):
                        S.op("dve", lambda e: e.tensor_tensor(out=x2t[tt2][:, dh * 512:(dh + 1) * 512], in0=pAcc[tt2 * 2 + dh][:],
                                                             in1=gtf_bc[:, dh * 512:(dh + 1) * 512], op=ALU.mult),
                             reads=[("pAcc", tt2 * 2 + dh), "gtf_bc"], writes=["x2t"])
                    if "peer" in debug:
                        if "peer" not in dbg:
                            dbg_out("peer", [T, D], force=True)
                        S.dma("sp", dbg["peer"][tsl, :], x2t[tt2][:], reads=["x2t"], writes=[("dbg_peer", blk, tt2)])
                    S.op("pool", lambda e: e.tensor_add(out=x2t[tt2][:], in0=x2t[tt2][:], in1=x1t[tt2][:]),
                         reads=["x2t", "x1t"], writes=["x2t"])
                    S.op("act", lambda e: e.activation(out=junk[:], in_=x2t[tt2][:], func=AF.Square), reads=["x2t"], writes=["ot"])
                    S.op("dve", lambda e: e.reduce_sum(out=fs[:, tt2:tt2 + 1], in_=junk[:], axis=AX.X), reads=["ot"], writes=[("fs", tt2)])
                    S.op("act", lambda e: e.activation(out=fr[:, tt2:tt2 + 1], in_=fs[:, tt2:tt2 + 1], func=AF.Sqrt, bias=1e-6, scale=1.0 / D),
                         reads=[("fs", tt2)], writes=[("fr", tt2)])
                    S.op("dve", lambda e: e.reciprocal(out=fr[:, tt2:tt2 + 1], in_=fr[:, tt2:tt2 + 1]), reads=[("fr", tt2)], writes=[("fr", tt2)])
                    S.op("dve", lambda e: e.scalar_tensor_tensor(out=ot[tt2][:], in0=x2t[tt2][:], scalar=fr[:, tt2:tt2 + 1], in1=fing_bc[:],
                                                                op0=ALU.mult, op1=ALU.mult),
                         reads=["x2t", ("fr", tt2), "fing"], writes=["ot"])
                    S.dma("sp", out_d[tsl, :], ot[tt2][:], reads=["ot"], writes=[("out", blk, tt2)])

            load_block(0)
            for i in range(NPF - 1):
                load_expert(i)
            for tb in range(TB // NTK):
                gbuild(0, tb)
            for blk in range(nblk):
                if blk + 1 < nblk:
                    load_block(blk + 1)
                s_stage(blk, 0)
                s_stage(blk, 1)
                for i in range(128):
                    nxt_e = blk * 128 + i + NPF - 1
                    if nxt_e < nblk * 128:
                        load_expert(nxt_e % 128)
                    v_stage(blk, i)
                    if i + 2 < 128:
                        s_stage(blk, i + 2)
                    if blk + 1 < nblk and i % 4 == 0:
                        if i >= 4:
                            gbuild_pe(blk + 1, i // 4 - 1)
                        gbuild_dve(blk + 1, i // 4)
                if blk + 1 < nblk:
                    gbuild_pe(blk + 1, TB // NTK - 1)
                epilogue(blk)
            S.barrier()
            S.finish([])
    print(f"[kernel] instructions={S.ninst} waits={S.nwaits}")
    return nc, dbg


def col(v, n):
    return np.ascontiguousarray(np.asarray(v, np.float32).reshape(n, 128).T)


def prep_inputs(inp):
    f = lambda k: np.asarray(inp[k], np.float32)
    cols = np.zeros((128, NCOL), np.float32)
    cols[:, G1:G1 + 8] = col(f("norm_mix_g")[0], 8)
    cols[:, G2:G2 + 8] = col(f("norm_ffn_g")[0], 8)
    cols[:, CW:CW + 124] = f("conv_dw_w")[0].reshape(31, 4, 128).transpose(2, 1, 0).reshape(128, 124)
    cols[:, CB:CB + 4] = col(f("conv_dw_b")[0], 4)
    cols[:, LW:LW + 4] = col(f("conv_ln_w")[0], 4)
    cols[:, LB:LB + 4] = col(f("conv_ln_b")[0], 4)
    cols[:, MU:MU + 14] = col(f("rwkv_mu")[0], 14)
    cols[:, W0:W0 + 4] = col(f("rwkv_w0")[0], 4)
    cols[:, A0:A0 + 4] = col(f("rwkv_a0")[0], 4)
    cols[:, KK:KK + 4] = col(f("rwkv_k_k")[0], 4)
    cols[:, KA:KA + 4] = col(f("rwkv_k_a")[0], 4)
    cols[:, RK:RK + 4] = col(f("rwkv_r_k")[0].reshape(-1), 4)
    cols[:, GW:GW + 4] = col(f("rwkv_gn_w")[0], 4)
    cols[:, GB:GB + 4] = col(f("rwkv_gn_b")[0], 4)
    shared = {
        "ada_w": np.ascontiguousarray(f("ada_w")[0]),
        "ada_b_col": col(f("ada_b")[0], 48),
        "cols": cols,
        "final_g_bc": np.ascontiguousarray(np.broadcast_to(f("final_g")[None, :], (128, D))),
        "w_in": np.ascontiguousarray(f("w_in")[0]),
        "wa2": np.ascontiguousarray(np.concatenate([f("rwkv_w2")[0], f("rwkv_a2")[0]], axis=0)),
        "g2": np.ascontiguousarray(f("rwkv_g2")[0]),
        "w_out": np.ascontiguousarray(f("w_out")[0]),
        "wqT": np.ascontiguousarray(f("peer_w_q")[0].T),
        "KT": np.ascontiguousarray(f("peer_sub_keys")[0].reshape(16, 128, 128).transpose(2, 0, 1).reshape(128, 2048)),
        "UTt": np.ascontiguousarray(f("peer_u")[0].reshape(128, 128, 8, 128).transpose(0, 3, 2, 1).reshape(128 * 128, 1024)),
        "V": np.ascontiguousarray(f("peer_v")[0]),
    }
    maps = []
    for b in range(NCORES):
        m = dict(shared)
        m["x"] = np.ascontiguousarray(f("x")[b])
        m["c_col"] = col(f("c")[b], 8)
        maps.append(m)
    return maps


_NC_CACHE = {}


def kernel(**inputs):
    maps = prep_inputs(inputs)
    if "nc" not in _NC_CACHE:
        _NC_CACHE["nc"] = build_nc()[0]
    nc = _NC_CACHE["nc"]
    res = run_bass_kernel_spmd(nc, maps, core_ids=list(range(NCORES)))
    return np.stack([np.asarray(r["out"], np.float32) for r in res.results], axis=0)
```

```python
import numpy as np
from contextlib import ExitStack
import concourse.bass as bass
import concourse.mybir as mybir
from concourse.bass_utils import run_bass_kernel_spmd

F32 = mybir.dt.float32
BF16 = mybir.dt.bfloat16
U16 = mybir.dt.uint16
I32 = mybir.dt.int32
AF = mybir.ActivationFunctionType
ALU = mybir.AluOpType
AX = mybir.AxisListType

T = 4096
D = 1024
NCORES = 8
C0 = 0.6065306597126334

G1, G2, CW, CB, LW, LB, MU, W0, A0, KK, KA, RK, GW, GB, NCOL = (
    0, 8, 16, 140, 144, 148, 152, 166, 170, 174, 178, 182, 186, 190, 194)


class Sync:
    NDMA = 12

    def __init__(self, nc, stack):
        self.nc = nc
        self.eng = {"pe": nc.tensor, "act": nc.scalar, "dve": nc.vector,
                    "pool": nc.gpsimd, "sp": nc.sync}
        self.sem, self.cnt = {}, {}
        self.semobj = {}
        for e in self.eng:
            self.sem[e] = stack.enter_context(nc.semaphore("s_" + e))
            self.semobj[self.sem[e].name] = self.sem[e]
            self.cnt[e] = 0
        self.dsem, self.dnext = {}, {}
        self.eng["cast"] = nc.gpsimd
        for q in ("sp", "pool", "act", "cast"):
            self.dsem[q] = [[stack.enter_context(nc.semaphore(f"d_{q}{i}")), 0]
                            for i in range(32 if q == "cast" else self.NDMA)]
            for s, _ in self.dsem[q]:
                self.semobj[s.name] = s
            self.dnext[q] = 0
        self.seen = {e: {} for e in self.eng}
        self.snap = {}
        self.lastw = {}
        self.readers = {}
        self.nwaits = 0
        self.ninst = 0

    def _need(self, e, ev, waits):
        if ev is None:
            return
        name, val = ev
        if self.seen[e].get(name, 0) >= val:
            return
        if e == "pe" and name == self.sem["pe"].name:
            return
        if waits.get(name, 0) < val:
            waits[name] = val

    def _do_waits(self, e, waits):
        for name, val in waits.items():
            if self.seen[e].get(name, 0) >= val:
                continue
            self.eng[e].wait_ge(self.semobj[name], val)
            self.nwaits += 1
            sn = self.snap.get((name, val))
            se = self.seen[e]
            if sn:
                for k, v in sn.items():
                    if se.get(k, 0) < v:
                        se[k] = v
            if se.get(name, 0) < val:
                se[name] = val

    def _deps(self, e, reads, writes):
        waits = {}
        for k in reads:
            self._need(e, self.lastw.get(k), waits)
        for k in writes:
            self._need(e, self.lastw.get(k), waits)
            for ev in self.readers.get(k, ()):
                self._need(e, ev, waits)
        self._do_waits(e, waits)

    def _record(self, e, ev, reads, writes):
        for k in reads:
            self.readers.setdefault(k, []).append(ev)
        for k in writes:
            self.lastw[k] = ev
            self.readers[k] = []
        self.snap[ev] = dict(self.seen[e])

    EXCL = {"psmod", "psb", "pT", "psA", "psB", "psC", "ps1", "ps2", "rpsA0", "rpsA1", "rpsM0", "rpsM1",
            "rW", "pO", "pT3", "pW", "pS", "pTr", "pAcc", "pSx", "pG"}

    def op(self, e, fn, reads=(), writes=()):
        ex = [k for k in reads if (k if isinstance(k, str) else k[0]) in self.EXCL]
        if ex:
            writes = list(writes) + ex
        self._deps(e, reads, writes)
        ins = fn(self.eng[e])
        self.cnt[e] += 1
        s = self.sem[e]
        ins.then_inc(s, 1)
        ev = (s.name, self.cnt[e])
        self._record(e, ev, reads, writes)
        self.ninst += 1
        return ev

    def dma(self, q, out, in_, reads=(), writes=(), **kw):
        pool = self.dsem[q]
        slot = pool[self.dnext[q] % len(pool)]
        self.dnext[q] += 1
        s = slot[0]
        waits = {}
        if slot[1] > 0:
            self._need(q, (s.name, slot[1]), waits)
        self._do_waits(q, waits)
        self._deps(q, reads, writes)
        ins = self.eng[q].dma_start(out=out, in_=in_, **kw)
        slot[1] += 16
        ins.then_inc(s, 16)
        ev = (s.name, slot[1])
        self._record(q, ev, reads, writes)
        self.ninst += 1
        return ev

    def barrier(self):
        evs = []
        for e in self.cnt:
            if self.cnt[e]:
                evs.append((self.sem[e].name, self.cnt[e]))
        for q in self.dsem:
            if q == "cast":
                continue
            for s, c in self.dsem[q]:
                if c:
                    evs.append((s.name, c))
        for e in self.eng:
            if e == "cast":
                continue
            waits = {}
            for ev in evs:
                self._need(e, ev, waits)
            self._do_waits(e, waits)
        keep = lambda k: isinstance(k, tuple) and k[0] in ("Ub", "Vb")
        self.lastw = {k: v for k, v in self.lastw.items() if keep(k)}
        self.readers = {k: v for k, v in self.readers.items() if keep(k)}

    def finish(self, keys):
        waits = {}
        for k in keys:
            self._need("sp", self.lastw.get(k), waits)
        self._do_waits("sp", waits)


def build_nc(debug=(), stop_after=None):
    nc = bass.Bass("TRN2", target_bir_lowering=False)

    def din(name, shape, dt=F32):
        return nc.dram_tensor(name, list(shape), dt, kind="ExternalInput").ap()

    def dscr(name, shape, dt=F32):
        return nc.dram_tensor(name, list(shape), dt, kind="Internal").ap()

    x_d = din("x", [T, D])
    ccol_d = din("c_col", [128, 8])
    adaw_d = din("ada_w", [D, 6 * D])
    adab_d = din("ada_b_col", [128, 48])
    cols_d = din("cols", [128, NCOL])
    fing_d = din("final_g_bc", [128, D])
    win_d = din("w_in", [D, 2816])
    wa2_d = din("wa2", [128, 512])
    g2_d = din("g2", [128, 512])
    wout_d = din("w_out", [D, D])
    wqT_d = din("wqT", [2048, D])
    KT_d = din("KT", [128, 16 * 128])
    UT_d = din("UTt", [128 * 128, 1024])
    V_d = din("V", [16384, D])
    out_d = nc.dram_tensor("out", [T, D], F32, kind="ExternalOutput").ap()

    yc_s = dscr("yc_s", [512, T])
    ycat_s = dscr("ycat_s", [1024, T], BF16)
    x1_s = dscr("x1_s", [T, D])
    u2T_s = dscr("u2T_s", [128, 8 * T], BF16)
    gT_s = dscr("gT_s", [128, T])
    iT_s = dscr("iT_s", [128, T])
    jT_s = dscr("jT_s", [128, T])
    Ub_s = dscr("Ub_s", [128 * 128, 1024], BF16)
    Vb_s = dscr("Vb_s", [16384, D], BF16)

    dbg = {}

    def dbg_out(name, shape, dt=F32, force=False):
        if name in debug or force:
            dbg[name] = nc.dram_tensor("dbg_" + name, list(shape), dt, kind="ExternalOutput").ap()
            return dbg[name]
        return None

    with ExitStack() as top:
        S = Sync(nc, top)

        def sbt(st, name, shape, dt=F32):
            return st.enter_context(nc.sbuf_tensor("s_" + name, list(shape), dt))

        def pst(st, name, shape=(128, 512), dt=F32):
            return st.enter_context(nc.psum_tensor("p_" + name, list(shape), dt))

        def aff(e, out, in_, scale, bias, reads, writes, func=None):
            if e == "act":
                f = func or AF.Identity
                return S.op("act", lambda en: en.activation(out=out, in_=in_, func=f, bias=bias, scale=scale),
                            reads=reads, writes=writes)
            assert func is None
            return S.op(e, lambda en: en.tensor_scalar(out=out, in0=in_, scalar1=scale, scalar2=bias,
                                                       op0=ALU.mult, op1=ALU.add), reads=reads, writes=writes)

        def build_gate_bc(st, dst, mof, pb, pbkey):
            dg = sbt(st, "dg_" + dst.name[2:], [128, 128])
            for cidx in range(8):
                S.op("dve", lambda e: e.tensor_scalar_mul(out=dg[:], in0=ident[:], scalar1=modT[:, mof + cidx:mof + cidx + 1]),
                     reads=["dg"], writes=["dg"])
                S.op("pe", lambda e: e.matmul(pb[:, 0:128], lhsT=ones_f[:], rhs=dg[:], start=True, stop=True),
                     reads=["dg"], writes=[pbkey])
                S.op("act", lambda e: e.copy(out=dst[:, cidx * 128:(cidx + 1) * 128], in_=pb[:, 0:128]),
                     reads=[pbkey], writes=[dst.name[2:]])

        cols = sbt(top, "cols", [128, NCOL])
        dcol = sbt(top, "dcol", [128, 48])
        OMM, OMKA, GSC1, GSC2 = 0, 14, 18, 26
        modT = sbt(top, "modT", [128, 48])
        ident = sbt(top, "ident", [128, 128])
        iota_f = sbt(top, "iota_f", [128, 128])
        ones_f = sbt(top, "ones_f", [128, 128])
        blk1 = sbt(top, "blk1", [128, 128])
        mask4 = sbt(top, "mask4", [128, 4, 128])
        maskL = sbt(top, "maskL", [128, 128])

        NCAST = 16
        cast_jobs = []
        for i in range(NCAST):
            r0, r1 = i * (16384 // NCAST), (i + 1) * (16384 // NCAST)
            cast_jobs.append((Ub_s[r0:r1, :], UT_d[r0:r1, :], ("Ub", i)))
            cast_jobs.append((Vb_s[r0:r1, :], V_d[r0:r1, :], ("Vb", i)))

        def issue_casts(n):
            for _ in range(n):
                if cast_jobs:
                    o_, i_, k_ = cast_jobs.pop(0)
                    S.dma("cast", o_, i_, writes=[k_])

        with ExitStack() as ph:
            io_i = sbt(ph, "io_i", [128, 128], I32)
            io_f = sbt(ph, "io_f", [128, 128])
            ccol = sbt(ph, "ccol", [128, 8])
            sc = sbt(ph, "silu_c", [128, 8])
            adab = sbt(ph, "adab", [128, 48])
            abuf = [sbt(ph, f"abuf{i}", [128, 2048]) for i in range(2)]
            dg = sbt(ph, "dg", [128, 128])
            psmod_t = pst(ph, "psmod", [128, 512])
            psmod = psmod_t[:, 0:384]
            psb = [pst(ph, f"psb{i}", [128, 512]) for i in range(2)]

            S.dma("sp", cols[:], cols_d, writes=["cols"])
            S.dma("sp", ccol[:], ccol_d, writes=["ccol"])
            S.dma("sp", adab[:], adab_d, writes=["adab"])
            S.op("pool", lambda e: e.iota(io_i[:], pattern=[[1, 128]], base=0, channel_multiplier=-1), writes=["io_i"])
            S.op("dve", lambda e: e.tensor_copy(out=io_f[:], in_=io_i[:]), reads=["io_i"], writes=["io_f"])
            S.op("dve", lambda e: e.tensor_single_scalar(out=ident[:], in_=io_f[:], scalar=0.0, op=ALU.is_equal),
                 reads=["io_f"], writes=["ident"])
            S.op("pool", lambda e: e.iota(io_i[:], pattern=[[1, 128]], base=0, channel_multiplier=0),
                 reads=["io_i"], writes=["io_i"])
            S.op("dve", lambda e: e.tensor_copy(out=iota_f[:], in_=io_i[:]), reads=["io_i"], writes=["iota_f"])
            S.op("pool", lambda e: e.memset(ones_f[:], 1.0), writes=["ones_f"])
            S.op("pool", lambda e: e.memset(blk1[:], 0.0), writes=["blk1"])
            S.op("pool", lambda e: e.memset(blk1[0:64, 0:64], 1.0), writes=["blk1"])
            S.op("pool", lambda e: e.memset(blk1[64:128, 64:128], 1.0), writes=["blk1"])
            S.op("dve", lambda e: e.scalar_tensor_tensor(out=mask4[:, 0, :], in0=io_f[:], scalar=0.0, in1=blk1[:],
                                                        op0=ALU.is_gt, op1=ALU.mult), reads=["io_f", "blk1"], writes=["mask4"])
            S.op("dve", lambda e: e.tensor_copy(out=mask4[:, 1, :], in_=mask4[:, 0, :]), reads=["mask4"], writes=["mask4"])
            S.op("dve", lambda e: e.scalar_tensor_tensor(out=mask4[:, 2, :], in0=io_f[:], scalar=0.0, in1=blk1[:],
                                                        op0=ALU.is_ge, op1=ALU.mult), reads=["io_f", "blk1", "mask4"], writes=["mask4"])
            S.op("dve", lambda e: e.tensor_copy(out=mask4[:, 3, :], in_=mask4[:, 2, :]), reads=["mask4"], writes=["mask4"])
            S.op("dve", lambda e: e.scalar_tensor_tensor(out=maskL[:], in0=io_f[:], scalar=0.0, in1=blk1[:],
                                                        op0=ALU.is_lt, op1=ALU.mult), reads=["io_f", "blk1"], writes=["maskL"])
            S.op("dve", lambda e: e.tensor_scalar(out=dcol[:, OMM:OMM + 14], in0=cols[:, MU:MU + 14], scalar1=-1.0, scalar2=1.0,
                                                 op0=ALU.mult, op1=ALU.add), reads=["cols"], writes=["dcol"])
            S.op("dve", lambda e: e.tensor_scalar(out=dcol[:, OMKA:OMKA + 4], in0=cols[:, KA:KA + 4], scalar1=-1.0, scalar2=1.0,
                                                 op0=ALU.mult, op1=ALU.add), reads=["cols", "dcol"], writes=["dcol"])
            S.op("act", lambda e: e.activation(out=sc[:], in_=ccol[:], func=AF.Silu), reads=["ccol"], writes=["silu_c"])
            n = 0
            for k in range(8):
                for pc in range(3):
                    ab = abuf[n % 2]
                    S.dma("sp", ab[:], adaw_d[k * 128:(k + 1) * 128, pc * 2048:(pc + 1) * 2048], writes=[("abuf", n % 2)])
                    for fl in range(16):
                        f = pc * 16 + fl
                        S.op("pe", lambda e: e.matmul(psmod[:, f * 8 + k:f * 8 + k + 1], lhsT=ab[:, fl * 128:(fl + 1) * 128],
                                                      rhs=sc[:, k:k + 1], start=True, stop=True),
                             reads=[("abuf", n % 2), "silu_c"], writes=["psmod"])
                    n += 1
            S.op("dve", lambda e: e.tensor_reduce(out=modT[:], in_=psmod.rearrange("p (f k) -> p f k", k=8),
                                                 axis=AX.X, op=ALU.add), reads=["psmod"], writes=["modT"])
            S.op("dve", lambda e: e.tensor_add(out=modT[:], in0=modT[:], in1=adab[:]), reads=["modT", "adab"], writes=["modT"])
            for (dst, gof, sof) in ((GSC1, G1, 8), (GSC2, G2, 32)):
                S.op("dve", lambda e: e.scalar_tensor_tensor(out=dcol[:, dst:dst + 8], in0=modT[:, sof:sof + 8], scalar=1.0,
                                                            in1=cols[:, gof:gof + 8], op0=ALU.add, op1=ALU.mult),
                     reads=["modT", "cols", "dcol"], writes=["dcol"])
            if "modT" in debug:
                S.dma("sp", dbg_out("modT", [128, 48]), modT[:], reads=["modT"], writes=["dbg_modT"])
            S.barrier()

        done = [False]

        def fin():
            S.barrier()
            S.finish([])
            done[0] = True

        if stop_after == 0:
            fin()

        if not done[0]:
          with ExitStack() as mx:
            uT = sbt(mx, "uT", [128, 8, T + 1], BF16)

            with ExitStack() as ph:
                xb = [sbt(ph, f"xb{i}", [128, D]) for i in range(2)]
                junk = sbt(ph, "junk", [128, D])
                xn = [sbt(ph, f"xn{i}", [128, D]) for i in range(2)]
                ss = sbt(ph, "ss", [128, 32])
                rs = sbt(ph, "rs", [128, 32])
                pT = [pst(ph, f"pT{i}", [128, 1024]) for i in range(2)]
                S.op("pool", lambda e: e.memset(uT[:, :, 0:1], 0.0), writes=["uT0"])
                def p1_front(tt):
                    b = tt % 2
                    S.dma("sp", xb[b][:], x_d[tt * 128:(tt + 1) * 128, :], writes=[("xb", b)])
                    S.op("act", lambda e: e.activation(out=junk[:], in_=xb[b][:], func=AF.Square),
                         reads=[("xb", b)], writes=["junk"])
                    S.op("dve", lambda e: e.reduce_sum(out=ss[:, tt:tt + 1], in_=junk[:], axis=AX.X),
                         reads=["junk"], writes=[("ss", tt)])
                    S.op("act", lambda e: e.activation(out=rs[:, tt:tt + 1], in_=ss[:, tt:tt + 1], func=AF.Sqrt,
                                                       bias=1e-6, scale=1.0 / D), reads=[("ss", tt)], writes=[("rs", tt)])
                    S.op("dve", lambda e: e.reciprocal(out=rs[:, tt:tt + 1], in_=rs[:, tt:tt + 1]),
                         reads=[("rs", tt)], writes=[("rs", tt)])
                    S.op("dve", lambda e: e.tensor_scalar_mul(out=xn[b][:], in0=xb[b][:], scalar1=rs[:, tt:tt + 1]),
                         reads=[("xb", b), ("rs", tt)], writes=[("xn", b)])

                p1_front(0)
                for tt in range(32):
                    b = tt % 2
                    if tt + 1 < 32:
                        p1_front(tt + 1)
                    for k in range(8):
                        S.op("pe", lambda e: e.transpose(pT[b][:, k * 128:(k + 1) * 128], xn[b][:, k * 128:(k + 1) * 128], ident[:]),
                             reads=[("xn", b)], writes=[("pT", b, k // 4)])
                    for k in range(8):
                        aff("act" if k < 4 else "dve", uT[:, k, 1 + tt * 128:1 + (tt + 1) * 128],
                            pT[b][:, k * 128:(k + 1) * 128], dcol[:, GSC1 + k:GSC1 + k + 1], modT[:, k:k + 1],
                            reads=[("pT", b, k // 4)], writes=[("uT", tt)])
                S.barrier()
            if "uT" in debug:
                du = dbg_out("uT", [128, 8 * T], BF16).rearrange("p (k t) -> p k t", k=8)
                for k in range(8):
                    S.dma("sp", du[:, k, :], uT[:, k, 1:T + 1], writes=[("dbg_uT", k)])
                S.barrier()
            if stop_after == 1:
                fin()

            def load_w(st_f32, dst_bf, f, key):
                S.dma("sp", st_f32[:], win_d[:, f * 128:(f + 1) * 128].rearrange("(k p) f -> p k f", p=128),
                      writes=[("wst", st_f32.name[2:])])
                S.op("pool", lambda e: e.tensor_copy(out=dst_bf[:], in_=st_f32[:]), reads=[("wst", st_f32.name[2:])], writes=[("wbf", key)])

            if not done[0]:
              with ExitStack() as ph:
                wst = [sbt(ph, f"wst{i}", [128, 8, 128]) for i in range(2)]
                wab = [sbt(ph, f"wab{i}", [128, 8, 128], BF16) for i in range(2)]
                diag = sbt(ph, "diag", [128, 31, 128], BF16)
                ypad = sbt(ph, "ypad", [128, 30 + T], BF16)
                sig = [sbt(ph, f"sig{i}", [128, 512]) for i in range(2)]
                ycb = [sbt(ph, f"ycb{i}", [128, 512]) for i in range(2)]
                psA = [pst(ph, f"psA{i}") for i in range(2)]
                psB = [pst(ph, f"psB{i}") for i in range(2)]
                psC = [pst(ph, f"psC{i}") for i in range(2)]
                S.op("pool", lambda e: e.memset(ypad[:, 0:30], 0.0), writes=["ypad0"])
                for ct in range(4):
                    issue_casts(2)
                    load_w(wst[0], wab[0], ct, 0)
                    load_w(wst[1], wab[1], 4 + ct, 1)
                    for k in range(31):
                        S.op("dve", lambda e: e.tensor_scalar_mul(out=diag[:, k, :], in0=ident[:],
                                                                  scalar1=cols[:, CW + ct * 31 + k:CW + ct * 31 + k + 1]),
                             writes=["diag"])
                    for tb in range(8):
                        b = tb % 2
                        t0 = tb * 512
                        for k in range(8):
                            S.op("pe", lambda e: e.matmul(psA[b][:], lhsT=wab[0][:, k, :], rhs=uT[:, k, 1 + t0:1 + t0 + 512],
                                                          start=(k == 0), stop=(k == 7)), reads=[("wbf", 0)], writes=[("psA", b)])
                        for k in range(8):
                            S.op("pe", lambda e: e.matmul(psB[b][:], lhsT=wab[1][:, k, :], rhs=uT[:, k, 1 + t0:1 + t0 + 512],
                                                          start=(k == 0), stop=(k == 7)), reads=[("wbf", 1)], writes=[("psB", b)])
                        S.op("act", lambda e: e.activation(out=sig[b][:], in_=psB[b][:], func=AF.Sigmoid),
                             reads=[("psB", b)], writes=[("sig", b)])
                        S.op("dve", lambda e: e.tensor_tensor(out=ypad[:, 30 + t0:30 + t0 + 512], in0=psA[b][:], in1=sig[b][:],
                                                             op=ALU.mult), reads=[("psA", b), ("sig", b)], writes=[("ypad", tb)])
                    for tb in range(8):
                        b = tb % 2
                        t0 = tb * 512
                        for k in range(31):
                            S.op("pe", lambda e: e.matmul(psC[b][:], lhsT=diag[:, k, :], rhs=ypad[:, t0 + k:t0 + k + 512],
                                                          start=(k == 0), stop=(k == 30)),
                                 reads=["diag", "ypad0", ("ypad", tb), ("ypad", max(tb - 1, 0))], writes=[("psC", b)])
                        aff("act", ycb[b][:], psC[b][:], 1.0, cols[:, CB + ct:CB + ct + 1], reads=[("psC", b)], writes=[("ycb", b)])
                        S.dma("sp", yc_s[ct * 128:(ct + 1) * 128, t0:t0 + 512], ycb[b][:], reads=[("ycb", b)],
                              writes=[("yc_s", ct, tb)])
                S.barrier()
            if "yc" in debug and not done[0]:
                S.dma("sp", dbg_out("yc", [512, T]), yc_s, writes=["dbg_yc"])
                S.barrier()
            if stop_after == 2 and not done[0]:
                fin()
            done_2a = True

            if not done[0]:
              with ExitStack() as ph:
                yb = [sbt(ph, f"yb{i}", [128, 4, 512]) for i in range(2)]
                sq = sbt(ph, "sq", [128, 4, 512])
                mean = sbt(ph, "mean", [128, 512])
                msq = sbt(ph, "msq", [128, 512])
                var = sbt(ph, "var", [128, 512])
                dd = [sbt(ph, f"dd{i}", [128, 512]) for i in range(2)]
                yo = [sbt(ph, f"yo{i}", [128, 512], BF16) for i in range(2)]
                ps1 = pst(ph, "ps1")
                ps2 = pst(ph, "ps2")
                for tb in range(8):
                    b = tb % 2
                    t0 = tb * 512
                    S.dma("sp", yb[b][:], yc_s[:, t0:t0 + 512].rearrange("(c p) t -> p c t", p=128), writes=[("yb", b)])
                    S.op("act", lambda e: e.activation(out=sq[:], in_=yb[b][:], func=AF.Square), reads=[("yb", b)], writes=["sq"])
                    for ct in range(4):
                        S.op("pe", lambda e: e.matmul(ps1[:], lhsT=ones_f[:], rhs=yb[b][:, ct, :], start=(ct == 0), stop=(ct == 3)),
                             reads=[("yb", b)], writes=["ps1"])
                    for ct in range(4):
                        S.op("pe", lambda e: e.matmul(ps2[:], lhsT=ones_f[:], rhs=sq[:, ct, :], start=(ct == 0), stop=(ct == 3)),
                             reads=["sq"], writes=["ps2"])
                    S.op("act", lambda e: e.mul(out=mean[:], in_=ps1[:], mul=1.0 / 512), reads=["ps1"], writes=["mean"])
                    S.op("pool", lambda e: e.tensor_mul(out=msq[:], in0=mean[:], in1=mean[:]), reads=["mean"], writes=["msq"])
                    S.op("dve", lambda e: e.scalar_tensor_tensor(out=var[:], in0=ps2[:], scalar=1.0 / 512, in1=msq[:],
                                                                op0=ALU.mult, op1=ALU.subtract), reads=["ps2", "msq"], writes=["var"])
                    S.op("act", lambda e: e.activation(out=var[:], in_=var[:], func=AF.Sqrt, bias=1e-5, scale=1.0),
                         reads=["var"], writes=["var"])
                    S.op("dve", lambda e: e.reciprocal(out=var[:], in_=var[:]), reads=["var"], writes=["var"])
                    for ct in range(4):
                        d2 = ct % 2
                        S.op("dve", lambda e: e.tensor_sub(out=dd[d2][:], in0=yb[b][:, ct, :], in1=mean[:]),
                             reads=[("yb", b), "mean"], writes=[("dd", d2)])
                        S.op("pool", lambda e: e.tensor_mul(out=dd[d2][:], in0=dd[d2][:], in1=var[:]),
                             reads=[("dd", d2), "var"], writes=[("dd", d2)])
                        aff("act", yo[d2][:], dd[d2][:], cols[:, LW + ct:LW + ct + 1], cols[:, LB + ct:LB + ct + 1],
                            reads=[("dd", d2)], writes=[("yo", d2)], func=AF.Silu)
                        S.dma("sp", ycat_s[ct * 128:(ct + 1) * 128, t0:t0 + 512], yo[d2][:], reads=[("yo", d2)],
                              writes=[("ycat", ct, tb)])
                S.barrier()
            if "ycat" in debug and stop_after == 3 and not done[0]:
                S.dma("sp", dbg_out("ycat", [1024, T], BF16), ycat_s, writes=["dbg_ycat"])
                S.barrier()
            if stop_after == 3 and not done[0]:
                fin()

            if not done[0]:
              with ExitStack() as ph:
                NB = 256
                NCH = NB // 64
                wst_ = sbt(ph, "rwst", [128, 8, 128]); wst = [wst_, wst_]
                wl_bf = sbt(ph, "wl_bf", [128, 8, 128], BF16)
                wr_bf = [sbt(ph, f"wr_bf{i}", [128, 8, 128], BF16) for i in range(3)]
                twal = sbt(ph, "twal", [128, T], BF16)
                sg = sbt(ph, "sg", [128, T], BF16)
                wa2b = sbt(ph, "wa2b", [128, 512], BF16)
                g2b = sbt(ph, "g2b", [128, 512], BF16)
                cmask = sbt(ph, "cmask", [128, NB])
                tmpx = [sbt(ph, f"tmpx{i}", [128, 256]) for i in range(2)]
                xsl = sbt(ph, "xsl", [128, 256])
                psA = [pst(ph, f"rpsA{i}") for i in range(2)]
                psM = [pst(ph, f"rpsM{i}") for i in range(2)]
                Wk = [pst(ph, f"rW{i}") for i in range(4)]

                S.op("pool", lambda e: e.memset(cmask[:], 1.0), writes=["cmask"])
                S.op("pool", lambda e: e.memset(cmask[:].rearrange("p (c k) -> p c k", k=64)[:, :, 0:1], 0.0),
                     reads=["cmask"], writes=["cmask"])

                halos = sbt(ph, "halos", [128, 4])
                npx = [0]

                def proj_xs1(wbf, wkey, fcol, t0, n, out_ap, okey, j3, first):
                    bi = npx[0] % 2
                    npx[0] += 1
                    pa = psA[bi]
                    pk_ = "rpsA%d" % bi
                    tx = tmpx[bi]
                    mu_c = cols[:, MU + fcol:MU + fcol + 1]
                    hk = ("halo", j3)
                    for k in range(8):
                        S.op("pe", lambda e: e.matmul(pa[:, 0:n], lhsT=wbf[:, k, :], rhs=uT[:, k, 1 + t0:1 + t0 + n],
                                                      start=(k == 0), stop=(k == 7)), reads=[wkey], writes=[pk_])
                    if first:
                        S.op("act", lambda e: e.memzero(halos[:, j3:j3 + 1]), writes=[hk])
                    S.op("act", lambda e: e.activation(out=tx[:, 0:1], in_=halos[:, j3:j3 + 1], func=AF.Copy, scale=mu_c),
                         reads=[hk], writes=[("tmpx", bi)])
                    S.op("act", lambda e: e.activation(out=tx[:, 1:n], in_=pa[:, 0:n - 1], func=AF.Copy, scale=mu_c),
                         reads=[pk_, ("tmpx", bi)], writes=[("tmpx", bi)])
                    S.op("act", lambda e: e.copy(out=halos[:, j3:j3 + 1], in_=pa[:, n - 1:n]), reads=[pk_, hk], writes=[hk])
                    S.op("dve", lambda e: e.scalar_tensor_tensor(out=out_ap, in0=pa[:, 0:n],
                                                                scalar=dcol[:, OMM + fcol:OMM + fcol + 1], in1=tx[:, 0:n],
                                                                op0=ALU.mult, op1=ALU.add),
                         reads=[pk_, ("tmpx", bi)], writes=[okey])

                def proj_xs(wbf, wkey, fcol, t0, n, out_ap, okey, slot):
                    pa, pb = psA[0], psA[1]
                    for k in range(8):
                        S.op("pe", lambda e: e.matmul(pa[:, 0:n], lhsT=wbf[:, k, :], rhs=uT[:, k, 1 + t0:1 + t0 + n],
                                                      start=(k == 0), stop=(k == 7)), reads=[wkey], writes=["rpsA0"])
                    for k in range(8):
                        S.op("pe", lambda e: e.matmul(pb[:, 0:n], lhsT=wbf[:, k, :], rhs=uT[:, k, t0:t0 + n],
                                                      start=(k == 0), stop=(k == 7)), reads=[wkey], writes=["rpsA1"])
                    tx = tmpx[slot % 2]
                    S.op("act", lambda e: e.activation(out=tx[:, 0:n], in_=pb[:, 0:n], func=AF.Copy,
                                                       scale=cols[:, MU + fcol:MU + fcol + 1]),
                         reads=["rpsA1"], writes=[("tmpx", slot % 2)])
                    S.op("dve", lambda e: e.scalar_tensor_tensor(out=out_ap, in0=pa[:, 0:n],
                                                                scalar=dcol[:, OMM + fcol:OMM + fcol + 1], in1=tx[:, 0:n],
                                                                op0=ALU.mult, op1=ALU.add),
                         reads=["rpsA0", ("tmpx", slot % 2)], writes=[okey])

                load_w(wst[0], wl_bf, 8 + 12, "l")
                for tb in range(16):
                    t0 = tb * 256
                    proj_xs1(wl_bf, ("wbf", "l"), 12, t0, 256, xsl[:], "xsl", 3, tb == 0)
                    S.op("act", lambda e: e.activation(out=twal[0:64, t0:t0 + 256], in_=xsl[0:64, :], func=AF.Tanh),
                         reads=["xsl"], writes=[("twal", tb)])
                    S.op("pool", lambda e: e.tensor_copy(out=twal[64:128, t0:t0 + 256], in_=xsl[64:128, :]),
                         reads=["xsl"], writes=[("twal2", tb)])
                load_w(wst[0], wl_bf, 8 + 13, "l")
                for tb in range(16):
                    t0 = tb * 256
                    proj_xs1(wl_bf, ("wbf", "l"), 13, t0, 256, xsl[:], "xsl", 3, tb == 0)
                    S.op("act", lambda e: e.activation(out=sg[:, t0:t0 + 256], in_=xsl[:], func=AF.Sigmoid),
                         reads=["xsl"], writes=[("sg", tb)])

                def wk(name, shape=(128, NB), dt=F32):
                    return sbt(ph, name, list(shape), dt)
                R2 = [wk("r_0"), wk("r_1")]; k0 = wk("k0"); V2 = [wk("v_0"), wk("v_1")]
                sgm = wk("sgm"); a_ = wk("a_"); GG2 = [wk("g_0"), wk("g_1")]
                cum = wk("cum"); cex = wk("cex"); dend = cex
                EIN2 = [wk("Ein0"), wk("Ein1")]; Eex = wk("Eex"); Einv = wk("Einv"); Eend = wk("Eend")
                kk = wk("kk"); rinv = wk("rinv"); kkn = wk("kkn")
                t1 = wk("t1"); kk2 = t1; K22 = [wk("k2_0"), wk("k2_1")]; bb = wk("bb"); rkr = wk("rkr")
                ysc = wk("ysc"); ysq = wk("ysq"); gmean = wk("gmean"); gvar = wk("gvar"); gtmp = wk("gtmp")
                yob = [sbt(ph, f"ryo{i}", [128, NB], BF16) for i in range(2)]
                PADS = [[wk(f"{nm}{pb}", (128, NCH, 128)) for nm in ("AP_", "BP_", "KP_", "RP_", "BC_", "KC_", "VP_")] for pb in range(2)]
                for pb_ in range(2):
                    for arr in PADS[pb_]:
                        S.op("pool", lambda e: e.memset(arr[:], 0.0), writes=[("padinit", arr.name)])
                tok = [wk(f"tok{c}", (128, 4, 128)) for c in range(NCH)]
                gram = [wk(f"gram{c}", (128, 4, 128)) for c in range(NCH)]
                NTm = [wk(f"NTm{c}", (128, 128)) for c in range(NCH)]
                Pm = [[wk(f"Pm{c}_{i}", (128, 2, 128)) for i in range(2)] for c in range(NCH)]
                Xm = [[wk(f"Xm{c}_{i}", (128, 128)) for i in range(2)] for c in range(NCH)]
                AW = [wk(f"AW{c}", (128, 2, 128)) for c in range(NCH)]
                AU = [wk(f"AU{c}", (128, 2, 128)) for c in range(NCH)]
                PTs = [wk(f"PTs{c}", (128, 128)) for c in range(NCH)]
                Qm = [wk(f"Qm{c}", (128, 128)) for c in range(NCH)]
                RH = [wk(f"RH{c}", (128, 128)) for c in range(NCH)]
                nwb = [0]
                wa2f = tok[0][:].rearrange("p a b -> p (a b)")
                g2f = gram[0][:].rearrange("p a b -> p (a b)")
                S.dma("sp", wa2f, wa2_d, writes=[("tok", 0)])
                S.dma("sp", g2f, g2_d, writes=[("gram", 0)])
                S.op("dve", lambda e: e.tensor_copy(out=wa2b[:], in_=wa2f), reads=[("tok", 0)], writes=["wa2b"])
                S.op("dve", lambda e: e.tensor_copy(out=g2b[:], in_=g2f), reads=[("gram", 0)], writes=["g2b"])
                STs = [wk(f"ST{i}", (128, 128)) for i in range(2)]

                for hp in range(4):
                    for j3, f in enumerate((8 + hp, 12 + hp, 16 + hp)):
                        load_w(wst[j3 % 2], wr_bf[j3], 8 + (f - 8), ("r", j3))
                    S.op("pool", lambda e: e.memset(STs[0][:], 0.0), writes=[("ST", 0)])
                    cs = slice(hp * 128, (hp + 1) * 128)
                    nch = [0]
                    def rw_prep(blk):
                        issue_casts(1)
                        pb = blk % 2
                        t0 = blk * NB
                        r_, k2, v_, g_, Ein = R2[pb], K22[pb], V2[pb], GG2[pb], EIN2[pb]
                        AP_, BP_, KP_, RP_, BC_, KC_, VP_ = PADS[pb]
                        padkeys = [(nm, pb, hh) for nm in ("AP_", "BP_", "KP_", "RP_", "BC_", "KC_", "VP_") for hh in range(2)]
                        pm, pm1 = psM[0], psM[1]
                        proj_xs1(wr_bf[0], ("wbf", ("r", 0)), hp, t0, NB, r_[:], ("r_", pb), 0, blk == 0)
                        proj_xs1(wr_bf[1], ("wbf", ("r", 1)), 4 + hp, t0, NB, k0[:], "k0", 1, blk == 0)
                        proj_xs1(wr_bf[2], ("wbf", ("r", 2)), 8 + hp, t0, NB, v_[:], ("v_", pb), 2, blk == 0)
                        S.op("pe", lambda e: e.matmul(pm[:, 0:NB], lhsT=wa2b[0:64, cs], rhs=twal[0:64, t0:t0 + NB], start=True, stop=True),
                             reads=["wa2b"], writes=["rpsM0"])
                        S.op("act", lambda e: e.activation(out=sgm[:], in_=pm[:, 0:NB], func=AF.Sigmoid, bias=cols[:, W0 + hp:W0 + hp + 1]),
                             reads=["rpsM0"], writes=["sgm"])
                        S.op("pe", lambda e: e.matmul(pm1[:, 0:NB], lhsT=wa2b[64:128, cs], rhs=twal[64:128, t0:t0 + NB], start=True, stop=True),
                             reads=["wa2b"], writes=["rpsM1"])
                        S.op("act", lambda e: e.activation(out=a_[:], in_=pm1[:, 0:NB], func=AF.Sigmoid, bias=cols[:, A0 + hp:A0 + hp + 1]),
                             reads=["rpsM1"], writes=["a_"])
                        S.op("pe", lambda e: e.matmul(pm[:, 0:NB], lhsT=g2b[:, cs], rhs=sg[:, t0:t0 + NB], start=True, stop=True),
                             reads=["g2b"], writes=["rpsM0"])
                        S.op("act", lambda e: e.copy(out=g_[:], in_=pm[:, 0:NB]), reads=["rpsM0"], writes=[("g_", pb)])
                        S.op("dve", lambda e: e.tensor_tensor_scan(out=cum[:], data0=cmask[:], data1=sgm[:], initial=0.0,
                                                                  op0=ALU.mult, op1=ALU.add), reads=["sgm", "cmask"], writes=["cum"])
                        S.op("pool", lambda e: e.tensor_sub(out=cex[:], in0=cum[:], in1=sgm[:]), reads=["cum", "sgm"], writes=["cex"])
                        S.op("act", lambda e: e.activation(out=Eex[:], in_=cex[:], func=AF.Exp, scale=-C0), reads=["cex"], writes=["Eex"])
                        S.op("act", lambda e: e.activation(out=Ein[:], in_=cum[:], func=AF.Exp, scale=-C0), reads=["cum"], writes=[("Ein", pb)])
                        S.op("act", lambda e: e.activation(out=Einv[:], in_=cum[:], func=AF.Exp, scale=C0), reads=["cum"], writes=["Einv"])
                        cum3 = cum[:].rearrange("p (c k) -> p c k", k=64)
                        S.op("dve", lambda e: e.tensor_sub(out=dend[:].rearrange("p (c k) -> p c k", k=64),
                                                          in0=cum3[:, :, 63:64].to_broadcast([128, NCH, 64]), in1=cum3),
                             reads=["cum"], writes=["cex"])
                        S.op("act", lambda e: e.activation(out=Eend[:], in_=dend[:], func=AF.Exp, scale=-C0), reads=["cex"], writes=["Eend"])
                        S.op("act", lambda e: e.activation(out=kk[:], in_=k0[:], func=AF.Copy, scale=cols[:, KK + hp:KK + hp + 1]),
                             reads=["k0"], writes=["kk"])
                        S.op("pool", lambda e: e.tensor_mul(out=kk2[:], in0=kk[:], in1=kk[:]), reads=["kk"], writes=["t1"])
                        S.op("pe", lambda e: e.matmul(pm1[:, 0:NB], lhsT=blk1[:], rhs=kk2[:], start=True, stop=True),
                             reads=["t1"], writes=["rpsM1"])
                        S.op("act", lambda e: e.activation(out=rinv[:], in_=pm1[:, 0:NB], func=AF.Sqrt), reads=["rpsM1"], writes=["rinv"])
                        S.op("dve", lambda e: e.tensor_scalar_max(out=rinv[:], in0=rinv[:], scalar1=1e-12), reads=["rinv"], writes=["rinv"])
                        S.op("dve", lambda e: e.reciprocal(out=rinv[:], in_=rinv[:]), reads=["rinv"], writes=["rinv"])
                        S.op("dve", lambda e: e.tensor_mul(out=kkn[:], in0=kk[:], in1=rinv[:]), reads=["kk", "rinv"], writes=["kkn"])
                        S.op("pool", lambda e: e.tensor_scalar(out=t1[:], in0=a_[:], scalar1=cols[:, KA + hp:KA + hp + 1],
                                                              scalar2=dcol[:, OMKA + hp:OMKA + hp + 1], op0=ALU.mult, op1=ALU.add),
                             reads=["a_"], writes=["t1"])
                        S.op("pool", lambda e: e.tensor_mul(out=k2[:], in0=k0[:], in1=t1[:]), reads=["k0", "t1"], writes=[("k2", pb)])
                        S.op("dve", lambda e: e.tensor_mul(out=bb[:], in0=kkn[:], in1=a_[:]), reads=["kkn", "a_"], writes=["bb"])
                        for hh in range(2):
                            ps_ = slice(hh * 64, hh * 64 + 64)

                            def v3(tile_):
                                return tile_[ps_, :].rearrange("p (c k) -> p c k", k=64)

                            def o3(arr):
                                return arr[ps_, :, hh * 64:hh * 64 + 64]
                            eA = "dve" if hh == 0 else "pool"
                            eB = "pool" if hh == 0 else "dve"
                            S.op("dve", lambda e: e.scalar_tensor_tensor(out=o3(AP_), in0=v3(kkn), scalar=-1.0, in1=v3(Eex),
                                                                      op0=ALU.mult, op1=ALU.mult), reads=["kkn", "Eex"], writes=[("AP_", pb, hh)])
                            S.op(eB, lambda e: e.tensor_mul(out=o3(BP_), in0=v3(bb), in1=v3(Einv)), reads=["bb", "Einv"], writes=[("BP_", pb, hh)])
                            S.op(eA, lambda e: e.tensor_mul(out=o3(KP_), in0=v3(k2), in1=v3(Einv)), reads=[("k2", pb), "Einv"], writes=[("KP_", pb, hh)])
                            S.op(eB, lambda e: e.tensor_mul(out=o3(RP_), in0=v3(r_), in1=v3(Ein)), reads=[("r_", pb), ("Ein", pb)], writes=[("RP_", pb, hh)])
                            S.op(eA, lambda e: e.tensor_mul(out=o3(BC_), in0=v3(bb), in1=v3(Eend)), reads=["bb", "Eend"], writes=[("BC_", pb, hh)])
                            S.op(eB, lambda e: e.tensor_mul(out=o3(KC_), in0=v3(k2), in1=v3(Eend)), reads=[("k2", pb), "Eend"], writes=[("KC_", pb, hh)])
                            S.op(eA, lambda e: e.tensor_copy(out=o3(VP_), in_=v3(v_)), reads=[("v_", pb)], writes=[("VP_", pb, hh)])
                        padkeys = [(nm, pb, hh) for nm in ("AP_", "BP_", "KP_", "RP_", "BC_", "KC_", "VP_") for hh in range(2)]

                    def rw_chunk(blk):
                        pb = blk % 2
                        t0 = blk * NB
                        r_, k2, v_, g_, Ein = R2[pb], K22[pb], V2[pb], GG2[pb], EIN2[pb]
                        AP_, BP_, KP_, RP_, BC_, KC_, VP_ = PADS[pb]
                        padkeys = [(nm, pb, hh) for nm in ("AP_", "BP_", "KP_", "RP_", "BC_", "KC_", "VP_") for hh in range(2)]
                        pm, pm1 = psM[0], psM[1]
                        NI = NCH
                        for CH in [range(c0, c0 + NI) for c0 in range(0, NCH, NI)]:
                          if True:

                            def wbank():
                                i_ = nwb[0] % 4
                                nwb[0] += 1
                                return Wk[i_], ("rW", i_)
                            for c in CH:
                                wb_, wk_ = wbank()
                                for q_, arr in enumerate((AP_, BC_, KC_, VP_)):
                                    S.op("pe", lambda e: e.transpose(wb_[:, q_ * 128:(q_ + 1) * 128], arr[:, c, :], ident[:]),
                                         reads=padkeys, writes=[wk_])
                                S.op("act", lambda e: e.copy(out=tok[c][:].rearrange("p a b -> p (a b)"), in_=wb_[:]), reads=[wk_], writes=[("tok", c)])
                            for c in CH:
                                wb_, wk_ = wbank()
                                for q_, (l_, r2) in enumerate(((BP_, AP_), (KP_, AP_), (BP_, RP_), (KP_, RP_))):
                                    S.op("pe", lambda e: e.matmul(wb_[:, q_ * 128:(q_ + 1) * 128], lhsT=l_[:, c, :], rhs=r2[:, c, :], start=True, stop=True),
                                         reads=padkeys, writes=[wk_])
                                S.op("dve", lambda e: e.tensor_tensor(out=gram[c][:].rearrange("p a b -> p (a b)"), in0=wb_[:],
                                                                     in1=mask4[:].rearrange("p a b -> p (a b)"), op=ALU.mult),
                                     reads=[wk_], writes=[("gram", c)])
                            for c in CH:
                                wb_, wk_ = wbank()
                                S.op("pe", lambda e: e.matmul(wb_[:, 0:128], lhsT=AP_[:, c, :], rhs=BP_[:, c, :], start=True, stop=True),
                                     reads=padkeys, writes=[wk_])
                                S.op("dve", lambda e: e.tensor_tensor(out=NTm[c][:], in0=wb_[:, 0:128], in1=maskL[:], op=ALU.mult),
                                     reads=[wk_], writes=[("NTm", c)])
                                S.op("pool", lambda e: e.tensor_add(out=Xm[c][0][:], in0=gram[c][:, 0, :], in1=ident[:]),
                                     reads=[("gram", c)], writes=[("Xm", c, 0)])
                            Pc = {c: gram[c][:, 0, :] for c in CH}
                            PTc = {c: NTm[c][:] for c in CH}
                            pk = {c: [("gram", c), ("NTm", c)] for c in CH}
                            xi = 0
                            for step in range(5):
                                for c in CH:
                                    wb_, wk_ = wbank()
                                    pp = Pm[c][step % 2]
                                    if step < 4:
                                        S.op("pe", lambda e: e.matmul(wb_[:, 0:128], lhsT=PTc[c], rhs=Pc[c], start=True, stop=True),
                                             reads=pk[c], writes=[wk_])
                                    S.op("pe", lambda e: e.matmul(wb_[:, 128:256], lhsT=Pc[c], rhs=PTc[c], start=True, stop=True),
                                         reads=pk[c], writes=[wk_])
                                    if step < 4:
                                        S.op("act", lambda e: e.copy(out=pp[:].rearrange("p a b -> p (a b)"), in_=wb_[:, 0:256]),
                                             reads=[wk_], writes=[("Pm", c, step % 2)])
                                    else:
                                        S.op("act", lambda e: e.copy(out=pp[:, 1, :], in_=wb_[:, 128:256]),
                                             reads=[wk_], writes=[("Pm", c, step % 2)])
                                    Pc[c], PTc[c] = pp[:, 0, :], pp[:, 1, :]
                                    pk[c] = [("Pm", c, step % 2)]
                                for c in CH:
                                    wb_, wk_ = wbank()
                                    S.op("pe", lambda e: e.matmul(wb_[:, 0:128], lhsT=PTc[c], rhs=Xm[c][xi][:], start=True, stop=True),
                                         reads=pk[c] + [("Xm", c, xi)], writes=[wk_])
                                    S.op("dve", lambda e: e.tensor_add(out=Xm[c][1 - xi][:], in0=wb_[:, 0:128], in1=Xm[c][xi][:]),
                                         reads=[wk_, ("Xm", c, xi)], writes=[("Xm", c, 1 - xi)])
                                xi = 1 - xi
                            for c in CH:
                                wb_, wk_ = wbank()
                                S.op("pe", lambda e: e.matmul(wb_[:, 0:128], lhsT=gram[c][:, 1, :], rhs=tok[c][:, 3, :], start=True, stop=True),
                                     reads=[("gram", c), ("tok", c)], writes=[wk_])
                                S.op("act", lambda e: e.copy(out=AW[c][:, 1, :], in_=wb_[:, 0:128]), reads=[wk_], writes=[("AW1", c)])
                                S.op("pool", lambda e: e.tensor_copy(out=AW[c][:, 0, :], in_=tok[c][:, 0, :]), reads=[("tok", c)], writes=[("AW0", c)])
                            for c in CH:
                                wb_, wk_ = wbank()
                                S.op("pe", lambda e: e.matmul(wb_[:, 0:256], lhsT=Xm[c][xi][:], rhs=AW[c][:].rearrange("p a b -> p (a b)"), start=True, stop=True),
                                     reads=[("Xm", c, xi), ("AW0", c), ("AW1", c)], writes=[wk_])
                                S.op("act", lambda e: e.copy(out=AU[c][:].rearrange("p a b -> p (a b)"), in_=wb_[:, 0:256]),
                                     reads=[wk_], writes=[("AU", c)])
                            for c in CH:
                                wb_, wk_ = wbank()
                                Ah, Uh = AU[c][:, 0, :], AU[c][:, 1, :]
                                BCt, KCt, Vt = tok[c][:, 1, :], tok[c][:, 2, :], tok[c][:, 3, :]
                                gcol = Ein[:, c * 64 + 63:c * 64 + 64]
                                S.op("pe", lambda e: e.matmul(wb_[:, 0:128], lhsT=Ah, rhs=BCt, start=True, stop=True),
                                     reads=[("AU", c), ("tok", c)], writes=[wk_])
                                S.op("pe", lambda e: e.matmul(wb_[:, 128:256], lhsT=BCt, rhs=Uh, start=True, stop=False),
                                     reads=[("AU", c), ("tok", c)], writes=[wk_])
                                S.op("pe", lambda e: e.matmul(wb_[:, 128:256], lhsT=KCt, rhs=Vt, start=False, stop=True),
                                     reads=[("tok", c)], writes=[wk_])
                                S.op("pe", lambda e: e.matmul(wb_[:, 256:384], lhsT=Ah, rhs=gram[c][:, 2, :], start=True, stop=True),
                                     reads=[("AU", c), ("gram", c)], writes=[wk_])
                                S.op("dve", lambda e: e.scalar_tensor_tensor(out=PTs[c][:], in0=ident[:], scalar=gcol, in1=wb_[:, 0:128],
                                                                            op0=ALU.mult, op1=ALU.add), reads=[wk_, ("Ein", pb)], writes=[("PTs", c)])
                                S.op("act", lambda e: e.copy(out=Qm[c][:], in_=wb_[:, 128:256]), reads=[wk_], writes=[("Qm", c)])
                                S.op("dve", lambda e: e.tensor_add(out=RH[c][:], in0=wb_[:, 256:384], in1=RP_[:, c, :]),
                                     reads=[wk_] + padkeys, writes=[("RH", c)])
                            for c in CH:
                                cur = STs[nch[0] % 2]
                                nxt = STs[(nch[0] + 1) % 2]
                                kcur, knxt = ("ST", nch[0] % 2), ("ST", (nch[0] + 1) % 2)
                                wb2_, wk2_ = wbank()
                                S.op("pe", lambda e: e.matmul(wb2_[:, 0:128], lhsT=PTs[c][:], rhs=cur[:], start=True, stop=True),
                                     reads=[("PTs", c), kcur], writes=[wk2_])
                                S.op("dve", lambda e: e.tensor_add(out=nxt[:], in0=wb2_[:, 0:128], in1=Qm[c][:]),
                                     reads=[wk2_, ("Qm", c)], writes=[knxt])
                                wb_, wk_ = wbank()
                                S.op("pe", lambda e: e.matmul(wb_[:, 0:128], lhsT=cur[:], rhs=RH[c][:], start=True, stop=False),
                                     reads=[kcur, ("RH", c)], writes=[wk_])
                                S.op("pe", lambda e: e.matmul(wb_[:, 0:128], lhsT=AU[c][:, 1, :], rhs=gram[c][:, 2, :], start=False, stop=False),
                                     reads=[("AU", c), ("gram", c)], writes=[wk_])
                                S.op("pe", lambda e: e.matmul(wb_[:, 0:128], lhsT=tok[c][:, 3, :], rhs=gram[c][:, 3, :], start=False, stop=True),
                                     reads=[("tok", c), ("gram", c)], writes=[wk_])
                                S.op("act", lambda e: e.copy(out=ysc[0:64, c * 64:c * 64 + 64], in_=wb_[0:64, 0:64]),
                                     reads=[wk_], writes=[("ysc", 0, c)])
                                S.op("act", lambda e: e.copy(out=ysc[64:128, c * 64:c * 64 + 64], in_=wb_[64:128, 64:128]),
                                     reads=[wk_], writes=[("ysc", 1, c)])
                                nch[0] += 1

                    def rw_gn(blk):
                        pb = blk % 2
                        t0 = blk * NB
                        r_, k2, v_, g_, Ein = R2[pb], K22[pb], V2[pb], GG2[pb], EIN2[pb]
                        AP_, BP_, KP_, RP_, BC_, KC_, VP_ = PADS[pb]
                        padkeys = [(nm, pb, hh) for nm in ("AP_", "BP_", "KP_", "RP_", "BC_", "KC_", "VP_") for hh in range(2)]
                        pm, pm1 = psM[0], psM[1]
                        ykeys = [("ysc", hh, c) for hh in range(2) for c in range(NCH)]
                        if "yscan" in debug:
                            S.dma("sp", dbg_out("yscan", [512, T])[hp * 128:(hp + 1) * 128, t0:t0 + NB] if "yscan" not in dbg else
                                  dbg["yscan"][hp * 128:(hp + 1) * 128, t0:t0 + NB], ysc[:], reads=ykeys, writes=[("dbg_yscan", hp, blk)])
                        S.op("act", lambda e: e.activation(out=ysq[:], in_=ysc[:], func=AF.Square), reads=ykeys, writes=["ysq"])
                        S.op("pe", lambda e: e.matmul(pm[:, 0:NB], lhsT=blk1[:], rhs=ysc[:], start=True, stop=True), reads=ykeys, writes=["rpsM0"])
                        S.op("pe", lambda e: e.matmul(pm1[:, 0:NB], lhsT=blk1[:], rhs=ysq[:], start=True, stop=True), reads=["ysq"], writes=["rpsM1"])
                        S.op("act", lambda e: e.mul(out=gmean[:], in_=pm[:, 0:NB], mul=1.0 / 64), reads=["rpsM0"], writes=["gmean"])
                        S.op("pool", lambda e: e.tensor_mul(out=gtmp[:], in0=gmean[:], in1=gmean[:]), reads=["gmean"], writes=["gtmp"])
                        S.op("dve", lambda e: e.scalar_tensor_tensor(out=gvar[:], in0=pm1[:, 0:NB], scalar=1.0 / 64, in1=gtmp[:],
                                                                    op0=ALU.mult, op1=ALU.subtract), reads=["rpsM1", "gtmp"], writes=["gvar"])
                        S.op("act", lambda e: e.activation(out=gvar[:], in_=gvar[:], func=AF.Sqrt, bias=64e-5, scale=1.0), reads=["gvar"], writes=["gvar"])
                        S.op("dve", lambda e: e.reciprocal(out=gvar[:], in_=gvar[:]), reads=["gvar"], writes=["gvar"])
                        S.op("dve", lambda e: e.tensor_sub(out=gtmp[:], in0=ysc[:], in1=gmean[:]), reads=ykeys + ["gmean", "gtmp"], writes=["gtmp"])
                        S.op("pool", lambda e: e.tensor_mul(out=gtmp[:], in0=gtmp[:], in1=gvar[:]), reads=["gtmp", "gvar"], writes=["gtmp"])
                        S.op("pool", lambda e: e.tensor_scalar(out=gtmp[:], in0=gtmp[:], scalar1=cols[:, GW + hp:GW + hp + 1],
                                                              scalar2=cols[:, GB + hp:GB + hp + 1], op0=ALU.mult, op1=ALU.add),
                             reads=["gtmp"], writes=["gtmp"])
                        S.op("dve", lambda e: e.scalar_tensor_tensor(out=rkr[:], in0=r_[:], scalar=cols[:, RK + hp:RK + hp + 1], in1=k2[:],
                                                                    op0=ALU.mult, op1=ALU.mult), reads=[("r_", pb), ("k2", pb)], writes=["rkr"])
                        S.op("pe", lambda e: e.matmul(pm[:, 0:NB], lhsT=blk1[:], rhs=rkr[:], start=True, stop=True), reads=["rkr"], writes=["rpsM0"])
                        S.op("dve", lambda e: e.tensor_tensor(out=rkr[:], in0=pm[:, 0:NB], in1=v_[:], op=ALU.mult),
                             reads=["rpsM0", ("v_", pb), "rkr"], writes=["rkr"])
                        S.op("pool", lambda e: e.tensor_add(out=gtmp[:], in0=gtmp[:], in1=rkr[:]), reads=["gtmp", "rkr"], writes=["gtmp"])
                        yo_ = yob[blk % 2]
                        S.op("dve", lambda e: e.tensor_mul(out=yo_[:], in0=gtmp[:], in1=g_[:]), reads=["gtmp", ("g_", pb)], writes=[("ryo", blk % 2)])
                        S.dma("sp", ycat_s[(4 + hp) * 128:(5 + hp) * 128, t0:t0 + NB], yo_[:], reads=[("ryo", blk % 2)],
                              writes=[("ycat", 4 + hp, blk)])
                    nblk_ = T // NB
                    rw_prep(0)
                    for blk in range(nblk_):
                        if blk + 1 < nblk_:
                            rw_prep(blk + 1)
                        rw_chunk(blk)
                        rw_gn(blk)
                S.barrier()
            if "ycat" in debug and stop_after == 4 and not done[0]:
                S.dma("sp", dbg_out("ycat", [1024, T], BF16), ycat_s, writes=["dbg_ycat"])
                S.barrier()
            if stop_after == 4 and not done[0]:
                fin()

        if not done[0]:
          issue_casts(len(cast_jobs))
          with ExitStack() as ph:
            wost = [sbt(ph, f"wost{i}", [128, D]) for i in range(2)]
            wob = sbt(ph, "wob", [128, 8, D], BF16)
            ycb = [sbt(ph, f"ycatb{i}", [128, 8, 128], BF16) for i in range(2)]
            xb = [sbt(ph, f"x3b{i}", [128, D]) for i in range(2)]
            x1b = [sbt(ph, f"x1b{i}", [128, D]) for i in range(2)]
            junk = sbt(ph, "junk3", [128, D])
            xn = [sbt(ph, f"xn3{i}", [128, D]) for i in range(2)]
            u2b = [sbt(ph, f"u2b{i}", [128, 8, 128], BF16) for i in range(2)]
            ss = sbt(ph, "ss3", [128, 32])
            rs = sbt(ph, "rs3", [128, 32])
            pO = [pst(ph, f"pO{i}", [128, 1024]) for i in range(2)]
            pT = [pst(ph, f"pT3{i}", [128, 1024]) for i in range(2)]
            gtm_bc = sbt(ph, "gtm_bc", [128, D])
            build_gate_bc(ph, gtm_bc, 16, pO[0], ("pO", 0, 0))
            for ct in range(8):
                S.dma("sp", wost[ct % 2][:], wout_d[ct * 128:(ct + 1) * 128, :], writes=[("wost", ct % 2)])
                S.op("pool", lambda e: e.tensor_copy(out=wob[:, ct, :], in_=wost[ct % 2][:]), reads=[("wost", ct % 2)], writes=["wob"])
            u2T3 = u2T_s.rearrange("p (k t) -> p k t", k=8)
            def p3_mm(tt):
                b = tt % 2
                ts_ = slice(tt * 128, (tt + 1) * 128)
                S.dma("sp", ycb[b][:], ycat_s[:, ts_].rearrange("(c p) t -> p c t", p=128), writes=[("ycatb", b)])
                S.dma("sp", xb[b][:], x_d[ts_, :], writes=[("x3b", b)])
                for dh in range(2):
                    for ct in range(8):
                        S.op("pe", lambda e: e.matmul(pO[b][:, dh * 512:(dh + 1) * 512], lhsT=ycb[b][:, ct, :],
                                                      rhs=wob[:, ct, dh * 512:(dh + 1) * 512], start=(ct == 0), stop=(ct == 7)),
                             reads=[("ycatb", b), "wob"], writes=[("pO", b, dh)])

            p3_mm(0)
            for tt in range(32):
                b = tt % 2
                ts_ = slice(tt * 128, (tt + 1) * 128)
                if tt + 1 < 32:
                    p3_mm(tt + 1)
                S.op("dve", lambda e: e.tensor_tensor(out=x1b[b][:], in0=pO[b][:], in1=gtm_bc[:], op=ALU.mult),
                     reads=[("pO", b, 0), ("pO", b, 1), "gtm_bc"], writes=[("x1b", b)])
                S.op("pool", lambda e: e.tensor_add(out=x1b[b][:], in0=x1b[b][:], in1=xb[b][:]), reads=[("x1b", b), ("x3b", b)], writes=[("x1b", b)])
                S.dma("sp", x1_s[ts_, :], x1b[b][:], reads=[("x1b", b)], writes=[("x1_s", tt)])
                S.op("act", lambda e: e.activation(out=junk[:], in_=x1b[b][:], func=AF.Square), reads=[("x1b", b)], writes=["junk3"])
                S.op("dve", lambda e: e.reduce_sum(out=ss[:, tt:tt + 1], in_=junk[:], axis=AX.X), reads=["junk3"], writes=[("ss3", tt)])
                S.op("act", lambda e: e.activation(out=rs[:, tt:tt + 1], in_=ss[:, tt:tt + 1], func=AF.Sqrt, bias=1e-6, scale=1.0 / D),
                     reads=[("ss3", tt)], writes=[("rs3", tt)])
                S.op("dve", lambda e: e.reciprocal(out=rs[:, tt:tt + 1], in_=rs[:, tt:tt + 1]), reads=[("rs3", tt)], writes=[("rs3", tt)])
                S.op("act", lambda e: e.activation(out=xn[b][:], in_=x1b[b][:], func=AF.Copy, scale=rs[:, tt:tt + 1]),
                     reads=[("x1b", b), ("rs3", tt)], writes=[("xn3", b)])
                for k in range(8):
                    S.op("pe", lambda e: e.transpose(pT[b][:, k * 128:(k + 1) * 128], xn[b][:, k * 128:(k + 1) * 128], ident[:]),
                         reads=[("xn3", b)], writes=[("pT3", b, k // 4)])
                for k in range(8):
                    aff("act" if k < 4 else "dve", u2b[b][:, k, :], pT[b][:, k * 128:(k + 1) * 128],
                        dcol[:, GSC2 + k:GSC2 + k + 1], modT[:, 24 + k:24 + k + 1], reads=[("pT3", b, k // 4)], writes=[("u2b", b)])
                S.dma("sp", u2T3[:, :, ts_], u2b[b][:], reads=[("u2b", b)], writes=[("u2T_s", tt)])
            S.barrier()
            if "x1" in debug:
                S.dma("sp", dbg_out("x1", [T, D]), x1_s, writes=["dbg_x1"])
                S.dma("sp", dbg_out("u2T", [128, 8 * T], BF16, force=True), u2T_s, writes=["dbg_u2T"])
                S.barrier()
        if stop_after == 5 and not done[0]:
            fin()

        if not done[0]:
          with ExitStack() as ph:
            Wc = sbt(ph, "Wc", [128, 8, 2048], BF16)
            KTs = sbt(ph, "KTs", [128, 16, 128])
            wqs = [sbt(ph, f"wqs{i}", [128, D]) for i in range(2)]
            u2t = [sbt(ph, f"u2t{i}", [128, 8, 128], BF16) for i in range(2)]
            sc2 = [sbt(ph, f"sc_{i}", [128, 16, 128]) for i in range(2)]
            wrk4 = [sbt(ph, f"wrk{i}", [128, 256]) for i in range(4)]
            vals = sbt(ph, "vals", [128, 16, 16])
            idxu = sbt(ph, "idxu", [128, 16, 16], U16)
            idxf = sbt(ph, "idxf", [128, 16, 16])
            cand = sbt(ph, "cand", [128, 8, 256])
            best = sbt(ph, "best", [128, 8, 16])
            posu = sbt(ph, "posu", [128, 8, 16], U16)
            posf = sbt(ph, "posf", [128, 8, 16])
            big = sbt(ph, "big", [128, 8, 16, 16])
            thr16 = sbt(ph, "thr16", [128, 16])
            io16 = sbt(ph, "io16", [128, 16])
            ak = sbt(ph, "ak", [128, 8, 16])
            bk = sbt(ph, "bk", [128, 8, 16])
            ik = sbt(ph, "ik", [128, 128])
            jk = sbt(ph, "jk", [128, 128])
            gk = sbt(ph, "gk", [128, 128])
            zs = sbt(ph, "zs", [128, 8])
            slT = [sbt(ph, f"slT{i}", [128, 3, 128]) for i in range(2)]
            pW = [pst(ph, f"pW{i}", [128, 512]) for i in range(2)]
            pS = [pst(ph, f"pS{i}", [128, 512]) for i in range(4)]
            pTr = pst(ph, "pTr", [128, 512])

            S.dma("sp", KTs[:].rearrange("p a b -> p (a b)"), KT_d, writes=["KTs"])
            S.op("dve", lambda e: e.tensor_scalar_mul(out=thr16[:], in0=iota_f[:, 0:16], scalar1=16.0), writes=["thr16"])
            S.op("dve", lambda e: e.tensor_copy(out=io16[:], in_=iota_f[:, 0:16]), writes=["io16"])
            for hp in range(16):
                S.dma("sp", wqs[hp % 2][:], wqT_d[hp * 128:(hp + 1) * 128, :], writes=[("wqs", hp % 2)])
                for dk in range(8):
                    S.op("pe", lambda e: e.matmul(pW[dk % 2][:, 0:128], lhsT=wqs[hp % 2][:, dk * 128:(dk + 1) * 128], rhs=KTs[:, hp, :],
                                                  start=True, stop=True), reads=[("wqs", hp % 2), "KTs"], writes=[("pW", dk % 2)])
                    S.op("act" if dk % 2 == 0 else "dve",
                         (lambda e: e.copy(out=Wc[:, dk, hp * 128:(hp + 1) * 128], in_=pW[dk % 2][:, 0:128])) if dk % 2 == 0 else
                         (lambda e: e.tensor_copy(out=Wc[:, dk, hp * 128:(hp + 1) * 128], in_=pW[dk % 2][:, 0:128])),
                         reads=[("pW", dk % 2)], writes=["Wc"])
            u2T3 = u2T_s.rearrange("p (k t) -> p k t", k=8)
            def p4_scores(tt):
                b = tt % 2
                ts_ = slice(tt * 128, (tt + 1) * 128)
                S.dma("sp", u2t[b][:], u2T3[:, :, ts_], writes=[("u2t", b)])
                sc_ = sc2[b]
                for q4 in range(4):
                    for dk in range(8):
                        S.op("pe", lambda e: e.matmul(pS[q4][:], lhsT=u2t[b][:, dk, :], rhs=Wc[:, dk, q4 * 512:(q4 + 1) * 512],
                                                      start=(dk == 0), stop=(dk == 7)), reads=[("u2t", b), "Wc"], writes=[("pS", q4)])
                    S.op("act", lambda e: e.copy(out=sc_[:, q4 * 4:(q4 + 1) * 4, :].rearrange("p a b -> p (a b)"), in_=pS[q4][:]),
                         reads=[("pS", q4)], writes=[("sc_", b, q4)])

            p4_scores(0)
            for tt in range(32):
                b = tt % 2
                ts_ = slice(tt * 128, (tt + 1) * 128)
                sc_ = sc2[b]
                if tt + 1 < 32:
                    p4_scores(tt + 1)
                if "scores" in debug and tt == 0:
                    S.dma("sp", dbg_out("scores", [128, 2048]), sc_[:].rearrange("p a b -> p (a b)"),
                          reads=[("sc_", b, q) for q in range(4)], writes=["dbg_scores"])
                NIT = 4
                ALLV = [("vals", g, hf) for g in range(16) for hf in range(2)]
                ALLI = [("idxu", g, hf) for g in range(16) for hf in range(2)]
                ALLB = [("best", h, hf) for h in range(8) for hf in range(2)]
                ALLP = [("posu", h, hf) for h in range(8) for hf in range(2)]
                for g0 in range(0, 16, NIT):
                    grp = range(g0, g0 + NIT)
                    for g16 in grp:
                        S.op("dve", lambda e: e.max(out=vals[:, g16, 0:8], in_=sc_[:, g16, :]), reads=[("sc_", b, g16 // 4)], writes=[("vals", g16, 0)])
                    for g16 in grp:
                        S.op("dve", lambda e: e.max_index(out=idxu[:, g16, 0:8], in_max=vals[:, g16, 0:8], in_values=sc_[:, g16, :]),
                             reads=[("sc_", b, g16 // 4), ("vals", g16, 0)], writes=[("idxu", g16, 0)])
                    for g16 in grp:
                        w_ = wrk4[g16 % NIT]
                        S.op("dve", lambda e: e.match_replace(out=w_[:, 0:128], in_to_replace=vals[:, g16, 0:8], in_values=sc_[:, g16, :],
                                                             imm_value=-1e30), reads=[("sc_", b, g16 // 4), ("vals", g16, 0)], writes=[("wrk", g16 % NIT)])
                    for g16 in grp:
                        w_ = wrk4[g16 % NIT]
                        S.op("dve", lambda e: e.max(out=vals[:, g16, 8:16], in_=w_[:, 0:128]), reads=[("wrk", g16 % NIT)], writes=[("vals", g16, 1)])
                    for g16 in grp:
                        w_ = wrk4[g16 % NIT]
                        S.op("dve", lambda e: e.max_index(out=idxu[:, g16, 8:16], in_max=vals[:, g16, 8:16], in_values=w_[:, 0:128]),
                             reads=[("wrk", g16 % NIT), ("vals", g16, 1)], writes=[("idxu", g16, 1)])
                S.op("pool", lambda e: e.tensor_copy(out=idxf[:], in_=idxu[:]), reads=ALLI, writes=["idxf"])
                v4 = vals[:].rearrange("p (h two) k -> p h two k", two=2)
                i4 = idxf[:].rearrange("p (h two) k -> p h two k", two=2)
                cand4 = cand[:].rearrange("p h (a b) -> p h a b", b=16)
                S.op("dve", lambda e: e.tensor_tensor(out=cand4, in0=v4[:, :, 0, :].unsqueeze(3).to_broadcast([128, 8, 16, 16]),
                                                     in1=v4[:, :, 1, :].unsqueeze(2).to_broadcast([128, 8, 16, 16]), op=ALU.add),
                     reads=ALLV, writes=["cand"])
                for h0 in range(0, 8, NIT):
                    grp = range(h0, h0 + NIT)
                    for h in grp:
                        S.op("dve", lambda e: e.max(out=best[:, h, 0:8], in_=cand[:, h, :]), reads=["cand"], writes=[("best", h, 0)])
                    for h in grp:
                        S.op("dve", lambda e: e.max_index(out=posu[:, h, 0:8], in_max=best[:, h, 0:8], in_values=cand[:, h, :]),
                             reads=["cand", ("best", h, 0)], writes=[("posu", h, 0)])
                    for h in grp:
                        w_ = wrk4[h % NIT]
                        S.op("dve", lambda e: e.match_replace(out=w_[:], in_to_replace=best[:, h, 0:8], in_values=cand[:, h, :],
                                                             imm_value=-1e30), reads=["cand", ("best", h, 0)], writes=[("wrk", h % NIT)])
                    for h in grp:
                        w_ = wrk4[h % NIT]
                        S.op("dve", lambda e: e.max(out=best[:, h, 8:16], in_=w_[:]), reads=[("wrk", h % NIT)], writes=[("best", h, 1)])
                    for h in grp:
                        w_ = wrk4[h % NIT]
                        S.op("dve", lambda e: e.max_index(out=posu[:, h, 8:16], in_max=best[:, h, 8:16], in_values=w_[:]),
                             reads=[("wrk", h % NIT), ("best", h, 1)], writes=[("posu", h, 1)])
                S.op("dve", lambda e: e.tensor_copy(out=posf[:], in_=posu[:]), reads=ALLP, writes=["posf"])
                gk3 = gk[:].rearrange("p (h k) -> p h k", k=16)
                S.op("pool", lambda e: e.tensor_sub(out=gk3, in0=best[:], in1=best[:, :, 0:1].to_broadcast([128, 8, 16])),
                     reads=ALLB, writes=["gk"])
                S.op("act", lambda e: e.activation(out=gk[:], in_=gk[:], func=AF.Exp), reads=["gk"], writes=["gk"])
                S.op("dve", lambda e: e.tensor_reduce(out=zs[:], in_=gk3, axis=AX.X, op=ALU.add), reads=["gk"], writes=["zs"])
                S.op("dve", lambda e: e.reciprocal(out=zs[:], in_=zs[:]), reads=["zs"], writes=["zs"])
                S.op("pool", lambda e: e.tensor_mul(out=gk3, in0=gk3, in1=zs[:].unsqueeze(2).to_broadcast([128, 8, 16])),
                     reads=["gk", "zs"], writes=["gk"])
                S.op("dve", lambda e: e.tensor_tensor(out=big[:], in0=posf[:].unsqueeze(3).to_broadcast([128, 8, 16, 16]),
                                                     in1=thr16[:].unsqueeze(1).unsqueeze(1).to_broadcast([128, 8, 16, 16]), op=ALU.is_ge),
                     reads=["posf", "thr16"], writes=["big"])
                S.op("dve", lambda e: e.tensor_reduce(out=ak[:], in_=big[:, :, :, 1:16], axis=AX.X, op=ALU.add), reads=["big"], writes=["ak"])
                S.op("dve", lambda e: e.scalar_tensor_tensor(out=bk[:], in0=ak[:], scalar=-16.0, in1=posf[:], op0=ALU.mult, op1=ALU.add),
                     reads=["ak", "posf"], writes=["bk"])
                for (sel, half, dst) in ((ak, 0, ik), (bk, 1, jk)):
                    S.op("dve", lambda e: e.tensor_tensor(out=big[:], in0=sel[:].unsqueeze(3).to_broadcast([128, 8, 16, 16]),
                                                         in1=io16[:].unsqueeze(1).unsqueeze(1).to_broadcast([128, 8, 16, 16]), op=ALU.is_equal),
                         reads=[sel.name[2:], "big"], writes=["big"])
                    S.op("dve", lambda e: e.tensor_mul(out=big[:], in0=big[:],
                                                      in1=i4[:, :, half, :].unsqueeze(2).to_broadcast([128, 8, 16, 16])),
                         reads=["big", "idxf"], writes=["big"])
                    S.op("dve", lambda e: e.tensor_reduce(out=dst[:].rearrange("p (h k) -> p h k", k=16), in_=big[:], axis=AX.X, op=ALU.add),
                         reads=["big"], writes=[dst.name[2:]])
                if "route" in debug and tt == 0:
                    S.dma("sp", dbg_out("r_g", [128, 128], force=True), gk[:], reads=["gk"], writes=["dbg_rg"])
                    S.dma("sp", dbg_out("r_i", [128, 128], force=True), ik[:], reads=["ik"], writes=["dbg_ri"])
                    S.dma("sp", dbg_out("r_j", [128, 128], force=True), jk[:], reads=["jk"], writes=["dbg_rj"])
                for q_, src in enumerate((gk, ik, jk)):
                    S.op("pe", lambda e: e.transpose(pTr[:, q_ * 128:(q_ + 1) * 128], src[:], ident[:]), reads=[src.name[2:]], writes=["pTr"])
                S.op("act", lambda e: e.copy(out=slT[b][:].rearrange("p a b -> p (a b)"), in_=pTr[:, 0:384]), reads=["pTr"], writes=[("slT", b)])
                S.dma("sp", gT_s[:, ts_], slT[b][:, 0, :], reads=[("slT", b)], writes=[("gT_s", tt)])
                S.dma("sp", iT_s[:, ts_], slT[b][:, 1, :], reads=[("slT", b)], writes=[("iT_s", tt)])
                S.dma("sp", jT_s[:, ts_], slT[b][:, 2, :], reads=[("slT", b)], writes=[("jT_s", tt)])
            S.barrier()
        if "slots" in debug and not done[0]:
            S.dma("sp", dbg_out("gT", [128, T], force=True), gT_s, writes=["dbg_gT"])
            S.dma("sp", dbg_out("iT", [128, T], force=True), iT_s, writes=["dbg_iT"])
            S.dma("sp", dbg_out("jT", [128, T], force=True), jT_s, writes=["dbg_jT"])
            S.barrier()
        if stop_after == 6 and not done[0]:
            fin()

        if not done[0]:
          with ExitStack() as ph:
            TB = 256
            NPF = 6
            NTK = 8
            Gs = [sbt(ph, f"G{i}", [128, TB, 128], BF16) for i in range(2)]
            u2k = [sbt(ph, f"u2k{i}", [128, 8, TB], BF16) for i in range(2)]
            slg = [sbt(ph, f"slg{i}", [128, TB]) for i in range(2)]
            sli = [sbt(ph, f"sli{i}", [128, TB]) for i in range(2)]
            slj = [sbt(ph, f"slj{i}", [128, TB]) for i in range(2)]
            Pb = [sbt(ph, f"Pb{i}", [128, NTK, 128], BF16) for i in range(2)]
            Qb = [sbt(ph, f"Qb{i}", [128, NTK, 128], BF16) for i in range(2)]
            Ub = [sbt(ph, f"Ub{i}", [128, 8, 128], BF16) for i in range(NPF)]
            Vb = [sbt(ph, f"Vb{i}", [128, D], BF16) for i in range(NPF)]
            hb = [sbt(ph, f"hb{i}", [128, TB], BF16) for i in range(2)]
            Wb = [sbt(ph, f"Wb{i}", [128, TB], BF16) for i in range(2)]
            x1t_ = sbt(ph, "x1t", [128, D]); x1t = [x1t_, x1t_]
            x2t_ = sbt(ph, "x2t", [128, D]); x2t = [x2t_, x2t_]
            fs = sbt(ph, "fs", [128, 2]); fr = sbt(ph, "fr", [128, 2])
            ot_ = sbt(ph, "ot", [128, D]); ot = [ot_, ot_]
            junk = ot_
            pAcc = [pst(ph, f"pAcc{i}", [128, 512]) for i in range(4)]
            pSx = [pst(ph, f"pSx{i}", [128, 512]) for i in range(2)]
            pG = [pst(ph, f"pG{i}", [128, 512]) for i in range(2)]
            gtf_bc = sbt(ph, "gtf_bc", [128, D])
            fing_bc = sbt(ph, "fing_bc", [128, D])
            S.dma("sp", fing_bc[:], fing_d, writes=["fing"])
            build_gate_bc(ph, gtf_bc, 40, pG[0], ("pG", 0))
            u2T3 = u2T_s.rearrange("p (k t) -> p k t", k=8)
            allU = [("Ub", i) for i in range(NCAST)]
            allV = [("Vb", i) for i in range(NCAST)]
            nload = [0]
            ngq = [0]
            nblk = T // TB

            def load_expert(i):
                s_ = nload[0] % NPF
                nload[0] += 1
                S.dma("sp", Ub[s_][:].rearrange("p k j -> p (k j)"), Ub_s[i * 128:(i + 1) * 128, :], reads=allU, writes=[("Ubt", s_)])
                S.dma("sp", Vb[s_][:], Vb_s[i * 128:(i + 1) * 128, :], reads=allV, writes=[("Vbt", s_)])

            def load_block(blk):
                gb = blk % 2
                t0 = blk * TB
                S.dma("sp", u2k[gb][:], u2T3[:, :, t0:t0 + TB], writes=[("u2k", gb)])
                S.dma("sp", slg[gb][:], gT_s[:, t0:t0 + TB], writes=[("slg", gb)])
                S.dma("sp", sli[gb][:], iT_s[:, t0:t0 + TB], writes=[("sli", gb)])
                S.dma("sp", slj[gb][:], jT_s[:, t0:t0 + TB], writes=[("slj", gb)])

            def gbuild_dve(blk, tb):
                gb = blk % 2
                pbuf = (blk * (TB // NTK) + tb) % 2
                ts_ = slice(tb * NTK, (tb + 1) * NTK)
                io_bc = iota_f[:].unsqueeze(1).to_broadcast([128, NTK, 128])
                S.op("dve", lambda e: e.tensor_tensor(out=Pb[pbuf][:], in0=io_bc,
                                                     in1=sli[gb][:, ts_].unsqueeze(2).to_broadcast([128, NTK, 128]), op=ALU.is_equal),
                     reads=[("sli", gb)], writes=[("Pb", pbuf)])
                S.op("pool", lambda e: e.tensor_tensor(out=Pb[pbuf][:], in0=Pb[pbuf][:],
                                                      in1=slg[gb][:, ts_].unsqueeze(2).to_broadcast([128, NTK, 128]), op=ALU.mult),
                     reads=[("slg", gb), ("Pb", pbuf)], writes=[("Pb", pbuf)])
                S.op("dve", lambda e: e.tensor_tensor(out=Qb[pbuf][:], in0=io_bc,
                                                     in1=slj[gb][:, ts_].unsqueeze(2).to_broadcast([128, NTK, 128]), op=ALU.is_equal),
                     reads=[("slj", gb)], writes=[("Qb", pbuf)])

            def gbuild_pe(blk, tb):
                gb = blk % 2
                pbuf = (blk * (TB // NTK) + tb) % 2
                for tq in range(NTK // 4):
                    pgi = ngq[0] % 2
                    ngq[0] += 1
                    pg = pG[pgi]
                    for t4 in range(4):
                        tl = tq * 4 + t4
                        S.op("pe", lambda e: e.matmul(pg[:, t4 * 128:(t4 + 1) * 128], lhsT=Qb[pbuf][:, tl, :], rhs=Pb[pbuf][:, tl, :],
                                                      start=True, stop=True), reads=[("Pb", pbuf), ("Qb", pbuf)], writes=[("pG", pgi)])
                    tg = tb * NTK + tq * 4
                    S.op("act", lambda e: e.copy(out=Gs[gb][:, tg:tg + 4, :].rearrange("p a b -> p (a b)"), in_=pg[:]),
                         reads=[("pG", pgi)], writes=[("G", gb, tb)])

            def gbuild(blk, tb):
                gbuild_dve(blk, tb)
                gbuild_pe(blk, tb)

            def s_stage(blk, i):
                gb = blk % 2
                s_ = (blk * 128 + i) % NPF
                b2 = i % 2
                for dk in range(8):
                    S.op("pe", lambda e: e.matmul(pSx[b2][:, 0:TB], lhsT=Ub[s_][:, dk, :], rhs=u2k[gb][:, dk, :], start=(dk == 0), stop=(dk == 7)),
                         reads=[("Ubt", s_), ("u2k", gb)], writes=[("pSx", b2)])
                S.op("act", lambda e: e.activation(out=hb[b2][:], in_=pSx[b2][:, 0:TB], func=AF.Gelu), reads=[("pSx", b2)], writes=[("hb", b2)])
                S.op("dve", lambda e: e.tensor_tensor(out=Wb[b2][:], in0=hb[b2][:], in1=Gs[gb][:, :, i], op=ALU.mult),
                     reads=[("hb", b2)] + ([("G", gb, tb) for tb in range(TB // NTK)] if i < 2 else []), writes=[("Wb", b2)])

            def v_stage(blk, i):
                s_ = (blk * 128 + i) % NPF
                b2 = i % 2
                for tt2 in range(TB // 128):
                    for dh in range(2):
                        S.op("pe", lambda e: e.matmul(pAcc[tt2 * 2 + dh][:], lhsT=Wb[b2][:, tt2 * 128:(tt2 + 1) * 128],
                                                      rhs=Vb[s_][:, dh * 512:(dh + 1) * 512], start=(i == 0), stop=(i == 127)),
                             reads=[("Wb", b2), ("Vbt", s_)], writes=[("pAcc", tt2 * 2 + dh)])

            def epilogue(blk):
                t0 = blk * TB
                for tt2 in range(TB // 128):
                    tsl = slice(t0 + tt2 * 128, t0 + (tt2 + 1) * 128)
                    S.dma("sp", x1t[tt2][:], x1_s[tsl, :], writes=["x1t"])
                    for dh in range(2):
                        S.op("dve", lambda e: e.tensor_tensor(out=x2t[tt2][:, dh * 512:(dh + 1) * 512], in0=pAcc[tt2 * 2 + dh][:],
                                                             in1=gtf_bc[:, dh * 512:(dh + 1) * 512], op=ALU.mult),
                             reads=[("pAcc", tt2 * 2 + dh), "gtf_bc"], writes=["x2t"])
                    if "peer" in debug:
                        if "peer" not in dbg:
                            dbg_out("peer", [T, D], force=True)
                        S.dma("sp", dbg["peer"][tsl, :], x2t[tt2][:], reads=["x2t"], writes=[("dbg_peer", blk, tt2)])
                    S.op("pool", lambda e: e.tensor_add(out=x2t[tt2][:], in0=x2t[tt2][:], in1=x1t[tt2][:]),
                         reads=["x2t", "x1t"], writes=["x2t"])
                    S.op("act", lambda e: e.activation(out=junk[:], in_=x2t[tt2][:], func=AF.Square), reads=["x2t"], writes=["ot"])
                    S.op("dve", lambda e: e.reduce_sum(out=fs[:, tt2:tt2 + 1], in_=junk[:], axis=AX.X), reads=["ot"], writes=[("fs", tt2)])
                    S.op("act", lambda e: e.activation(out=fr[:, tt2:tt2 + 1], in_=fs[:, tt2:tt2 + 1], func=AF.Sqrt, bias=1e-6, scale=1.0 / D),
                         reads=[("fs", tt2)], writes=[("fr", tt2)])
                    S.op("dve", lambda e: e.reciprocal(out=fr[:, tt2:tt2 + 1], in_=fr[:, tt2:tt2 + 1]), reads=[("fr", tt2)], writes=[("fr", tt2)])
                    S.op("dve", lambda e: e.scalar_tensor_tensor(out=ot[tt2][:], in0=x2t[tt2][:], scalar=fr[:, tt2:tt2 + 1], in1=fing_bc[:],
                                                                op0=ALU.mult, op1=ALU.mult),
                         reads=["x2t", ("fr", tt2), "fing"], writes=["ot"])
                    S.dma("sp", out_d[tsl, :], ot[tt2][:], reads=["ot"], writes=[("out", blk, tt2)])

            load_block(0)
            for i in range(NPF - 1):
                load_expert(i)
            for tb in range(TB // NTK):
                gbuild(0, tb)
            for blk in range(nblk):
                if blk + 1 < nblk:
                    load_block(blk + 1)
                s_stage(blk, 0)
                s_stage(blk, 1)
                for i in range(128):
                    nxt_e = blk * 128 + i + NPF - 1
                    if nxt_e < nblk * 128:
                        load_expert(nxt_e % 128)
                    v_stage(blk, i)
                    if i + 2 < 128:
                        s_stage(blk, i + 2)
                    if blk + 1 < nblk and i % 4 == 0:
                        if i >= 4:
                            gbuild_pe(blk + 1, i // 4 - 1)
                        gbuild_dve(blk + 1, i // 4)
                if blk + 1 < nblk:
                    gbuild_pe(blk + 1, TB // NTK - 1)
                epilogue(blk)
            S.barrier()
            S.finish([])
    print(f"[kernel] instructions={S.ninst} waits={S.nwaits}")
    return nc, dbg


def col(v, n):
    return np.ascontiguousarray(np.asarray(v, np.float32).reshape(n, 128).T)


def prep_inputs(inp):
    f = lambda k: np.asarray(inp[k], np.float32)
    cols = np.zeros((128, NCOL), np.float32)
    cols[:, G1:G1 + 8] = col(f("norm_mix_g")[0], 8)
    cols[:, G2:G2 + 8] = col(f("norm_ffn_g")[0], 8)
    cols[:, CW:CW + 124] = f("conv_dw_w")[0].reshape(31, 4, 128).transpose(2, 1, 0).reshape(128, 124)
    cols[:, CB:CB + 4] = col(f("conv_dw_b")[0], 4)
    cols[:, LW:LW + 4] = col(f("conv_ln_w")[0], 4)
    cols[:, LB:LB + 4] = col(f("conv_ln_b")[0], 4)
    cols[:, MU:MU + 14] = col(f("rwkv_mu")[0], 14)
    cols[:, W0:W0 + 4] = col(f("rwkv_w0")[0], 4)
    cols[:, A0:A0 + 4] = col(f("rwkv_a0")[0], 4)
    cols[:, KK:KK + 4] = col(f("rwkv_k_k")[0], 4)
    cols[:, KA:KA + 4] = col(f("rwkv_k_a")[0], 4)
    cols[:, RK:RK + 4] = col(f("rwkv_r_k")[0].reshape(-1), 4)
    cols[:, GW:GW + 4] = col(f("rwkv_gn_w")[0], 4)
    cols[:, GB:GB + 4] = col(f("rwkv_gn_b")[0], 4)
    shared = {
        "ada_w": np.ascontiguousarray(f("ada_w")[0]),
        "ada_b_col": col(f("ada_b")[0], 48),
        "cols": cols,
        "final_g_bc": np.ascontiguousarray(np.broadcast_to(f("final_g")[None, :], (128, D))),
        "w_in": np.ascontiguousarray(f("w_in")[0]),
        "wa2": np.ascontiguousarray(np.concatenate([f("rwkv_w2")[0], f("rwkv_a2")[0]], axis=0)),
        "g2": np.ascontiguousarray(f("rwkv_g2")[0]),
        "w_out": np.ascontiguousarray(f("w_out")[0]),
        "wqT": np.ascontiguousarray(f("peer_w_q")[0].T),
        "KT": np.ascontiguousarray(f("peer_sub_keys")[0].reshape(16, 128, 128).transpose(2, 0, 1).reshape(128, 2048)),
        "UTt": np.ascontiguousarray(f("peer_u")[0].reshape(128, 128, 8, 128).transpose(0, 3, 2, 1).reshape(128 * 128, 1024)),
        "V": np.ascontiguousarray(f("peer_v")[0]),
    }
    maps = []
    for b in range(NCORES):
        m = dict(shared)
        m["x"] = np.ascontiguousarray(f("x")[b])
        m["c_col"] = col(f("c")[b], 8)
        maps.append(m)
    return maps


_NC_CACHE = {}


def kernel(**inputs):
    maps = prep_inputs(inputs)
    if "nc" not in _NC_CACHE:
        _NC_CACHE["nc"] = build_nc()[0]
    nc = _NC_CACHE["nc"]
    res = run_bass_kernel_spmd(nc, maps, core_ids=list(range(NCORES)))
    return np.stack([np.asarray(r["out"], np.float32) for r in res.results], axis=0)
```

```python
import numpy as np
from contextlib import ExitStack
import concourse.bass as bass
import concourse.mybir as mybir
from concourse.bass_utils import run_bass_kernel_spmd

F32 = mybir.dt.float32
BF16 = mybir.dt.bfloat16
U16 = mybir.dt.uint16
I32 = mybir.dt.int32
AF = mybir.ActivationFunctionType
ALU = mybir.AluOpType
AX = mybir.AxisListType

T = 4096
D = 1024
NCORES = 8
C0 = 0.6065306597126334

G1, G2, CW, CB, LW, LB, MU, W0, A0, KK, KA, RK, GW, GB, NCOL = (
    0, 8, 16, 140, 144, 148, 152, 166, 170, 174, 178, 182, 186, 190, 194)


class Sync:
    NDMA = 12

    def __init__(self, nc, stack):
        self.nc = nc
        self.eng = {"pe": nc.tensor, "act": nc.scalar, "dve": nc.vector,
                    "pool": nc.gpsimd, "sp": nc.sync}
        self.sem, self.cnt = {}, {}
        self.semobj = {}
        for e in self.eng:
            self.sem[e] = stack.enter_context(nc.semaphore("s_" + e))
            self.semobj[self.sem[e].name] = self.sem[e]
            self.cnt[e] = 0
        self.dsem, self.dnext = {}, {}
        self.eng["cast"] = nc.gpsimd
        for q in ("sp", "pool", "act", "cast"):
            self.dsem[q] = [[stack.enter_context(nc.semaphore(f"d_{q}{i}")), 0]
                            for i in range(32 if q == "cast" else self.NDMA)]
            for s, _ in self.dsem[q]:
                self.semobj[s.name] = s
            self.dnext[q] = 0
        self.seen = {e: {} for e in self.eng}
        self.snap = {}
        self.lastw = {}
        self.readers = {}
        self.nwaits = 0
        self.ninst = 0

    def _need(self, e, ev, waits):
        if ev is None:
            return
        name, val = ev
        if self.seen[e].get(name, 0) >= val:
            return
        if e == "pe" and name == self.sem["pe"].name:
            return
        if waits.get(name, 0) < val:
            waits[name] = val

    def _do_waits(self, e, waits):
        for name, val in waits.items():
            if self.seen[e].get(name, 0) >= val:
                continue
            self.eng[e].wait_ge(self.semobj[name], val)
            self.nwaits += 1
            sn = self.snap.get((name, val))
            se = self.seen[e]
            if sn:
                for k, v in sn.items():
                    if se.get(k, 0) < v:
                        se[k] = v
            if se.get(name, 0) < val:
                se[name] = val

    def _deps(self, e, reads, writes):
        waits = {}
        for k in reads:
            self._need(e, self.lastw.get(k), waits)
        for k in writes:
            self._need(e, self.lastw.get(k), waits)
            for ev in self.readers.get(k, ()):
                self._need(e, ev, waits)
        self._do_waits(e, waits)

    def _record(self, e, ev, reads, writes):
        for k in reads:
            self.readers.setdefault(k, []).append(ev)
        for k in writes:
            self.lastw[k] = ev
            self.readers[k] = []
        self.snap[ev] = dict(self.seen[e])

    EXCL = {"psmod", "psb", "pT", "psA", "psB", "psC", "ps1", "ps2", "rpsA0", "rpsA1", "rpsM0", "rpsM1",
            "rW", "pO", "pT3", "pW", "pS", "pTr", "pAcc", "pSx", "pG"}

    def op(self, e, fn, reads=(), writes=()):
        ex = [k for k in reads if (k if isinstance(k, str) else k[0]) in self.EXCL]
        if ex:
            writes = list(writes) + ex
        self._deps(e, reads, writes)
        ins = fn(self.eng[e])
        self.cnt[e] += 1
        s = self.sem[e]
        ins.then_inc(s, 1)
        ev = (s.name, self.cnt[e])
        self._record(e, ev, reads, writes)
        self.ninst += 1
        return ev

    def dma(self, q, out, in_, reads=(), writes=(), **kw):
        pool = self.dsem[q]
        slot = pool[self.dnext[q] % len(pool)]
        self.dnext[q] += 1
        s = slot[0]
        waits = {}
        if slot[1] > 0:
            self._need(q, (s.name, slot[1]), waits)
        self._do_waits(q, waits)
        self._deps(q, reads, writes)
        ins = self.eng[q].dma_start(out=out, in_=in_, **kw)
        slot[1] += 16
        ins.then_inc(s, 16)
        ev = (s.name, slot[1])
        self._record(q, ev, reads, writes)
        self.ninst += 1
        return ev

    def barrier(self):
        evs = []
        for e in self.cnt:
            if self.cnt[e]:
                evs.append((self.sem[e].name, self.cnt[e]))
        for q in self.dsem:
            if q == "cast":
                continue
            for s, c in self.dsem[q]:
                if c:
                    evs.append((s.name, c))
        for e in self.eng:
            if e == "cast":
                continue
            waits = {}
            for ev in evs:
                self._need(e, ev, waits)
            self._do_waits(e, waits)
        keep = lambda k: isinstance(k, tuple) and k[0] in ("Ub", "Vb")
        self.lastw = {k: v for k, v in self.lastw.items() if keep(k)}
        self.readers = {k: v for k, v in self.readers.items() if keep(k)}

    def finish(self, keys):
        waits = {}
        for k in keys:
            self._need("sp", self.lastw.get(k), waits)
        self._do_waits("sp", waits)


def build_nc(debug=(), stop_after=None):
    nc = bass.Bass("TRN2", target_bir_lowering=False)

    def din(name, shape, dt=F32):
        return nc.dram_tensor(name, list(shape), dt, kind="ExternalInput").ap()

    def dscr(name, shape, dt=F32):
        return nc.dram_tensor(name, list(shape), dt, kind="Internal").ap()

    x_d = din("x", [T, D])
    ccol_d = din("c_col", [128, 8])
    adaw_d = din("ada_w", [D, 6 * D])
    adab_d = din("ada_b_col", [128, 48])
    cols_d = din("cols", [128, NCOL])
    fing_d = din("final_g_bc", [128, D])
    win_d = din("w_in", [D, 2816])
    wa2_d = din("wa2", [128, 512])
    g2_d = din("g2", [128, 512])
    wout_d = din("w_out", [D, D])
    wqT_d = din("wqT", [2048, D])
    KT_d = din("KT", [128, 16 * 128])
    UT_d = din("UTt", [128 * 128, 1024])
    V_d = din("V", [16384, D])
    out_d = nc.dram_tensor("out", [T, D], F32, kind="ExternalOutput").ap()

    yc_s = dscr("yc_s", [512, T])
    ycat_s = dscr("ycat_s", [1024, T], BF16)
    x1_s = dscr("x1_s", [T, D])
    u2T_s = dscr("u2T_s", [128, 8 * T], BF16)
    gT_s = dscr("gT_s", [128, T])
    iT_s = dscr("iT_s", [128, T])
    jT_s = dscr("jT_s", [128, T])
    Ub_s = dscr("Ub_s", [128 * 128, 1024], BF16)
    Vb_s = dscr("Vb_s", [16384, D], BF16)

    dbg = {}

    def dbg_out(name, shape, dt=F32, force=False):
        if name in debug or force:
            dbg[name] = nc.dram_tensor("dbg_" + name, list(shape), dt, kind="ExternalOutput").ap()
            return dbg[name]
        return None

    with ExitStack() as top:
        S = Sync(nc, top)

        def sbt(st, name, shape, dt=F32):
            return st.enter_context(nc.sbuf_tensor("s_" + name, list(shape), dt))

        def pst(st, name, shape=(128, 512), dt=F32):
            return st.enter_context(nc.psum_tensor("p_" + name, list(shape), dt))

        def aff(e, out, in_, scale, bias, reads, writes, func=None):
            if e == "act":
                f = func or AF.Identity
                return S.op("act", lambda en: en.activation(out=out, in_=in_, func=f, bias=bias, scale=scale),
                            reads=reads, writes=writes)
            assert func is None
            return S.op(e, lambda en: en.tensor_scalar(out=out, in0=in_, scalar1=scale, scalar2=bias,
                                                       op0=ALU.mult, op1=ALU.add), reads=reads, writes=writes)

        def build_gate_bc(st, dst, mof, pb, pbkey):
            dg = sbt(st, "dg_" + dst.name[2:], [128, 128])
            for cidx in range(8):
                S.op("dve", lambda e: e.tensor_scalar_mul(out=dg[:], in0=ident[:], scalar1=modT[:, mof + cidx:mof + cidx + 1]),
                     reads=["dg"], writes=["dg"])
                S.op("pe", lambda e: e.matmul(pb[:, 0:128], lhsT=ones_f[:], rhs=dg[:], start=True, stop=True),
                     reads=["dg"], writes=[pbkey])
                S.op("act", lambda e: e.copy(out=dst[:, cidx * 128:(cidx + 1) * 128], in_=pb[:, 0:128]),
                     reads=[pbkey], writes=[dst.name[2:]])

        cols = sbt(top, "cols", [128, NCOL])
        dcol = sbt(top, "dcol", [128, 48])
        OMM, OMKA, GSC1, GSC2 = 0, 14, 18, 26
        modT = sbt(top, "modT", [128, 48])
        ident = sbt(top, "ident", [128, 128])
        iota_f = sbt(top, "iota_f", [128, 128])
        ones_f = sbt(top, "ones_f", [128, 128])
        blk1 = sbt(top, "blk1", [128, 128])
        mask4 = sbt(top, "mask4", [128, 4, 128])
        maskL = sbt(top, "maskL", [128, 128])

        NCAST = 16
        cast_jobs = []
        for i in range(NCAST):
            r0, r1 = i * (16384 // NCAST), (i + 1) * (16384 // NCAST)
            cast_jobs.append((Ub_s[r0:r1, :], UT_d[r0:r1, :], ("Ub", i)))
            cast_jobs.append((Vb_s[r0:r1, :], V_d[r0:r1, :], ("Vb", i)))

        def issue_casts(n):
            for _ in range(n):
                if cast_jobs:
                    o_, i_, k_ = cast_jobs.pop(0)
                    S.dma("cast", o_, i_, writes=[k_])

        with ExitStack() as ph:
            io_i = sbt(ph, "io_i", [128, 128], I32)
            io_f = sbt(ph, "io_f", [128, 128])
            ccol = sbt(ph, "ccol", [128, 8])
            sc = sbt(ph, "silu_c", [128, 8])
            adab = sbt(ph, "adab", [128, 48])
            abuf = [sbt(ph, f"abuf{i}", [128, 2048]) for i in range(2)]
            dg = sbt(ph, "dg", [128, 128])
            psmod_t = pst(ph, "psmod", [128, 512])
            psmod = psmod_t[:, 0:384]
            psb = [pst(ph, f"psb{i}", [128, 512]) for i in range(2)]

            S.dma("sp", cols[:], cols_d, writes=["cols"])
            S.dma("sp", ccol[:], ccol_d, writes=["ccol"])
            S.dma("sp", adab[:], adab_d, writes=["adab"])
            S.op("pool", lambda e: e.iota(io_i[:], pattern=[[1, 128]], base=0, channel_multiplier=-1), writes=["io_i"])
            S.op("dve", lambda e: e.tensor_copy(out=io_f[:], in_=io_i[:]), reads=["io_i"], writes=["io_f"])
            S.op("dve", lambda e: e.tensor_single_scalar(out=ident[:], in_=io_f[:], scalar=0.0, op=ALU.is_equal),
                 reads=["io_f"], writes=["ident"])
            S.op("pool", lambda e: e.iota(io_i[:], pattern=[[1, 128]], base=0, channel_multiplier=0),
                 reads=["io_i"], writes=["io_i"])
            S.op("dve", lambda e: e.tensor_copy(out=iota_f[:], in_=io_i[:]), reads=["io_i"], writes=["iota_f"])
            S.op("pool", lambda e: e.memset(ones_f[:], 1.0), writes=["ones_f"])
            S.op("pool", lambda e: e.memset(blk1[:], 0.0), writes=["blk1"])
            S.op("pool", lambda e: e.memset(blk1[0:64, 0:64], 1.0), writes=["blk1"])
            S.op("pool", lambda e: e.memset(blk1[64:128, 64:128], 1.0), writes=["blk1"])
            S.op("dve", lambda e: e.scalar_tensor_tensor(out=mask4[:, 0, :], in0=io_f[:], scalar=0.0, in1=blk1[:],
                                                        op0=ALU.is_gt, op1=ALU.mult), reads=["io_f", "blk1"], writes=["mask4"])
            S.op("dve", lambda e: e.tensor_copy(out=mask4[:, 1, :], in_=mask4[:, 0, :]), reads=["mask4"], writes=["mask4"])
            S.op("dve", lambda e: e.scalar_tensor_tensor(out=mask4[:, 2, :], in0=io_f[:], scalar=0.0, in1=blk1[:],
                                                        op0=ALU.is_ge, op1=ALU.mult), reads=["io_f", "blk1", "mask4"], writes=["mask4"])
            S.op("dve", lambda e: e.tensor_copy(out=mask4[:, 3, :], in_=mask4[:, 2, :]), reads=["mask4"], writes=["mask4"])
            S.op("dve", lambda e: e.scalar_tensor_tensor(out=maskL[:], in0=io_f[:], scalar=0.0, in1=blk1[:],
                                                        op0=ALU.is_lt, op1=ALU.mult), reads=["io_f", "blk1"], writes=["maskL"])
            S.op("dve", lambda e: e.tensor_scalar(out=dcol[:, OMM:OMM + 14], in0=cols[:, MU:MU + 14], scalar1=-1.0, scalar2=1.0,
                                                 op0=ALU.mult, op1=ALU.add), reads=["cols"], writes=["dcol"])
            S.op("dve", lambda e: e.tensor_scalar(out=dcol[:, OMKA:OMKA + 4], in0=cols[:, KA:KA + 4], scalar1=-1.0, scalar2=1.0,
                                                 op0=ALU.mult, op1=ALU.add), reads=["cols", "dcol"], writes=["dcol"])
            S.op("act", lambda e: e.activation(out=sc[:], in_=ccol[:], func=AF.Silu), reads=["ccol"], writes=["silu_c"])
            n = 0
            for k in range(8):
                for pc in range(3):
                    ab = abuf[n % 2]
                    S.dma("sp", ab[:], adaw_d[k * 128:(k + 1) * 128, pc * 2048:(pc + 1) * 2048], writes=[("abuf", n % 2)])
                    for fl in range(16):
                        f = pc * 16 + fl
                        S.op("pe", lambda e: e.matmul(psmod[:, f * 8 + k:f * 8 + k + 1], lhsT=ab[:, fl * 128:(fl + 1) * 128],
                                                      rhs=sc[:, k:k + 1], start=True, stop=True),
                             reads=[("abuf", n % 2), "silu_c"], writes=["psmod"])
                    n += 1
            S.op("dve", lambda e: e.tensor_reduce(out=modT[:], in_=psmod.rearrange("p (f k) -> p f k", k=8),
                                                 axis=AX.X, op=ALU.add), reads=["psmod"], writes=["modT"])
            S.op("dve", lambda e: e.tensor_add(out=modT[:], in0=modT[:], in1=adab[:]), reads=["modT", "adab"], writes=["modT"])
            for (dst, gof, sof) in ((GSC1, G1, 8), (GSC2, G2, 32)):
                S.op("dve", lambda e: e.scalar_tensor_tensor(out=dcol[:, dst:dst + 8], in0=modT[:, sof:sof + 8], scalar=1.0,
                                                            in1=cols[:, gof:gof + 8], op0=ALU.add, op1=ALU.mult),
                     reads=["modT", "cols", "dcol"], writes=["dcol"])
            if "modT" in debug:
                S.dma("sp", dbg_out("modT", [128, 48]), modT[:], reads=["modT"], writes=["dbg_modT"])
            S.barrier()

        done = [False]

        def fin():
            S.barrier()
            S.finish([])
            done[0] = True

        if stop_after == 0:
            fin()

        if not done[0]:
          with ExitStack() as mx:
            uT = sbt(mx, "uT", [128, 8, T + 1], BF16)

            with ExitStack() as ph:
                xb = [sbt(ph, f"xb{i}", [128, D]) for i in range(2)]
                junk = sbt(ph, "junk", [128, D])
                xn = [sbt(ph, f"xn{i}", [128, D]) for i in range(2)]
                ss = sbt(ph, "ss", [128, 32])
                rs = sbt(ph, "rs", [128, 32])
                pT = [pst(ph, f"pT{i}", [128, 1024]) for i in range(2)]
                S.op("pool", lambda e: e.memset(uT[:, :, 0:1], 0.0), writes=["uT0"])
                def p1_front(tt):
                    b = tt % 2
                    S.dma("sp", xb[b][:], x_d[tt * 128:(tt + 1) * 128, :], writes=[("xb", b)])
                    S.op("act", lambda e: e.activation(out=junk[:], in_=xb[b][:], func=AF.Square),
                         reads=[("xb", b)], writes=["junk"])
                    S.op("dve", lambda e: e.reduce_sum(out=ss[:, tt:tt + 1], in_=junk[:], axis=AX.X),
                         reads=["junk"], writes=[("ss", tt)])
                    S.op("act", lambda e: e.activation(out=rs[:, tt:tt + 1], in_=ss[:, tt:tt + 1], func=AF.Sqrt,
                                                       bias=1e-6, scale=1.0 / D), reads=[("ss", tt)], writes=[("rs", tt)])
                    S.op("dve", lambda e: e.reciprocal(out=rs[:, tt:tt + 1], in_=rs[:, tt:tt + 1]),
                         reads=[("rs", tt)], writes=[("rs", tt)])
                    S.op("dve", lambda e: e.tensor_scalar_mul(out=xn[b][:], in0=xb[b][:], scalar1=rs[:, tt:tt + 1]),
                         reads=[("xb", b), ("rs", tt)], writes=[("xn", b)])

                p1_front(0)
                for tt in range(32):
                    b = tt % 2
                    if tt + 1 < 32:
                        p1_front(tt + 1)
                    for k in range(8):
                        S.op("pe", lambda e: e.transpose(pT[b][:, k * 128:(k + 1) * 128], xn[b][:, k * 128:(k + 1) * 128], ident[:]),
                             reads=[("xn", b)], writes=[("pT", b, k // 4)])
                    for k in range(8):
                        aff("act" if k < 4 else "dve", uT[:, k, 1 + tt * 128:1 + (tt + 1) * 128],
                            pT[b][:, k * 128:(k + 1) * 128], dcol[:, GSC1 + k:GSC1 + k + 1], modT[:, k:k + 1],
                            reads=[("pT", b, k // 4)], writes=[("uT", tt)])
                S.barrier()
            if "uT" in debug:
                du = dbg_out("uT", [128, 8 * T], BF16).rearrange("p (k t) -> p k t", k=8)
                for k in range(8):
                    S.dma("sp", du[:, k, :], uT[:, k, 1:T + 1], writes=[("dbg_uT", k)])
                S.barrier()
            if stop_after == 1:
                fin()

            def load_w(st_f32, dst_bf, f, key):
                S.dma("sp", st_f32[:], win_d[:, f * 128:(f + 1) * 128].rearrange("(k p) f -> p k f", p=128),
                      writes=[("wst", st_f32.name[2:])])
                S.op("pool", lambda e: e.tensor_copy(out=dst_bf[:], in_=st_f32[:]), reads=[("wst", st_f32.name[2:])], writes=[("wbf", key)])

            if not done[0]:
              with ExitStack() as ph:
                wst = [sbt(ph, f"wst{i}", [128, 8, 128]) for i in range(2)]
                wab = [sbt(ph, f"wab{i}", [128, 8, 128], BF16) for i in range(2)]
                diag = sbt(ph, "diag", [128, 31, 128], BF16)
                ypad = sbt(ph, "ypad", [128, 30 + T], BF16)
                sig = [sbt(ph, f"sig{i}", [128, 512]) for i in range(2)]
                ycb = [sbt(ph, f"ycb{i}", [128, 512]) for i in range(2)]
                psA = [pst(ph, f"psA{i}") for i in range(2)]
                psB = [pst(ph, f"psB{i}") for i in range(2)]
                psC = [pst(ph, f"psC{i}") for i in range(2)]
                S.op("pool", lambda e: e.memset(ypad[:, 0:30], 0.0), writes=["ypad0"])
                for ct in range(4):
                    issue_casts(2)
                    load_w(wst[0], wab[0], ct, 0)
                    load_w(wst[1], wab[1], 4 + ct, 1)
                    for k in range(31):
                        S.op("dve", lambda e: e.tensor_scalar_mul(out=diag[:, k, :], in0=ident[:],
                                                                  scalar1=cols[:, CW + ct * 31 + k:CW + ct * 31 + k + 1]),
                             writes=["diag"])
                    for tb in range(8):
                        b = tb % 2
                        t0 = tb * 512
                        for k in range(8):
                            S.op("pe", lambda e: e.matmul(psA[b][:], lhsT=wab[0][:, k, :], rhs=uT[:, k, 1 + t0:1 + t0 + 512],
                                                          start=(k == 0), stop=(k == 7)), reads=[("wbf", 0)], writes=[("psA", b)])
                        for k in range(8):
                            S.op("pe", lambda e: e.matmul(psB[b][:], lhsT=wab[1][:, k, :], rhs=uT[:, k, 1 + t0:1 + t0 + 512],
                                                          start=(k == 0), stop=(k == 7)), reads=[("wbf", 1)], writes=[("psB", b)])
                        S.op("act", lambda e: e.activation(out=sig[b][:], in_=psB[b][:], func=AF.Sigmoid),
                             reads=[("psB", b)], writes=[("sig", b)])
                        S.op("dve", lambda e: e.tensor_tensor(out=ypad[:, 30 + t0:30 + t0 + 512], in0=psA[b][:], in1=sig[b][:],
                                                             op=ALU.mult), reads=[("psA", b), ("sig", b)], writes=[("ypad", tb)])
                    for tb in range(8):
                        b = tb % 2
                        t0 = tb * 512
                        for k in range(31):
                            S.op("pe", lambda e: e.matmul(psC[b][:], lhsT=diag[:, k, :], rhs=ypad[:, t0 + k:t0 + k + 512],
                                                          start=(k == 0), stop=(k == 30)),
                                 reads=["diag", "ypad0", ("ypad", tb), ("ypad", max(tb - 1, 0))], writes=[("psC", b)])
                        aff("act", ycb[b][:], psC[b][:], 1.0, cols[:, CB + ct:CB + ct + 1], reads=[("psC", b)], writes=[("ycb", b)])
                        S.dma("sp", yc_s[ct * 128:(ct + 1) * 128, t0:t0 + 512], ycb[b][:], reads=[("ycb", b)],
                              writes=[("yc_s", ct, tb)])
                S.barrier()
            if "yc" in debug and not done[0]:
                S.dma("sp", dbg_out("yc", [512, T]), yc_s, writes=["dbg_yc"])
                S.barrier()
            if stop_after == 2 and not done[0]:
                fin()
            done_2a = True

            if not done[0]:
              with ExitStack() as ph:
                yb = [sbt(ph, f"yb{i}", [128, 4, 512]) for i in range(2)]
                sq = sbt(ph, "sq", [128, 4, 512])
                mean = sbt(ph, "mean", [128, 512])
                msq = sbt(ph, "msq", [128, 512])
                var = sbt(ph, "var", [128, 512])
                dd = [sbt(ph, f"dd{i}", [128, 512]) for i in range(2)]
                yo = [sbt(ph, f"yo{i}", [128, 512], BF16) for i in range(2)]
                ps1 = pst(ph, "ps1")
                ps2 = pst(ph, "ps2")
                for tb in range(8):
                    b = tb % 2
                    t0 = tb * 512
                    S.dma("sp", yb[b][:], yc_s[:, t0:t0 + 512].rearrange("(c p) t -> p c t", p=128), writes=[("yb", b)])
                    S.op("act", lambda e: e.activation(out=sq[:], in_=yb[b][:], func=AF.Square), reads=[("yb", b)], writes=["sq"])
                    for ct in range(4):
                        S.op("pe", lambda e: e.matmul(ps1[:], lhsT=ones_f[:], rhs=yb[b][:, ct, :], start=(ct == 0), stop=(ct == 3)),
                             reads=[("yb", b)], writes=["ps1"])
                    for ct in range(4):
                        S.op("pe", lambda e: e.matmul(ps2[:], lhsT=ones_f[:], rhs=sq[:, ct, :], start=(ct == 0), stop=(ct == 3)),
                             reads=["sq"], writes=["ps2"])
                    S.op("act", lambda e: e.mul(out=mean[:], in_=ps1[:], mul=1.0 / 512), reads=["ps1"], writes=["mean"])
                    S.op("pool", lambda e: e.tensor_mul(out=msq[:], in0=mean[:], in1=mean[:]), reads=["mean"], writes=["msq"])
                    S.op("dve", lambda e: e.scalar_tensor_tensor(out=var[:], in0=ps2[:], scalar=1.0 / 512, in1=msq[:],
                                                                op0=ALU.mult, op1=ALU.subtract), reads=["ps2", "msq"], writes=["var"])
                    S.op("act", lambda e: e.activation(out=var[:], in_=var[:], func=AF.Sqrt, bias=1e-5, scale=1.0),
                         reads=["var"], writes=["var"])
                    S.op("dve", lambda e: e.reciprocal(out=var[:], in_=var[:]), reads=["var"], writes=["var"])
                    for ct in range(4):
                        d2 = ct % 2
                        S.op("dve", lambda e: e.tensor_sub(out=dd[d2][:], in0=yb[b][:, ct, :], in1=mean[:]),
                             reads=[("yb", b), "mean"], writes=[("dd", d2)])
                        S.op("pool", lambda e: e.tensor_mul(out=dd[d2][:], in0=dd[d2][:], in1=var[:]),
                             reads=[("dd", d2), "var"], writes=[("dd", d2)])
                        aff("act", yo[d2][:], dd[d2][:], cols[:, LW + ct:LW + ct + 1], cols[:, LB + ct:LB + ct + 1],
                            reads=[("dd", d2)], writes=[("yo", d2)], func=AF.Silu)
                        S.dma("sp", ycat_s[ct * 128:(ct + 1) * 128, t0:t0 + 512], yo[d2][:], reads=[("yo", d2)],
                              writes=[("ycat", ct, tb)])
                S.barrier()
            if "ycat" in debug and stop_after == 3 and not done[0]:
                S.dma("sp", dbg_out("ycat", [1024, T], BF16), ycat_s, writes=["dbg_ycat"])
                S.barrier()
            if stop_after == 3 and not done[0]:
                fin()

            if not done[0]:
              with ExitStack() as ph:
                NB = 256
                NCH = NB // 64
                wst_ = sbt(ph, "rwst", [128, 8, 128]); wst = [wst_, wst_]
                wl_bf = sbt(ph, "wl_bf", [128, 8, 128], BF16)
                wr_bf = [sbt(ph, f"wr_bf{i}", [128, 8, 128], BF16) for i in range(3)]
                twal = sbt(ph, "twal", [128, T], BF16)
                sg = sbt(ph, "sg", [128, T], BF16)
                wa2b = sbt(ph, "wa2b", [128, 512], BF16)
                g2b = sbt(ph, "g2b", [128, 512], BF16)
                cmask = sbt(ph, "cmask", [128, NB])
                tmpx = [sbt(ph, f"tmpx{i}", [128, 256]) for i in range(2)]
                xsl = sbt(ph, "xsl", [128, 256])
                psA = [pst(ph, f"rpsA{i}") for i in range(2)]
                psM = [pst(ph, f"rpsM{i}") for i in range(2)]
                Wk = [pst(ph, f"rW{i}") for i in range(4)]

                S.op("pool", lambda e: e.memset(cmask[:], 1.0), writes=["cmask"])
                S.op("pool", lambda e: e.memset(cmask[:].rearrange("p (c k) -> p c k", k=64)[:, :, 0:1], 0.0),
                     reads=["cmask"], writes=["cmask"])

                halos = sbt(ph, "halos", [128, 4])
                npx = [0]

                def proj_xs1(wbf, wkey, fcol, t0, n, out_ap, okey, j3, first):
                    bi = npx[0] % 2
                    npx[0] += 1
                    pa = psA[bi]
                    pk_ = "rpsA%d" % bi
                    tx = tmpx[bi]
                    mu_c = cols[:, MU + fcol:MU + fcol + 1]
                    hk = ("halo", j3)
                    for k in range(8):
                        S.op("pe", lambda e: e.matmul(pa[:, 0:n], lhsT=wbf[:, k, :], rhs=uT[:, k, 1 + t0:1 + t0 + n],
                                                      start=(k == 0), stop=(k == 7)), reads=[wkey], writes=[pk_])
                    if first:
                        S.op("act", lambda e: e.memzero(halos[:, j3:j3 + 1]), writes=[hk])
                    S.op("act", lambda e: e.activation(out=tx[:, 0:1], in_=halos[:, j3:j3 + 1], func=AF.Copy, scale=mu_c),
                         reads=[hk], writes=[("tmpx", bi)])
                    S.op("act", lambda e: e.activation(out=tx[:, 1:n], in_=pa[:, 0:n - 1], func=AF.Copy, scale=mu_c),
                         reads=[pk_, ("tmpx", bi)], writes=[("tmpx", bi)])
                    S.op("act", lambda e: e.copy(out=halos[:, j3:j3 + 1], in_=pa[:, n - 1:n]), reads=[pk_, hk], writes=[hk])
                    S.op("dve", lambda e: e.scalar_tensor_tensor(out=out_ap, in0=pa[:, 0:n],
                                                                scalar=dcol[:, OMM + fcol:OMM + fcol + 1], in1=tx[:, 0:n],
                                                                op0=ALU.mult, op1=ALU.add),
                         reads=[pk_, ("tmpx", bi)], writes=[okey])

                def proj_xs(wbf, wkey, fcol, t0, n, out_ap, okey, slot):
                    pa, pb = psA[0], psA[1]
                    for k in range(8):
                        S.op("pe", lambda e: e.matmul(pa[:, 0:n], lhsT=wbf[:, k, :], rhs=uT[:, k, 1 + t0:1 + t0 + n],
                                                      start=(k == 0), stop=(k == 7)), reads=[wkey], writes=["rpsA0"])
                    for k in range(8):
                        S.op("pe", lambda e: e.matmul(pb[:, 0:n], lhsT=wbf[:, k, :], rhs=uT[:, k, t0:t0 + n],
                                                      start=(k == 0), stop=(k == 7)), reads=[wkey], writes=["rpsA1"])
                    tx = tmpx[slot % 2]
                    S.op("act", lambda e: e.activation(out=tx[:, 0:n], in_=pb[:, 0:n], func=AF.Copy,
                                                       scale=cols[:, MU + fcol:MU + fcol + 1]),
                         reads=["rpsA1"], writes=[("tmpx", slot % 2)])
                    S.op("dve", lambda e: e.scalar_tensor_tensor(out=out_ap, in0=pa[:, 0:n],
                                                                scalar=dcol[:, OMM + fcol:OMM + fcol + 1], in1=tx[:, 0:n],
                                                                op0=ALU.mult, op1=ALU.add),
                         reads=["rpsA0", ("tmpx", slot % 2)], writes=[okey])

                load_w(wst[0], wl_bf, 8 + 12, "l")
                for tb in range(16):
                    t0 = tb * 256
                    proj_xs1(wl_bf, ("wbf", "l"), 12, t0, 256, xsl[:], "xsl", 3, tb == 0)
                    S.op("act", lambda e: e.activation(out=twal[0:64, t0:t0 + 256], in_=xsl[0:64, :], func=AF.Tanh),
                         reads=["xsl"], writes=[("twal", tb)])
                    S.op("pool", lambda e: e.tensor_copy(out=twal[64:128, t0:t0 + 256], in_=xsl[64:128, :]),
                         reads=["xsl"], writes=[("twal2", tb)])
                load_w(wst[0], wl_bf, 8 + 13, "l")
                for tb in range(16):
                    t0 = tb * 256
                    proj_xs1(wl_bf, ("wbf", "l"), 13, t0, 256, xsl[:], "xsl", 3, tb == 0)
                    S.op("act", lambda e: e.activation(out=sg[:, t0:t0 + 256], in_=xsl[:], func=AF.Sigmoid),
                         reads=["xsl"], writes=[("sg", tb)])

                def wk(name, shape=(128, NB), dt=F32):
                    return sbt(ph, name, list(shape), dt)
                R2 = [wk("r_0"), wk("r_1")]; k0 = wk("k0"); V2 = [wk("v_0"), wk("v_1")]
                sgm = wk("sgm"); a_ = wk("a_"); GG2 = [wk("g_0"), wk("g_1")]
                cum = wk("cum"); cex = wk("cex"); dend = cex
                EIN2 = [wk("Ein0"), wk("Ein1")]; Eex = wk("Eex"); Einv = wk("Einv"); Eend = wk("Eend")
                kk = wk("kk"); rinv = wk("rinv"); kkn = wk("kkn")
                t1 = wk("t1"); kk2 = t1; K22 = [wk("k2_0"), wk("k2_1")]; bb = wk("bb"); rkr = wk("rkr")
                ysc = wk("ysc"); ysq = wk("ysq"); gmean = wk("gmean"); gvar = wk("gvar"); gtmp = wk("gtmp")
                yob = [sbt(ph, f"ryo{i}", [128, NB], BF16) for i in range(2)]
                PADS = [[wk(f"{nm}{pb}", (128, NCH, 128)) for nm in ("AP_", "BP_", "KP_", "RP_", "BC_", "KC_", "VP_")] for pb in range(2)]
                for pb_ in range(2):
                    for arr in PADS[pb_]:
                        S.op("pool", lambda e: e.memset(arr[:], 0.0), writes=[("padinit", arr.name)])
                tok = [wk(f"tok{c}", (128, 4, 128)) for c in range(NCH)]
                gram = [wk(f"gram{c}", (128, 4, 128)) for c in range(NCH)]
                NTm = [wk(f"NTm{c}", (128, 128)) for c in range(NCH)]
                Pm = [[wk(f"Pm{c}_{i}", (128, 2, 128)) for i in range(2)] for c in range(NCH)]
                Xm = [[wk(f"Xm{c}_{i}", (128, 128)) for i in range(2)] for c in range(NCH)]
                AW = [wk(f"AW{c}", (128, 2, 128)) for c in range(NCH)]
                AU = [wk(f"AU{c}", (128, 2, 128)) for c in range(NCH)]
                PTs = [wk(f"PTs{c}", (128, 128)) for c in range(NCH)]
                Qm = [wk(f"Qm{c}", (128, 128)) for c in range(NCH)]
                RH = [wk(f"RH{c}", (128, 128)) for c in range(NCH)]
                nwb = [0]
                wa2f = tok[0][:].rearrange("p a b -> p (a b)")
                g2f = gram[0][:].rearrange("p a b -> p (a b)")
                S.dma("sp", wa2f, wa2_d, writes=[("tok", 0)])
                S.dma("sp", g2f, g2_d, writes=[("gram", 0)])
                S.op("dve", lambda e: e.tensor_copy(out=wa2b[:], in_=wa2f), reads=[("tok", 0)], writes=["wa2b"])
                S.op("dve", lambda e: e.tensor_copy(out=g2b[:], in_=g2f), reads=[("gram", 0)], writes=["g2b"])
                STs = [wk(f"ST{i}", (128, 128)) for i in range(2)]

                for hp in range(4):
                    for j3, f in enumerate((8 + hp, 12 + hp, 16 + hp)):
                        load_w(wst[j3 % 2], wr_bf[j3], 8 + (f - 8), ("r", j3))
                    S.op("pool", lambda e: e.memset(STs[0][:], 0.0), writes=[("ST", 0)])
                    cs = slice(hp * 128, (hp + 1) * 128)
                    nch = [0]
                    def rw_prep(blk):
                        issue_casts(1)
                        pb = blk % 2
                        t0 = blk * NB
                        r_, k2, v_, g_, Ein = R2[pb], K22[pb], V2[pb], GG2[pb], EIN2[pb]
                        AP_, BP_, KP_, RP_, BC_, KC_, VP_ = PADS[pb]
                        padkeys = [(nm, pb, hh) for nm in ("AP_", "BP_", "KP_", "RP_", "BC_", "KC_", "VP_") for hh in range(2)]
                        pm, pm1 = psM[0], psM[1]
                        proj_xs1(wr_bf[0], ("wbf", ("r", 0)), hp, t0, NB, r_[:], ("r_", pb), 0, blk == 0)
                        proj_xs1(wr_bf[1], ("wbf", ("r", 1)), 4 + hp, t0, NB, k0[:], "k0", 1, blk == 0)
                        proj_xs1(wr_bf[2], ("wbf", ("r", 2)), 8 + hp, t0, NB, v_[:], ("v_", pb), 2, blk == 0)
                        S.op("pe", lambda e: e.matmul(pm[:, 0:NB], lhsT=wa2b[0:64, cs], rhs=twal[0:64, t0:t0 + NB], start=True, stop=True),
                             reads=["wa2b"], writes=["rpsM0"])
                        S.op("act", lambda e: e.activation(out=sgm[:], in_=pm[:, 0:NB], func=AF.Sigmoid, bias=cols[:, W0 + hp:W0 + hp + 1]),
                             reads=["rpsM0"], writes=["sgm"])
                        S.op("pe", lambda e: e.matmul(pm1[:, 0:NB], lhsT=wa2b[64:128, cs], rhs=twal[64:128, t0:t0 + NB], start=True, stop=True),
                             reads=["wa2b"], writes=["rpsM1"])
                        S.op("act", lambda e: e.activation(out=a_[:], in_=pm1[:, 0:NB], func=AF.Sigmoid, bias=cols[:, A0 + hp:A0 + hp + 1]),
                             reads=["rpsM1"], writes=["a_"])
                        S.op("pe", lambda e: e.matmul(pm[:, 0:NB], lhsT=g2b[:, cs], rhs=sg[:, t0:t0 + NB], start=True, stop=True),
                             reads=["g2b"], writes=["rpsM0"])
                        S.op("act", lambda e: e.copy(out=g_[:], in_=pm[:, 0:NB]), reads=["rpsM0"], writes=[("g_", pb)])
                        S.op("dve", lambda e: e.tensor_tensor_scan(out=cum[:], data0=cmask[:], data1=sgm[:], initial=0.0,
                                                                  op0=ALU.mult, op1=ALU.add), reads=["sgm", "cmask"], writes=["cum"])
                        S.op("pool", lambda e: e.tensor_sub(out=cex[:], in0=cum[:], in1=sgm[:]), reads=["cum", "sgm"], writes=["cex"])
                        S.op("act", lambda e: e.activation(out=Eex[:], in_=cex[:], func=AF.Exp, scale=-C0), reads=["cex"], writes=["Eex"])
                        S.op("act", lambda e: e.activation(out=Ein[:], in_=cum[:], func=AF.Exp, scale=-C0), reads=["cum"], writes=[("Ein", pb)])
                        S.op("act", lambda e: e.activation(out=Einv[:], in_=cum[:], func=AF.Exp, scale=C0), reads=["cum"], writes=["Einv"])
                        cum3 = cum[:].rearrange("p (c k) -> p c k", k=64)
                        S.op("dve", lambda e: e.tensor_sub(out=dend[:].rearrange("p (c k) -> p c k", k=64),
                                                          in0=cum3[:, :, 63:64].to_broadcast([128, NCH, 64]), in1=cum3),
                             reads=["cum"], writes=["cex"])
                        S.op("act", lambda e: e.activation(out=Eend[:], in_=dend[:], func=AF.Exp, scale=-C0), reads=["cex"], writes=["Eend"])
                        S.op("act", lambda e: e.activation(out=kk[:], in_=k0[:], func=AF.Copy, scale=cols[:, KK + hp:KK + hp + 1]),
                             reads=["k0"], writes=["kk"])
                        S.op("pool", lambda e: e.tensor_mul(out=kk2[:], in0=kk[:], in1=kk[:]), reads=["kk"], writes=["t1"])
                        S.op("pe", lambda e: e.matmul(pm1[:, 0:NB], lhsT=blk1[:], rhs=kk2[:], start=True, stop=True),
                             reads=["t1"], writes=["rpsM1"])
                        S.op("act", lambda e: e.activation(out=rinv[:], in_=pm1[:, 0:NB], func=AF.Sqrt), reads=["rpsM1"], writes=["rinv"])
                        S.op("dve", lambda e: e.tensor_scalar_max(out=rinv[:], in0=rinv[:], scalar1=1e-12), reads=["rinv"], writes=["rinv"])
                        S.op("dve", lambda e: e.reciprocal(out=rinv[:], in_=rinv[:]), reads=["rinv"], writes=["rinv"])
                        S.op("dve", lambda e: e.tensor_mul(out=kkn[:], in0=kk[:], in1=rinv[:]), reads=["kk", "rinv"], writes=["kkn"])
                        S.op("pool", lambda e: e.tensor_scalar(out=t1[:], in0=a_[:], scalar1=cols[:, KA + hp:KA + hp + 1],
                                                              scalar2=dcol[:, OMKA + hp:OMKA + hp + 1], op0=ALU.mult, op1=ALU.add),
                             reads=["a_"], writes=["t1"])
                        S.op("pool", lambda e: e.tensor_mul(out=k2[:], in0=k0[:], in1=t1[:]), reads=["k0", "t1"], writes=[("k2", pb)])
                        S.op("dve", lambda e: e.tensor_mul(out=bb[:], in0=kkn[:], in1=a_[:]), reads=["kkn", "a_"], writes=["bb"])
                        for hh in range(2):
                            ps_ = slice(hh * 64, hh * 64 + 64)

                            def v3(tile_):
                                return tile_[ps_, :].rearrange("p (c k) -> p c k", k=64)

                            def o3(arr):
                                return arr[ps_, :, hh * 64:hh * 64 + 64]
                            eA = "dve" if hh == 0 else "pool"
                            eB = "pool" if hh == 0 else "dve"
                            S.op("dve", lambda e: e.scalar_tensor_tensor(out=o3(AP_), in0=v3(kkn), scalar=-1.0, in1=v3(Eex),
                                                                      op0=ALU.mult, op1=ALU.mult), reads=["kkn", "Eex"], writes=[("AP_", pb, hh)])
                            S.op(eB, lambda e: e.tensor_mul(out=o3(BP_), in0=v3(bb), in1=v3(Einv)), reads=["bb", "Einv"], writes=[("BP_", pb, hh)])
                            S.op(eA, lambda e: e.tensor_mul(out=o3(KP_), in0=v3(k2), in1=v3(Einv)), reads=[("k2", pb), "Einv"], writes=[("KP_", pb, hh)])
                            S.op(eB, lambda e: e.tensor_mul(out=o3(RP_), in0=v3(r_), in1=v3(Ein)), reads=[("r_", pb), ("Ein", pb)], writes=[("RP_", pb, hh)])
                            S.op(eA, lambda e: e.tensor_mul(out=o3(BC_), in0=v3(bb), in1=v3(Eend)), reads=["bb", "Eend"], writes=[("BC_", pb, hh)])
                            S.op(eB, lambda e: e.tensor_mul(out=o3(KC_), in0=v3(k2), in1=v3(Eend)), reads=[("k2", pb), "Eend"], writes=[("KC_", pb, hh)])
                            S.op(eA, lambda e: e.tensor_copy(out=o3(VP_), in_=v3(v_)), reads=[("v_", pb)], writes=[("VP_", pb, hh)])
                        padkeys = [(nm, pb, hh) for nm in ("AP_", "BP_", "KP_", "RP_", "BC_", "KC_", "VP_") for hh in range(2)]

                    def rw_chunk(blk):
                        pb = blk % 2
                        t0 = blk * NB
                        r_, k2, v_, g_, Ein = R2[pb], K22[pb], V2[pb], GG2[pb], EIN2[pb]
                        AP_, BP_, KP_, RP_, BC_, KC_, VP_ = PADS[pb]
                        padkeys = [(nm, pb, hh) for nm in ("AP_", "BP_", "KP_", "RP_", "BC_", "KC_", "VP_") for hh in range(2)]
                        pm, pm1 = psM[0], psM[1]
                        NI = NCH
                        for CH in [range(c0, c0 + NI) for c0 in range(0, NCH, NI)]:
                          if True:

                            def wbank():
                                i_ = nwb[0] % 4
                                nwb[0] += 1
                                return Wk[i_], ("rW", i_)
                            for c in CH:
                                wb_, wk_ = wbank()
                                for q_, arr in enumerate((AP_, BC_, KC_, VP_)):
                                    S.op("pe", lambda e: e.transpose(wb_[:, q_ * 128:(q_ + 1) * 128], arr[:, c, :], ident[:]),
                                         reads=padkeys, writes=[wk_])
                                S.op("act", lambda e: e.copy(out=tok[c][:].rearrange("p a b -> p (a b)"), in_=wb_[:]), reads=[wk_], writes=[("tok", c)])
                            for c in CH:
                                wb_, wk_ = wbank()
                                for q_, (l_, r2) in enumerate(((BP_, AP_), (KP_, AP_), (BP_, RP_), (KP_, RP_))):
                                    S.op("pe", lambda e: e.matmul(wb_[:, q_ * 128:(q_ + 1) * 128], lhsT=l_[:, c, :], rhs=r2[:, c, :], start=True, stop=True),
                                         reads=padkeys, writes=[wk_])
                                S.op("dve", lambda e: e.tensor_tensor(out=gram[c][:].rearrange("p a b -> p (a b)"), in0=wb_[:],
                                                                     in1=mask4[:].rearrange("p a b -> p (a b)"), op=ALU.mult),
                                     reads=[wk_], writes=[("gram", c)])
                            for c in CH:
                                wb_, wk_ = wbank()
                                S.op("pe", lambda e: e.matmul(wb_[:, 0:128], lhsT=AP_[:, c, :], rhs=BP_[:, c, :], start=True, stop=True),
                                     reads=padkeys, writes=[wk_])
                                S.op("dve", lambda e: e.tensor_tensor(out=NTm[c][:], in0=wb_[:, 0:128], in1=maskL[:], op=ALU.mult),
                                     reads=[wk_], writes=[("NTm", c)])
                                S.op("pool", lambda e: e.tensor_add(out=Xm[c][0][:], in0=gram[c][:, 0, :], in1=ident[:]),
                                     reads=[("gram", c)], writes=[("Xm", c, 0)])
                            Pc = {c: gram[c][:, 0, :] for c in CH}
                            PTc = {c: NTm[c][:] for c in CH}
                            pk = {c: [("gram", c), ("NTm", c)] for c in CH}
                            xi = 0
                            pt_step = {}

                            def do_sq(step):
                                for c in CH:
                                    wb_, wk_ = wbank()
                                    pp = Pm[c][step % 2]
                                    if step < 4:
                                        S.op("pe", lambda e: e.matmul(wb_[:, 0:128], lhsT=PTc[c], rhs=Pc[c], start=True, stop=True),
                                             reads=pk[c], writes=[wk_])
                                    S.op("pe", lambda e: e.matmul(wb_[:, 128:256], lhsT=Pc[c], rhs=PTc[c], start=True, stop=True),
                                         reads=pk[c], writes=[wk_])
                                    if step < 4:
                                        S.op("act", lambda e: e.copy(out=pp[:].rearrange("p a b -> p (a b)"), in_=wb_[:, 0:256]),
                                             reads=[wk_], writes=[("Pm", c, step % 2)])
                                    else:
                                        S.op("act", lambda e: e.copy(out=pp[:, 1, :], in_=wb_[:, 128:256]),
                                             reads=[wk_], writes=[("Pm", c, step % 2)])
                                    Pc[c], PTc[c] = pp[:, 0, :], pp[:, 1, :]
                                    pk[c] = [("Pm", c, step % 2)]
                                pt_step[step] = ({c: PTc[c] for c in CH}, {c: list(pk[c]) for c in CH})

                            def do_xu(step, xi_):
                                pts, pks = pt_step[step]
                                for c in CH:
                                    wb_, wk_ = wbank()
                                    S.op("pe", lambda e: e.matmul(wb_[:, 0:128], lhsT=pts[c], rhs=Xm[c][xi_][:], start=True, stop=True),
                                         reads=pks[c] + [("Xm", c, xi_)], writes=[wk_])
                                    S.op("dve", lambda e: e.tensor_add(out=Xm[c][1 - xi_][:], in0=wb_[:, 0:128], in1=Xm[c][xi_][:]),
                                         reads=[wk_, ("Xm", c, xi_)], writes=[("Xm", c, 1 - xi_)])

                            do_sq(0)
                            for step in range(5):
                                if step + 1 < 5:
                                    do_sq(step + 1)
                                do_xu(step, xi)
                                xi = 1 - xi
                            for c in CH:
                                wb_, wk_ = wbank()
                                S.op("pe", lambda e: e.matmul(wb_[:, 0:128], lhsT=gram[c][:, 1, :], rhs=tok[c][:, 3, :], start=True, stop=True),
                                     reads=[("gram", c), ("tok", c)], writes=[wk_])
                                S.op("act", lambda e: e.copy(out=AW[c][:, 1, :], in_=wb_[:, 0:128]), reads=[wk_], writes=[("AW1", c)])
                                S.op("pool", lambda e: e.tensor_copy(out=AW[c][:, 0, :], in_=tok[c][:, 0, :]), reads=[("tok", c)], writes=[("AW0", c)])
                            for c in CH:
                                wb_, wk_ = wbank()
                                S.op("pe", lambda e: e.matmul(wb_[:, 0:256], lhsT=Xm[c][xi][:], rhs=AW[c][:].rearrange("p a b -> p (a b)"), start=True, stop=True),
                                     reads=[("Xm", c, xi), ("AW0", c), ("AW1", c)], writes=[wk_])
                                S.op("act", lambda e: e.copy(out=AU[c][:].rearrange("p a b -> p (a b)"), in_=wb_[:, 0:256]),
                                     reads=[wk_], writes=[("AU", c)])
                            for c in CH:
                                wb_, wk_ = wbank()
                                Ah, Uh = AU[c][:, 0, :], AU[c][:, 1, :]
                                BCt, KCt, Vt = tok[c][:, 1, :], tok[c][:, 2, :], tok[c][:, 3, :]
                                gcol = Ein[:, c * 64 + 63:c * 64 + 64]
                                S.op("pe", lambda e: e.matmul(wb_[:, 0:128], lhsT=Ah, rhs=BCt, start=True, stop=True),
                                     reads=[("AU", c), ("tok", c)], writes=[wk_])
                                S.op("pe", lambda e: e.matmul(wb_[:, 128:256], lhsT=BCt, rhs=Uh, start=True, stop=False),
                                     reads=[("AU", c), ("tok", c)], writes=[wk_])
                                S.op("pe", lambda e: e.matmul(wb_[:, 128:256], lhsT=KCt, rhs=Vt, start=False, stop=True),
                                     reads=[("tok", c)], writes=[wk_])
                                S.op("pe", lambda e: e.matmul(wb_[:, 256:384], lhsT=Ah, rhs=gram[c][:, 2, :], start=True, stop=True),
                                     reads=[("AU", c), ("gram", c)], writes=[wk_])
                                S.op("dve", lambda e: e.scalar_tensor_tensor(out=PTs[c][:], in0=ident[:], scalar=gcol, in1=wb_[:, 0:128],
                                                                            op0=ALU.mult, op1=ALU.add), reads=[wk_, ("Ein", pb)], writes=[("PTs", c)])
                                S.op("act", lambda e: e.copy(out=Qm[c][:], in_=wb_[:, 128:256]), reads=[wk_], writes=[("Qm", c)])
                                S.op("dve", lambda e: e.tensor_add(out=RH[c][:], in0=wb_[:, 256:384], in1=RP_[:, c, :]),
                                     reads=[wk_] + padkeys, writes=[("RH", c)])
                            for c in CH:
                                cur = STs[nch[0] % 2]
                                nxt = STs[(nch[0] + 1) % 2]
                                kcur, knxt = ("ST", nch[0] % 2), ("ST", (nch[0] + 1) % 2)
                                wb2_, wk2_ = wbank()
                                S.op("pe", lambda e: e.matmul(wb2_[:, 0:128], lhsT=PTs[c][:], rhs=cur[:], start=True, stop=True),
                                     reads=[("PTs", c), kcur], writes=[wk2_])
                                S.op("dve", lambda e: e.tensor_add(out=nxt[:], in0=wb2_[:, 0:128], in1=Qm[c][:]),
                                     reads=[wk2_, ("Qm", c)], writes=[knxt])
                                wb_, wk_ = wbank()
                                S.op("pe", lambda e: e.matmul(wb_[:, 0:128], lhsT=cur[:], rhs=RH[c][:], start=True, stop=False),
                                     reads=[kcur, ("RH", c)], writes=[wk_])
                                S.op("pe", lambda e: e.matmul(wb_[:, 0:128], lhsT=AU[c][:, 1, :], rhs=gram[c][:, 2, :], start=False, stop=False),
                                     reads=[("AU", c), ("gram", c)], writes=[wk_])
                                S.op("pe", lambda e: e.matmul(wb_[:, 0:128], lhsT=tok[c][:, 3, :], rhs=gram[c][:, 3, :], start=False, stop=True),
                                     reads=[("tok", c), ("gram", c)], writes=[wk_])
                                S.op("act", lambda e: e.copy(out=ysc[0:64, c * 64:c * 64 + 64], in_=wb_[0:64, 0:64]),
                                     reads=[wk_], writes=[("ysc", 0, c)])
                                S.op("act", lambda e: e.copy(out=ysc[64:128, c * 64:c * 64 + 64], in_=wb_[64:128, 64:128]),
                                     reads=[wk_], writes=[("ysc", 1, c)])
                                nch[0] += 1

                    def rw_gn(blk):
                        pb = blk % 2
                        t0 = blk * NB
                        r_, k2, v_, g_, Ein = R2[pb], K22[pb], V2[pb], GG2[pb], EIN2[pb]
                        AP_, BP_, KP_, RP_, BC_, KC_, VP_ = PADS[pb]
                        padkeys = [(nm, pb, hh) for nm in ("AP_", "BP_", "KP_", "RP_", "BC_", "KC_", "VP_") for hh in range(2)]
                        pm, pm1 = psM[0], psM[1]
                        ykeys = [("ysc", hh, c) for hh in range(2) for c in range(NCH)]
                        if "yscan" in debug:
                            S.dma("sp", dbg_out("yscan", [512, T])[hp * 128:(hp + 1) * 128, t0:t0 + NB] if "yscan" not in dbg else
                                  dbg["yscan"][hp * 128:(hp + 1) * 128, t0:t0 + NB], ysc[:], reads=ykeys, writes=[("dbg_yscan", hp, blk)])
                        S.op("act", lambda e: e.activation(out=ysq[:], in_=ysc[:], func=AF.Square), reads=ykeys, writes=["ysq"])
                        S.op("pe", lambda e: e.matmul(pm[:, 0:NB], lhsT=blk1[:], rhs=ysc[:], start=True, stop=True), reads=ykeys, writes=["rpsM0"])
                        S.op("pe", lambda e: e.matmul(pm1[:, 0:NB], lhsT=blk1[:], rhs=ysq[:], start=True, stop=True), reads=["ysq"], writes=["rpsM1"])
                        S.op("act", lambda e: e.mul(out=gmean[:], in_=pm[:, 0:NB], mul=1.0 / 64), reads=["rpsM0"], writes=["gmean"])
                        S.op("pool", lambda e: e.tensor_mul(out=gtmp[:], in0=gmean[:], in1=gmean[:]), reads=["gmean"], writes=["gtmp"])
                        S.op("dve", lambda e: e.scalar_tensor_tensor(out=gvar[:], in0=pm1[:, 0:NB], scalar=1.0 / 64, in1=gtmp[:],
                                                                    op0=ALU.mult, op1=ALU.subtract), reads=["rpsM1", "gtmp"], writes=["gvar"])
                        S.op("act", lambda e: e.activation(out=gvar[:], in_=gvar[:], func=AF.Sqrt, bias=64e-5, scale=1.0), reads=["gvar"], writes=["gvar"])
                        S.op("dve", lambda e: e.reciprocal(out=gvar[:], in_=gvar[:]), reads=["gvar"], writes=["gvar"])
                        S.op("dve", lambda e: e.tensor_sub(out=gtmp[:], in0=ysc[:], in1=gmean[:]), reads=ykeys + ["gmean", "gtmp"], writes=["gtmp"])
                        S.op("pool", lambda e: e.tensor_mul(out=gtmp[:], in0=gtmp[:], in1=gvar[:]), reads=["gtmp", "gvar"], writes=["gtmp"])
                        S.op("pool", lambda e: e.tensor_scalar(out=gtmp[:], in0=gtmp[:], scalar1=cols[:, GW + hp:GW + hp + 1],
                                                              scalar2=cols[:, GB + hp:GB + hp + 1], op0=ALU.mult, op1=ALU.add),
                             reads=["gtmp"], writes=["gtmp"])
                        S.op("dve", lambda e: e.scalar_tensor_tensor(out=rkr[:], in0=r_[:], scalar=cols[:, RK + hp:RK + hp + 1], in1=k2[:],
                                                                    op0=ALU.mult, op1=ALU.mult), reads=[("r_", pb), ("k2", pb)], writes=["rkr"])
                        S.op("pe", lambda e: e.matmul(pm[:, 0:NB], lhsT=blk1[:], rhs=rkr[:], start=True, stop=True), reads=["rkr"], writes=["rpsM0"])
                        S.op("dve", lambda e: e.tensor_tensor(out=rkr[:], in0=pm[:, 0:NB], in1=v_[:], op=ALU.mult),
                             reads=["rpsM0", ("v_", pb), "rkr"], writes=["rkr"])
                        S.op("pool", lambda e: e.tensor_add(out=gtmp[:], in0=gtmp[:], in1=rkr[:]), reads=["gtmp", "rkr"], writes=["gtmp"])
                        yo_ = yob[blk % 2]
                        S.op("dve", lambda e: e.tensor_mul(out=yo_[:], in0=gtmp[:], in1=g_[:]), reads=["gtmp", ("g_", pb)], writes=[("ryo", blk % 2)])
                        S.dma("sp", ycat_s[(4 + hp) * 128:(5 + hp) * 128, t0:t0 + NB], yo_[:], reads=[("ryo", blk % 2)],
                              writes=[("ycat", 4 + hp, blk)])
                    nblk_ = T // NB
                    rw_prep(0)
                    for blk in range(nblk_):
                        if blk + 1 < nblk_:
                            rw_prep(blk + 1)
                        rw_chunk(blk)
                        rw_gn(blk)
                S.barrier()
            if "ycat" in debug and stop_after == 4 and not done[0]:
                S.dma("sp", dbg_out("ycat", [1024, T], BF16), ycat_s, writes=["dbg_ycat"])
                S.barrier()
            if stop_after == 4 and not done[0]:
                fin()

        if not done[0]:
          issue_casts(len(cast_jobs))
          with ExitStack() as ph:
            wost = [sbt(ph, f"wost{i}", [128, D]) for i in range(2)]
            wob = sbt(ph, "wob", [128, 8, D], BF16)
            ycb = [sbt(ph, f"ycatb{i}", [128, 8, 128], BF16) for i in range(2)]
            xb = [sbt(ph, f"x3b{i}", [128, D]) for i in range(2)]
            x1b = [sbt(ph, f"x1b{i}", [128, D]) for i in range(2)]
            junk = sbt(ph, "junk3", [128, D])
            xn = [sbt(ph, f"xn3{i}", [128, D]) for i in range(2)]
            u2b = [sbt(ph, f"u2b{i}", [128, 8, 128], BF16) for i in range(2)]
            ss = sbt(ph, "ss3", [128, 32])
            rs = sbt(ph, "rs3", [128, 32])
            pO = [pst(ph, f"pO{i}", [128, 1024]) for i in range(2)]
            pT = [pst(ph, f"pT3{i}", [128, 1024]) for i in range(2)]
            gtm_bc = sbt(ph, "gtm_bc", [128, D])
            build_gate_bc(ph, gtm_bc, 16, pO[0], ("pO", 0, 0))
            for ct in range(8):
                S.dma("sp", wost[ct % 2][:], wout_d[ct * 128:(ct + 1) * 128, :], writes=[("wost", ct % 2)])
                S.op("pool", lambda e: e.tensor_copy(out=wob[:, ct, :], in_=wost[ct % 2][:]), reads=[("wost", ct % 2)], writes=["wob"])
            u2T3 = u2T_s.rearrange("p (k t) -> p k t", k=8)
            def p3_mm(tt):
                b = tt % 2
                ts_ = slice(tt * 128, (tt + 1) * 128)
                S.dma("sp", ycb[b][:], ycat_s[:, ts_].rearrange("(c p) t -> p c t", p=128), writes=[("ycatb", b)])
                S.dma("sp", xb[b][:], x_d[ts_, :], writes=[("x3b", b)])
                for dh in range(2):
                    for ct in range(8):
                        S.op("pe", lambda e: e.matmul(pO[b][:, dh * 512:(dh + 1) * 512], lhsT=ycb[b][:, ct, :],
                                                      rhs=wob[:, ct, dh * 512:(dh + 1) * 512], start=(ct == 0), stop=(ct == 7)),
                             reads=[("ycatb", b), "wob"], writes=[("pO", b, dh)])

            p3_mm(0)
            for tt in range(32):
                b = tt % 2
                ts_ = slice(tt * 128, (tt + 1) * 128)
                if tt + 1 < 32:
                    p3_mm(tt + 1)
                S.op("dve", lambda e: e.tensor_tensor(out=x1b[b][:], in0=pO[b][:], in1=gtm_bc[:], op=ALU.mult),
                     reads=[("pO", b, 0), ("pO", b, 1), "gtm_bc"], writes=[("x1b", b)])
                S.op("pool", lambda e: e.tensor_add(out=x1b[b][:], in0=x1b[b][:], in1=xb[b][:]), reads=[("x1b", b), ("x3b", b)], writes=[("x1b", b)])
                S.dma("sp", x1_s[ts_, :], x1b[b][:], reads=[("x1b", b)], writes=[("x1_s", tt)])
                S.op("act", lambda e: e.activation(out=junk[:], in_=x1b[b][:], func=AF.Square), reads=[("x1b", b)], writes=["junk3"])
                S.op("dve", lambda e: e.reduce_sum(out=ss[:, tt:tt + 1], in_=junk[:], axis=AX.X), reads=["junk3"], writes=[("ss3", tt)])
                S.op("act", lambda e: e.activation(out=rs[:, tt:tt + 1], in_=ss[:, tt:tt + 1], func=AF.Sqrt, bias=1e-6, scale=1.0 / D),
                     reads=[("ss3", tt)], writes=[("rs3", tt)])
                S.op("dve", lambda e: e.reciprocal(out=rs[:, tt:tt + 1], in_=rs[:, tt:tt + 1]), reads=[("rs3", tt)], writes=[("rs3", tt)])
                S.op("act", lambda e: e.activation(out=xn[b][:], in_=x1b[b][:], func=AF.Copy, scale=rs[:, tt:tt + 1]),
                     reads=[("x1b", b), ("rs3", tt)], writes=[("xn3", b)])
                for k in range(8):
                    S.op("pe", lambda e: e.transpose(pT[b][:, k * 128:(k + 1) * 128], xn[b][:, k * 128:(k + 1) * 128], ident[:]),
                         reads=[("xn3", b)], writes=[("pT3", b, k // 4)])
                for k in range(8):
                    aff("act" if k < 4 else "dve", u2b[b][:, k, :], pT[b][:, k * 128:(k + 1) * 128],
                        dcol[:, GSC2 + k:GSC2 + k + 1], modT[:, 24 + k:24 + k + 1], reads=[("pT3", b, k // 4)], writes=[("u2b", b)])
                S.dma("sp", u2T3[:, :, ts_], u2b[b][:], reads=[("u2b", b)], writes=[("u2T_s", tt)])
            S.barrier()
            if "x1" in debug:
                S.dma("sp", dbg_out("x1", [T, D]), x1_s, writes=["dbg_x1"])
                S.dma("sp", dbg_out("u2T", [128, 8 * T], BF16, force=True), u2T_s, writes=["dbg_u2T"])
                S.barrier()
        if stop_after == 5 and not done[0]:
            fin()

        if not done[0]:
          with ExitStack() as ph:
            Wc = sbt(ph, "Wc", [128, 8, 2048], BF16)
            KTs = sbt(ph, "KTs", [128, 16, 128])
            wqs = [sbt(ph, f"wqs{i}", [128, D]) for i in range(2)]
            u2t = [sbt(ph, f"u2t{i}", [128, 8, 128], BF16) for i in range(2)]
            sc2 = [sbt(ph, f"sc_{i}", [128, 16, 128]) for i in range(2)]
            wrk4 = [sbt(ph, f"wrk{i}", [128, 256]) for i in range(4)]
            vals = sbt(ph, "vals", [128, 16, 16])
            idxu = sbt(ph, "idxu", [128, 16, 16], U16)
            idxf = sbt(ph, "idxf", [128, 16, 16])
            cand = sbt(ph, "cand", [128, 8, 256])
            best = sbt(ph, "best", [128, 8, 16])
            posu = sbt(ph, "posu", [128, 8, 16], U16)
            posf = sbt(ph, "posf", [128, 8, 16])
            big = sbt(ph, "big", [128, 8, 16, 16])
            thr16 = sbt(ph, "thr16", [128, 16])
            io16 = sbt(ph, "io16", [128, 16])
            ak = sbt(ph, "ak", [128, 8, 16])
            bk = sbt(ph, "bk", [128, 8, 16])
            ik = sbt(ph, "ik", [128, 128])
            jk = sbt(ph, "jk", [128, 128])
            gk = sbt(ph, "gk", [128, 128])
            zs = sbt(ph, "zs", [128, 8])
            slT = [sbt(ph, f"slT{i}", [128, 3, 128]) for i in range(2)]
            pW = [pst(ph, f"pW{i}", [128, 512]) for i in range(2)]
            pS = [pst(ph, f"pS{i}", [128, 512]) for i in range(4)]
            pTr = pst(ph, "pTr", [128, 512])

            S.dma("sp", KTs[:].rearrange("p a b -> p (a b)"), KT_d, writes=["KTs"])
            S.op("dve", lambda e: e.tensor_scalar_mul(out=thr16[:], in0=iota_f[:, 0:16], scalar1=16.0), writes=["thr16"])
            S.op("dve", lambda e: e.tensor_copy(out=io16[:], in_=iota_f[:, 0:16]), writes=["io16"])
            for hp in range(16):
                S.dma("sp", wqs[hp % 2][:], wqT_d[hp * 128:(hp + 1) * 128, :], writes=[("wqs", hp % 2)])
                for dk in range(8):
                    S.op("pe", lambda e: e.matmul(pW[dk % 2][:, 0:128], lhsT=wqs[hp % 2][:, dk * 128:(dk + 1) * 128], rhs=KTs[:, hp, :],
                                                  start=True, stop=True), reads=[("wqs", hp % 2), "KTs"], writes=[("pW", dk % 2)])
                    S.op("act" if dk % 2 == 0 else "dve",
                         (lambda e: e.copy(out=Wc[:, dk, hp * 128:(hp + 1) * 128], in_=pW[dk % 2][:, 0:128])) if dk % 2 == 0 else
                         (lambda e: e.tensor_copy(out=Wc[:, dk, hp * 128:(hp + 1) * 128], in_=pW[dk % 2][:, 0:128])),
                         reads=[("pW", dk % 2)], writes=["Wc"])
            u2T3 = u2T_s.rearrange("p (k t) -> p k t", k=8)
            def p4_scores(tt):
                b = tt % 2
                ts_ = slice(tt * 128, (tt + 1) * 128)
                S.dma("sp", u2t[b][:], u2T3[:, :, ts_], writes=[("u2t", b)])
                sc_ = sc2[b]
                for q4 in range(4):
                    for dk in range(8):
                        S.op("pe", lambda e: e.matmul(pS[q4][:], lhsT=u2t[b][:, dk, :], rhs=Wc[:, dk, q4 * 512:(q4 + 1) * 512],
                                                      start=(dk == 0), stop=(dk == 7)), reads=[("u2t", b), "Wc"], writes=[("pS", q4)])
                    S.op("act", lambda e: e.copy(out=sc_[:, q4 * 4:(q4 + 1) * 4, :].rearrange("p a b -> p (a b)"), in_=pS[q4][:]),
                         reads=[("pS", q4)], writes=[("sc_", b, q4)])

            p4_scores(0)
            for tt in range(32):
                b = tt % 2
                ts_ = slice(tt * 128, (tt + 1) * 128)
                sc_ = sc2[b]
                if tt + 1 < 32:
                    p4_scores(tt + 1)
                if "scores" in debug and tt == 0:
                    S.dma("sp", dbg_out("scores", [128, 2048]), sc_[:].rearrange("p a b -> p (a b)"),
                          reads=[("sc_", b, q) for q in range(4)], writes=["dbg_scores"])
                NIT = 4
                ALLV = [("vals", g, hf) for g in range(16) for hf in range(2)]
                ALLI = [("idxu", g, hf) for g in range(16) for hf in range(2)]
                ALLB = [("best", h, hf) for h in range(8) for hf in range(2)]
                ALLP = [("posu", h, hf) for h in range(8) for hf in range(2)]
                for g0 in range(0, 16, NIT):
                    grp = range(g0, g0 + NIT)
                    for g16 in grp:
                        S.op("dve", lambda e: e.max(out=vals[:, g16, 0:8], in_=sc_[:, g16, :]), reads=[("sc_", b, g16 // 4)], writes=[("vals", g16, 0)])
                    for g16 in grp:
                        S.op("dve", lambda e: e.max_index(out=idxu[:, g16, 0:8], in_max=vals[:, g16, 0:8], in_values=sc_[:, g16, :]),
                             reads=[("sc_", b, g16 // 4), ("vals", g16, 0)], writes=[("idxu", g16, 0)])
                    for g16 in grp:
                        w_ = wrk4[g16 % NIT]
                        S.op("dve", lambda e: e.match_replace(out=w_[:, 0:128], in_to_replace=vals[:, g16, 0:8], in_values=sc_[:, g16, :],
                                                             imm_value=-1e30), reads=[("sc_", b, g16 // 4), ("vals", g16, 0)], writes=[("wrk", g16 % NIT)])
                    for g16 in grp:
                        w_ = wrk4[g16 % NIT]
                        S.op("dve", lambda e: e.max(out=vals[:, g16, 8:16], in_=w_[:, 0:128]), reads=[("wrk", g16 % NIT)], writes=[("vals", g16, 1)])
                    for g16 in grp:
                        w_ = wrk4[g16 % NIT]
                        S.op("dve", lambda e: e.max_index(out=idxu[:, g16, 8:16], in_max=vals[:, g16, 8:16], in_values=w_[:, 0:128]),
                             reads=[("wrk", g16 % NIT), ("vals", g16, 1)], writes=[("idxu", g16, 1)])
                S.op("pool", lambda e: e.tensor_copy(out=idxf[:], in_=idxu[:]), reads=ALLI, writes=["idxf"])
                v4 = vals[:].rearrange("p (h two) k -> p h two k", two=2)
                i4 = idxf[:].rearrange("p (h two) k -> p h two k", two=2)
                cand4 = cand[:].rearrange("p h (a b) -> p h a b", b=16)
                S.op("dve", lambda e: e.tensor_tensor(out=cand4, in0=v4[:, :, 0, :].unsqueeze(3).to_broadcast([128, 8, 16, 16]),
                                                     in1=v4[:, :, 1, :].unsqueeze(2).to_broadcast([128, 8, 16, 16]), op=ALU.add),
                     reads=ALLV, writes=["cand"])
                for h0 in range(0, 8, NIT):
                    grp = range(h0, h0 + NIT)
                    for h in grp:
                        S.op("dve", lambda e: e.max(out=best[:, h, 0:8], in_=cand[:, h, :]), reads=["cand"], writes=[("best", h, 0)])
                    for h in grp:
                        S.op("dve", lambda e: e.max_index(out=posu[:, h, 0:8], in_max=best[:, h, 0:8], in_values=cand[:, h, :]),
                             reads=["cand", ("best", h, 0)], writes=[("posu", h, 0)])
                    for h in grp:
                        w_ = wrk4[h % NIT]
                        S.op("dve", lambda e: e.match_replace(out=w_[:], in_to_replace=best[:, h, 0:8], in_values=cand[:, h, :],
                                                             imm_value=-1e30), reads=["cand", ("best", h, 0)], writes=[("wrk", h % NIT)])
                    for h in grp:
                        w_ = wrk4[h % NIT]
                        S.op("dve", lambda e: e.max(out=best[:, h, 8:16], in_=w_[:]), reads=[("wrk", h % NIT)], writes=[("best", h, 1)])
                    for h in grp:
                        w_ = wrk4[h % NIT]
                        S.op("dve", lambda e: e.max_index(out=posu[:, h, 8:16], in_max=best[:, h, 8:16], in_values=w_[:]),
                             reads=[("wrk", h % NIT), ("best", h, 1)], writes=[("posu", h, 1)])
                S.op("dve", lambda e: e.tensor_copy(out=posf[:], in_=posu[:]), reads=ALLP, writes=["posf"])
                gk3 = gk[:].rearrange("p (h k) -> p h k", k=16)
                S.op("pool", lambda e: e.tensor_sub(out=gk3, in0=best[:], in1=best[:, :, 0:1].to_broadcast([128, 8, 16])),
                     reads=ALLB, writes=["gk"])
                S.op("act", lambda e: e.activation(out=gk[:], in_=gk[:], func=AF.Exp), reads=["gk"], writes=["gk"])
                S.op("dve", lambda e: e.tensor_reduce(out=zs[:], in_=gk3, axis=AX.X, op=ALU.add), reads=["gk"], writes=["zs"])
                S.op("dve", lambda e: e.reciprocal(out=zs[:], in_=zs[:]), reads=["zs"], writes=["zs"])
                S.op("pool", lambda e: e.tensor_mul(out=gk3, in0=gk3, in1=zs[:].unsqueeze(2).to_broadcast([128, 8, 16])),
                     reads=["gk", "zs"], writes=["gk"])
                S.op("dve", lambda e: e.tensor_tensor(out=big[:], in0=posf[:].unsqueeze(3).to_broadcast([128, 8, 16, 16]),
                                                     in1=thr16[:].unsqueeze(1).unsqueeze(1).to_broadcast([128, 8, 16, 16]), op=ALU.is_ge),
                     reads=["posf", "thr16"], writes=["big"])
                S.op("dve", lambda e: e.tensor_reduce(out=ak[:], in_=big[:, :, :, 1:16], axis=AX.X, op=ALU.add), reads=["big"], writes=["ak"])
                S.op("dve", lambda e: e.scalar_tensor_tensor(out=bk[:], in0=ak[:], scalar=-16.0, in1=posf[:], op0=ALU.mult, op1=ALU.add),
                     reads=["ak", "posf"], writes=["bk"])
                for (sel, half, dst) in ((ak, 0, ik), (bk, 1, jk)):
                    S.op("dve", lambda e: e.tensor_tensor(out=big[:], in0=sel[:].unsqueeze(3).to_broadcast([128, 8, 16, 16]),
                                                         in1=io16[:].unsqueeze(1).unsqueeze(1).to_broadcast([128, 8, 16, 16]), op=ALU.is_equal),
                         reads=[sel.name[2:], "big"], writes=["big"])
                    S.op("dve", lambda e: e.tensor_mul(out=big[:], in0=big[:],
                                                      in1=i4[:, :, half, :].unsqueeze(2).to_broadcast([128, 8, 16, 16])),
                         reads=["big", "idxf"], writes=["big"])
                    S.op("dve", lambda e: e.tensor_reduce(out=dst[:].rearrange("p (h k) -> p h k", k=16), in_=big[:], axis=AX.X, op=ALU.add),
                         reads=["big"], writes=[dst.name[2:]])
                if "route" in debug and tt == 0:
                    S.dma("sp", dbg_out("r_g", [128, 128], force=True), gk[:], reads=["gk"], writes=["dbg_rg"])
                    S.dma("sp", dbg_out("r_i", [128, 128], force=True), ik[:], reads=["ik"], writes=["dbg_ri"])
                    S.dma("sp", dbg_out("r_j", [128, 128], force=True), jk[:], reads=["jk"], writes=["dbg_rj"])
                for q_, src in enumerate((gk, ik, jk)):
                    S.op("pe", lambda e: e.transpose(pTr[:, q_ * 128:(q_ + 1) * 128], src[:], ident[:]), reads=[src.name[2:]], writes=["pTr"])
                S.op("act", lambda e: e.copy(out=slT[b][:].rearrange("p a b -> p (a b)"), in_=pTr[:, 0:384]), reads=["pTr"], writes=[("slT", b)])
                S.dma("sp", gT_s[:, ts_], slT[b][:, 0, :], reads=[("slT", b)], writes=[("gT_s", tt)])
                S.dma("sp", iT_s[:, ts_], slT[b][:, 1, :], reads=[("slT", b)], writes=[("iT_s", tt)])
                S.dma("sp", jT_s[:, ts_], slT[b][:, 2, :], reads=[("slT", b)], writes=[("jT_s", tt)])
            S.barrier()
        if "slots" in debug and not done[0]:
            S.dma("sp", dbg_out("gT", [128, T], force=True), gT_s, writes=["dbg_gT"])
            S.dma("sp", dbg_out("iT", [128, T], force=True), iT_s, writes=["dbg_iT"])
            S.dma("sp", dbg_out("jT", [128, T], force=True), jT_s, writes=["dbg_jT"])
            S.barrier()
        if stop_after == 6 and not done[0]:
            fin()

        if not done[0]:
          with ExitStack() as ph:
            TB = 256
            NPF = 6
            NTK = 8
            Gs = [sbt(ph, f"G{i}", [128, TB, 128], BF16) for i in range(2)]
            u2k = [sbt(ph, f"u2k{i}", [128, 8, TB], BF16) for i in range(2)]
            slg = [sbt(ph, f"slg{i}", [128, TB]) for i in range(2)]
            sli = [sbt(ph, f"sli{i}", [128, TB]) for i in range(2)]
            slj = [sbt(ph, f"slj{i}", [128, TB]) for i in range(2)]
            Pb = [sbt(ph, f"Pb{i}", [128, NTK, 128], BF16) for i in range(2)]
            Qb = [sbt(ph, f"Qb{i}", [128, NTK, 128], BF16) for i in range(2)]
            Ub = [sbt(ph, f"Ub{i}", [128, 8, 128], BF16) for i in range(NPF)]
            Vb = [sbt(ph, f"Vb{i}", [128, D], BF16) for i in range(NPF)]
            hb = [sbt(ph, f"hb{i}", [128, TB], BF16) for i in range(2)]
            Wb = [sbt(ph, f"Wb{i}", [128, TB], BF16) for i in range(2)]
            x1t_ = sbt(ph, "x1t", [128, D]); x1t = [x1t_, x1t_]
            x2t_ = sbt(ph, "x2t", [128, D]); x2t = [x2t_, x2t_]
            fs = sbt(ph, "fs", [128, 2]); fr = sbt(ph, "fr", [128, 2])
            ot_ = sbt(ph, "ot", [128, D]); ot = [ot_, ot_]
            junk = ot_
            pAcc = [pst(ph, f"pAcc{i}", [128, 512]) for i in range(4)]
            pSx = [pst(ph, f"pSx{i}", [128, 512]) for i in range(2)]
            pG = [pst(ph, f"pG{i}", [128, 512]) for i in range(2)]
            gtf_bc = sbt(ph, "gtf_bc", [128, D])
            fing_bc = sbt(ph, "fing_bc", [128, D])
            S.dma("sp", fing_bc[:], fing_d, writes=["fing"])
            build_gate_bc(ph, gtf_bc, 40, pG[0], ("pG", 0))
            u2T3 = u2T_s.rearrange("p (k t) -> p k t", k=8)
            allU = [("Ub", i) for i in range(NCAST)]
            allV = [("Vb", i) for i in range(NCAST)]
            nload = [0]
            ngq = [0]
            nblk = T // TB

            def load_expert(i):
                s_ = nload[0] % NPF
                nload[0] += 1
                S.dma("sp", Ub[s_][:].rearrange("p k j -> p (k j)"), Ub_s[i * 128:(i + 1) * 128, :], reads=allU, writes=[("Ubt", s_)])
                S.dma("sp", Vb[s_][:], Vb_s[i * 128:(i + 1) * 128, :], reads=allV, writes=[("Vbt", s_)])

            def load_block(blk):
                gb = blk % 2
                t0 = blk * TB
                S.dma("sp", u2k[gb][:], u2T3[:, :, t0:t0 + TB], writes=[("u2k", gb)])
                S.dma("sp", slg[gb][:], gT_s[:, t0:t0 + TB], writes=[("slg", gb)])
                S.dma("sp", sli[gb][:], iT_s[:, t0:t0 + TB], writes=[("sli", gb)])
                S.dma("sp", slj[gb][:], jT_s[:, t0:t0 + TB], writes=[("slj", gb)])

            def gbuild_dve(blk, tb):
                gb = blk % 2
                pbuf = (blk * (TB // NTK) + tb) % 2
                ts_ = slice(tb * NTK, (tb + 1) * NTK)
                io_bc = iota_f[:].unsqueeze(1).to_broadcast([128, NTK, 128])
                S.op("dve", lambda e: e.tensor_tensor(out=Pb[pbuf][:], in0=io_bc,
                                                     in1=sli[gb][:, ts_].unsqueeze(2).to_broadcast([128, NTK, 128]), op=ALU.is_equal),
                     reads=[("sli", gb)], writes=[("Pb", pbuf)])
                S.op("pool", lambda e: e.tensor_tensor(out=Pb[pbuf][:], in0=Pb[pbuf][:],
                                                      in1=slg[gb][:, ts_].unsqueeze(2).to_broadcast([128, NTK, 128]), op=ALU.mult),
                     reads=[("slg", gb), ("Pb", pbuf)], writes=[("Pb", pbuf)])
                S.op("dve", lambda e: e.tensor_tensor(out=Qb[pbuf][:], in0=io_bc,
                                                     in1=slj[gb][:, ts_].unsqueeze(2).to_broadcast([128, NTK, 128]), op=ALU.is_equal),
                     reads=[("slj", gb)], writes=[("Qb", pbuf)])

            def gbuild_pe(blk, tb):
                gb = blk % 2
                pbuf = (blk * (TB // NTK) + tb) % 2
                for tq in range(NTK // 4):
                    pgi = ngq[0] % 2
                    ngq[0] += 1
                    pg = pG[pgi]
                    for t4 in range(4):
                        tl = tq * 4 + t4
                        S.op("pe", lambda e: e.matmul(pg[:, t4 * 128:(t4 + 1) * 128], lhsT=Qb[pbuf][:, tl, :], rhs=Pb[pbuf][:, tl, :],
                                                      start=True, stop=True), reads=[("Pb", pbuf), ("Qb", pbuf)], writes=[("pG", pgi)])
                    tg = tb * NTK + tq * 4
                    S.op("act", lambda e: e.copy(out=Gs[gb][:, tg:tg + 4, :].rearrange("p a b -> p (a b)"), in_=pg[:]),
                         reads=[("pG", pgi)], writes=[("G", gb, tb)])

            def gbuild(blk, tb):
                gbuild_dve(blk, tb)
                gbuild_pe(blk, tb)

            def s_stage(blk, i):
                gb = blk % 2
                s_ = (blk * 128 + i) % NPF
                b2 = i % 2
                for dk in range(8):
                    S.op("pe", lambda e: e.matmul(pSx[b2][:, 0:TB], lhsT=Ub[s_][:, dk, :], rhs=u2k[gb][:, dk, :], start=(dk == 0), stop=(dk == 7)),
                         reads=[("Ubt", s_), ("u2k", gb)], writes=[("pSx", b2)])
                S.op("act", lambda e: e.activation(out=hb[b2][:], in_=pSx[b2][:, 0:TB], func=AF.Gelu), reads=[("pSx", b2)], writes=[("hb", b2)])
                S.op("dve", lambda e: e.tensor_tensor(out=Wb[b2][:], in0=hb[b2][:], in1=Gs[gb][:, :, i], op=ALU.mult),
                     reads=[("hb", b2)] + ([("G", gb, tb) for tb in range(TB // NTK)] if i < 2 else []), writes=[("Wb", b2)])

            def v_stage(blk, i):
                s_ = (blk * 128 + i) % NPF
                b2 = i % 2
                for tt2 in range(TB // 128):
                    for dh in range(2):
                        S.op("pe", lambda e: e.matmul(pAcc[tt2 * 2 + dh][:], lhsT=Wb[b2][:, tt2 * 128:(tt2 + 1) * 128],
                                                      rhs=Vb[s_][:, dh * 512:(dh + 1) * 512], start=(i == 0), stop=(i == 127)),
                             reads=[("Wb", b2), ("Vbt", s_)], writes=[("pAcc", tt2 * 2 + dh)])

            def epilogue(blk):
                t0 = blk * TB
                for tt2 in range(TB // 128):
                    tsl = slice(t0 + tt2 * 128, t0 + (tt2 + 1) * 128)
                    S.dma("sp", x1t[tt2][:], x1_s[tsl, :], writes=["x1t"])
                    for dh in range(2):
                        S.op("dve", lambda e: e.tensor_tensor(out=x2t[tt2][:, dh * 512:(dh + 1) * 512], in0=pAcc[tt2 * 2 + dh][:],
                                                             in1=gtf_bc[:, dh * 512:(dh + 1) * 512], op=ALU.mult),
                             reads=[("pAcc", tt2 * 2 + dh), "gtf_bc"], writes=["x2t"])
                    if "peer" in debug:
                        if "peer" not in dbg:
                            dbg_out("peer", [T, D], force=True)
                        S.dma("sp", dbg["peer"][tsl, :], x2t[tt2][:], reads=["x2t"], writes=[("dbg_peer", blk, tt2)])
                    S.op("pool", lambda e: e.tensor_add(out=x2t[tt2][:], in0=x2t[tt2][:], in1=x1t[tt2][:]),
                         reads=["x2t", "x1t"], writes=["x2t"])
                    S.op("act", lambda e: e.activation(out=junk[:], in_=x2t[tt2][:], func=AF.Square), reads=["x2t"], writes=["ot"])
                    S.op("dve", lambda e: e.reduce_sum(out=fs[:, tt2:tt2 + 1], in_=junk[:], axis=AX.X), reads=["ot"], writes=[("fs", tt2)])
                    S.op("act", lambda e: e.activation(out=fr[:, tt2:tt2 + 1], in_=fs[:, tt2:tt2 + 1], func=AF.Sqrt, bias=1e-6, scale=1.0 / D),
                         reads=[("fs", tt2)], writes=[("fr", tt2)])
                    S.op("dve", lambda e: e.reciprocal(out=fr[:, tt2:tt2 + 1], in_=fr[:, tt2:tt2 + 1]), reads=[("fr", tt2)], writes=[("fr", tt2)])
                    S.op("dve", lambda e: e.scalar_tensor_tensor(out=ot[tt2][:], in0=x2t[tt2][:], scalar=fr[:, tt2:tt2 + 1], in1=fing_bc[:],
                                                                op0=ALU.mult, op1=ALU.mult),
                         reads=["x2t", ("fr", tt2), "fing"], writes=["ot"])
                    S.dma("sp", out_d[tsl, :], ot[tt2][:], reads=["ot"], writes=[("out", blk, tt2)])

            load_block(0)
            for i in range(NPF - 1):
                load_expert(i)
            for tb in range(TB // NTK):
                gbuild(0, tb)
            for blk in range(nblk):
                if blk + 1 < nblk:
                    load_block(blk + 1)
                s_stage(blk, 0)
                s_stage(blk, 1)
                for i in range(128):
                    nxt_e = blk * 128 + i + NPF - 1
                    if nxt_e < nblk * 128:
                        load_expert(nxt_e % 128)
                    v_stage(blk, i)
                    if i + 2 < 128:
                        s_stage(blk, i + 2)
                    if blk + 1 < nblk and i % 4 == 0:
                        if i >= 4:
                            gbuild_pe(blk + 1, i // 4 - 1)
                        gbuild_dve(blk + 1, i // 4)
                if blk + 1 < nblk:
                    gbuild_pe(blk + 1, TB // NTK - 1)
                epilogue(blk)
            S.barrier()
            S.finish([])
    print(f"[kernel] instructions={S.ninst} waits={S.nwaits}")
    return nc, dbg


def col(v, n):
    return np.ascontiguousarray(np.asarray(v, np.float32).reshape(n, 128).T)


def prep_inputs(inp):
    f = lambda k: np.asarray(inp[k], np.float32)
    cols = np.zeros((128, NCOL), np.float32)
    cols[:, G1:G1 + 8] = col(f("norm_mix_g")[0], 8)
    cols[:, G2:G2 + 8] = col(f("norm_ffn_g")[0], 8)
    cols[:, CW:CW + 124] = f("conv_dw_w")[0].reshape(31, 4, 128).transpose(2, 1, 0).reshape(128, 124)
    cols[:, CB:CB + 4] = col(f("conv_dw_b")[0], 4)
    cols[:, LW:LW + 4] = col(f("conv_ln_w")[0], 4)
    cols[:, LB:LB + 4] = col(f("conv_ln_b")[0], 4)
    cols[:, MU:MU + 14] = col(f("rwkv_mu")[0], 14)
    cols[:, W0:W0 + 4] = col(f("rwkv_w0")[0], 4)
    cols[:, A0:A0 + 4] = col(f("rwkv_a0")[0], 4)
    cols[:, KK:KK + 4] = col(f("rwkv_k_k")[0], 4)
    cols[:, KA:KA + 4] = col(f("rwkv_k_a")[0], 4)
    cols[:, RK:RK + 4] = col(f("rwkv_r_k")[0].reshape(-1), 4)
    cols[:, GW:GW + 4] = col(f("rwkv_gn_w")[0], 4)
    cols[:, GB:GB + 4] = col(f("rwkv_gn_b")[0], 4)
    shared = {
        "ada_w": np.ascontiguousarray(f("ada_w")[0]),
        "ada_b_col": col(f("ada_b")[0], 48),
        "cols": cols,
        "final_g_bc": np.ascontiguousarray(np.broadcast_to(f("final_g")[None, :], (128, D))),
        "w_in": np.ascontiguousarray(f("w_in")[0]),
        "wa2": np.ascontiguousarray(np.concatenate([f("rwkv_w2")[0], f("rwkv_a2")[0]], axis=0)),
        "g2": np.ascontiguousarray(f("rwkv_g2")[0]),
        "w_out": np.ascontiguousarray(f("w_out")[0]),
        "wqT": np.ascontiguousarray(f("peer_w_q")[0].T),
        "KT": np.ascontiguousarray(f("peer_sub_keys")[0].reshape(16, 128, 128).transpose(2, 0, 1).reshape(128, 2048)),
        "UTt": np.ascontiguousarray(f("peer_u")[0].reshape(128, 128, 8, 128).transpose(0, 3, 2, 1).reshape(128 * 128, 1024)),
        "V": np.ascontiguousarray(f("peer_v")[0]),
    }
    maps = []
    for b in range(NCORES):
        m = dict(shared)
        m["x"] = np.ascontiguousarray(f("x")[b])
        m["c_col"] = col(f("c")[b], 8)
        maps.append(m)
    return maps


_NC_CACHE = {}


def kernel(**inputs):
    maps = prep_inputs(inputs)
    if "nc" not in _NC_CACHE:
        _NC_CACHE["nc"] = build_nc()[0]
    nc = _NC_CACHE["nc"]
    res = run_bass_kernel_spmd(nc, maps, core_ids=list(range(NCORES)))
    return np.stack([np.asarray(r["out"], np.float32) for r in res.results], axis=0)
```

```python
import numpy as np
from contextlib import ExitStack
import concourse.bass as bass
import concourse.mybir as mybir
from concourse.bass_utils import run_bass_kernel_spmd

F32 = mybir.dt.float32
BF16 = mybir.dt.bfloat16
U16 = mybir.dt.uint16
I32 = mybir.dt.int32
AF = mybir.ActivationFunctionType
ALU = mybir.AluOpType
AX = mybir.AxisListType

T = 4096
D = 1024
NCORES = 8
C0 = 0.6065306597126334

G1, G2, CW, CB, LW, LB, MU, W0, A0, KK, KA, RK, GW, GB, NCOL = (
    0, 8, 16, 140, 144, 148, 152, 166, 170, 174, 178, 182, 186, 190, 194)


class Sync:
    NDMA = 12

    def __init__(self, nc, stack):
        self.nc = nc
        self.eng = {"pe": nc.tensor, "act": nc.scalar, "dve": nc.vector,
                    "pool": nc.gpsimd, "sp": nc.sync}
        self.sem, self.cnt = {}, {}
        self.semobj = {}
        for e in self.eng:
            self.sem[e] = stack.enter_context(nc.semaphore("s_" + e))
            self.semobj[self.sem[e].name] = self.sem[e]
            self.cnt[e] = 0
        self.dsem, self.dnext = {}, {}
        self.eng["cast"] = nc.gpsimd
        for q in ("sp", "pool", "act", "cast"):
            self.dsem[q] = [[stack.enter_context(nc.semaphore(f"d_{q}{i}")), 0]
                            for i in range(32 if q == "cast" else self.NDMA)]
            for s, _ in self.dsem[q]:
                self.semobj[s.name] = s
            self.dnext[q] = 0
        self.seen = {e: {} for e in self.eng}
        self.snap = {}
        self.lastw = {}
        self.readers = {}
        self.nwaits = 0
        self.ninst = 0

    def _need(self, e, ev, waits):
        if ev is None:
            return
        name, val = ev
        if self.seen[e].get(name, 0) >= val:
            return
        if e == "pe" and name == self.sem["pe"].name:
            return
        if waits.get(name, 0) < val:
            waits[name] = val

    def _do_waits(self, e, waits):
        for name, val in waits.items():
            if self.seen[e].get(name, 0) >= val:
                continue
            self.eng[e].wait_ge(self.semobj[name], val)
            self.nwaits += 1
            sn = self.snap.get((name, val))
            se = self.seen[e]
            if sn:
                for k, v in sn.items():
                    if se.get(k, 0) < v:
                        se[k] = v
            if se.get(name, 0) < val:
                se[name] = val

    def _deps(self, e, reads, writes):
        waits = {}
        for k in reads:
            self._need(e, self.lastw.get(k), waits)
        for k in writes:
            self._need(e, self.lastw.get(k), waits)
            for ev in self.readers.get(k, ()):
                self._need(e, ev, waits)
        self._do_waits(e, waits)

    def _record(self, e, ev, reads, writes):
        for k in reads:
            self.readers.setdefault(k, []).append(ev)
        for k in writes:
            self.lastw[k] = ev
            self.readers[k] = []
        self.snap[ev] = dict(self.seen[e])

    EXCL = {"psmod", "psb", "pT", "psA", "psB", "psC", "ps1", "ps2", "rpsA0", "rpsA1", "rpsM0", "rpsM1",
            "rW", "pO", "pT3", "pW", "pS", "pTr", "pAcc", "pSx", "pG"}

    def op(self, e, fn, reads=(), writes=()):
        ex = [k for k in reads if (k if isinstance(k, str) else k[0]) in self.EXCL]
        if ex:
            writes = list(writes) + ex
        self._deps(e, reads, writes)
        ins = fn(self.eng[e])
        self.cnt[e] += 1
        s = self.sem[e]
        ins.then_inc(s, 1)
        ev = (s.name, self.cnt[e])
        self._record(e, ev, reads, writes)
        self.ninst += 1
        return ev

    def dma(self, q, out, in_, reads=(), writes=(), **kw):
        pool = self.dsem[q]
        slot = pool[self.dnext[q] % len(pool)]
        self.dnext[q] += 1
        s = slot[0]
        waits = {}
        if slot[1] > 0:
            self._need(q, (s.name, slot[1]), waits)
        self._do_waits(q, waits)
        self._deps(q, reads, writes)
        ins = self.eng[q].dma_start(out=out, in_=in_, **kw)
        slot[1] += 16
        ins.then_inc(s, 16)
        ev = (s.name, slot[1])
        self._record(q, ev, reads, writes)
        self.ninst += 1
        return ev

    def barrier(self):
        evs = []
        for e in self.cnt:
            if self.cnt[e]:
                evs.append((self.sem[e].name, self.cnt[e]))
        for q in self.dsem:
            if q == "cast":
                continue
            for s, c in self.dsem[q]:
                if c:
                    evs.append((s.name, c))
        for e in self.eng:
            if e == "cast":
                continue
            waits = {}
            for ev in evs:
                self._need(e, ev, waits)
            self._do_waits(e, waits)
        keep = lambda k: isinstance(k, tuple) and k[0] in ("Ub", "Vb")
        self.lastw = {k: v for k, v in self.lastw.items() if keep(k)}
        self.readers = {k: v for k, v in self.readers.items() if keep(k)}

    def finish(self, keys):
        waits = {}
        for k in keys:
            self._need("sp", self.lastw.get(k), waits)
        self._do_waits("sp", waits)


def build_nc(debug=(), stop_after=None):
    nc = bass.Bass("TRN2", target_bir_lowering=False)

    def din(name, shape, dt=F32):
        return nc.dram_tensor(name, list(shape), dt, kind="ExternalInput").ap()

    def dscr(name, shape, dt=F32):
        return nc.dram_tensor(name, list(shape), dt, kind="Internal").ap()

    x_d = din("x", [T, D])
    ccol_d = din("c_col", [128, 8])
    adaw_d = din("ada_w", [D, 6 * D])
    adab_d = din("ada_b_col", [128, 48])
    cols_d = din("cols", [128, NCOL])
    fing_d = din("final_g_bc", [128, D])
    win_d = din("w_in", [D, 2816])
    wa2_d = din("wa2", [128, 512])
    g2_d = din("g2", [128, 512])
    wout_d = din("w_out", [D, D])
    wqT_d = din("wqT", [2048, D])
    KT_d = din("KT", [128, 16 * 128])
    UT_d = din("UTt", [128 * 128, 1024])
    V_d = din("V", [16384, D])
    out_d = nc.dram_tensor("out", [T, D], F32, kind="ExternalOutput").ap()

    yc_s = dscr("yc_s", [512, T])
    ycat_s = dscr("ycat_s", [1024, T], BF16)
    x1_s = dscr("x1_s", [T, D])
    u2T_s = dscr("u2T_s", [128, 8 * T], BF16)
    gT_s = dscr("gT_s", [128, T])
    iT_s = dscr("iT_s", [128, T])
    jT_s = dscr("jT_s", [128, T])
    UV_s = dscr("UV_s", [128 * 128, 2048], BF16)

    dbg = {}

    def dbg_out(name, shape, dt=F32, force=False):
        if name in debug or force:
            dbg[name] = nc.dram_tensor("dbg_" + name, list(shape), dt, kind="ExternalOutput").ap()
            return dbg[name]
        return None

    with ExitStack() as top:
        S = Sync(nc, top)

        def sbt(st, name, shape, dt=F32):
            return st.enter_context(nc.sbuf_tensor("s_" + name, list(shape), dt))

        def pst(st, name, shape=(128, 512), dt=F32):
            return st.enter_context(nc.psum_tensor("p_" + name, list(shape), dt))

        def aff(e, out, in_, scale, bias, reads, writes, func=None):
            if e == "act":
                f = func or AF.Identity
                return S.op("act", lambda en: en.activation(out=out, in_=in_, func=f, bias=bias, scale=scale),
                            reads=reads, writes=writes)
            assert func is None
            return S.op(e, lambda en: en.tensor_scalar(out=out, in0=in_, scalar1=scale, scalar2=bias,
                                                       op0=ALU.mult, op1=ALU.add), reads=reads, writes=writes)

        def build_gate_bc(st, dst, mof, pb, pbkey):
            dg = sbt(st, "dg_" + dst.name[2:], [128, 128])
            for cidx in range(8):
                S.op("dve", lambda e: e.tensor_scalar_mul(out=dg[:], in0=ident[:], scalar1=modT[:, mof + cidx:mof + cidx + 1]),
                     reads=["dg"], writes=["dg"])
                S.op("pe", lambda e: e.matmul(pb[:, 0:128], lhsT=ones_f[:], rhs=dg[:], start=True, stop=True),
                     reads=["dg"], writes=[pbkey])
                S.op("act", lambda e: e.copy(out=dst[:, cidx * 128:(cidx + 1) * 128], in_=pb[:, 0:128]),
                     reads=[pbkey], writes=[dst.name[2:]])

        cols = sbt(top, "cols", [128, NCOL])
        dcol = sbt(top, "dcol", [128, 48])
        OMM, OMKA, GSC1, GSC2 = 0, 14, 18, 26
        modT = sbt(top, "modT", [128, 48])
        ident = sbt(top, "ident", [128, 128])
        iota_f = sbt(top, "iota_f", [128, 128])
        ones_f = sbt(top, "ones_f", [128, 128])
        blk1 = sbt(top, "blk1", [128, 128])
        mask4 = sbt(top, "mask4", [128, 4, 128])
        maskL = sbt(top, "maskL", [128, 128])

        NCAST = 16
        cast_jobs = []
        for i in range(NCAST):
            r0, r1 = i * (16384 // NCAST), (i + 1) * (16384 // NCAST)
            cast_jobs.append((UV_s[r0:r1, 0:1024], UT_d[r0:r1, :], ("Ub", i)))
            cast_jobs.append((UV_s[r0:r1, 1024:2048], V_d[r0:r1, :], ("Vb", i)))

        def issue_casts(n):
            for _ in range(n):
                if cast_jobs:
                    o_, i_, k_ = cast_jobs.pop(0)
                    S.dma("cast", o_, i_, writes=[k_])

        with ExitStack() as ph:
            io_i = sbt(ph, "io_i", [128, 128], I32)
            io_f = sbt(ph, "io_f", [128, 128])
            ccol = sbt(ph, "ccol", [128, 8])
            sc = sbt(ph, "silu_c", [128, 8])
            adab = sbt(ph, "adab", [128, 48])
            abuf = [sbt(ph, f"abuf{i}", [128, 2048]) for i in range(2)]
            dg = sbt(ph, "dg", [128, 128])
            psmod_t = pst(ph, "psmod", [128, 512])
            psmod = psmod_t[:, 0:384]
            psb = [pst(ph, f"psb{i}", [128, 512]) for i in range(2)]

            S.dma("sp", cols[:], cols_d, writes=["cols"])
            S.dma("sp", ccol[:], ccol_d, writes=["ccol"])
            S.dma("sp", adab[:], adab_d, writes=["adab"])
            S.op("pool", lambda e: e.iota(io_i[:], pattern=[[1, 128]], base=0, channel_multiplier=-1), writes=["io_i"])
            S.op("dve", lambda e: e.tensor_copy(out=io_f[:], in_=io_i[:]), reads=["io_i"], writes=["io_f"])
            S.op("dve", lambda e: e.tensor_single_scalar(out=ident[:], in_=io_f[:], scalar=0.0, op=ALU.is_equal),
                 reads=["io_f"], writes=["ident"])
            S.op("pool", lambda e: e.iota(io_i[:], pattern=[[1, 128]], base=0, channel_multiplier=0),
                 reads=["io_i"], writes=["io_i"])
            S.op("dve", lambda e: e.tensor_copy(out=iota_f[:], in_=io_i[:]), reads=["io_i"], writes=["iota_f"])
            S.op("pool", lambda e: e.memset(ones_f[:], 1.0), writes=["ones_f"])
            S.op("pool", lambda e: e.memset(blk1[:], 0.0), writes=["blk1"])
            S.op("pool", lambda e: e.memset(blk1[0:64, 0:64], 1.0), writes=["blk1"])
            S.op("pool", lambda e: e.memset(blk1[64:128, 64:128], 1.0), writes=["blk1"])
            S.op("dve", lambda e: e.scalar_tensor_tensor(out=mask4[:, 0, :], in0=io_f[:], scalar=0.0, in1=blk1[:],
                                                        op0=ALU.is_gt, op1=ALU.mult), reads=["io_f", "blk1"], writes=["mask4"])
            S.op("dve", lambda e: e.tensor_copy(out=mask4[:, 1, :], in_=mask4[:, 0, :]), reads=["mask4"], writes=["mask4"])
            S.op("dve", lambda e: e.scalar_tensor_tensor(out=mask4[:, 2, :], in0=io_f[:], scalar=0.0, in1=blk1[:],
                                                        op0=ALU.is_ge, op1=ALU.mult), reads=["io_f", "blk1", "mask4"], writes=["mask4"])
            S.op("dve", lambda e: e.tensor_copy(out=mask4[:, 3, :], in_=mask4[:, 2, :]), reads=["mask4"], writes=["mask4"])
            S.op("dve", lambda e: e.scalar_tensor_tensor(out=maskL[:], in0=io_f[:], scalar=0.0, in1=blk1[:],
                                                        op0=ALU.is_lt, op1=ALU.mult), reads=["io_f", "blk1"], writes=["maskL"])
            S.op("dve", lambda e: e.tensor_scalar(out=dcol[:, OMM:OMM + 14], in0=cols[:, MU:MU + 14], scalar1=-1.0, scalar2=1.0,
                                                 op0=ALU.mult, op1=ALU.add), reads=["cols"], writes=["dcol"])
            S.op("dve", lambda e: e.tensor_scalar(out=dcol[:, OMKA:OMKA + 4], in0=cols[:, KA:KA + 4], scalar1=-1.0, scalar2=1.0,
                                                 op0=ALU.mult, op1=ALU.add), reads=["cols", "dcol"], writes=["dcol"])
            S.op("act", lambda e: e.activation(out=sc[:], in_=ccol[:], func=AF.Silu), reads=["ccol"], writes=["silu_c"])
            n = 0
            for k in range(8):
                for pc in range(3):
                    ab = abuf[n % 2]
                    S.dma("sp", ab[:], adaw_d[k * 128:(k + 1) * 128, pc * 2048:(pc + 1) * 2048], writes=[("abuf", n % 2)])
                    for fl in range(16):
                        f = pc * 16 + fl
                        S.op("pe", lambda e: e.matmul(psmod[:, f * 8 + k:f * 8 + k + 1], lhsT=ab[:, fl * 128:(fl + 1) * 128],
                                                      rhs=sc[:, k:k + 1], start=True, stop=True),
                             reads=[("abuf", n % 2), "silu_c"], writes=["psmod"])
                    n += 1
            S.op("dve", lambda e: e.tensor_reduce(out=modT[:], in_=psmod.rearrange("p (f k) -> p f k", k=8),
                                                 axis=AX.X, op=ALU.add), reads=["psmod"], writes=["modT"])
            S.op("dve", lambda e: e.tensor_add(out=modT[:], in0=modT[:], in1=adab[:]), reads=["modT", "adab"], writes=["modT"])
            for (dst, gof, sof) in ((GSC1, G1, 8), (GSC2, G2, 32)):
                S.op("dve", lambda e: e.scalar_tensor_tensor(out=dcol[:, dst:dst + 8], in0=modT[:, sof:sof + 8], scalar=1.0,
                                                            in1=cols[:, gof:gof + 8], op0=ALU.add, op1=ALU.mult),
                     reads=["modT", "cols", "dcol"], writes=["dcol"])
            if "modT" in debug:
                S.dma("sp", dbg_out("modT", [128, 48]), modT[:], reads=["modT"], writes=["dbg_modT"])
            S.barrier()

        done = [False]

        def fin():
            S.barrier()
            S.finish([])
            done[0] = True

        if stop_after == 0:
            fin()

        if not done[0]:
          with ExitStack() as mx:
            uT = sbt(mx, "uT", [128, 8, T + 1], BF16)

            with ExitStack() as ph:
                xb = [sbt(ph, f"xb{i}", [128, D]) for i in range(2)]
                junk = sbt(ph, "junk", [128, D])
                xn = [sbt(ph, f"xn{i}", [128, D]) for i in range(2)]
                ss = sbt(ph, "ss", [128, 32])
                rs = sbt(ph, "rs", [128, 32])
                pT = [pst(ph, f"pT{i}", [128, 1024]) for i in range(2)]
                S.op("pool", lambda e: e.memset(uT[:, :, 0:1], 0.0), writes=["uT0"])
                def p1_front(tt):
                    b = tt % 2
                    S.dma("sp", xb[b][:], x_d[tt * 128:(tt + 1) * 128, :], writes=[("xb", b)])
                    S.op("act", lambda e: e.activation(out=junk[:], in_=xb[b][:], func=AF.Square),
                         reads=[("xb", b)], writes=["junk"])
                    S.op("dve", lambda e: e.reduce_sum(out=ss[:, tt:tt + 1], in_=junk[:], axis=AX.X),
                         reads=["junk"], writes=[("ss", tt)])
                    S.op("act", lambda e: e.activation(out=rs[:, tt:tt + 1], in_=ss[:, tt:tt + 1], func=AF.Sqrt,
                                                       bias=1e-6, scale=1.0 / D), reads=[("ss", tt)], writes=[("rs", tt)])
                    S.op("dve", lambda e: e.reciprocal(out=rs[:, tt:tt + 1], in_=rs[:, tt:tt + 1]),
                         reads=[("rs", tt)], writes=[("rs", tt)])
                    S.op("dve", lambda e: e.tensor_scalar_mul(out=xn[b][:], in0=xb[b][:], scalar1=rs[:, tt:tt + 1]),
                         reads=[("xb", b), ("rs", tt)], writes=[("xn", b)])

                p1_front(0)
                for tt in range(32):
                    b = tt % 2
                    if tt + 1 < 32:
                        p1_front(tt + 1)
                    for k in range(8):
                        S.op("pe", lambda e: e.transpose(pT[b][:, k * 128:(k + 1) * 128], xn[b][:, k * 128:(k + 1) * 128], ident[:]),
                             reads=[("xn", b)], writes=[("pT", b, k // 4)])
                    for k in range(8):
                        aff("act" if k < 4 else "dve", uT[:, k, 1 + tt * 128:1 + (tt + 1) * 128],
                            pT[b][:, k * 128:(k + 1) * 128], dcol[:, GSC1 + k:GSC1 + k + 1], modT[:, k:k + 1],
                            reads=[("pT", b, k // 4)], writes=[("uT", tt)])
                S.barrier()
            if "uT" in debug:
                du = dbg_out("uT", [128, 8 * T], BF16).rearrange("p (k t) -> p k t", k=8)
                for k in range(8):
                    S.dma("sp", du[:, k, :], uT[:, k, 1:T + 1], writes=[("dbg_uT", k)])
                S.barrier()
            if stop_after == 1:
                fin()

            def load_w(st_f32, dst_bf, f, key):
                S.dma("sp", st_f32[:], win_d[:, f * 128:(f + 1) * 128].rearrange("(k p) f -> p k f", p=128),
                      writes=[("wst", st_f32.name[2:])])
                S.op("pool", lambda e: e.tensor_copy(out=dst_bf[:], in_=st_f32[:]), reads=[("wst", st_f32.name[2:])], writes=[("wbf", key)])

            if not done[0]:
              with ExitStack() as ph:
                wst = [sbt(ph, f"wst{i}", [128, 8, 128]) for i in range(2)]
                wab = [sbt(ph, f"wab{i}", [128, 8, 128], BF16) for i in range(2)]
                diag = sbt(ph, "diag", [128, 31, 128], BF16)
                ypad = sbt(ph, "ypad", [128, 30 + T], BF16)
                sig = [sbt(ph, f"sig{i}", [128, 512]) for i in range(2)]
                ycb = [sbt(ph, f"ycb{i}", [128, 512]) for i in range(2)]
                psA = [pst(ph, f"psA{i}") for i in range(2)]
                psB = [pst(ph, f"psB{i}") for i in range(2)]
                psC = [pst(ph, f"psC{i}") for i in range(2)]
                S.op("pool", lambda e: e.memset(ypad[:, 0:30], 0.0), writes=["ypad0"])
                for ct in range(4):
                    issue_casts(2)
                    load_w(wst[0], wab[0], ct, 0)
                    load_w(wst[1], wab[1], 4 + ct, 1)
                    for k in range(31):
                        S.op("dve", lambda e: e.tensor_scalar_mul(out=diag[:, k, :], in0=ident[:],
                                                                  scalar1=cols[:, CW + ct * 31 + k:CW + ct * 31 + k + 1]),
                             writes=["diag"])
                    for tb in range(8):
                        b = tb % 2
                        t0 = tb * 512
                        for k in range(8):
                            S.op("pe", lambda e: e.matmul(psA[b][:], lhsT=wab[0][:, k, :], rhs=uT[:, k, 1 + t0:1 + t0 + 512],
                                                          start=(k == 0), stop=(k == 7)), reads=[("wbf", 0)], writes=[("psA", b)])
                        for k in range(8):
                            S.op("pe", lambda e: e.matmul(psB[b][:], lhsT=wab[1][:, k, :], rhs=uT[:, k, 1 + t0:1 + t0 + 512],
                                                          start=(k == 0), stop=(k == 7)), reads=[("wbf", 1)], writes=[("psB", b)])
                        S.op("act", lambda e: e.activation(out=sig[b][:], in_=psB[b][:], func=AF.Sigmoid),
                             reads=[("psB", b)], writes=[("sig", b)])
                        S.op("dve", lambda e: e.tensor_tensor(out=ypad[:, 30 + t0:30 + t0 + 512], in0=psA[b][:], in1=sig[b][:],
                                                             op=ALU.mult), reads=[("psA", b), ("sig", b)], writes=[("ypad", tb)])
                    for tb in range(8):
                        b = tb % 2
                        t0 = tb * 512
                        for k in range(31):
                            S.op("pe", lambda e: e.matmul(psC[b][:], lhsT=diag[:, k, :], rhs=ypad[:, t0 + k:t0 + k + 512],
                                                          start=(k == 0), stop=(k == 30)),
                                 reads=["diag", "ypad0", ("ypad", tb), ("ypad", max(tb - 1, 0))], writes=[("psC", b)])
                        aff("act", ycb[b][:], psC[b][:], 1.0, cols[:, CB + ct:CB + ct + 1], reads=[("psC", b)], writes=[("ycb", b)])
                        S.dma("sp", yc_s[ct * 128:(ct + 1) * 128, t0:t0 + 512], ycb[b][:], reads=[("ycb", b)],
                              writes=[("yc_s", ct, tb)])
                S.barrier()
            if "yc" in debug and not done[0]:
                S.dma("sp", dbg_out("yc", [512, T]), yc_s, writes=["dbg_yc"])
                S.barrier()
            if stop_after == 2 and not done[0]:
                fin()
            done_2a = True

            if not done[0]:
              with ExitStack() as ph:
                yb = [sbt(ph, f"yb{i}", [128, 4, 512]) for i in range(2)]
                sq = sbt(ph, "sq", [128, 4, 512])
                mean = sbt(ph, "mean", [128, 512])
                msq = sbt(ph, "msq", [128, 512])
                var = sbt(ph, "var", [128, 512])
                dd = [sbt(ph, f"dd{i}", [128, 512]) for i in range(2)]
                yo = [sbt(ph, f"yo{i}", [128, 512], BF16) for i in range(2)]
                ps1 = pst(ph, "ps1")
                ps2 = pst(ph, "ps2")
                for tb in range(8):
                    b = tb % 2
                    t0 = tb * 512
                    S.dma("sp", yb[b][:], yc_s[:, t0:t0 + 512].rearrange("(c p) t -> p c t", p=128), writes=[("yb", b)])
                    S.op("act", lambda e: e.activation(out=sq[:], in_=yb[b][:], func=AF.Square), reads=[("yb", b)], writes=["sq"])
                    for ct in range(4):
                        S.op("pe", lambda e: e.matmul(ps1[:], lhsT=ones_f[:], rhs=yb[b][:, ct, :], start=(ct == 0), stop=(ct == 3)),
                             reads=[("yb", b)], writes=["ps1"])
                    for ct in range(4):
                        S.op("pe", lambda e: e.matmul(ps2[:], lhsT=ones_f[:], rhs=sq[:, ct, :], start=(ct == 0), stop=(ct == 3)),
                             reads=["sq"], writes=["ps2"])
                    S.op("act", lambda e: e.mul(out=mean[:], in_=ps1[:], mul=1.0 / 512), reads=["ps1"], writes=["mean"])
                    S.op("pool", lambda e: e.tensor_mul(out=msq[:], in0=mean[:], in1=mean[:]), reads=["mean"], writes=["msq"])
                    S.op("dve", lambda e: e.scalar_tensor_tensor(out=var[:], in0=ps2[:], scalar=1.0 / 512, in1=msq[:],
                                                                op0=ALU.mult, op1=ALU.subtract), reads=["ps2", "msq"], writes=["var"])
                    S.op("act", lambda e: e.activation(out=var[:], in_=var[:], func=AF.Sqrt, bias=1e-5, scale=1.0),
                         reads=["var"], writes=["var"])
                    S.op("dve", lambda e: e.reciprocal(out=var[:], in_=var[:]), reads=["var"], writes=["var"])
                    for ct in range(4):
                        d2 = ct % 2
                        S.op("dve", lambda e: e.tensor_sub(out=dd[d2][:], in0=yb[b][:, ct, :], in1=mean[:]),
                             reads=[("yb", b), "mean"], writes=[("dd", d2)])
                        S.op("pool", lambda e: e.tensor_mul(out=dd[d2][:], in0=dd[d2][:], in1=var[:]),
                             reads=[("dd", d2), "var"], writes=[("dd", d2)])
                        aff("act", yo[d2][:], dd[d2][:], cols[:, LW + ct:LW + ct + 1], cols[:, LB + ct:LB + ct + 1],
                            reads=[("dd", d2)], writes=[("yo", d2)], func=AF.Silu)
                        S.dma("sp", ycat_s[ct * 128:(ct + 1) * 128, t0:t0 + 512], yo[d2][:], reads=[("yo", d2)],
                              writes=[("ycat", ct, tb)])
                S.barrier()
            if "ycat" in debug and stop_after == 3 and not done[0]:
                S.dma("sp", dbg_out("ycat", [1024, T], BF16), ycat_s, writes=["dbg_ycat"])
                S.barrier()
            if stop_after == 3 and not done[0]:
                fin()

            if not done[0]:
              with ExitStack() as ph:
                NB = 256
                NCH = NB // 64
                wst_ = sbt(ph, "rwst", [128, 8, 128]); wst = [wst_, wst_]
                wl_bf = sbt(ph, "wl_bf", [128, 8, 128], BF16)
                wr_bf = [sbt(ph, f"wr_bf{i}", [128, 8, 128], BF16) for i in range(3)]
                twal = sbt(ph, "twal", [128, T], BF16)
                sg = sbt(ph, "sg", [128, T], BF16)
                wa2b = sbt(ph, "wa2b", [128, 512], BF16)
                g2b = sbt(ph, "g2b", [128, 512], BF16)
                cmask = sbt(ph, "cmask", [128, NB])
                tmpx = [sbt(ph, f"tmpx{i}", [128, 256]) for i in range(2)]
                xsl = sbt(ph, "xsl", [128, 256])
                psA = [pst(ph, f"rpsA{i}") for i in range(2)]
                psM = [pst(ph, f"rpsM{i}") for i in range(2)]
                Wk = [pst(ph, f"rW{i}") for i in range(4)]

                S.op("pool", lambda e: e.memset(cmask[:], 1.0), writes=["cmask"])
                S.op("pool", lambda e: e.memset(cmask[:].rearrange("p (c k) -> p c k", k=64)[:, :, 0:1], 0.0),
                     reads=["cmask"], writes=["cmask"])

                halos = sbt(ph, "halos", [128, 4])
                npx = [0]

                def proj_xs1(wbf, wkey, fcol, t0, n, out_ap, okey, j3, first):
                    bi = npx[0] % 2
                    npx[0] += 1
                    pa = psA[bi]
                    pk_ = "rpsA%d" % bi
                    tx = tmpx[bi]
                    mu_c = cols[:, MU + fcol:MU + fcol + 1]
                    hk = ("halo", j3)
                    for k in range(8):
                        S.op("pe", lambda e: e.matmul(pa[:, 0:n], lhsT=wbf[:, k, :], rhs=uT[:, k, 1 + t0:1 + t0 + n],
                                                      start=(k == 0), stop=(k == 7)), reads=[wkey], writes=[pk_])
                    if first:
                        S.op("act", lambda e: e.memzero(halos[:, j3:j3 + 1]), writes=[hk])
                    S.op("act", lambda e: e.activation(out=tx[:, 0:1], in_=halos[:, j3:j3 + 1], func=AF.Copy, scale=mu_c),
                         reads=[hk], writes=[("tmpx", bi)])
                    S.op("act", lambda e: e.activation(out=tx[:, 1:n], in_=pa[:, 0:n - 1], func=AF.Copy, scale=mu_c),
                         reads=[pk_, ("tmpx", bi)], writes=[("tmpx", bi)])
                    S.op("act", lambda e: e.copy(out=halos[:, j3:j3 + 1], in_=pa[:, n - 1:n]), reads=[pk_, hk], writes=[hk])
                    S.op("dve", lambda e: e.scalar_tensor_tensor(out=out_ap, in0=pa[:, 0:n],
                                                                scalar=dcol[:, OMM + fcol:OMM + fcol + 1], in1=tx[:, 0:n],
                                                                op0=ALU.mult, op1=ALU.add),
                         reads=[pk_, ("tmpx", bi)], writes=[okey])

                def proj_xs(wbf, wkey, fcol, t0, n, out_ap, okey, slot):
                    pa, pb = psA[0], psA[1]
                    for k in range(8):
                        S.op("pe", lambda e: e.matmul(pa[:, 0:n], lhsT=wbf[:, k, :], rhs=uT[:, k, 1 + t0:1 + t0 + n],
                                                      start=(k == 0), stop=(k == 7)), reads=[wkey], writes=["rpsA0"])
                    for k in range(8):
                        S.op("pe", lambda e: e.matmul(pb[:, 0:n], lhsT=wbf[:, k, :], rhs=uT[:, k, t0:t0 + n],
                                                      start=(k == 0), stop=(k == 7)), reads=[wkey], writes=["rpsA1"])
                    tx = tmpx[slot % 2]
                    S.op("act", lambda e: e.activation(out=tx[:, 0:n], in_=pb[:, 0:n], func=AF.Copy,
                                                       scale=cols[:, MU + fcol:MU + fcol + 1]),
                         reads=["rpsA1"], writes=[("tmpx", slot % 2)])
                    S.op("dve", lambda e: e.scalar_tensor_tensor(out=out_ap, in0=pa[:, 0:n],
                                                                scalar=dcol[:, OMM + fcol:OMM + fcol + 1], in1=tx[:, 0:n],
                                                                op0=ALU.mult, op1=ALU.add),
                         reads=["rpsA0", ("tmpx", slot % 2)], writes=[okey])

                load_w(wst[0], wl_bf, 8 + 12, "l")
                for tb in range(16):
                    t0 = tb * 256
                    proj_xs1(wl_bf, ("wbf", "l"), 12, t0, 256, xsl[:], "xsl", 3, tb == 0)
                    S.op("act", lambda e: e.activation(out=twal[0:64, t0:t0 + 256], in_=xsl[0:64, :], func=AF.Tanh),
                         reads=["xsl"], writes=[("twal", tb)])
                    S.op("pool", lambda e: e.tensor_copy(out=twal[64:128, t0:t0 + 256], in_=xsl[64:128, :]),
                         reads=["xsl"], writes=[("twal2", tb)])
                load_w(wst[0], wl_bf, 8 + 13, "l")
                for tb in range(16):
                    t0 = tb * 256
                    proj_xs1(wl_bf, ("wbf", "l"), 13, t0, 256, xsl[:], "xsl", 3, tb == 0)
                    S.op("act", lambda e: e.activation(out=sg[:, t0:t0 + 256], in_=xsl[:], func=AF.Sigmoid),
                         reads=["xsl"], writes=[("sg", tb)])

                def wk(name, shape=(128, NB), dt=F32):
                    return sbt(ph, name, list(shape), dt)
                R2 = [wk("r_0"), wk("r_1")]; k0 = wk("k0"); V2 = [wk("v_0"), wk("v_1")]
                sgm = wk("sgm"); a_ = wk("a_"); GG2 = [wk("g_0"), wk("g_1")]
                cum = wk("cum"); cex = wk("cex"); dend = cex
                EIN2 = [wk("Ein0"), wk("Ein1")]; Eex = wk("Eex"); Einv = wk("Einv"); Eend = wk("Eend")
                kk = wk("kk"); rinv = wk("rinv"); kkn = wk("kkn")
                t1 = wk("t1"); kk2 = t1; K22 = [wk("k2_0"), wk("k2_1")]; bb = wk("bb"); rkr = wk("rkr")
                ysc = wk("ysc"); ysq = wk("ysq"); gmean = wk("gmean"); gvar = wk("gvar"); gtmp = wk("gtmp")
                yob = [sbt(ph, f"ryo{i}", [128, NB], BF16) for i in range(2)]
                PADS = [[wk(f"{nm}{pb}", (128, NCH, 128)) for nm in ("AP_", "BP_", "KP_", "RP_", "BC_", "KC_", "VP_")] for pb in range(2)]
                for pb_ in range(2):
                    for arr in PADS[pb_]:
                        S.op("pool", lambda e: e.memset(arr[:], 0.0), writes=[("padinit", arr.name)])
                tok = [wk(f"tok{c}", (128, 4, 128)) for c in range(NCH)]
                gram = [wk(f"gram{c}", (128, 4, 128)) for c in range(NCH)]
                NTm = [wk(f"NTm{c}", (128, 128)) for c in range(NCH)]
                Pm = [[wk(f"Pm{c}_{i}", (128, 2, 128)) for i in range(2)] for c in range(NCH)]
                Xm = [[wk(f"Xm{c}_{i}", (128, 128)) for i in range(2)] for c in range(NCH)]
                AW = [wk(f"AW{c}", (128, 2, 128)) for c in range(NCH)]
                AU = [wk(f"AU{c}", (128, 2, 128)) for c in range(NCH)]
                PTs = [wk(f"PTs{c}", (128, 128)) for c in range(NCH)]
                Qm = [wk(f"Qm{c}", (128, 128)) for c in range(NCH)]
                RH = [wk(f"RH{c}", (128, 128)) for c in range(NCH)]
                nwb = [0]
                wa2f = tok[0][:].rearrange("p a b -> p (a b)")
                g2f = gram[0][:].rearrange("p a b -> p (a b)")
                S.dma("sp", wa2f, wa2_d, writes=[("tok", 0)])
                S.dma("sp", g2f, g2_d, writes=[("gram", 0)])
                S.op("dve", lambda e: e.tensor_copy(out=wa2b[:], in_=wa2f), reads=[("tok", 0)], writes=["wa2b"])
                S.op("dve", lambda e: e.tensor_copy(out=g2b[:], in_=g2f), reads=[("gram", 0)], writes=["g2b"])
                STs = [wk(f"ST{i}", (128, 128)) for i in range(2)]

                for hp in range(4):
                    for j3, f in enumerate((8 + hp, 12 + hp, 16 + hp)):
                        load_w(wst[j3 % 2], wr_bf[j3], 8 + (f - 8), ("r", j3))
                    S.op("pool", lambda e: e.memset(STs[0][:], 0.0), writes=[("ST", 0)])
                    cs = slice(hp * 128, (hp + 1) * 128)
                    nch = [0]
                    def rw_prep(blk):
                        issue_casts(1)
                        pb = blk % 2
                        t0 = blk * NB
                        r_, k2, v_, g_, Ein = R2[pb], K22[pb], V2[pb], GG2[pb], EIN2[pb]
                        AP_, BP_, KP_, RP_, BC_, KC_, VP_ = PADS[pb]
                        padkeys = [(nm, pb, hh) for nm in ("AP_", "BP_", "KP_", "RP_", "BC_", "KC_", "VP_") for hh in range(2)]
                        pm, pm1 = psM[0], psM[1]
                        proj_xs1(wr_bf[0], ("wbf", ("r", 0)), hp, t0, NB, r_[:], ("r_", pb), 0, blk == 0)
                        proj_xs1(wr_bf[1], ("wbf", ("r", 1)), 4 + hp, t0, NB, k0[:], "k0", 1, blk == 0)
                        proj_xs1(wr_bf[2], ("wbf", ("r", 2)), 8 + hp, t0, NB, v_[:], ("v_", pb), 2, blk == 0)
                        S.op("pe", lambda e: e.matmul(pm[:, 0:NB], lhsT=wa2b[0:64, cs], rhs=twal[0:64, t0:t0 + NB], start=True, stop=True),
                             reads=["wa2b"], writes=["rpsM0"])
                        S.op("act", lambda e: e.activation(out=sgm[:], in_=pm[:, 0:NB], func=AF.Sigmoid, bias=cols[:, W0 + hp:W0 + hp + 1]),
                             reads=["rpsM0"], writes=["sgm"])
                        S.op("pe", lambda e: e.matmul(pm1[:, 0:NB], lhsT=wa2b[64:128, cs], rhs=twal[64:128, t0:t0 + NB], start=True, stop=True),
                             reads=["wa2b"], writes=["rpsM1"])
                        S.op("act", lambda e: e.activation(out=a_[:], in_=pm1[:, 0:NB], func=AF.Sigmoid, bias=cols[:, A0 + hp:A0 + hp + 1]),
                             reads=["rpsM1"], writes=["a_"])
                        S.op("pe", lambda e: e.matmul(pm[:, 0:NB], lhsT=g2b[:, cs], rhs=sg[:, t0:t0 + NB], start=True, stop=True),
                             reads=["g2b"], writes=["rpsM0"])
                        S.op("act", lambda e: e.copy(out=g_[:], in_=pm[:, 0:NB]), reads=["rpsM0"], writes=[("g_", pb)])
                        S.op("dve", lambda e: e.tensor_tensor_scan(out=cum[:], data0=cmask[:], data1=sgm[:], initial=0.0,
                                                                  op0=ALU.mult, op1=ALU.add), reads=["sgm", "cmask"], writes=["cum"])
                        S.op("pool", lambda e: e.tensor_sub(out=cex[:], in0=cum[:], in1=sgm[:]), reads=["cum", "sgm"], writes=["cex"])
                        S.op("act", lambda e: e.activation(out=Eex[:], in_=cex[:], func=AF.Exp, scale=-C0), reads=["cex"], writes=["Eex"])
                        S.op("act", lambda e: e.activation(out=Ein[:], in_=cum[:], func=AF.Exp, scale=-C0), reads=["cum"], writes=[("Ein", pb)])
                        S.op("act", lambda e: e.activation(out=Einv[:], in_=cum[:], func=AF.Exp, scale=C0), reads=["cum"], writes=["Einv"])
                        cum3 = cum[:].rearrange("p (c k) -> p c k", k=64)
                        S.op("dve", lambda e: e.tensor_sub(out=dend[:].rearrange("p (c k) -> p c k", k=64),
                                                          in0=cum3[:, :, 63:64].to_broadcast([128, NCH, 64]), in1=cum3),
                             reads=["cum"], writes=["cex"])
                        S.op("act", lambda e: e.activation(out=Eend[:], in_=dend[:], func=AF.Exp, scale=-C0), reads=["cex"], writes=["Eend"])
                        S.op("act", lambda e: e.activation(out=kk[:], in_=k0[:], func=AF.Copy, scale=cols[:, KK + hp:KK + hp + 1]),
                             reads=["k0"], writes=["kk"])
                        S.op("pool", lambda e: e.tensor_mul(out=kk2[:], in0=kk[:], in1=kk[:]), reads=["kk"], writes=["t1"])
                        S.op("pe", lambda e: e.matmul(pm1[:, 0:NB], lhsT=blk1[:], rhs=kk2[:], start=True, stop=True),
                             reads=["t1"], writes=["rpsM1"])
                        S.op("act", lambda e: e.activation(out=rinv[:], in_=pm1[:, 0:NB], func=AF.Sqrt), reads=["rpsM1"], writes=["rinv"])
                        S.op("dve", lambda e: e.tensor_scalar_max(out=rinv[:], in0=rinv[:], scalar1=1e-12), reads=["rinv"], writes=["rinv"])
                        S.op("dve", lambda e: e.reciprocal(out=rinv[:], in_=rinv[:]), reads=["rinv"], writes=["rinv"])
                        S.op("dve", lambda e: e.tensor_mul(out=kkn[:], in0=kk[:], in1=rinv[:]), reads=["kk", "rinv"], writes=["kkn"])
                        S.op("pool", lambda e: e.tensor_scalar(out=t1[:], in0=a_[:], scalar1=cols[:, KA + hp:KA + hp + 1],
                                                              scalar2=dcol[:, OMKA + hp:OMKA + hp + 1], op0=ALU.mult, op1=ALU.add),
                             reads=["a_"], writes=["t1"])
                        S.op("pool", lambda e: e.tensor_mul(out=k2[:], in0=k0[:], in1=t1[:]), reads=["k0", "t1"], writes=[("k2", pb)])
                        S.op("dve", lambda e: e.tensor_mul(out=bb[:], in0=kkn[:], in1=a_[:]), reads=["kkn", "a_"], writes=["bb"])
                        for hh in range(2):
                            ps_ = slice(hh * 64, hh * 64 + 64)

                            def v3(tile_):
                                return tile_[ps_, :].rearrange("p (c k) -> p c k", k=64)

                            def o3(arr):
                                return arr[ps_, :, hh * 64:hh * 64 + 64]
                            eA = "dve" if hh == 0 else "pool"
                            eB = "pool" if hh == 0 else "dve"
                            S.op("dve", lambda e: e.scalar_tensor_tensor(out=o3(AP_), in0=v3(kkn), scalar=-1.0, in1=v3(Eex),
                                                                      op0=ALU.mult, op1=ALU.mult), reads=["kkn", "Eex"], writes=[("AP_", pb, hh)])
                            S.op(eB, lambda e: e.tensor_mul(out=o3(BP_), in0=v3(bb), in1=v3(Einv)), reads=["bb", "Einv"], writes=[("BP_", pb, hh)])
                            S.op(eA, lambda e: e.tensor_mul(out=o3(KP_), in0=v3(k2), in1=v3(Einv)), reads=[("k2", pb), "Einv"], writes=[("KP_", pb, hh)])
                            S.op(eB, lambda e: e.tensor_mul(out=o3(RP_), in0=v3(r_), in1=v3(Ein)), reads=[("r_", pb), ("Ein", pb)], writes=[("RP_", pb, hh)])
                            S.op(eA, lambda e: e.tensor_mul(out=o3(BC_), in0=v3(bb), in1=v3(Eend)), reads=["bb", "Eend"], writes=[("BC_", pb, hh)])
                            S.op(eB, lambda e: e.tensor_mul(out=o3(KC_), in0=v3(k2), in1=v3(Eend)), reads=[("k2", pb), "Eend"], writes=[("KC_", pb, hh)])
                            S.op(eA, lambda e: e.tensor_copy(out=o3(VP_), in_=v3(v_)), reads=[("v_", pb)], writes=[("VP_", pb, hh)])
                        padkeys = [(nm, pb, hh) for nm in ("AP_", "BP_", "KP_", "RP_", "BC_", "KC_", "VP_") for hh in range(2)]

                    def rw_chunk(blk):
                        pb = blk % 2
                        t0 = blk * NB
                        r_, k2, v_, g_, Ein = R2[pb], K22[pb], V2[pb], GG2[pb], EIN2[pb]
                        AP_, BP_, KP_, RP_, BC_, KC_, VP_ = PADS[pb]
                        padkeys = [(nm, pb, hh) for nm in ("AP_", "BP_", "KP_", "RP_", "BC_", "KC_", "VP_") for hh in range(2)]
                        pm, pm1 = psM[0], psM[1]
                        NI = NCH
                        for CH in [range(c0, c0 + NI) for c0 in range(0, NCH, NI)]:
                          if True:

                            def wbank():
                                i_ = nwb[0] % 4
                                nwb[0] += 1
                                return Wk[i_], ("rW", i_)
                            for c in CH:
                                wb_, wk_ = wbank()
                                for q_, arr in enumerate((AP_, BC_, KC_, VP_)):
                                    S.op("pe", lambda e: e.transpose(wb_[:, q_ * 128:(q_ + 1) * 128], arr[:, c, :], ident[:]),
                                         reads=padkeys, writes=[wk_])
                                S.op("act", lambda e: e.copy(out=tok[c][:].rearrange("p a b -> p (a b)"), in_=wb_[:]), reads=[wk_], writes=[("tok", c)])
                            for c in CH:
                                wb_, wk_ = wbank()
                                for q_, (l_, r2) in enumerate(((BP_, AP_), (KP_, AP_), (BP_, RP_), (KP_, RP_))):
                                    S.op("pe", lambda e: e.matmul(wb_[:, q_ * 128:(q_ + 1) * 128], lhsT=l_[:, c, :], rhs=r2[:, c, :], start=True, stop=True),
                                         reads=padkeys, writes=[wk_])
                                S.op("dve", lambda e: e.tensor_tensor(out=gram[c][:].rearrange("p a b -> p (a b)"), in0=wb_[:],
                                                                     in1=mask4[:].rearrange("p a b -> p (a b)"), op=ALU.mult),
                                     reads=[wk_], writes=[("gram", c)])
                            for c in CH:
                                wb_, wk_ = wbank()
                                S.op("pe", lambda e: e.matmul(wb_[:, 0:128], lhsT=AP_[:, c, :], rhs=BP_[:, c, :], start=True, stop=True),
                                     reads=padkeys, writes=[wk_])
                                S.op("dve", lambda e: e.tensor_tensor(out=NTm[c][:], in0=wb_[:, 0:128], in1=maskL[:], op=ALU.mult),
                                     reads=[wk_], writes=[("NTm", c)])
                                S.op("pool", lambda e: e.tensor_add(out=Xm[c][0][:], in0=gram[c][:, 0, :], in1=ident[:]),
                                     reads=[("gram", c)], writes=[("Xm", c, 0)])
                            Pc = {c: gram[c][:, 0, :] for c in CH}
                            PTc = {c: NTm[c][:] for c in CH}
                            pk = {c: [("gram", c), ("NTm", c)] for c in CH}
                            xi = 0
                            for step in range(5):
                                for c in CH:
                                    wb_, wk_ = wbank()
                                    pp = Pm[c][step % 2]
                                    if step < 4:
                                        S.op("pe", lambda e: e.matmul(wb_[:, 0:128], lhsT=PTc[c], rhs=Pc[c], start=True, stop=True),
                                             reads=pk[c], writes=[wk_])
                                    S.op("pe", lambda e: e.matmul(wb_[:, 128:256], lhsT=Pc[c], rhs=PTc[c], start=True, stop=True),
                                         reads=pk[c], writes=[wk_])
                                    if step < 4:
                                        S.op("act", lambda e: e.copy(out=pp[:].rearrange("p a b -> p (a b)"), in_=wb_[:, 0:256]),
                                             reads=[wk_], writes=[("Pm", c, step % 2)])
                                    else:
                                        S.op("act", lambda e: e.copy(out=pp[:, 1, :], in_=wb_[:, 128:256]),
                                             reads=[wk_], writes=[("Pm", c, step % 2)])
                                    Pc[c], PTc[c] = pp[:, 0, :], pp[:, 1, :]
                                    pk[c] = [("Pm", c, step % 2)]
                                for c in CH:
                                    wb_, wk_ = wbank()
                                    S.op("pe", lambda e: e.matmul(wb_[:, 0:128], lhsT=PTc[c], rhs=Xm[c][xi][:], start=True, stop=True),
                                         reads=pk[c] + [("Xm", c, xi)], writes=[wk_])
                                    S.op("dve", lambda e: e.tensor_add(out=Xm[c][1 - xi][:], in0=wb_[:, 0:128], in1=Xm[c][xi][:]),
                                         reads=[wk_, ("Xm", c, xi)], writes=[("Xm", c, 1 - xi)])
                                xi = 1 - xi
                            for c in CH:
                                wb_, wk_ = wbank()
                                S.op("pe", lambda e: e.matmul(wb_[:, 0:128], lhsT=gram[c][:, 1, :], rhs=tok[c][:, 3, :], start=True, stop=True),
                                     reads=[("gram", c), ("tok", c)], writes=[wk_])
                                S.op("act", lambda e: e.copy(out=AW[c][:, 1, :], in_=wb_[:, 0:128]), reads=[wk_], writes=[("AW1", c)])
                                S.op("pool", lambda e: e.tensor_copy(out=AW[c][:, 0, :], in_=tok[c][:, 0, :]), reads=[("tok", c)], writes=[("AW0", c)])
                            for c in CH:
                                wb_, wk_ = wbank()
                                S.op("pe", lambda e: e.matmul(wb_[:, 0:256], lhsT=Xm[c][xi][:], rhs=AW[c][:].rearrange("p a b -> p (a b)"), start=True, stop=True),
                                     reads=[("Xm", c, xi), ("AW0", c), ("AW1", c)], writes=[wk_])
                                S.op("act", lambda e: e.copy(out=AU[c][:].rearrange("p a b -> p (a b)"), in_=wb_[:, 0:256]),
                                     reads=[wk_], writes=[("AU", c)])
                            for c in CH:
                                wb_, wk_ = wbank()
                                Ah, Uh = AU[c][:, 0, :], AU[c][:, 1, :]
                                BCt, KCt, Vt = tok[c][:, 1, :], tok[c][:, 2, :], tok[c][:, 3, :]
                                gcol = Ein[:, c * 64 + 63:c * 64 + 64]
                                S.op("pe", lambda e: e.matmul(wb_[:, 0:128], lhsT=Ah, rhs=BCt, start=True, stop=True),
                                     reads=[("AU", c), ("tok", c)], writes=[wk_])
                                S.op("pe", lambda e: e.matmul(wb_[:, 128:256], lhsT=BCt, rhs=Uh, start=True, stop=False),
                                     reads=[("AU", c), ("tok", c)], writes=[wk_])
                                S.op("pe", lambda e: e.matmul(wb_[:, 128:256], lhsT=KCt, rhs=Vt, start=False, stop=True),
                                     reads=[("tok", c)], writes=[wk_])
                                S.op("pe", lambda e: e.matmul(wb_[:, 256:384], lhsT=Ah, rhs=gram[c][:, 2, :], start=True, stop=True),
                                     reads=[("AU", c), ("gram", c)], writes=[wk_])
                                S.op("dve", lambda e: e.scalar_tensor_tensor(out=PTs[c][:], in0=ident[:], scalar=gcol, in1=wb_[:, 0:128],
                                                                            op0=ALU.mult, op1=ALU.add), reads=[wk_, ("Ein", pb)], writes=[("PTs", c)])
                                S.op("act", lambda e: e.copy(out=Qm[c][:], in_=wb_[:, 128:256]), reads=[wk_], writes=[("Qm", c)])
                                S.op("dve", lambda e: e.tensor_add(out=RH[c][:], in0=wb_[:, 256:384], in1=RP_[:, c, :]),
                                     reads=[wk_] + padkeys, writes=[("RH", c)])
                            for c in CH:
                                cur = STs[nch[0] % 2]
                                nxt = STs[(nch[0] + 1) % 2]
                                kcur, knxt = ("ST", nch[0] % 2), ("ST", (nch[0] + 1) % 2)
                                wb2_, wk2_ = wbank()
                                S.op("pe", lambda e: e.matmul(wb2_[:, 0:128], lhsT=PTs[c][:], rhs=cur[:], start=True, stop=True),
                                     reads=[("PTs", c), kcur], writes=[wk2_])
                                S.op("dve", lambda e: e.tensor_add(out=nxt[:], in0=wb2_[:, 0:128], in1=Qm[c][:]),
                                     reads=[wk2_, ("Qm", c)], writes=[knxt])
                                wb_, wk_ = wbank()
                                S.op("pe", lambda e: e.matmul(wb_[:, 0:128], lhsT=cur[:], rhs=RH[c][:], start=True, stop=False),
                                     reads=[kcur, ("RH", c)], writes=[wk_])
                                S.op("pe", lambda e: e.matmul(wb_[:, 0:128], lhsT=AU[c][:, 1, :], rhs=gram[c][:, 2, :], start=False, stop=False),
                                     reads=[("AU", c), ("gram", c)], writes=[wk_])
                                S.op("pe", lambda e: e.matmul(wb_[:, 0:128], lhsT=tok[c][:, 3, :], rhs=gram[c][:, 3, :], start=False, stop=True),
                                     reads=[("tok", c), ("gram", c)], writes=[wk_])
                                S.op("act", lambda e: e.copy(out=ysc[0:64, c * 64:c * 64 + 64], in_=wb_[0:64, 0:64]),
                                     reads=[wk_], writes=[("ysc", 0, c)])
                                S.op("act", lambda e: e.copy(out=ysc[64:128, c * 64:c * 64 + 64], in_=wb_[64:128, 64:128]),
                                     reads=[wk_], writes=[("ysc", 1, c)])
                                nch[0] += 1

                    def rw_gn(blk):
                        pb = blk % 2
                        t0 = blk * NB
                        r_, k2, v_, g_, Ein = R2[pb], K22[pb], V2[pb], GG2[pb], EIN2[pb]
                        AP_, BP_, KP_, RP_, BC_, KC_, VP_ = PADS[pb]
                        padkeys = [(nm, pb, hh) for nm in ("AP_", "BP_", "KP_", "RP_", "BC_", "KC_", "VP_") for hh in range(2)]
                        pm, pm1 = psM[0], psM[1]
                        ykeys = [("ysc", hh, c) for hh in range(2) for c in range(NCH)]
                        if "yscan" in debug:
                            S.dma("sp", dbg_out("yscan", [512, T])[hp * 128:(hp + 1) * 128, t0:t0 + NB] if "yscan" not in dbg else
                                  dbg["yscan"][hp * 128:(hp + 1) * 128, t0:t0 + NB], ysc[:], reads=ykeys, writes=[("dbg_yscan", hp, blk)])
                        S.op("act", lambda e: e.activation(out=ysq[:], in_=ysc[:], func=AF.Square), reads=ykeys, writes=["ysq"])
                        S.op("pe", lambda e: e.matmul(pm[:, 0:NB], lhsT=blk1[:], rhs=ysc[:], start=True, stop=True), reads=ykeys, writes=["rpsM0"])
                        S.op("pe", lambda e: e.matmul(pm1[:, 0:NB], lhsT=blk1[:], rhs=ysq[:], start=True, stop=True), reads=["ysq"], writes=["rpsM1"])
                        S.op("act", lambda e: e.mul(out=gmean[:], in_=pm[:, 0:NB], mul=1.0 / 64), reads=["rpsM0"], writes=["gmean"])
                        S.op("pool", lambda e: e.tensor_mul(out=gtmp[:], in0=gmean[:], in1=gmean[:]), reads=["gmean"], writes=["gtmp"])
                        S.op("dve", lambda e: e.scalar_tensor_tensor(out=gvar[:], in0=pm1[:, 0:NB], scalar=1.0 / 64, in1=gtmp[:],
                                                                    op0=ALU.mult, op1=ALU.subtract), reads=["rpsM1", "gtmp"], writes=["gvar"])
                        S.op("act", lambda e: e.activation(out=gvar[:], in_=gvar[:], func=AF.Sqrt, bias=64e-5, scale=1.0), reads=["gvar"], writes=["gvar"])
                        S.op("dve", lambda e: e.reciprocal(out=gvar[:], in_=gvar[:]), reads=["gvar"], writes=["gvar"])
                        S.op("dve", lambda e: e.tensor_sub(out=gtmp[:], in0=ysc[:], in1=gmean[:]), reads=ykeys + ["gmean", "gtmp"], writes=["gtmp"])
                        S.op("pool", lambda e: e.tensor_mul(out=gtmp[:], in0=gtmp[:], in1=gvar[:]), reads=["gtmp", "gvar"], writes=["gtmp"])
                        S.op("pool", lambda e: e.tensor_scalar(out=gtmp[:], in0=gtmp[:], scalar1=cols[:, GW + hp:GW + hp + 1],
                                                              scalar2=cols[:, GB + hp:GB + hp + 1], op0=ALU.mult, op1=ALU.add),
                             reads=["gtmp"], writes=["gtmp"])
                        S.op("dve", lambda e: e.scalar_tensor_tensor(out=rkr[:], in0=r_[:], scalar=cols[:, RK + hp:RK + hp + 1], in1=k2[:],
                                                                    op0=ALU.mult, op1=ALU.mult), reads=[("r_", pb), ("k2", pb)], writes=["rkr"])
                        S.op("pe", lambda e: e.matmul(pm[:, 0:NB], lhsT=blk1[:], rhs=rkr[:], start=True, stop=True), reads=["rkr"], writes=["rpsM0"])
                        S.op("dve", lambda e: e.tensor_tensor(out=rkr[:], in0=pm[:, 0:NB], in1=v_[:], op=ALU.mult),
                             reads=["rpsM0", ("v_", pb), "rkr"], writes=["rkr"])
                        S.op("pool", lambda e: e.tensor_add(out=gtmp[:], in0=gtmp[:], in1=rkr[:]), reads=["gtmp", "rkr"], writes=["gtmp"])
                        yo_ = yob[blk % 2]
                        S.op("dve", lambda e: e.tensor_mul(out=yo_[:], in0=gtmp[:], in1=g_[:]), reads=["gtmp", ("g_", pb)], writes=[("ryo", blk % 2)])
                        S.dma("sp", ycat_s[(4 + hp) * 128:(5 + hp) * 128, t0:t0 + NB], yo_[:], reads=[("ryo", blk % 2)],
                              writes=[("ycat", 4 + hp, blk)])
                    nblk_ = T // NB
                    rw_prep(0)
                    for blk in range(nblk_):
                        if blk + 1 < nblk_:
                            rw_prep(blk + 1)
                        rw_chunk(blk)
                        rw_gn(blk)
                S.barrier()
            if "ycat" in debug and stop_after == 4 and not done[0]:
                S.dma("sp", dbg_out("ycat", [1024, T], BF16), ycat_s, writes=["dbg_ycat"])
                S.barrier()
            if stop_after == 4 and not done[0]:
                fin()

        if not done[0]:
          issue_casts(len(cast_jobs))
          with ExitStack() as ph:
            wost = [sbt(ph, f"wost{i}", [128, D]) for i in range(2)]
            wob = sbt(ph, "wob", [128, 8, D], BF16)
            ycb = [sbt(ph, f"ycatb{i}", [128, 8, 128], BF16) for i in range(2)]
            xb = [sbt(ph, f"x3b{i}", [128, D]) for i in range(2)]
            x1b = [sbt(ph, f"x1b{i}", [128, D]) for i in range(2)]
            junk = sbt(ph, "junk3", [128, D])
            xn = [sbt(ph, f"xn3{i}", [128, D]) for i in range(2)]
            u2b = [sbt(ph, f"u2b{i}", [128, 8, 128], BF16) for i in range(2)]
            ss = sbt(ph, "ss3", [128, 32])
            rs = sbt(ph, "rs3", [128, 32])
            pO = [pst(ph, f"pO{i}", [128, 1024]) for i in range(2)]
            pT = [pst(ph, f"pT3{i}", [128, 1024]) for i in range(2)]
            gtm_bc = sbt(ph, "gtm_bc", [128, D])
            build_gate_bc(ph, gtm_bc, 16, pO[0], ("pO", 0, 0))
            for ct in range(8):
                S.dma("sp", wost[ct % 2][:], wout_d[ct * 128:(ct + 1) * 128, :], writes=[("wost", ct % 2)])
                S.op("pool", lambda e: e.tensor_copy(out=wob[:, ct, :], in_=wost[ct % 2][:]), reads=[("wost", ct % 2)], writes=["wob"])
            u2T3 = u2T_s.rearrange("p (k t) -> p k t", k=8)
            def p3_mm(tt):
                b = tt % 2
                ts_ = slice(tt * 128, (tt + 1) * 128)
                S.dma("sp", ycb[b][:], ycat_s[:, ts_].rearrange("(c p) t -> p c t", p=128), writes=[("ycatb", b)])
                S.dma("sp", xb[b][:], x_d[ts_, :], writes=[("x3b", b)])
                for dh in range(2):
                    for ct in range(8):
                        S.op("pe", lambda e: e.matmul(pO[b][:, dh * 512:(dh + 1) * 512], lhsT=ycb[b][:, ct, :],
                                                      rhs=wob[:, ct, dh * 512:(dh + 1) * 512], start=(ct == 0), stop=(ct == 7)),
                             reads=[("ycatb", b), "wob"], writes=[("pO", b, dh)])

            p3_mm(0)
            for tt in range(32):
                b = tt % 2
                ts_ = slice(tt * 128, (tt + 1) * 128)
                if tt + 1 < 32:
                    p3_mm(tt + 1)
                S.op("dve", lambda e: e.tensor_tensor(out=x1b[b][:], in0=pO[b][:], in1=gtm_bc[:], op=ALU.mult),
                     reads=[("pO", b, 0), ("pO", b, 1), "gtm_bc"], writes=[("x1b", b)])
                S.op("pool", lambda e: e.tensor_add(out=x1b[b][:], in0=x1b[b][:], in1=xb[b][:]), reads=[("x1b", b), ("x3b", b)], writes=[("x1b", b)])
                S.dma("sp", x1_s[ts_, :], x1b[b][:], reads=[("x1b", b)], writes=[("x1_s", tt)])
                S.op("act", lambda e: e.activation(out=junk[:], in_=x1b[b][:], func=AF.Square), reads=[("x1b", b)], writes=["junk3"])
                S.op("dve", lambda e: e.reduce_sum(out=ss[:, tt:tt + 1], in_=junk[:], axis=AX.X), reads=["junk3"], writes=[("ss3", tt)])
                S.op("act", lambda e: e.activation(out=rs[:, tt:tt + 1], in_=ss[:, tt:tt + 1], func=AF.Sqrt, bias=1e-6, scale=1.0 / D),
                     reads=[("ss3", tt)], writes=[("rs3", tt)])
                S.op("dve", lambda e: e.reciprocal(out=rs[:, tt:tt + 1], in_=rs[:, tt:tt + 1]), reads=[("rs3", tt)], writes=[("rs3", tt)])
                S.op("act", lambda e: e.activation(out=xn[b][:], in_=x1b[b][:], func=AF.Copy, scale=rs[:, tt:tt + 1]),
                     reads=[("x1b", b), ("rs3", tt)], writes=[("xn3", b)])
                for k in range(8):
                    S.op("pe", lambda e: e.transpose(pT[b][:, k * 128:(k + 1) * 128], xn[b][:, k * 128:(k + 1) * 128], ident[:]),
                         reads=[("xn3", b)], writes=[("pT3", b, k // 4)])
                for k in range(8):
                    aff("act" if k < 4 else "dve", u2b[b][:, k, :], pT[b][:, k * 128:(k + 1) * 128],
                        dcol[:, GSC2 + k:GSC2 + k + 1], modT[:, 24 + k:24 + k + 1], reads=[("pT3", b, k // 4)], writes=[("u2b", b)])
                S.dma("sp", u2T3[:, :, ts_], u2b[b][:], reads=[("u2b", b)], writes=[("u2T_s", tt)])
            S.barrier()
            if "x1" in debug:
                S.dma("sp", dbg_out("x1", [T, D]), x1_s, writes=["dbg_x1"])
                S.dma("sp", dbg_out("u2T", [128, 8 * T], BF16, force=True), u2T_s, writes=["dbg_u2T"])
                S.barrier()
        if stop_after == 5 and not done[0]:
            fin()

        if not done[0]:
          with ExitStack() as ph:
            Wc = sbt(ph, "Wc", [128, 8, 2048], BF16)
            KTs = sbt(ph, "KTs", [128, 16, 128])
            wqs = [sbt(ph, f"wqs{i}", [128, D]) for i in range(2)]
            u2t = [sbt(ph, f"u2t{i}", [128, 8, 128], BF16) for i in range(2)]
            sc2 = [sbt(ph, f"sc_{i}", [128, 16, 128]) for i in range(2)]
            wrk4 = [sbt(ph, f"wrk{i}", [128, 256]) for i in range(4)]
            vals = sbt(ph, "vals", [128, 16, 16])
            idxu = sbt(ph, "idxu", [128, 16, 16], U16)
            idxf = sbt(ph, "idxf", [128, 16, 16])
            cand = sbt(ph, "cand", [128, 8, 256])
            best = sbt(ph, "best", [128, 8, 16])
            posu = sbt(ph, "posu", [128, 8, 16], U16)
            posf = sbt(ph, "posf", [128, 8, 16])
            big = sbt(ph, "big", [128, 8, 16, 16])
            thr16 = sbt(ph, "thr16", [128, 16])
            io16 = sbt(ph, "io16", [128, 16])
            ak = sbt(ph, "ak", [128, 8, 16])
            bk = sbt(ph, "bk", [128, 8, 16])
            ik = sbt(ph, "ik", [128, 128])
            jk = sbt(ph, "jk", [128, 128])
            gk = sbt(ph, "gk", [128, 128])
            zs = sbt(ph, "zs", [128, 8])
            slT = [sbt(ph, f"slT{i}", [128, 3, 128]) for i in range(2)]
            pW = [pst(ph, f"pW{i}", [128, 512]) for i in range(2)]
            pS = [pst(ph, f"pS{i}", [128, 512]) for i in range(4)]
            pTr = pst(ph, "pTr", [128, 512])

            S.dma("sp", KTs[:].rearrange("p a b -> p (a b)"), KT_d, writes=["KTs"])
            S.op("dve", lambda e: e.tensor_scalar_mul(out=thr16[:], in0=iota_f[:, 0:16], scalar1=16.0), writes=["thr16"])
            S.op("dve", lambda e: e.tensor_copy(out=io16[:], in_=iota_f[:, 0:16]), writes=["io16"])
            for hp in range(16):
                S.dma("sp", wqs[hp % 2][:], wqT_d[hp * 128:(hp + 1) * 128, :], writes=[("wqs", hp % 2)])
                for dk in range(8):
                    S.op("pe", lambda e: e.matmul(pW[dk % 2][:, 0:128], lhsT=wqs[hp % 2][:, dk * 128:(dk + 1) * 128], rhs=KTs[:, hp, :],
                                                  start=True, stop=True), reads=[("wqs", hp % 2), "KTs"], writes=[("pW", dk % 2)])
                    S.op("act" if dk % 2 == 0 else "dve",
                         (lambda e: e.copy(out=Wc[:, dk, hp * 128:(hp + 1) * 128], in_=pW[dk % 2][:, 0:128])) if dk % 2 == 0 else
                         (lambda e: e.tensor_copy(out=Wc[:, dk, hp * 128:(hp + 1) * 128], in_=pW[dk % 2][:, 0:128])),
                         reads=[("pW", dk % 2)], writes=["Wc"])
            u2T3 = u2T_s.rearrange("p (k t) -> p k t", k=8)
            def p4_scores(tt):
                b = tt % 2
                ts_ = slice(tt * 128, (tt + 1) * 128)
                S.dma("sp", u2t[b][:], u2T3[:, :, ts_], writes=[("u2t", b)])
                sc_ = sc2[b]
                for q4 in range(4):
                    for dk in range(8):
                        S.op("pe", lambda e: e.matmul(pS[q4][:], lhsT=u2t[b][:, dk, :], rhs=Wc[:, dk, q4 * 512:(q4 + 1) * 512],
                                                      start=(dk == 0), stop=(dk == 7)), reads=[("u2t", b), "Wc"], writes=[("pS", q4)])
                    S.op("act", lambda e: e.copy(out=sc_[:, q4 * 4:(q4 + 1) * 4, :].rearrange("p a b -> p (a b)"), in_=pS[q4][:]),
                         reads=[("pS", q4)], writes=[("sc_", b, q4)])

            p4_scores(0)
            for tt in range(32):
                b = tt % 2
                ts_ = slice(tt * 128, (tt + 1) * 128)
                sc_ = sc2[b]
                if tt + 1 < 32:
                    p4_scores(tt + 1)
                if "scores" in debug and tt == 0:
                    S.dma("sp", dbg_out("scores", [128, 2048]), sc_[:].rearrange("p a b -> p (a b)"),
                          reads=[("sc_", b, q) for q in range(4)], writes=["dbg_scores"])
                NIT = 4
                ALLV = [("vals", g, hf) for g in range(16) for hf in range(2)]
                ALLI = [("idxu", g, hf) for g in range(16) for hf in range(2)]
                ALLB = [("best", h, hf) for h in range(8) for hf in range(2)]
                ALLP = [("posu", h, hf) for h in range(8) for hf in range(2)]
                for g0 in range(0, 16, NIT):
                    grp = range(g0, g0 + NIT)
                    for g16 in grp:
                        S.op("dve", lambda e: e.max(out=vals[:, g16, 0:8], in_=sc_[:, g16, :]), reads=[("sc_", b, g16 // 4)], writes=[("vals", g16, 0)])
                    for g16 in grp:
                        S.op("dve", lambda e: e.max_index(out=idxu[:, g16, 0:8], in_max=vals[:, g16, 0:8], in_values=sc_[:, g16, :]),
                             reads=[("sc_", b, g16 // 4), ("vals", g16, 0)], writes=[("idxu", g16, 0)])
                    for g16 in grp:
                        w_ = wrk4[g16 % NIT]
                        S.op("dve", lambda e: e.match_replace(out=w_[:, 0:128], in_to_replace=vals[:, g16, 0:8], in_values=sc_[:, g16, :],
                                                             imm_value=-1e30), reads=[("sc_", b, g16 // 4), ("vals", g16, 0)], writes=[("wrk", g16 % NIT)])
                    for g16 in grp:
                        w_ = wrk4[g16 % NIT]
                        S.op("dve", lambda e: e.max(out=vals[:, g16, 8:16], in_=w_[:, 0:128]), reads=[("wrk", g16 % NIT)], writes=[("vals", g16, 1)])
                    for g16 in grp:
                        w_ = wrk4[g16 % NIT]
                        S.op("dve", lambda e: e.max_index(out=idxu[:, g16, 8:16], in_max=vals[:, g16, 8:16], in_values=w_[:, 0:128]),
                             reads=[("wrk", g16 % NIT), ("vals", g16, 1)], writes=[("idxu", g16, 1)])
                S.op("pool", lambda e: e.tensor_copy(out=idxf[:], in_=idxu[:]), reads=ALLI, writes=["idxf"])
                v4 = vals[:].rearrange("p (h two) k -> p h two k", two=2)
                i4 = idxf[:].rearrange("p (h two) k -> p h two k", two=2)
                cand4 = cand[:].rearrange("p h (a b) -> p h a b", b=16)
                S.op("dve", lambda e: e.tensor_tensor(out=cand4, in0=v4[:, :, 0, :].unsqueeze(3).to_broadcast([128, 8, 16, 16]),
                                                     in1=v4[:, :, 1, :].unsqueeze(2).to_broadcast([128, 8, 16, 16]), op=ALU.add),
                     reads=ALLV, writes=["cand"])
                for h0 in range(0, 8, NIT):
                    grp = range(h0, h0 + NIT)
                    for h in grp:
                        S.op("dve", lambda e: e.max(out=best[:, h, 0:8], in_=cand[:, h, :]), reads=["cand"], writes=[("best", h, 0)])
                    for h in grp:
                        S.op("dve", lambda e: e.max_index(out=posu[:, h, 0:8], in_max=best[:, h, 0:8], in_values=cand[:, h, :]),
                             reads=["cand", ("best", h, 0)], writes=[("posu", h, 0)])
                    for h in grp:
                        w_ = wrk4[h % NIT]
                        S.op("dve", lambda e: e.match_replace(out=w_[:], in_to_replace=best[:, h, 0:8], in_values=cand[:, h, :],
                                                             imm_value=-1e30), reads=["cand", ("best", h, 0)], writes=[("wrk", h % NIT)])
                    for h in grp:
                        w_ = wrk4[h % NIT]
                        S.op("dve", lambda e: e.max(out=best[:, h, 8:16], in_=w_[:]), reads=[("wrk", h % NIT)], writes=[("best", h, 1)])
                    for h in grp:
                        w_ = wrk4[h % NIT]
                        S.op("dve", lambda e: e.max_index(out=posu[:, h, 8:16], in_max=best[:, h, 8:16], in_values=w_[:]),
                             reads=[("wrk", h % NIT), ("best", h, 1)], writes=[("posu", h, 1)])
                S.op("dve", lambda e: e.tensor_copy(out=posf[:], in_=posu[:]), reads=ALLP, writes=["posf"])
                gk3 = gk[:].rearrange("p (h k) -> p h k", k=16)
                S.op("pool", lambda e: e.tensor_sub(out=gk3, in0=best[:], in1=best[:, :, 0:1].to_broadcast([128, 8, 16])),
                     reads=ALLB, writes=["gk"])
                S.op("act", lambda e: e.activation(out=gk[:], in_=gk[:], func=AF.Exp), reads=["gk"], writes=["gk"])
                S.op("dve", lambda e: e.tensor_reduce(out=zs[:], in_=gk3, axis=AX.X, op=ALU.add), reads=["gk"], writes=["zs"])
                S.op("dve", lambda e: e.reciprocal(out=zs[:], in_=zs[:]), reads=["zs"], writes=["zs"])
                S.op("pool", lambda e: e.tensor_mul(out=gk3, in0=gk3, in1=zs[:].unsqueeze(2).to_broadcast([128, 8, 16])),
                     reads=["gk", "zs"], writes=["gk"])
                S.op("dve", lambda e: e.tensor_tensor(out=big[:], in0=posf[:].unsqueeze(3).to_broadcast([128, 8, 16, 16]),
                                                     in1=thr16[:].unsqueeze(1).unsqueeze(1).to_broadcast([128, 8, 16, 16]), op=ALU.is_ge),
                     reads=["posf", "thr16"], writes=["big"])
                S.op("dve", lambda e: e.tensor_reduce(out=ak[:], in_=big[:, :, :, 1:16], axis=AX.X, op=ALU.add), reads=["big"], writes=["ak"])
                S.op("dve", lambda e: e.scalar_tensor_tensor(out=bk[:], in0=ak[:], scalar=-16.0, in1=posf[:], op0=ALU.mult, op1=ALU.add),
                     reads=["ak", "posf"], writes=["bk"])
                for (sel, half, dst) in ((ak, 0, ik), (bk, 1, jk)):
                    S.op("dve", lambda e: e.tensor_tensor(out=big[:], in0=sel[:].unsqueeze(3).to_broadcast([128, 8, 16, 16]),
                                                         in1=io16[:].unsqueeze(1).unsqueeze(1).to_broadcast([128, 8, 16, 16]), op=ALU.is_equal),
                         reads=[sel.name[2:], "big"], writes=["big"])
                    S.op("dve", lambda e: e.tensor_mul(out=big[:], in0=big[:],
                                                      in1=i4[:, :, half, :].unsqueeze(2).to_broadcast([128, 8, 16, 16])),
                         reads=["big", "idxf"], writes=["big"])
                    S.op("dve", lambda e: e.tensor_reduce(out=dst[:].rearrange("p (h k) -> p h k", k=16), in_=big[:], axis=AX.X, op=ALU.add),
                         reads=["big"], writes=[dst.name[2:]])
                if "route" in debug and tt == 0:
                    S.dma("sp", dbg_out("r_g", [128, 128], force=True), gk[:], reads=["gk"], writes=["dbg_rg"])
                    S.dma("sp", dbg_out("r_i", [128, 128], force=True), ik[:], reads=["ik"], writes=["dbg_ri"])
                    S.dma("sp", dbg_out("r_j", [128, 128], force=True), jk[:], reads=["jk"], writes=["dbg_rj"])
                for q_, src in enumerate((gk, ik, jk)):
                    S.op("pe", lambda e: e.transpose(pTr[:, q_ * 128:(q_ + 1) * 128], src[:], ident[:]), reads=[src.name[2:]], writes=["pTr"])
                S.op("act", lambda e: e.copy(out=slT[b][:].rearrange("p a b -> p (a b)"), in_=pTr[:, 0:384]), reads=["pTr"], writes=[("slT", b)])
                S.dma("sp", gT_s[:, ts_], slT[b][:, 0, :], reads=[("slT", b)], writes=[("gT_s", tt)])
                S.dma("sp", iT_s[:, ts_], slT[b][:, 1, :], reads=[("slT", b)], writes=[("iT_s", tt)])
                S.dma("sp", jT_s[:, ts_], slT[b][:, 2, :], reads=[("slT", b)], writes=[("jT_s", tt)])
            S.barrier()
        if "slots" in debug and not done[0]:
            S.dma("sp", dbg_out("gT", [128, T], force=True), gT_s, writes=["dbg_gT"])
            S.dma("sp", dbg_out("iT", [128, T], force=True), iT_s, writes=["dbg_iT"])
            S.dma("sp", dbg_out("jT", [128, T], force=True), jT_s, writes=["dbg_jT"])
            S.barrier()
        if stop_after == 6 and not done[0]:
            fin()

        if not done[0]:
          with ExitStack() as ph:
            TB = 256
            NPF = 6
            NTK = 8
            Gs = [sbt(ph, f"G{i}", [128, TB, 128], BF16) for i in range(2)]
            u2k = [sbt(ph, f"u2k{i}", [128, 8, TB], BF16) for i in range(2)]
            slg = [sbt(ph, f"slg{i}", [128, TB]) for i in range(2)]
            sli = [sbt(ph, f"sli{i}", [128, TB]) for i in range(2)]
            slj = [sbt(ph, f"slj{i}", [128, TB]) for i in range(2)]
            Pb = [sbt(ph, f"Pb{i}", [128, NTK, 128], BF16) for i in range(2)]
            Qb = [sbt(ph, f"Qb{i}", [128, NTK, 128], BF16) for i in range(2)]
            UVb = [sbt(ph, f"UVb{i}", [128, 2048], BF16) for i in range(NPF)]
            Ub = [t_[:, 0:1024].rearrange("p (k j) -> p k j", k=8) for t_ in UVb]
            Vb = [t_[:, 1024:2048] for t_ in UVb]
            hb = [sbt(ph, f"hb{i}", [128, TB], BF16) for i in range(2)]
            Wb = [sbt(ph, f"Wb{i}", [128, TB], BF16) for i in range(2)]
            x1t_ = sbt(ph, "x1t", [128, D]); x1t = [x1t_, x1t_]
            x2t_ = sbt(ph, "x2t", [128, D]); x2t = [x2t_, x2t_]
            fs = sbt(ph, "fs", [128, 2]); fr = sbt(ph, "fr", [128, 2])
            ot_ = sbt(ph, "ot", [128, D]); ot = [ot_, ot_]
            junk = ot_
            pAcc = [pst(ph, f"pAcc{i}", [128, 512]) for i in range(4)]
            pSx = [pst(ph, f"pSx{i}", [128, 512]) for i in range(2)]
            pG = [pst(ph, f"pG{i}", [128, 512]) for i in range(2)]
            gtf_bc = sbt(ph, "gtf_bc", [128, D])
            fing_bc = sbt(ph, "fing_bc", [128, D])
            S.dma("sp", fing_bc[:], fing_d, writes=["fing"])
            build_gate_bc(ph, gtf_bc, 40, pG[0], ("pG", 0))
            u2T3 = u2T_s.rearrange("p (k t) -> p k t", k=8)
            allU = [("Ub", i) for i in range(NCAST)]
            allV = [("Vb", i) for i in range(NCAST)]
            nload = [0]
            ngq = [0]
            nblk = T // TB

            def load_expert(i):
                s_ = nload[0] % NPF
                nload[0] += 1
                S.dma("sp", UVb[s_][:], UV_s[i * 128:(i + 1) * 128, :], reads=allU + allV, writes=[("UVt", s_)])

            def load_block(blk):
                gb = blk % 2
                t0 = blk * TB
                S.dma("sp", u2k[gb][:], u2T3[:, :, t0:t0 + TB], writes=[("u2k", gb)])
                S.dma("sp", slg[gb][:], gT_s[:, t0:t0 + TB], writes=[("slg", gb)])
                S.dma("sp", sli[gb][:], iT_s[:, t0:t0 + TB], writes=[("sli", gb)])
                S.dma("sp", slj[gb][:], jT_s[:, t0:t0 + TB], writes=[("slj", gb)])

            def gbuild_dve(blk, tb):
                gb = blk % 2
                pbuf = (blk * (TB // NTK) + tb) % 2
                ts_ = slice(tb * NTK, (tb + 1) * NTK)
                io_bc = iota_f[:].unsqueeze(1).to_broadcast([128, NTK, 128])
                S.op("dve", lambda e: e.tensor_tensor(out=Pb[pbuf][:], in0=io_bc,
                                                     in1=sli[gb][:, ts_].unsqueeze(2).to_broadcast([128, NTK, 128]), op=ALU.is_equal),
                     reads=[("sli", gb)], writes=[("Pb", pbuf)])
                S.op("pool", lambda e: e.tensor_tensor(out=Pb[pbuf][:], in0=Pb[pbuf][:],
                                                      in1=slg[gb][:, ts_].unsqueeze(2).to_broadcast([128, NTK, 128]), op=ALU.mult),
                     reads=[("slg", gb), ("Pb", pbuf)], writes=[("Pb", pbuf)])
                S.op("dve", lambda e: e.tensor_tensor(out=Qb[pbuf][:], in0=io_bc,
                                                     in1=slj[gb][:, ts_].unsqueeze(2).to_broadcast([128, NTK, 128]), op=ALU.is_equal),
                     reads=[("slj", gb)], writes=[("Qb", pbuf)])

            def gbuild_pe(blk, tb):
                gb = blk % 2
                pbuf = (blk * (TB // NTK) + tb) % 2
                for tq in range(NTK // 4):
                    pgi = ngq[0] % 2
                    ngq[0] += 1
                    pg = pG[pgi]
                    for t4 in range(4):
                        tl = tq * 4 + t4
                        S.op("pe", lambda e: e.matmul(pg[:, t4 * 128:(t4 + 1) * 128], lhsT=Qb[pbuf][:, tl, :], rhs=Pb[pbuf][:, tl, :],
                                                      start=True, stop=True), reads=[("Pb", pbuf), ("Qb", pbuf)], writes=[("pG", pgi)])
                    tg = tb * NTK + tq * 4
                    S.op("act", lambda e: e.copy(out=Gs[gb][:, tg:tg + 4, :].rearrange("p a b -> p (a b)"), in_=pg[:]),
                         reads=[("pG", pgi)], writes=[("G", gb, tb)])

            def gbuild(blk, tb):
                gbuild_dve(blk, tb)
                gbuild_pe(blk, tb)

            def s_stage(blk, i):
                gb = blk % 2
                s_ = (blk * 128 + i) % NPF
                b2 = i % 2
                for dk in range(8):
                    S.op("pe", lambda e: e.matmul(pSx[b2][:, 0:TB], lhsT=Ub[s_][:, dk, :], rhs=u2k[gb][:, dk, :], start=(dk == 0), stop=(dk == 7)),
                         reads=[("UVt", s_), ("u2k", gb)], writes=[("pSx", b2)])
                S.op("act", lambda e: e.activation(out=hb[b2][:], in_=pSx[b2][:, 0:TB], func=AF.Gelu), reads=[("pSx", b2)], writes=[("hb", b2)])
                S.op("dve", lambda e: e.tensor_tensor(out=Wb[b2][:], in0=hb[b2][:], in1=Gs[gb][:, :, i], op=ALU.mult),
                     reads=[("hb", b2)] + ([("G", gb, tb) for tb in range(TB // NTK)] if i < 2 else []), writes=[("Wb", b2)])

            def v_stage(blk, i):
                s_ = (blk * 128 + i) % NPF
                b2 = i % 2
                for tt2 in range(TB // 128):
                    for dh in range(2):
                        S.op("pe", lambda e: e.matmul(pAcc[tt2 * 2 + dh][:], lhsT=Wb[b2][:, tt2 * 128:(tt2 + 1) * 128],
                                                      rhs=Vb[s_][:, dh * 512:(dh + 1) * 512], start=(i == 0), stop=(i == 127)),
                             reads=[("Wb", b2), ("UVt", s_)], writes=[("pAcc", tt2 * 2 + dh)])

            def epilogue(blk):
                t0 = blk * TB
                for tt2 in range(TB // 128):
                    tsl = slice(t0 + tt2 * 128, t0 + (tt2 + 1) * 128)
                    S.dma("sp", x1t[tt2][:], x1_s[tsl, :], writes=["x1t"])
                    for dh in range(2):
                        S.op("dve", lambda e: e.tensor_tensor(out=x2t[tt2][:, dh * 512:(dh + 1) * 512], in0=pAcc[tt2 * 2 + dh][:],
                                                             in1=gtf_bc[:, dh * 512:(dh + 1) * 512], op=ALU.mult),
                             reads=[("pAcc", tt2 * 2 + dh), "gtf_bc"], writes=["x2t"])
                    if "peer" in debug:
                        if "peer" not in dbg:
                            dbg_out("peer", [T, D], force=True)
                        S.dma("sp", dbg["peer"][tsl, :], x2t[tt2][:], reads=["x2t"], writes=[("dbg_peer", blk, tt2)])
                    S.op("pool", lambda e: e.tensor_add(out=x2t[tt2][:], in0=x2t[tt2][:], in1=x1t[tt2][:]),
                         reads=["x2t", "x1t"], writes=["x2t"])
                    S.op("act", lambda e: e.activation(out=junk[:], in_=x2t[tt2][:], func=AF.Square), reads=["x2t"], writes=["ot"])
                    S.op("dve", lambda e: e.reduce_sum(out=fs[:, tt2:tt2 + 1], in_=junk[:], axis=AX.X), reads=["ot"], writes=[("fs", tt2)])
                    S.op("act", lambda e: e.activation(out=fr[:, tt2:tt2 + 1], in_=fs[:, tt2:tt2 + 1], func=AF.Sqrt, bias=1e-6, scale=1.0 / D),
                         reads=[("fs", tt2)], writes=[("fr", tt2)])
                    S.op("dve", lambda e: e.reciprocal(out=fr[:, tt2:tt2 + 1], in_=fr[:, tt2:tt2 + 1]), reads=[("fr", tt2)], writes=[("fr", tt2)])
                    S.op("dve", lambda e: e.scalar_tensor_tensor(out=ot[tt2][:], in0=x2t[tt2][:], scalar=fr[:, tt2:tt2 + 1], in1=fing_bc[:],
                                                                op0=ALU.mult, op1=ALU.mult),
                         reads=["x2t", ("fr", tt2), "fing"], writes=["ot"])
                    S.dma("sp", out_d[tsl, :], ot[tt2][:], reads=["ot"], writes=[("out", blk, tt2)])

            load_block(0)
            for i in range(NPF - 1):
                load_expert(i)
            for tb in range(TB // NTK):
                gbuild(0, tb)
            for blk in range(nblk):
                if blk + 1 < nblk:
                    load_block(blk + 1)
                s_stage(blk, 0)
                s_stage(blk, 1)
                for i in range(128):
                    nxt_e = blk * 128 + i + NPF - 1
                    if nxt_e < nblk * 128:
                        load_expert(nxt_e % 128)
                    v_stage(blk, i)
                    if i + 2 < 128:
                        s_stage(blk, i + 2)
                    if blk + 1 < nblk and i % 4 == 0:
                        if i >= 4:
                            gbuild_pe(blk + 1, i // 4 - 1)
                        gbuild_dve(blk + 1, i // 4)
                if blk + 1 < nblk:
                    gbuild_pe(blk + 1, TB // NTK - 1)
                epilogue(blk)
            S.barrier()
            S.finish([])
    print(f"[kernel] instructions={S.ninst} waits={S.nwaits}")
    return nc, dbg


def col(v, n):
    return np.ascontiguousarray(np.asarray(v, np.float32).reshape(n, 128).T)


def prep_inputs(inp):
    f = lambda k: np.asarray(inp[k], np.float32)
    cols = np.zeros((128, NCOL), np.float32)
    cols[:, G1:G1 + 8] = col(f("norm_mix_g")[0], 8)
    cols[:, G2:G2 + 8] = col(f("norm_ffn_g")[0], 8)
    cols[:, CW:CW + 124] = f("conv_dw_w")[0].reshape(31, 4, 128).transpose(2, 1, 0).reshape(128, 124)
    cols[:, CB:CB + 4] = col(f("conv_dw_b")[0], 4)
    cols[:, LW:LW + 4] = col(f("conv_ln_w")[0], 4)
    cols[:, LB:LB + 4] = col(f("conv_ln_b")[0], 4)
    cols[:, MU:MU + 14] = col(f("rwkv_mu")[0], 14)
    cols[:, W0:W0 + 4] = col(f("rwkv_w0")[0], 4)
    cols[:, A0:A0 + 4] = col(f("rwkv_a0")[0], 4)
    cols[:, KK:KK + 4] = col(f("rwkv_k_k")[0], 4)
    cols[:, KA:KA + 4] = col(f("rwkv_k_a")[0], 4)
    cols[:, RK:RK + 4] = col(f("rwkv_r_k")[0].reshape(-1), 4)
    cols[:, GW:GW + 4] = col(f("rwkv_gn_w")[0], 4)
    cols[:, GB:GB + 4] = col(f("rwkv_gn_b")[0], 4)
    shared = {
        "ada_w": np.ascontiguousarray(f("ada_w")[0]),
        "ada_b_col": col(f("ada_b")[0], 48),
        "cols": cols,
        "final_g_bc": np.ascontiguousarray(np.broadcast_to(f("final_g")[None, :], (128, D))),
        "w_in": np.ascontiguousarray(f("w_in")[0]),
        "wa2": np.ascontiguousarray(np.concatenate([f("rwkv_w2")[0], f("rwkv_a2")[0]], axis=0)),
        "g2": np.ascontiguousarray(f("rwkv_g2")[0]),
        "w_out": np.ascontiguousarray(f("w_out")[0]),
        "wqT": np.ascontiguousarray(f("peer_w_q")[0].T),
        "KT": np.ascontiguousarray(f("peer_sub_keys")[0].reshape(16, 128, 128).transpose(2, 0, 1).reshape(128, 2048)),
        "UTt": np.ascontiguousarray(f("peer_u")[0].reshape(128, 128, 8, 128).transpose(0, 3, 2, 1).reshape(128 * 128, 1024)),
        "V": np.ascontiguousarray(f("peer_v")[0]),
    }
    maps = []
    for b in range(NCORES):
        m = dict(shared)
        m["x"] = np.ascontiguousarray(f("x")[b])
        m["c_col"] = col(f("c")[b], 8)
        maps.append(m)
    return maps


_NC_CACHE = {}


def kernel(**inputs):
    maps = prep_inputs(inputs)
    if "nc" not in _NC_CACHE:
        _NC_CACHE["nc"] = build_nc()[0]
    nc = _NC_CACHE["nc"]
    res = run_bass_kernel_spmd(nc, maps, core_ids=list(range(NCORES)))
    return np.stack([np.asarray(r["out"], np.float32) for r in res.results], axis=0)
```

```python
import numpy as np
from contextlib import ExitStack
import concourse.bass as bass
import concourse.mybir as mybir
from concourse.bass_utils import run_bass_kernel_spmd

F32 = mybir.dt.float32
BF16 = mybir.dt.bfloat16
U16 = mybir.dt.uint16
I32 = mybir.dt.int32
AF = mybir.ActivationFunctionType
ALU = mybir.AluOpType
AX = mybir.AxisListType

T = 4096
D = 1024
NCORES = 8
C0 = 0.6065306597126334

G1, G2, CW, CB, LW, LB, MU, W0, A0, KK, KA, RK, GW, GB, NCOL = (
    0, 8, 16, 140, 144, 148, 152, 166, 170, 174, 178, 182, 186, 190, 194)


class Sync:
    NDMA = 12

    def __init__(self, nc, stack):
        self.nc = nc
        self.eng = {"pe": nc.tensor, "act": nc.scalar, "dve": nc.vector,
                    "pool": nc.gpsimd, "sp": nc.sync}
        self.sem, self.cnt = {}, {}
        self.semobj = {}
        for e in self.eng:
            self.sem[e] = stack.enter_context(nc.semaphore("s_" + e))
            self.semobj[self.sem[e].name] = self.sem[e]
            self.cnt[e] = 0
        self.dsem, self.dnext = {}, {}
        self.eng["cast"] = nc.gpsimd
        for q in ("sp", "pool", "act", "cast"):
            self.dsem[q] = [[stack.enter_context(nc.semaphore(f"d_{q}{i}")), 0]
                            for i in range(32 if q == "cast" else self.NDMA)]
            for s, _ in self.dsem[q]:
                self.semobj[s.name] = s
            self.dnext[q] = 0
        self.seen = {e: {} for e in self.eng}
        self.snap = {}
        self.lastw = {}
        self.readers = {}
        self.nwaits = 0
        self.ninst = 0

    def _need(self, e, ev, waits):
        if ev is None:
            return
        name, val = ev
        if self.seen[e].get(name, 0) >= val:
            return
        if e == "pe" and name == self.sem["pe"].name:
            return
        if waits.get(name, 0) < val:
            waits[name] = val

    def _do_waits(self, e, waits):
        for name, val in waits.items():
            if self.seen[e].get(name, 0) >= val:
                continue
            self.eng[e].wait_ge(self.semobj[name], val)
            self.nwaits += 1
            sn = self.snap.get((name, val))
            se = self.seen[e]
            if sn:
                for k, v in sn.items():
                    if se.get(k, 0) < v:
                        se[k] = v
            if se.get(name, 0) < val:
                se[name] = val

    def _deps(self, e, reads, writes):
        waits = {}
        for k in reads:
            self._need(e, self.lastw.get(k), waits)
        for k in writes:
            self._need(e, self.lastw.get(k), waits)
            for ev in self.readers.get(k, ()):
                self._need(e, ev, waits)
        self._do_waits(e, waits)

    def _record(self, e, ev, reads, writes):
        for k in reads:
            self.readers.setdefault(k, []).append(ev)
        for k in writes:
            self.lastw[k] = ev
            self.readers[k] = []
        self.snap[ev] = dict(self.seen[e])

    EXCL = {"psmod", "psb", "pT", "psA", "psB", "psC", "ps1", "ps2", "rpsA0", "rpsA1", "rpsM0", "rpsM1",
            "rW", "pO", "pT3", "pW", "pS", "pTr", "pAcc", "pSx", "pG"}

    def op(self, e, fn, reads=(), writes=()):
        ex = [k for k in reads if (k if isinstance(k, str) else k[0]) in self.EXCL]
        if ex:
            writes = list(writes) + ex
        self._deps(e, reads, writes)
        ins = fn(self.eng[e])
        self.cnt[e] += 1
        s = self.sem[e]
        ins.then_inc(s, 1)
        ev = (s.name, self.cnt[e])
        self._record(e, ev, reads, writes)
        self.ninst += 1
        return ev

    def dma(self, q, out, in_, reads=(), writes=(), **kw):
        pool = self.dsem[q]
        slot = pool[self.dnext[q] % len(pool)]
        self.dnext[q] += 1
        s = slot[0]
        waits = {}
        if slot[1] > 0:
            self._need(q, (s.name, slot[1]), waits)
        self._do_waits(q, waits)
        self._deps(q, reads, writes)
        ins = self.eng[q].dma_start(out=out, in_=in_, **kw)
        slot[1] += 16
        ins.then_inc(s, 16)
        ev = (s.name, slot[1])
        self._record(q, ev, reads, writes)
        self.ninst += 1
        return ev

    def barrier(self):
        evs = []
        for e in self.cnt:
            if self.cnt[e]:
                evs.append((self.sem[e].name, self.cnt[e]))
        for q in self.dsem:
            if q == "cast":
                continue
            for s, c in self.dsem[q]:
                if c:
                    evs.append((s.name, c))
        for e in self.eng:
            if e == "cast":
                continue
            waits = {}
            for ev in evs:
                self._need(e, ev, waits)
            self._do_waits(e, waits)
        keep = lambda k: isinstance(k, tuple) and k[0] in ("Ub", "Vb")
        self.lastw = {k: v for k, v in self.lastw.items() if keep(k)}
        self.readers = {k: v for k, v in self.readers.items() if keep(k)}

    def finish(self, keys):
        waits = {}
        for k in keys:
            self._need("sp", self.lastw.get(k), waits)
        self._do_waits("sp", waits)


def build_nc(debug=(), stop_after=None):
    nc = bass.Bass("TRN2", target_bir_lowering=False)

    def din(name, shape, dt=F32):
        return nc.dram_tensor(name, list(shape), dt, kind="ExternalInput").ap()

    def dscr(name, shape, dt=F32):
        return nc.dram_tensor(name, list(shape), dt, kind="Internal").ap()

    x_d = din("x", [T, D])
    ccol_d = din("c_col", [128, 8])
    adaw_d = din("ada_w", [D, 6 * D])
    adab_d = din("ada_b_col", [128, 48])
    cols_d = din("cols", [128, NCOL])
    fing_d = din("final_g_bc", [128, D])
    win_d = din("w_in", [D, 2816])
    wa2_d = din("wa2", [128, 512])
    g2_d = din("g2", [128, 512])
    wout_d = din("w_out", [D, D])
    wqT_d = din("wqT", [2048, D])
    KT_d = din("KT", [128, 16 * 128])
    UT_d = din("UTt", [128 * 128, 1024])
    V_d = din("V", [16384, D])
    out_d = nc.dram_tensor("out", [T, D], F32, kind="ExternalOutput").ap()

    yc_s = dscr("yc_s", [512, T])
    ycat_s = dscr("ycat_s", [1024, T], BF16)
    x1_s = dscr("x1_s", [T, D])
    u2T_s = dscr("u2T_s", [128, 8 * T], BF16)
    gT_s = dscr("gT_s", [128, T])
    iT_s = dscr("iT_s", [128, T])
    jT_s = dscr("jT_s", [128, T])
    UV_s = dscr("UV_s", [128 * 128, 2048], BF16)

    dbg = {}

    def dbg_out(name, shape, dt=F32, force=False):
        if name in debug or force:
            dbg[name] = nc.dram_tensor("dbg_" + name, list(shape), dt, kind="ExternalOutput").ap()
            return dbg[name]
        return None

    with ExitStack() as top:
        S = Sync(nc, top)

        def sbt(st, name, shape, dt=F32):
            return st.enter_context(nc.sbuf_tensor("s_" + name, list(shape), dt))

        def pst(st, name, shape=(128, 512), dt=F32):
            return st.enter_context(nc.psum_tensor("p_" + name, list(shape), dt))

        def aff(e, out, in_, scale, bias, reads, writes, func=None):
            if e == "act":
                f = func or AF.Identity
                return S.op("act", lambda en: en.activation(out=out, in_=in_, func=f, bias=bias, scale=scale),
                            reads=reads, writes=writes)
            assert func is None
            return S.op(e, lambda en: en.tensor_scalar(out=out, in0=in_, scalar1=scale, scalar2=bias,
                                                       op0=ALU.mult, op1=ALU.add), reads=reads, writes=writes)

        def build_gate_bc(st, dst, mof, pb, pbkey):
            dg = sbt(st, "dg_" + dst.name[2:], [128, 128])
            for cidx in range(8):
                S.op("dve", lambda e: e.tensor_scalar_mul(out=dg[:], in0=ident[:], scalar1=modT[:, mof + cidx:mof + cidx + 1]),
                     reads=["dg"], writes=["dg"])
                S.op("pe", lambda e: e.matmul(pb[:, 0:128], lhsT=ones_f[:], rhs=dg[:], start=True, stop=True),
                     reads=["dg"], writes=[pbkey])
                S.op("act", lambda e: e.copy(out=dst[:, cidx * 128:(cidx + 1) * 128], in_=pb[:, 0:128]),
                     reads=[pbkey], writes=[dst.name[2:]])

        cols = sbt(top, "cols", [128, NCOL])
        dcol = sbt(top, "dcol", [128, 48])
        OMM, OMKA, GSC1, GSC2 = 0, 14, 18, 26
        modT = sbt(top, "modT", [128, 48])
        ident = sbt(top, "ident", [128, 128])
        iota_f = sbt(top, "iota_f", [128, 128])
        ones_f = sbt(top, "ones_f", [128, 128])
        blk1 = sbt(top, "blk1", [128, 128])
        mask4 = sbt(top, "mask4", [128, 4, 128])
        maskL = sbt(top, "maskL", [128, 128])

        NCAST = 16
        cast_jobs = []
        for i in range(NCAST):
            r0, r1 = i * (16384 // NCAST), (i + 1) * (16384 // NCAST)
            cast_jobs.append((UV_s[r0:r1, 0:1024], UT_d[r0:r1, :], ("Ub", i)))
            cast_jobs.append((UV_s[r0:r1, 1024:2048], V_d[r0:r1, :], ("Vb", i)))

        def issue_casts(n):
            for _ in range(n):
                if cast_jobs:
                    o_, i_, k_ = cast_jobs.pop(0)
                    S.dma("cast", o_, i_, writes=[k_])

        with ExitStack() as ph:
            io_i = sbt(ph, "io_i", [128, 128], I32)
            io_f = sbt(ph, "io_f", [128, 128])
            ccol = sbt(ph, "ccol", [128, 8])
            sc = sbt(ph, "silu_c", [128, 8])
            adab = sbt(ph, "adab", [128, 48])
            abuf = [sbt(ph, f"abuf{i}", [128, 2048]) for i in range(2)]
            dg = sbt(ph, "dg", [128, 128])
            psmod_t = pst(ph, "psmod", [128, 512])
            psmod = psmod_t[:, 0:384]
            psb = [pst(ph, f"psb{i}", [128, 512]) for i in range(2)]

            S.dma("sp", cols[:], cols_d, writes=["cols"])
            S.dma("sp", ccol[:], ccol_d, writes=["ccol"])
            S.dma("sp", adab[:], adab_d, writes=["adab"])
            S.op("pool", lambda e: e.iota(io_i[:], pattern=[[1, 128]], base=0, channel_multiplier=-1), writes=["io_i"])
            S.op("dve", lambda e: e.tensor_copy(out=io_f[:], in_=io_i[:]), reads=["io_i"], writes=["io_f"])
            S.op("dve", lambda e: e.tensor_single_scalar(out=ident[:], in_=io_f[:], scalar=0.0, op=ALU.is_equal),
                 reads=["io_f"], writes=["ident"])
            S.op("pool", lambda e: e.iota(io_i[:], pattern=[[1, 128]], base=0, channel_multiplier=0),
                 reads=["io_i"], writes=["io_i"])
            S.op("dve", lambda e: e.tensor_copy(out=iota_f[:], in_=io_i[:]), reads=["io_i"], writes=["iota_f"])
            S.op("pool", lambda e: e.memset(ones_f[:], 1.0), writes=["ones_f"])
            S.op("pool", lambda e: e.memset(blk1[:], 0.0), writes=["blk1"])
            S.op("pool", lambda e: e.memset(blk1[0:64, 0:64], 1.0), writes=["blk1"])
            S.op("pool", lambda e: e.memset(blk1[64:128, 64:128], 1.0), writes=["blk1"])
            S.op("dve", lambda e: e.scalar_tensor_tensor(out=mask4[:, 0, :], in0=io_f[:], scalar=0.0, in1=blk1[:],
                                                        op0=ALU.is_gt, op1=ALU.mult), reads=["io_f", "blk1"], writes=["mask4"])
            S.op("dve", lambda e: e.tensor_copy(out=mask4[:, 1, :], in_=mask4[:, 0, :]), reads=["mask4"], writes=["mask4"])
            S.op("dve", lambda e: e.scalar_tensor_tensor(out=mask4[:, 2, :], in0=io_f[:], scalar=0.0, in1=blk1[:],
                                                        op0=ALU.is_ge, op1=ALU.mult), reads=["io_f", "blk1", "mask4"], writes=["mask4"])
            S.op("dve", lambda e: e.tensor_copy(out=mask4[:, 3, :], in_=mask4[:, 2, :]), reads=["mask4"], writes=["mask4"])
            S.op("dve", lambda e: e.scalar_tensor_tensor(out=maskL[:], in0=io_f[:], scalar=0.0, in1=blk1[:],
                                                        op0=ALU.is_lt, op1=ALU.mult), reads=["io_f", "blk1"], writes=["maskL"])
            S.op("dve", lambda e: e.tensor_scalar(out=dcol[:, OMM:OMM + 14], in0=cols[:, MU:MU + 14], scalar1=-1.0, scalar2=1.0,
                                                 op0=ALU.mult, op1=ALU.add), reads=["cols"], writes=["dcol"])
            S.op("dve", lambda e: e.tensor_scalar(out=dcol[:, OMKA:OMKA + 4], in0=cols[:, KA:KA + 4], scalar1=-1.0, scalar2=1.0,
                                                 op0=ALU.mult, op1=ALU.add), reads=["cols", "dcol"], writes=["dcol"])
            S.op("act", lambda e: e.activation(out=sc[:], in_=ccol[:], func=AF.Silu), reads=["ccol"], writes=["silu_c"])
            n = 0
            for k in range(8):
                for pc in range(3):
                    ab = abuf[n % 2]
                    S.dma("sp", ab[:], adaw_d[k * 128:(k + 1) * 128, pc * 2048:(pc + 1) * 2048], writes=[("abuf", n % 2)])
                    for fl in range(16):
                        f = pc * 16 + fl
                        S.op("pe", lambda e: e.matmul(psmod[:, f * 8 + k:f * 8 + k + 1], lhsT=ab[:, fl * 128:(fl + 1) * 128],
                                                      rhs=sc[:, k:k + 1], start=True, stop=True),
                             reads=[("abuf", n % 2), "silu_c"], writes=["psmod"])
                    n += 1
            S.op("dve", lambda e: e.tensor_reduce(out=modT[:], in_=psmod.rearrange("p (f k) -> p f k", k=8),
                                                 axis=AX.X, op=ALU.add), reads=["psmod"], writes=["modT"])
            S.op("dve", lambda e: e.tensor_add(out=modT[:], in0=modT[:], in1=adab[:]), reads=["modT", "adab"], writes=["modT"])
            for (dst, gof, sof) in ((GSC1, G1, 8), (GSC2, G2, 32)):
                S.op("dve", lambda e: e.scalar_tensor_tensor(out=dcol[:, dst:dst + 8], in0=modT[:, sof:sof + 8], scalar=1.0,
                                                            in1=cols[:, gof:gof + 8], op0=ALU.add, op1=ALU.mult),
                     reads=["modT", "cols", "dcol"], writes=["dcol"])
            if "modT" in debug:
                S.dma("sp", dbg_out("modT", [128, 48]), modT[:], reads=["modT"], writes=["dbg_modT"])
            S.barrier()

        done = [False]

        def fin():
            S.barrier()
            S.finish([])
            done[0] = True

        if stop_after == 0:
            fin()

        if not done[0]:
          with ExitStack() as mx:
            uT = sbt(mx, "uT", [128, 8, T + 1], BF16)

            with ExitStack() as ph:
                xb = [sbt(ph, f"xb{i}", [128, D]) for i in range(2)]
                junk = sbt(ph, "junk", [128, D])
                xn = [sbt(ph, f"xn{i}", [128, D]) for i in range(2)]
                ss = sbt(ph, "ss", [128, 32])
                rs = sbt(ph, "rs", [128, 32])
                pT = [pst(ph, f"pT{i}", [128, 1024]) for i in range(2)]
                S.op("pool", lambda e: e.memset(uT[:, :, 0:1], 0.0), writes=["uT0"])
                def p1_front(tt):
                    b = tt % 2
                    S.dma("sp", xb[b][:], x_d[tt * 128:(tt + 1) * 128, :], writes=[("xb", b)])
                    S.op("act", lambda e: e.activation(out=junk[:], in_=xb[b][:], func=AF.Square),
                         reads=[("xb", b)], writes=["junk"])
                    S.op("dve", lambda e: e.reduce_sum(out=ss[:, tt:tt + 1], in_=junk[:], axis=AX.X),
                         reads=["junk"], writes=[("ss", tt)])
                    S.op("act", lambda e: e.activation(out=rs[:, tt:tt + 1], in_=ss[:, tt:tt + 1], func=AF.Sqrt,
                                                       bias=1e-6, scale=1.0 / D), reads=[("ss", tt)], writes=[("rs", tt)])
                    S.op("dve", lambda e: e.reciprocal(out=rs[:, tt:tt + 1], in_=rs[:, tt:tt + 1]),
                         reads=[("rs", tt)], writes=[("rs", tt)])
                    S.op("dve", lambda e: e.tensor_scalar_mul(out=xn[b][:], in0=xb[b][:], scalar1=rs[:, tt:tt + 1]),
                         reads=[("xb", b), ("rs", tt)], writes=[("xn", b)])

                p1_front(0)
                for tt in range(32):
                    b = tt % 2
                    if tt + 1 < 32:
                        p1_front(tt + 1)
                    for k in range(8):
                        S.op("pe", lambda e: e.transpose(pT[b][:, k * 128:(k + 1) * 128], xn[b][:, k * 128:(k + 1) * 128], ident[:]),
                             reads=[("xn", b)], writes=[("pT", b, k // 4)])
                    for k in range(8):
                        aff("act" if k < 4 else "dve", uT[:, k, 1 + tt * 128:1 + (tt + 1) * 128],
                            pT[b][:, k * 128:(k + 1) * 128], dcol[:, GSC1 + k:GSC1 + k + 1], modT[:, k:k + 1],
                            reads=[("pT", b, k // 4)], writes=[("uT", tt)])
                S.barrier()
            if "uT" in debug:
                du = dbg_out("uT", [128, 8 * T], BF16).rearrange("p (k t) -> p k t", k=8)
                for k in range(8):
                    S.dma("sp", du[:, k, :], uT[:, k, 1:T + 1], writes=[("dbg_uT", k)])
                S.barrier()
            if stop_after == 1:
                fin()

            def load_w(st_f32, dst_bf, f, key):
                S.dma("sp", st_f32[:], win_d[:, f * 128:(f + 1) * 128].rearrange("(k p) f -> p k f", p=128),
                      writes=[("wst", st_f32.name[2:])])
                S.op("pool", lambda e: e.tensor_copy(out=dst_bf[:], in_=st_f32[:]), reads=[("wst", st_f32.name[2:])], writes=[("wbf", key)])

            if not done[0]:
              with ExitStack() as ph:
                wst = [sbt(ph, f"wst{i}", [128, 8, 128]) for i in range(2)]
                wab = [sbt(ph, f"wab{i}", [128, 8, 128], BF16) for i in range(2)]
                diag = sbt(ph, "diag", [128, 31, 128], BF16)
                ypad = sbt(ph, "ypad", [128, 30 + T], BF16)
                sig = [sbt(ph, f"sig{i}", [128, 512]) for i in range(2)]
                ycb = [sbt(ph, f"ycb{i}", [128, 512]) for i in range(2)]
                psA = [pst(ph, f"psA{i}") for i in range(2)]
                psB = [pst(ph, f"psB{i}") for i in range(2)]
                psC = [pst(ph, f"psC{i}") for i in range(2)]
                S.op("pool", lambda e: e.memset(ypad[:, 0:30], 0.0), writes=["ypad0"])
                for ct in range(4):
                    issue_casts(2)
                    load_w(wst[0], wab[0], ct, 0)
                    load_w(wst[1], wab[1], 4 + ct, 1)
                    for k in range(31):
                        S.op("dve", lambda e: e.tensor_scalar_mul(out=diag[:, k, :], in0=ident[:],
                                                                  scalar1=cols[:, CW + ct * 31 + k:CW + ct * 31 + k + 1]),
                             writes=["diag"])
                    for tb in range(8):
                        b = tb % 2
                        t0 = tb * 512
                        for k in range(8):
                            S.op("pe", lambda e: e.matmul(psA[b][:], lhsT=wab[0][:, k, :], rhs=uT[:, k, 1 + t0:1 + t0 + 512],
                                                          start=(k == 0), stop=(k == 7)), reads=[("wbf", 0)], writes=[("psA", b)])
                        for k in range(8):
                            S.op("pe", lambda e: e.matmul(psB[b][:], lhsT=wab[1][:, k, :], rhs=uT[:, k, 1 + t0:1 + t0 + 512],
                                                          start=(k == 0), stop=(k == 7)), reads=[("wbf", 1)], writes=[("psB", b)])
                        S.op("act", lambda e: e.activation(out=sig[b][:], in_=psB[b][:], func=AF.Sigmoid),
                             reads=[("psB", b)], writes=[("sig", b)])
                        S.op("dve", lambda e: e.tensor_tensor(out=ypad[:, 30 + t0:30 + t0 + 512], in0=psA[b][:], in1=sig[b][:],
                                                             op=ALU.mult), reads=[("psA", b), ("sig", b)], writes=[("ypad", tb)])
                    for tb in range(8):
                        b = tb % 2
                        t0 = tb * 512
                        for k in range(31):
                            S.op("pe", lambda e: e.matmul(psC[b][:], lhsT=diag[:, k, :], rhs=ypad[:, t0 + k:t0 + k + 512],
                                                          start=(k == 0), stop=(k == 30)),
                                 reads=["diag", "ypad0", ("ypad", tb), ("ypad", max(tb - 1, 0))], writes=[("psC", b)])
                        aff("act", ycb[b][:], psC[b][:], 1.0, cols[:, CB + ct:CB + ct + 1], reads=[("psC", b)], writes=[("ycb", b)])
                        S.dma("sp", yc_s[ct * 128:(ct + 1) * 128, t0:t0 + 512], ycb[b][:], reads=[("ycb", b)],
                              writes=[("yc_s", ct, tb)])
                S.barrier()
            if "yc" in debug and not done[0]:
                S.dma("sp", dbg_out("yc", [512, T]), yc_s, writes=["dbg_yc"])
                S.barrier()
            if stop_after == 2 and not done[0]:
                fin()
            done_2a = True

            if not done[0]:
              with ExitStack() as ph:
                yb = [sbt(ph, f"yb{i}", [128, 4, 512]) for i in range(2)]
                sq = sbt(ph, "sq", [128, 4, 512])
                mean = sbt(ph, "mean", [128, 512])
                msq = sbt(ph, "msq", [128, 512])
                var = sbt(ph, "var", [128, 512])
                dd = [sbt(ph, f"dd{i}", [128, 512]) for i in range(2)]
                yo = [sbt(ph, f"yo{i}", [128, 512], BF16) for i in range(2)]
                ps1 = pst(ph, "ps1")
                ps2 = pst(ph, "ps2")
                for tb in range(8):
                    b = tb % 2
                    t0 = tb * 512
                    S.dma("sp", yb[b][:], yc_s[:, t0:t0 + 512].rearrange("(c p) t -> p c t", p=128), writes=[("yb", b)])
                    S.op("act", lambda e: e.activation(out=sq[:], in_=yb[b][:], func=AF.Square), reads=[("yb", b)], writes=["sq"])
                    for ct in range(4):
                        S.op("pe", lambda e: e.matmul(ps1[:], lhsT=ones_f[:], rhs=yb[b][:, ct, :], start=(ct == 0), stop=(ct == 3)),
                             reads=[("yb", b)], writes=["ps1"])
                    for ct in range(4):
                        S.op("pe", lambda e: e.matmul(ps2[:], lhsT=ones_f[:], rhs=sq[:, ct, :], start=(ct == 0), stop=(ct == 3)),
                             reads=["sq"], writes=["ps2"])
                    S.op("act", lambda e: e.mul(out=mean[:], in_=ps1[:], mul=1.0 / 512), reads=["ps1"], writes=["mean"])
                    S.op("pool", lambda e: e.tensor_mul(out=msq[:], in0=mean[:], in1=mean[:]), reads=["mean"], writes=["msq"])
                    S.op("dve", lambda e: e.scalar_tensor_tensor(out=var[:], in0=ps2[:], scalar=1.0 / 512, in1=msq[:],
                                                                op0=ALU.mult, op1=ALU.subtract), reads=["ps2", "msq"], writes=["var"])
                    S.op("act", lambda e: e.activation(out=var[:], in_=var[:], func=AF.Sqrt, bias=1e-5, scale=1.0),
                         reads=["var"], writes=["var"])
                    S.op("dve", lambda e: e.reciprocal(out=var[:], in_=var[:]), reads=["var"], writes=["var"])
                    for ct in range(4):
                        d2 = ct % 2
                        S.op("dve", lambda e: e.tensor_sub(out=dd[d2][:], in0=yb[b][:, ct, :], in1=mean[:]),
                             reads=[("yb", b), "mean"], writes=[("dd", d2)])
                        S.op("pool", lambda e: e.tensor_mul(out=dd[d2][:], in0=dd[d2][:], in1=var[:]),
                             reads=[("dd", d2), "var"], writes=[("dd", d2)])
                        aff("act", yo[d2][:], dd[d2][:], cols[:, LW + ct:LW + ct + 1], cols[:, LB + ct:LB + ct + 1],
                            reads=[("dd", d2)], writes=[("yo", d2)], func=AF.Silu)
                        S.dma("sp", ycat_s[ct * 128:(ct + 1) * 128, t0:t0 + 512], yo[d2][:], reads=[("yo", d2)],
                              writes=[("ycat", ct, tb)])
                S.barrier()
            if "ycat" in debug and stop_after == 3 and not done[0]:
                S.dma("sp", dbg_out("ycat", [1024, T], BF16), ycat_s, writes=["dbg_ycat"])
                S.barrier()
            if stop_after == 3 and not done[0]:
                fin()

            if not done[0]:
              with ExitStack() as ph:
                NB = 256
                NCH = NB // 64
                wst_ = sbt(ph, "rwst", [128, 8, 128]); wst = [wst_, wst_]
                wl_bf = sbt(ph, "wl_bf", [128, 8, 128], BF16)
                wr_bf = [sbt(ph, f"wr_bf{i}", [128, 8, 128], BF16) for i in range(3)]
                twal = sbt(ph, "twal", [128, T], BF16)
                sg = sbt(ph, "sg", [128, T], BF16)
                wa2b = sbt(ph, "wa2b", [128, 512], BF16)
                g2b = sbt(ph, "g2b", [128, 512], BF16)
                cmask = sbt(ph, "cmask", [128, NB])
                tmpx = [sbt(ph, f"tmpx{i}", [128, 256]) for i in range(2)]
                xsl = sbt(ph, "xsl", [128, 256])
                psA = [pst(ph, f"rpsA{i}") for i in range(2)]
                psM = [pst(ph, f"rpsM{i}") for i in range(2)]
                Wk = [pst(ph, f"rW{i}") for i in range(4)]

                S.op("pool", lambda e: e.memset(cmask[:], 1.0), writes=["cmask"])
                S.op("pool", lambda e: e.memset(cmask[:].rearrange("p (c k) -> p c k", k=64)[:, :, 0:1], 0.0),
                     reads=["cmask"], writes=["cmask"])

                halos = sbt(ph, "halos", [128, 4])
                npx = [0]

                def proj_xs1(wbf, wkey, fcol, t0, n, out_ap, okey, j3, first):
                    bi = npx[0] % 2
                    npx[0] += 1
                    pa = psA[bi]
                    pk_ = "rpsA%d" % bi
                    tx = tmpx[bi]
                    mu_c = cols[:, MU + fcol:MU + fcol + 1]
                    hk = ("halo", j3)
                    for k in range(8):
                        S.op("pe", lambda e: e.matmul(pa[:, 0:n], lhsT=wbf[:, k, :], rhs=uT[:, k, 1 + t0:1 + t0 + n],
                                                      start=(k == 0), stop=(k == 7)), reads=[wkey], writes=[pk_])
                    if first:
                        S.op("act", lambda e: e.memzero(halos[:, j3:j3 + 1]), writes=[hk])
                    S.op("act", lambda e: e.activation(out=tx[:, 0:1], in_=halos[:, j3:j3 + 1], func=AF.Copy, scale=mu_c),
                         reads=[hk], writes=[("tmpx", bi)])
                    S.op("act", lambda e: e.activation(out=tx[:, 1:n], in_=pa[:, 0:n - 1], func=AF.Copy, scale=mu_c),
                         reads=[pk_, ("tmpx", bi)], writes=[("tmpx", bi)])
                    S.op("act", lambda e: e.copy(out=halos[:, j3:j3 + 1], in_=pa[:, n - 1:n]), reads=[pk_, hk], writes=[hk])
                    S.op("dve", lambda e: e.scalar_tensor_tensor(out=out_ap, in0=pa[:, 0:n],
                                                                scalar=dcol[:, OMM + fcol:OMM + fcol + 1], in1=tx[:, 0:n],
                                                                op0=ALU.mult, op1=ALU.add),
                         reads=[pk_, ("tmpx", bi)], writes=[okey])

                def proj_xs(wbf, wkey, fcol, t0, n, out_ap, okey, slot):
                    pa, pb = psA[0], psA[1]
                    for k in range(8):
                        S.op("pe", lambda e: e.matmul(pa[:, 0:n], lhsT=wbf[:, k, :], rhs=uT[:, k, 1 + t0:1 + t0 + n],
                                                      start=(k == 0), stop=(k == 7)), reads=[wkey], writes=["rpsA0"])
                    for k in range(8):
                        S.op("pe", lambda e: e.matmul(pb[:, 0:n], lhsT=wbf[:, k, :], rhs=uT[:, k, t0:t0 + n],
                                                      start=(k == 0), stop=(k == 7)), reads=[wkey], writes=["rpsA1"])
                    tx = tmpx[slot % 2]
                    S.op("act", lambda e: e.activation(out=tx[:, 0:n], in_=pb[:, 0:n], func=AF.Copy,
                                                       scale=cols[:, MU + fcol:MU + fcol + 1]),
                         reads=["rpsA1"], writes=[("tmpx", slot % 2)])
                    S.op("dve", lambda e: e.scalar_tensor_tensor(out=out_ap, in0=pa[:, 0:n],
                                                                scalar=dcol[:, OMM + fcol:OMM + fcol + 1], in1=tx[:, 0:n],
                                                                op0=ALU.mult, op1=ALU.add),
                         reads=["rpsA0", ("tmpx", slot % 2)], writes=[okey])

                load_w(wst[0], wl_bf, 8 + 12, "l")
                for tb in range(16):
                    t0 = tb * 256
                    proj_xs1(wl_bf, ("wbf", "l"), 12, t0, 256, xsl[:], "xsl", 3, tb == 0)
                    S.op("act", lambda e: e.activation(out=twal[0:64, t0:t0 + 256], in_=xsl[0:64, :], func=AF.Tanh),
                         reads=["xsl"], writes=[("twal", tb)])
                    S.op("pool", lambda e: e.tensor_copy(out=twal[64:128, t0:t0 + 256], in_=xsl[64:128, :]),
                         reads=["xsl"], writes=[("twal2", tb)])
                load_w(wst[0], wl_bf, 8 + 13, "l")
                for tb in range(16):
                    t0 = tb * 256
                    proj_xs1(wl_bf, ("wbf", "l"), 13, t0, 256, xsl[:], "xsl", 3, tb == 0)
                    S.op("act", lambda e: e.activation(out=sg[:, t0:t0 + 256], in_=xsl[:], func=AF.Sigmoid),
                         reads=["xsl"], writes=[("sg", tb)])

                def wk(name, shape=(128, NB), dt=F32):
                    return sbt(ph, name, list(shape), dt)
                R2 = [wk("r_0"), wk("r_1")]; k0 = wk("k0"); V2 = [wk("v_0"), wk("v_1")]
                sgm = wk("sgm"); a_ = wk("a_"); GG2 = [wk("g_0"), wk("g_1")]
                cum = wk("cum"); cex = wk("cex"); dend = cex
                EIN2 = [wk("Ein0"), wk("Ein1")]; Eex = wk("Eex"); Einv = wk("Einv"); Eend = wk("Eend")
                kk = wk("kk"); rinv = wk("rinv"); kkn = wk("kkn")
                t1 = wk("t1"); kk2 = t1; K22 = [wk("k2_0"), wk("k2_1")]; bb = wk("bb"); rkr = wk("rkr")
                ysc = wk("ysc"); ysq = wk("ysq"); gmean = wk("gmean"); gvar = wk("gvar"); gtmp = wk("gtmp")
                yob = [sbt(ph, f"ryo{i}", [128, NB], BF16) for i in range(2)]
                PADS = [[wk(f"{nm}{pb}", (128, NCH, 128)) for nm in ("AP_", "BP_", "KP_", "RP_", "BC_", "KC_", "VP_")] for pb in range(2)]
                for pb_ in range(2):
                    for arr in PADS[pb_]:
                        S.op("pool", lambda e: e.memset(arr[:], 0.0), writes=[("padinit", arr.name)])
                tok = [wk(f"tok{c}", (128, 4, 128)) for c in range(NCH)]
                gram = [wk(f"gram{c}", (128, 4, 128)) for c in range(NCH)]
                NTm = [wk(f"NTm{c}", (128, 128)) for c in range(NCH)]
                Pm = [[wk(f"Pm{c}_{i}", (128, 2, 128)) for i in range(2)] for c in range(NCH)]
                Xm = [[wk(f"Xm{c}_{i}", (128, 128)) for i in range(2)] for c in range(NCH)]
                AW = [wk(f"AW{c}", (128, 2, 128)) for c in range(NCH)]
                AU = [wk(f"AU{c}", (128, 2, 128)) for c in range(NCH)]
                PTs = [wk(f"PTs{c}", (128, 128)) for c in range(NCH)]
                Qm = [wk(f"Qm{c}", (128, 128)) for c in range(NCH)]
                RH = [wk(f"RH{c}", (128, 128)) for c in range(NCH)]
                nwb = [0]
                wa2f = tok[0][:].rearrange("p a b -> p (a b)")
                g2f = gram[0][:].rearrange("p a b -> p (a b)")
                S.dma("sp", wa2f, wa2_d, writes=[("tok", 0)])
                S.dma("sp", g2f, g2_d, writes=[("gram", 0)])
                S.op("dve", lambda e: e.tensor_copy(out=wa2b[:], in_=wa2f), reads=[("tok", 0)], writes=["wa2b"])
                S.op("dve", lambda e: e.tensor_copy(out=g2b[:], in_=g2f), reads=[("gram", 0)], writes=["g2b"])
                STs = [wk(f"ST{i}", (128, 128)) for i in range(2)]

                for hp in range(4):
                    for j3, f in enumerate((8 + hp, 12 + hp, 16 + hp)):
                        load_w(wst[j3 % 2], wr_bf[j3], 8 + (f - 8), ("r", j3))
                    S.op("pool", lambda e: e.memset(STs[0][:], 0.0), writes=[("ST", 0)])
                    cs = slice(hp * 128, (hp + 1) * 128)
                    nch = [0]
                    def rw_prep(blk):
                        issue_casts(1)
                        pb = blk % 2
                        t0 = blk * NB
                        r_, k2, v_, g_, Ein = R2[pb], K22[pb], V2[pb], GG2[pb], EIN2[pb]
                        AP_, BP_, KP_, RP_, BC_, KC_, VP_ = PADS[pb]
                        padkeys = [(nm, pb, hh) for nm in ("AP_", "BP_", "KP_", "RP_", "BC_", "KC_", "VP_") for hh in range(2)]
                        pm, pm1 = psM[0], psM[1]
                        proj_xs1(wr_bf[0], ("wbf", ("r", 0)), hp, t0, NB, r_[:], ("r_", pb), 0, blk == 0)
                        proj_xs1(wr_bf[1], ("wbf", ("r", 1)), 4 + hp, t0, NB, k0[:], "k0", 1, blk == 0)
                        proj_xs1(wr_bf[2], ("wbf", ("r", 2)), 8 + hp, t0, NB, v_[:], ("v_", pb), 2, blk == 0)
                        S.op("pe", lambda e: e.matmul(pm[:, 0:NB], lhsT=wa2b[0:64, cs], rhs=twal[0:64, t0:t0 + NB], start=True, stop=True),
                             reads=["wa2b"], writes=["rpsM0"])
                        S.op("act", lambda e: e.activation(out=sgm[:], in_=pm[:, 0:NB], func=AF.Sigmoid, bias=cols[:, W0 + hp:W0 + hp + 1]),
                             reads=["rpsM0"], writes=["sgm"])
                        S.op("pe", lambda e: e.matmul(pm1[:, 0:NB], lhsT=wa2b[64:128, cs], rhs=twal[64:128, t0:t0 + NB], start=True, stop=True),
                             reads=["wa2b"], writes=["rpsM1"])
                        S.op("act", lambda e: e.activation(out=a_[:], in_=pm1[:, 0:NB], func=AF.Sigmoid, bias=cols[:, A0 + hp:A0 + hp + 1]),
                             reads=["rpsM1"], writes=["a_"])
                        S.op("pe", lambda e: e.matmul(pm[:, 0:NB], lhsT=g2b[:, cs], rhs=sg[:, t0:t0 + NB], start=True, stop=True),
                             reads=["g2b"], writes=["rpsM0"])
                        S.op("act", lambda e: e.copy(out=g_[:], in_=pm[:, 0:NB]), reads=["rpsM0"], writes=[("g_", pb)])
                        S.op("dve", lambda e: e.tensor_tensor_scan(out=cum[:], data0=cmask[:], data1=sgm[:], initial=0.0,
                                                                  op0=ALU.mult, op1=ALU.add), reads=["sgm", "cmask"], writes=["cum"])
                        S.op("pool", lambda e: e.tensor_sub(out=cex[:], in0=cum[:], in1=sgm[:]), reads=["cum", "sgm"], writes=["cex"])
                        S.op("act", lambda e: e.activation(out=Eex[:], in_=cex[:], func=AF.Exp, scale=-C0), reads=["cex"], writes=["Eex"])
                        S.op("act", lambda e: e.activation(out=Ein[:], in_=cum[:], func=AF.Exp, scale=-C0), reads=["cum"], writes=[("Ein", pb)])
                        S.op("act", lambda e: e.activation(out=Einv[:], in_=cum[:], func=AF.Exp, scale=C0), reads=["cum"], writes=["Einv"])
                        cum3 = cum[:].rearrange("p (c k) -> p c k", k=64)
                        S.op("dve", lambda e: e.tensor_sub(out=dend[:].rearrange("p (c k) -> p c k", k=64),
                                                          in0=cum3[:, :, 63:64].to_broadcast([128, NCH, 64]), in1=cum3),
                             reads=["cum"], writes=["cex"])
                        S.op("act", lambda e: e.activation(out=Eend[:], in_=dend[:], func=AF.Exp, scale=-C0), reads=["cex"], writes=["Eend"])
                        S.op("act", lambda e: e.activation(out=kk[:], in_=k0[:], func=AF.Copy, scale=cols[:, KK + hp:KK + hp + 1]),
                             reads=["k0"], writes=["kk"])
                        S.op("pool", lambda e: e.tensor_mul(out=kk2[:], in0=kk[:], in1=kk[:]), reads=["kk"], writes=["t1"])
                        S.op("pe", lambda e: e.matmul(pm1[:, 0:NB], lhsT=blk1[:], rhs=kk2[:], start=True, stop=True),
                             reads=["t1"], writes=["rpsM1"])
                        S.op("act", lambda e: e.activation(out=rinv[:], in_=pm1[:, 0:NB], func=AF.Sqrt), reads=["rpsM1"], writes=["rinv"])
                        S.op("dve", lambda e: e.tensor_scalar_max(out=rinv[:], in0=rinv[:], scalar1=1e-12), reads=["rinv"], writes=["rinv"])
                        S.op("dve", lambda e: e.reciprocal(out=rinv[:], in_=rinv[:]), reads=["rinv"], writes=["rinv"])
                        S.op("dve", lambda e: e.tensor_mul(out=kkn[:], in0=kk[:], in1=rinv[:]), reads=["kk", "rinv"], writes=["kkn"])
                        S.op("pool", lambda e: e.tensor_scalar(out=t1[:], in0=a_[:], scalar1=cols[:, KA + hp:KA + hp + 1],
                                                              scalar2=dcol[:, OMKA + hp:OMKA + hp + 1], op0=ALU.mult, op1=ALU.add),
                             reads=["a_"], writes=["t1"])
                        S.op("pool", lambda e: e.tensor_mul(out=k2[:], in0=k0[:], in1=t1[:]), reads=["k0", "t1"], writes=[("k2", pb)])
                        S.op("dve", lambda e: e.tensor_mul(out=bb[:], in0=kkn[:], in1=a_[:]), reads=["kkn", "a_"], writes=["bb"])
                        for hh in range(2):
                            ps_ = slice(hh * 64, hh * 64 + 64)

                            def v3(tile_):
                                return tile_[ps_, :].rearrange("p (c k) -> p c k", k=64)

                            def o3(arr):
                                return arr[ps_, :, hh * 64:hh * 64 + 64]
                            eA = "dve" if hh == 0 else "pool"
                            eB = "pool" if hh == 0 else "dve"
                            S.op("dve", lambda e: e.scalar_tensor_tensor(out=o3(AP_), in0=v3(kkn), scalar=-1.0, in1=v3(Eex),
                                                                      op0=ALU.mult, op1=ALU.mult), reads=["kkn", "Eex"], writes=[("AP_", pb, hh)])
                            S.op(eB, lambda e: e.tensor_mul(out=o3(BP_), in0=v3(bb), in1=v3(Einv)), reads=["bb", "Einv"], writes=[("BP_", pb, hh)])
                            S.op(eA, lambda e: e.tensor_mul(out=o3(KP_), in0=v3(k2), in1=v3(Einv)), reads=[("k2", pb), "Einv"], writes=[("KP_", pb, hh)])
                            S.op(eB, lambda e: e.tensor_mul(out=o3(RP_), in0=v3(r_), in1=v3(Ein)), reads=[("r_", pb), ("Ein", pb)], writes=[("RP_", pb, hh)])
                            S.op(eA, lambda e: e.tensor_mul(out=o3(BC_), in0=v3(bb), in1=v3(Eend)), reads=["bb", "Eend"], writes=[("BC_", pb, hh)])
                            S.op(eB, lambda e: e.tensor_mul(out=o3(KC_), in0=v3(k2), in1=v3(Eend)), reads=[("k2", pb), "Eend"], writes=[("KC_", pb, hh)])
                            S.op(eA, lambda e: e.tensor_copy(out=o3(VP_), in_=v3(v_)), reads=[("v_", pb)], writes=[("VP_", pb, hh)])
                        padkeys = [(nm, pb, hh) for nm in ("AP_", "BP_", "KP_", "RP_", "BC_", "KC_", "VP_") for hh in range(2)]

                    def rw_chunk(blk):
                        pb = blk % 2
                        t0 = blk * NB
                        r_, k2, v_, g_, Ein = R2[pb], K22[pb], V2[pb], GG2[pb], EIN2[pb]
                        AP_, BP_, KP_, RP_, BC_, KC_, VP_ = PADS[pb]
                        padkeys = [(nm, pb, hh) for nm in ("AP_", "BP_", "KP_", "RP_", "BC_", "KC_", "VP_") for hh in range(2)]
                        pm, pm1 = psM[0], psM[1]
                        NI = NCH
                        for CH in [range(c0, c0 + NI) for c0 in range(0, NCH, NI)]:
                          if True:

                            def wbank():
                                i_ = nwb[0] % 4
                                nwb[0] += 1
                                return Wk[i_], ("rW", i_)
                            for c in CH:
                                wb_, wk_ = wbank()
                                for q_, arr in enumerate((AP_, BC_, KC_, VP_)):
                                    S.op("pe", lambda e: e.transpose(wb_[:, q_ * 128:(q_ + 1) * 128], arr[:, c, :], ident[:]),
                                         reads=padkeys, writes=[wk_])
                                S.op("act", lambda e: e.copy(out=tok[c][:].rearrange("p a b -> p (a b)"), in_=wb_[:]), reads=[wk_], writes=[("tok", c)])
                            for c in CH:
                                wb_, wk_ = wbank()
                                for q_, (l_, r2) in enumerate(((BP_, AP_), (KP_, AP_), (BP_, RP_), (KP_, RP_))):
                                    S.op("pe", lambda e: e.matmul(wb_[:, q_ * 128:(q_ + 1) * 128], lhsT=l_[:, c, :], rhs=r2[:, c, :], start=True, stop=True),
                                         reads=padkeys, writes=[wk_])
                                S.op("dve", lambda e: e.tensor_tensor(out=gram[c][:].rearrange("p a b -> p (a b)"), in0=wb_[:],
                                                                     in1=mask4[:].rearrange("p a b -> p (a b)"), op=ALU.mult),
                                     reads=[wk_], writes=[("gram", c)])
                            for c in CH:
                                wb_, wk_ = wbank()
                                S.op("pe", lambda e: e.matmul(wb_[:, 0:128], lhsT=AP_[:, c, :], rhs=BP_[:, c, :], start=True, stop=True),
                                     reads=padkeys, writes=[wk_])
                                S.op("dve", lambda e: e.tensor_tensor(out=NTm[c][:], in0=wb_[:, 0:128], in1=maskL[:], op=ALU.mult),
                                     reads=[wk_], writes=[("NTm", c)])
                                S.op("pool", lambda e: e.tensor_add(out=Xm[c][0][:], in0=gram[c][:, 0, :], in1=ident[:]),
                                     reads=[("gram", c)], writes=[("Xm", c, 0)])
                            Pc = {c: gram[c][:, 0, :] for c in CH}
                            PTc = {c: NTm[c][:] for c in CH}
                            pk = {c: [("gram", c), ("NTm", c)] for c in CH}
                            xi = 0
                            for step in range(5):
                                for c in CH:
                                    wb_, wk_ = wbank()
                                    pp = Pm[c][step % 2]
                                    if step < 4:
                                        S.op("pe", lambda e: e.matmul(wb_[:, 0:128], lhsT=PTc[c], rhs=Pc[c], start=True, stop=True),
                                             reads=pk[c], writes=[wk_])
                                    S.op("pe", lambda e: e.matmul(wb_[:, 128:256], lhsT=Pc[c], rhs=PTc[c], start=True, stop=True),
                                         reads=pk[c], writes=[wk_])
                                    if step < 4:
                                        S.op("act", lambda e: e.copy(out=pp[:].rearrange("p a b -> p (a b)"), in_=wb_[:, 0:256]),
                                             reads=[wk_], writes=[("Pm", c, step % 2)])
                                    else:
                                        S.op("act", lambda e: e.copy(out=pp[:, 1, :], in_=wb_[:, 128:256]),
                                             reads=[wk_], writes=[("Pm", c, step % 2)])
                                    Pc[c], PTc[c] = pp[:, 0, :], pp[:, 1, :]
                                    pk[c] = [("Pm", c, step % 2)]
                                for c in CH:
                                    wb_, wk_ = wbank()
                                    S.op("pe", lambda e: e.matmul(wb_[:, 0:128], lhsT=PTc[c], rhs=Xm[c][xi][:], start=True, stop=True),
                                         reads=pk[c] + [("Xm", c, xi)], writes=[wk_])
                                    S.op("dve", lambda e: e.tensor_add(out=Xm[c][1 - xi][:], in0=wb_[:, 0:128], in1=Xm[c][xi][:]),
                                         reads=[wk_, ("Xm", c, xi)], writes=[("Xm", c, 1 - xi)])
                                xi = 1 - xi
                            for c in CH:
                                wb_, wk_ = wbank()
                                S.op("pe", lambda e: e.matmul(wb_[:, 0:128], lhsT=gram[c][:, 1, :], rhs=tok[c][:, 3, :], start=True, stop=True),
                                     reads=[("gram", c), ("tok", c)], writes=[wk_])
                                S.op("act", lambda e: e.copy(out=AW[c][:, 1, :], in_=wb_[:, 0:128]), reads=[wk_], writes=[("AW1", c)])
                                S.op("pool", lambda e: e.tensor_copy(out=AW[c][:, 0, :], in_=tok[c][:, 0, :]), reads=[("tok", c)], writes=[("AW0", c)])
                            for c in CH:
                                wb_, wk_ = wbank()
                                S.op("pe", lambda e: e.matmul(wb_[:, 0:256], lhsT=Xm[c][xi][:], rhs=AW[c][:].rearrange("p a b -> p (a b)"), start=True, stop=True),
                                     reads=[("Xm", c, xi), ("AW0", c), ("AW1", c)], writes=[wk_])
                                S.op("act", lambda e: e.copy(out=AU[c][:].rearrange("p a b -> p (a b)"), in_=wb_[:, 0:256]),
                                     reads=[wk_], writes=[("AU", c)])
                            for c in CH:
                                wb_, wk_ = wbank()
                                Ah, Uh = AU[c][:, 0, :], AU[c][:, 1, :]
                                BCt, KCt, Vt = tok[c][:, 1, :], tok[c][:, 2, :], tok[c][:, 3, :]
                                gcol = Ein[:, c * 64 + 63:c * 64 + 64]
                                S.op("pe", lambda e: e.matmul(wb_[:, 0:128], lhsT=Ah, rhs=BCt, start=True, stop=True),
                                     reads=[("AU", c), ("tok", c)], writes=[wk_])
                                S.op("pe", lambda e: e.matmul(wb_[:, 128:256], lhsT=BCt, rhs=Uh, start=True, stop=False),
                                     reads=[("AU", c), ("tok", c)], writes=[wk_])
                                S.op("pe", lambda e: e.matmul(wb_[:, 128:256], lhsT=KCt, rhs=Vt, start=False, stop=True),
                                     reads=[("tok", c)], writes=[wk_])
                                S.op("pe", lambda e: e.matmul(wb_[:, 256:384], lhsT=Ah, rhs=gram[c][:, 2, :], start=True, stop=True),
                                     reads=[("AU", c), ("gram", c)], writes=[wk_])
                                S.op("dve", lambda e: e.scalar_tensor_tensor(out=PTs[c][:], in0=ident[:], scalar=gcol, in1=wb_[:, 0:128],
                                                                            op0=ALU.mult, op1=ALU.add), reads=[wk_, ("Ein", pb)], writes=[("PTs", c)])
                                S.op("act", lambda e: e.copy(out=Qm[c][:], in_=wb_[:, 128:256]), reads=[wk_], writes=[("Qm", c)])
                                S.op("dve", lambda e: e.tensor_add(out=RH[c][:], in0=wb_[:, 256:384], in1=RP_[:, c, :]),
                                     reads=[wk_] + padkeys, writes=[("RH", c)])
                            for c in CH:
                                cur = STs[nch[0] % 2]
                                nxt = STs[(nch[0] + 1) % 2]
                                kcur, knxt = ("ST", nch[0] % 2), ("ST", (nch[0] + 1) % 2)
                                wb2_, wk2_ = wbank()
                                S.op("pe", lambda e: e.matmul(wb2_[:, 0:128], lhsT=PTs[c][:], rhs=cur[:], start=True, stop=True),
                                     reads=[("PTs", c), kcur], writes=[wk2_])
                                S.op("dve", lambda e: e.tensor_add(out=nxt[:], in0=wb2_[:, 0:128], in1=Qm[c][:]),
                                     reads=[wk2_, ("Qm", c)], writes=[knxt])
                                wb_, wk_ = wbank()
                                S.op("pe", lambda e: e.matmul(wb_[:, 0:128], lhsT=cur[:], rhs=RH[c][:], start=True, stop=False),
                                     reads=[kcur, ("RH", c)], writes=[wk_])
                                S.op("pe", lambda e: e.matmul(wb_[:, 0:128], lhsT=AU[c][:, 1, :], rhs=gram[c][:, 2, :], start=False, stop=False),
                                     reads=[("AU", c), ("gram", c)], writes=[wk_])
                                S.op("pe", lambda e: e.matmul(wb_[:, 0:128], lhsT=tok[c][:, 3, :], rhs=gram[c][:, 3, :], start=False, stop=True),
                                     reads=[("tok", c), ("gram", c)], writes=[wk_])
                                S.op("act", lambda e: e.copy(out=ysc[0:64, c * 64:c * 64 + 64], in_=wb_[0:64, 0:64]),
                                     reads=[wk_], writes=[("ysc", 0, c)])
                                S.op("act", lambda e: e.copy(out=ysc[64:128, c * 64:c * 64 + 64], in_=wb_[64:128, 64:128]),
                                     reads=[wk_], writes=[("ysc", 1, c)])
                                nch[0] += 1

                    def rw_gn(blk):
                        pb = blk % 2
                        t0 = blk * NB
                        r_, k2, v_, g_, Ein = R2[pb], K22[pb], V2[pb], GG2[pb], EIN2[pb]
                        AP_, BP_, KP_, RP_, BC_, KC_, VP_ = PADS[pb]
                        padkeys = [(nm, pb, hh) for nm in ("AP_", "BP_", "KP_", "RP_", "BC_", "KC_", "VP_") for hh in range(2)]
                        pm, pm1 = psM[0], psM[1]
                        ykeys = [("ysc", hh, c) for hh in range(2) for c in range(NCH)]
                        if "yscan" in debug:
                            S.dma("sp", dbg_out("yscan", [512, T])[hp * 128:(hp + 1) * 128, t0:t0 + NB] if "yscan" not in dbg else
                                  dbg["yscan"][hp * 128:(hp + 1) * 128, t0:t0 + NB], ysc[:], reads=ykeys, writes=[("dbg_yscan", hp, blk)])
                        S.op("act", lambda e: e.activation(out=ysq[:], in_=ysc[:], func=AF.Square), reads=ykeys, writes=["ysq"])
                        S.op("pe", lambda e: e.matmul(pm[:, 0:NB], lhsT=blk1[:], rhs=ysc[:], start=True, stop=True), reads=ykeys, writes=["rpsM0"])
                        S.op("pe", lambda e: e.matmul(pm1[:, 0:NB], lhsT=blk1[:], rhs=ysq[:], start=True, stop=True), reads=["ysq"], writes=["rpsM1"])
                        S.op("act", lambda e: e.mul(out=gmean[:], in_=pm[:, 0:NB], mul=1.0 / 64), reads=["rpsM0"], writes=["gmean"])
                        S.op("pool", lambda e: e.tensor_mul(out=gtmp[:], in0=gmean[:], in1=gmean[:]), reads=["gmean"], writes=["gtmp"])
                        S.op("dve", lambda e: e.scalar_tensor_tensor(out=gvar[:], in0=pm1[:, 0:NB], scalar=1.0 / 64, in1=gtmp[:],
                                                                    op0=ALU.mult, op1=ALU.subtract), reads=["rpsM1", "gtmp"], writes=["gvar"])
                        S.op("act", lambda e: e.activation(out=gvar[:], in_=gvar[:], func=AF.Sqrt, bias=64e-5, scale=1.0), reads=["gvar"], writes=["gvar"])
                        S.op("dve", lambda e: e.reciprocal(out=gvar[:], in_=gvar[:]), reads=["gvar"], writes=["gvar"])
                        S.op("dve", lambda e: e.tensor_sub(out=gtmp[:], in0=ysc[:], in1=gmean[:]), reads=ykeys + ["gmean", "gtmp"], writes=["gtmp"])
                        S.op("pool", lambda e: e.tensor_mul(out=gtmp[:], in0=gtmp[:], in1=gvar[:]), reads=["gtmp", "gvar"], writes=["gtmp"])
                        S.op("pool", lambda e: e.tensor_scalar(out=gtmp[:], in0=gtmp[:], scalar1=cols[:, GW + hp:GW + hp + 1],
                                                              scalar2=cols[:, GB + hp:GB + hp + 1], op0=ALU.mult, op1=ALU.add),
                             reads=["gtmp"], writes=["gtmp"])
                        S.op("dve", lambda e: e.scalar_tensor_tensor(out=rkr[:], in0=r_[:], scalar=cols[:, RK + hp:RK + hp + 1], in1=k2[:],
                                                                    op0=ALU.mult, op1=ALU.mult), reads=[("r_", pb), ("k2", pb)], writes=["rkr"])
                        S.op("pe", lambda e: e.matmul(pm[:, 0:NB], lhsT=blk1[:], rhs=rkr[:], start=True, stop=True), reads=["rkr"], writes=["rpsM0"])
                        S.op("dve", lambda e: e.tensor_tensor(out=rkr[:], in0=pm[:, 0:NB], in1=v_[:], op=ALU.mult),
                             reads=["rpsM0", ("v_", pb), "rkr"], writes=["rkr"])
                        S.op("pool", lambda e: e.tensor_add(out=gtmp[:], in0=gtmp[:], in1=rkr[:]), reads=["gtmp", "rkr"], writes=["gtmp"])
                        yo_ = yob[blk % 2]
                        S.op("dve", lambda e: e.tensor_mul(out=yo_[:], in0=gtmp[:], in1=g_[:]), reads=["gtmp", ("g_", pb)], writes=[("ryo", blk % 2)])
                        S.dma("sp", ycat_s[(4 + hp) * 128:(5 + hp) * 128, t0:t0 + NB], yo_[:], reads=[("ryo", blk % 2)],
                              writes=[("ycat", 4 + hp, blk)])
                    nblk_ = T // NB
                    rw_prep(0)
                    for blk in range(nblk_):
                        if blk + 1 < nblk_:
                            rw_prep(blk + 1)
                        rw_chunk(blk)
                        rw_gn(blk)
                S.barrier()
            if "ycat" in debug and stop_after == 4 and not done[0]:
                S.dma("sp", dbg_out("ycat", [1024, T], BF16), ycat_s, writes=["dbg_ycat"])
                S.barrier()
            if stop_after == 4 and not done[0]:
                fin()

        if not done[0]:
          issue_casts(len(cast_jobs))
          with ExitStack() as ph:
            wost = [sbt(ph, f"wost{i}", [128, D]) for i in range(2)]
            wob = sbt(ph, "wob", [128, 8, D], BF16)
            ycb = [sbt(ph, f"ycatb{i}", [128, 8, 128], BF16) for i in range(2)]
            xb = [sbt(ph, f"x3b{i}", [128, D]) for i in range(2)]
            x1b = [sbt(ph, f"x1b{i}", [128, D]) for i in range(2)]
            junk = sbt(ph, "junk3", [128, D])
            xn = [sbt(ph, f"xn3{i}", [128, D]) for i in range(2)]
            u2b = [sbt(ph, f"u2b{i}", [128, 8, 128], BF16) for i in range(2)]
            ss = sbt(ph, "ss3", [128, 32])
            rs = sbt(ph, "rs3", [128, 32])
            pO = [pst(ph, f"pO{i}", [128, 1024]) for i in range(2)]
            pT = [pst(ph, f"pT3{i}", [128, 1024]) for i in range(2)]
            gtm_bc = sbt(ph, "gtm_bc", [128, D])
            build_gate_bc(ph, gtm_bc, 16, pO[0], ("pO", 0, 0))
            for ct in range(8):
                S.dma("sp", wost[ct % 2][:], wout_d[ct * 128:(ct + 1) * 128, :], writes=[("wost", ct % 2)])
                S.op("pool", lambda e: e.tensor_copy(out=wob[:, ct, :], in_=wost[ct % 2][:]), reads=[("wost", ct % 2)], writes=["wob"])
            u2T3 = u2T_s.rearrange("p (k t) -> p k t", k=8)
            def p3_mm(tt):
                b = tt % 2
                ts_ = slice(tt * 128, (tt + 1) * 128)
                S.dma("sp", ycb[b][:], ycat_s[:, ts_].rearrange("(c p) t -> p c t", p=128), writes=[("ycatb", b)])
                S.dma("sp", xb[b][:], x_d[ts_, :], writes=[("x3b", b)])
                for dh in range(2):
                    for ct in range(8):
                        S.op("pe", lambda e: e.matmul(pO[b][:, dh * 512:(dh + 1) * 512], lhsT=ycb[b][:, ct, :],
                                                      rhs=wob[:, ct, dh * 512:(dh + 1) * 512], start=(ct == 0), stop=(ct == 7)),
                             reads=[("ycatb", b), "wob"], writes=[("pO", b, dh)])

            p3_mm(0)
            for tt in range(32):
                b = tt % 2
                ts_ = slice(tt * 128, (tt + 1) * 128)
                if tt + 1 < 32:
                    p3_mm(tt + 1)
                S.op("dve", lambda e: e.tensor_tensor(out=x1b[b][:], in0=pO[b][:], in1=gtm_bc[:], op=ALU.mult),
                     reads=[("pO", b, 0), ("pO", b, 1), "gtm_bc"], writes=[("x1b", b)])
                S.op("pool", lambda e: e.tensor_add(out=x1b[b][:], in0=x1b[b][:], in1=xb[b][:]), reads=[("x1b", b), ("x3b", b)], writes=[("x1b", b)])
                S.dma("sp", x1_s[ts_, :], x1b[b][:], reads=[("x1b", b)], writes=[("x1_s", tt)])
                S.op("act", lambda e: e.activation(out=junk[:], in_=x1b[b][:], func=AF.Square), reads=[("x1b", b)], writes=["junk3"])
                S.op("dve", lambda e: e.reduce_sum(out=ss[:, tt:tt + 1], in_=junk[:], axis=AX.X), reads=["junk3"], writes=[("ss3", tt)])
                S.op("act", lambda e: e.activation(out=rs[:, tt:tt + 1], in_=ss[:, tt:tt + 1], func=AF.Sqrt, bias=1e-6, scale=1.0 / D),
                     reads=[("ss3", tt)], writes=[("rs3", tt)])
                S.op("dve", lambda e: e.reciprocal(out=rs[:, tt:tt + 1], in_=rs[:, tt:tt + 1]), reads=[("rs3", tt)], writes=[("rs3", tt)])
                S.op("act", lambda e: e.activation(out=xn[b][:], in_=x1b[b][:], func=AF.Copy, scale=rs[:, tt:tt + 1]),
                     reads=[("x1b", b), ("rs3", tt)], writes=[("xn3", b)])
                for k in range(8):
                    S.op("pe", lambda e: e.transpose(pT[b][:, k * 128:(k + 1) * 128], xn[b][:, k * 128:(k + 1) * 128], ident[:]),
                         reads=[("xn3", b)], writes=[("pT3", b, k // 4)])
                for k in range(8):
                    aff("act" if k < 4 else "dve", u2b[b][:, k, :], pT[b][:, k * 128:(k + 1) * 128],
                        dcol[:, GSC2 + k:GSC2 + k + 1], modT[:, 24 + k:24 + k + 1], reads=[("pT3", b, k // 4)], writes=[("u2b", b)])
                S.dma("sp", u2T3[:, :, ts_], u2b[b][:], reads=[("u2b", b)], writes=[("u2T_s", tt)])
            S.barrier()
            if "x1" in debug:
                S.dma("sp", dbg_out("x1", [T, D]), x1_s, writes=["dbg_x1"])
                S.dma("sp", dbg_out("u2T", [128, 8 * T], BF16, force=True), u2T_s, writes=["dbg_u2T"])
                S.barrier()
        if stop_after == 5 and not done[0]:
            fin()

        if not done[0]:
          with ExitStack() as ph:
            Wc = sbt(ph, "Wc", [128, 8, 2048], BF16)
            KTs = sbt(ph, "KTs", [128, 16, 128])
            wqs = [sbt(ph, f"wqs{i}", [128, D]) for i in range(2)]
            u2t = [sbt(ph, f"u2t{i}", [128, 8, 128], BF16) for i in range(2)]
            sc2 = [sbt(ph, f"sc_{i}", [128, 16, 128]) for i in range(2)]
            wrk4 = [sbt(ph, f"wrk{i}", [128, 256]) for i in range(4)]
            vals = sbt(ph, "vals", [128, 16, 16])
            idxu = sbt(ph, "idxu", [128, 16, 16], U16)
            idxf = sbt(ph, "idxf", [128, 16, 16])
            cand = sbt(ph, "cand", [128, 8, 256])
            best = sbt(ph, "best", [128, 8, 16])
            posu = sbt(ph, "posu", [128, 8, 16], U16)
            posf = sbt(ph, "posf", [128, 8, 16])
            big = sbt(ph, "big", [128, 8, 16, 16])
            thr16 = sbt(ph, "thr16", [128, 16])
            io16 = sbt(ph, "io16", [128, 16])
            ak = sbt(ph, "ak", [128, 8, 16])
            bk = sbt(ph, "bk", [128, 8, 16])
            ik = sbt(ph, "ik", [128, 128])
            jk = sbt(ph, "jk", [128, 128])
            gk = sbt(ph, "gk", [128, 128])
            zs = sbt(ph, "zs", [128, 8])
            slT = [sbt(ph, f"slT{i}", [128, 3, 128]) for i in range(2)]
            pW = [pst(ph, f"pW{i}", [128, 512]) for i in range(2)]
            pS = [pst(ph, f"pS{i}", [128, 512]) for i in range(4)]
            pTr = pst(ph, "pTr", [128, 512])

            S.dma("sp", KTs[:].rearrange("p a b -> p (a b)"), KT_d, writes=["KTs"])
            S.op("dve", lambda e: e.tensor_scalar_mul(out=thr16[:], in0=iota_f[:, 0:16], scalar1=16.0), writes=["thr16"])
            S.op("dve", lambda e: e.tensor_copy(out=io16[:], in_=iota_f[:, 0:16]), writes=["io16"])
            for hp in range(16):
                S.dma("sp", wqs[hp % 2][:], wqT_d[hp * 128:(hp + 1) * 128, :], writes=[("wqs", hp % 2)])
                for dk in range(8):
                    S.op("pe", lambda e: e.matmul(pW[dk % 2][:, 0:128], lhsT=wqs[hp % 2][:, dk * 128:(dk + 1) * 128], rhs=KTs[:, hp, :],
                                                  start=True, stop=True), reads=[("wqs", hp % 2), "KTs"], writes=[("pW", dk % 2)])
                    S.op("act" if dk % 2 == 0 else "dve",
                         (lambda e: e.copy(out=Wc[:, dk, hp * 128:(hp + 1) * 128], in_=pW[dk % 2][:, 0:128])) if dk % 2 == 0 else
                         (lambda e: e.tensor_copy(out=Wc[:, dk, hp * 128:(hp + 1) * 128], in_=pW[dk % 2][:, 0:128])),
                         reads=[("pW", dk % 2)], writes=["Wc"])
            u2T3 = u2T_s.rearrange("p (k t) -> p k t", k=8)
            def p4_scores(tt):
                b = tt % 2
                ts_ = slice(tt * 128, (tt + 1) * 128)
                S.dma("sp", u2t[b][:], u2T3[:, :, ts_], writes=[("u2t", b)])
                sc_ = sc2[b]
                for q4 in range(4):
                    for dk in range(8):
                        S.op("pe", lambda e: e.matmul(pS[q4][:], lhsT=u2t[b][:, dk, :], rhs=Wc[:, dk, q4 * 512:(q4 + 1) * 512],
                                                      start=(dk == 0), stop=(dk == 7)), reads=[("u2t", b), "Wc"], writes=[("pS", q4)])
                    S.op("act", lambda e: e.copy(out=sc_[:, q4 * 4:(q4 + 1) * 4, :].rearrange("p a b -> p (a b)"), in_=pS[q4][:]),
                         reads=[("pS", q4)], writes=[("sc_", b, q4)])

            p4_scores(0)
            for tt in range(32):
                b = tt % 2
                ts_ = slice(tt * 128, (tt + 1) * 128)
                sc_ = sc2[b]
                if tt + 1 < 32:
                    p4_scores(tt + 1)
                if "scores" in debug and tt == 0:
                    S.dma("sp", dbg_out("scores", [128, 2048]), sc_[:].rearrange("p a b -> p (a b)"),
                          reads=[("sc_", b, q) for q in range(4)], writes=["dbg_scores"])
                NIT = 4
                ALLV = [("vals", g, hf) for g in range(16) for hf in range(2)]
                ALLI = [("idxu", g, hf) for g in range(16) for hf in range(2)]
                ALLB = [("best", h, hf) for h in range(8) for hf in range(2)]
                ALLP = [("posu", h, hf) for h in range(8) for hf in range(2)]
                for g0 in range(0, 16, NIT):
                    grp = range(g0, g0 + NIT)
                    for g16 in grp:
                        S.op("dve", lambda e: e.max(out=vals[:, g16, 0:8], in_=sc_[:, g16, :]), reads=[("sc_", b, g16 // 4)], writes=[("vals", g16, 0)])
                    for g16 in grp:
                        S.op("dve", lambda e: e.max_index(out=idxu[:, g16, 0:8], in_max=vals[:, g16, 0:8], in_values=sc_[:, g16, :]),
                             reads=[("sc_", b, g16 // 4), ("vals", g16, 0)], writes=[("idxu", g16, 0)])
                    for g16 in grp:
                        w_ = wrk4[g16 % NIT]
                        S.op("dve", lambda e: e.match_replace(out=w_[:, 0:128], in_to_replace=vals[:, g16, 0:8], in_values=sc_[:, g16, :],
                                                             imm_value=-1e30), reads=[("sc_", b, g16 // 4), ("vals", g16, 0)], writes=[("wrk", g16 % NIT)])
                    for g16 in grp:
                        w_ = wrk4[g16 % NIT]
                        S.op("dve", lambda e: e.max(out=vals[:, g16, 8:16], in_=w_[:, 0:128]), reads=[("wrk", g16 % NIT)], writes=[("vals", g16, 1)])
                    for g16 in grp:
                        w_ = wrk4[g16 % NIT]
                        S.op("dve", lambda e: e.max_index(out=idxu[:, g16, 8:16], in_max=vals[:, g16, 8:16], in_values=w_[:, 0:128]),
                             reads=[("wrk", g16 % NIT), ("vals", g16, 1)], writes=[("idxu", g16, 1)])
                S.op("pool", lambda e: e.tensor_copy(out=idxf[:], in_=idxu[:]), reads=ALLI, writes=["idxf"])
                v4 = vals[:].rearrange("p (h two) k -> p h two k", two=2)
                i4 = idxf[:].rearrange("p (h two) k -> p h two k", two=2)
                cand4 = cand[:].rearrange("p h (a b) -> p h a b", b=16)
                S.op("dve", lambda e: e.tensor_tensor(out=cand4, in0=v4[:, :, 0, :].unsqueeze(3).to_broadcast([128, 8, 16, 16]),
                                                     in1=v4[:, :, 1, :].unsqueeze(2).to_broadcast([128, 8, 16, 16]), op=ALU.add),
                     reads=ALLV, writes=["cand"])
                for h0 in range(0, 8, NIT):
                    grp = range(h0, h0 + NIT)
                    for h in grp:
                        S.op("dve", lambda e: e.max(out=best[:, h, 0:8], in_=cand[:, h, :]), reads=["cand"], writes=[("best", h, 0)])
                    for h in grp:
                        S.op("dve", lambda e: e.max_index(out=posu[:, h, 0:8], in_max=best[:, h, 0:8], in_values=cand[:, h, :]),
                             reads=["cand", ("best", h, 0)], writes=[("posu", h, 0)])
                    for h in grp:
                        w_ = wrk4[h % NIT]
                        S.op("dve", lambda e: e.match_replace(out=w_[:], in_to_replace=best[:, h, 0:8], in_values=cand[:, h, :],
                                                             imm_value=-1e30), reads=["cand", ("best", h, 0)], writes=[("wrk", h % NIT)])
                    for h in grp:
                        w_ = wrk4[h % NIT]
                        S.op("dve", lambda e: e.max(out=best[:, h, 8:16], in_=w_[:]), reads=[("wrk", h % NIT)], writes=[("best", h, 1)])
                    for h in grp:
                        w_ = wrk4[h % NIT]
                        S.op("dve", lambda e: e.max_index(out=posu[:, h, 8:16], in_max=best[:, h, 8:16], in_values=w_[:]),
                             reads=[("wrk", h % NIT), ("best", h, 1)], writes=[("posu", h, 1)])
                S.op("dve", lambda e: e.tensor_copy(out=posf[:], in_=posu[:]), reads=ALLP, writes=["posf"])
                gk3 = gk[:].rearrange("p (h k) -> p h k", k=16)
                S.op("pool", lambda e: e.tensor_sub(out=gk3, in0=best[:], in1=best[:, :, 0:1].to_broadcast([128, 8, 16])),
                     reads=ALLB, writes=["gk"])
                S.op("act", lambda e: e.activation(out=gk[:], in_=gk[:], func=AF.Exp), reads=["gk"], writes=["gk"])
                S.op("dve", lambda e: e.tensor_reduce(out=zs[:], in_=gk3, axis=AX.X, op=ALU.add), reads=["gk"], writes=["zs"])
                S.op("dve", lambda e: e.reciprocal(out=zs[:], in_=zs[:]), reads=["zs"], writes=["zs"])
                S.op("pool", lambda e: e.tensor_mul(out=gk3, in0=gk3, in1=zs[:].unsqueeze(2).to_broadcast([128, 8, 16])),
                     reads=["gk", "zs"], writes=["gk"])
                S.op("dve", lambda e: e.tensor_tensor(out=big[:], in0=posf[:].unsqueeze(3).to_broadcast([128, 8, 16, 16]),
                                                     in1=thr16[:].unsqueeze(1).unsqueeze(1).to_broadcast([128, 8, 16, 16]), op=ALU.is_ge),
                     reads=["posf", "thr16"], writes=["big"])
                S.op("dve", lambda e: e.tensor_reduce(out=ak[:], in_=big[:, :, :, 1:16], axis=AX.X, op=ALU.add), reads=["big"], writes=["ak"])
                S.op("dve", lambda e: e.scalar_tensor_tensor(out=bk[:], in0=ak[:], scalar=-16.0, in1=posf[:], op0=ALU.mult, op1=ALU.add),
                     reads=["ak", "posf"], writes=["bk"])
                for (sel, half, dst) in ((ak, 0, ik), (bk, 1, jk)):
                    S.op("dve", lambda e: e.tensor_tensor(out=big[:], in0=sel[:].unsqueeze(3).to_broadcast([128, 8, 16, 16]),
                                                         in1=io16[:].unsqueeze(1).unsqueeze(1).to_broadcast([128, 8, 16, 16]), op=ALU.is_equal),
                         reads=[sel.name[2:], "big"], writes=["big"])
                    S.op("dve", lambda e: e.tensor_mul(out=big[:], in0=big[:],
                                                      in1=i4[:, :, half, :].unsqueeze(2).to_broadcast([128, 8, 16, 16])),
                         reads=["big", "idxf"], writes=["big"])
                    S.op("dve", lambda e: e.tensor_reduce(out=dst[:].rearrange("p (h k) -> p h k", k=16), in_=big[:], axis=AX.X, op=ALU.add),
                         reads=["big"], writes=[dst.name[2:]])
                if "route" in debug and tt == 0:
                    S.dma("sp", dbg_out("r_g", [128, 128], force=True), gk[:], reads=["gk"], writes=["dbg_rg"])
                    S.dma("sp", dbg_out("r_i", [128, 128], force=True), ik[:], reads=["ik"], writes=["dbg_ri"])
                    S.dma("sp", dbg_out("r_j", [128, 128], force=True), jk[:], reads=["jk"], writes=["dbg_rj"])
                for q_, src in enumerate((gk, ik, jk)):
                    S.op("pe", lambda e: e.transpose(pTr[:, q_ * 128:(q_ + 1) * 128], src[:], ident[:]), reads=[src.name[2:]], writes=["pTr"])
                S.op("act", lambda e: e.copy(out=slT[b][:].rearrange("p a b -> p (a b)"), in_=pTr[:, 0:384]), reads=["pTr"], writes=[("slT", b)])
                S.dma("sp", gT_s[:, ts_], slT[b][:, 0, :], reads=[("slT", b)], writes=[("gT_s", tt)])
                S.dma("sp", iT_s[:, ts_], slT[b][:, 1, :], reads=[("slT", b)], writes=[("iT_s", tt)])
                S.dma("sp", jT_s[:, ts_], slT[b][:, 2, :], reads=[("slT", b)], writes=[("jT_s", tt)])
            S.barrier()
        if "slots" in debug and not done[0]:
            S.dma("sp", dbg_out("gT", [128, T], force=True), gT_s, writes=["dbg_gT"])
            S.dma("sp", dbg_out("iT", [128, T], force=True), iT_s, writes=["dbg_iT"])
            S.dma("sp", dbg_out("jT", [128, T], force=True), jT_s, writes=["dbg_jT"])
            S.barrier()
        if stop_after == 6 and not done[0]:
            fin()

        if not done[0]:
          with ExitStack() as ph:
            TB = 256
            NPF = 6
            NTK = 8
            Gs = [sbt(ph, f"G{i}", [128, TB, 128], BF16) for i in range(2)]
            u2k = [sbt(ph, f"u2k{i}", [128, 8, TB], BF16) for i in range(2)]
            slg = [sbt(ph, f"slg{i}", [128, TB]) for i in range(2)]
            sli = [sbt(ph, f"sli{i}", [128, TB]) for i in range(2)]
            slj = [sbt(ph, f"slj{i}", [128, TB]) for i in range(2)]
            Pb = [sbt(ph, f"Pb{i}", [128, NTK, 128], BF16) for i in range(2)]
            Qb = [sbt(ph, f"Qb{i}", [128, NTK, 128], BF16) for i in range(2)]
            UVb = [sbt(ph, f"UVb{i}", [128, 2048], BF16) for i in range(NPF)]
            Ub = [t_[:, 0:1024].rearrange("p (k j) -> p k j", k=8) for t_ in UVb]
            Vb = [t_[:, 1024:2048] for t_ in UVb]
            hb = [sbt(ph, f"hb{i}", [128, TB], BF16) for i in range(2)]
            Wb = [sbt(ph, f"Wb{i}", [128, TB], BF16) for i in range(2)]
            x1t_ = sbt(ph, "x1t", [128, D]); x1t = [x1t_, x1t_]
            x2t = [sbt(ph, f"x2t{i}", [128, D]) for i in range(2)]
            fs = sbt(ph, "fs", [128, 2]); fr = sbt(ph, "fr", [128, 2])
            ot_ = sbt(ph, "ot", [128, D]); ot = [ot_, ot_]
            junk = ot_
            pAcc = [pst(ph, f"pAcc{i}", [128, 512]) for i in range(4)]
            pSx = [pst(ph, f"pSx{i}", [128, 512]) for i in range(2)]
            pG = [pst(ph, f"pG{i}", [128, 512]) for i in range(2)]
            gtf_bc = sbt(ph, "gtf_bc", [128, D])
            fing_bc = sbt(ph, "fing_bc", [128, D])
            S.dma("sp", fing_bc[:], fing_d, writes=["fing"])
            build_gate_bc(ph, gtf_bc, 40, pG[0], ("pG", 0))
            u2T3 = u2T_s.rearrange("p (k t) -> p k t", k=8)
            allU = [("Ub", i) for i in range(NCAST)]
            allV = [("Vb", i) for i in range(NCAST)]
            nload = [0]
            ngq = [0]
            nblk = T // TB

            def load_expert(i):
                s_ = nload[0] % NPF
                nload[0] += 1
                S.dma("sp", UVb[s_][:], UV_s[i * 128:(i + 1) * 128, :], reads=allU + allV, writes=[("UVt", s_)])

            def load_block(blk):
                gb = blk % 2
                t0 = blk * TB
                S.dma("sp", u2k[gb][:], u2T3[:, :, t0:t0 + TB], writes=[("u2k", gb)])
                S.dma("sp", slg[gb][:], gT_s[:, t0:t0 + TB], writes=[("slg", gb)])
                S.dma("sp", sli[gb][:], iT_s[:, t0:t0 + TB], writes=[("sli", gb)])
                S.dma("sp", slj[gb][:], jT_s[:, t0:t0 + TB], writes=[("slj", gb)])

            def gbuild_dve(blk, tb):
                gb = blk % 2
                pbuf = (blk * (TB // NTK) + tb) % 2
                ts_ = slice(tb * NTK, (tb + 1) * NTK)
                io_bc = iota_f[:].unsqueeze(1).to_broadcast([128, NTK, 128])
                S.op("dve", lambda e: e.tensor_tensor(out=Pb[pbuf][:], in0=io_bc,
                                                     in1=sli[gb][:, ts_].unsqueeze(2).to_broadcast([128, NTK, 128]), op=ALU.is_equal),
                     reads=[("sli", gb)], writes=[("Pb", pbuf)])
                S.op("pool", lambda e: e.tensor_tensor(out=Pb[pbuf][:], in0=Pb[pbuf][:],
                                                      in1=slg[gb][:, ts_].unsqueeze(2).to_broadcast([128, NTK, 128]), op=ALU.mult),
                     reads=[("slg", gb), ("Pb", pbuf)], writes=[("Pb", pbuf)])
                S.op("dve", lambda e: e.tensor_tensor(out=Qb[pbuf][:], in0=io_bc,
                                                     in1=slj[gb][:, ts_].unsqueeze(2).to_broadcast([128, NTK, 128]), op=ALU.is_equal),
                     reads=[("slj", gb)], writes=[("Qb", pbuf)])

            def gbuild_pe(blk, tb):
                gb = blk % 2
                pbuf = (blk * (TB // NTK) + tb) % 2
                for tq in range(NTK // 4):
                    pgi = ngq[0] % 2
                    ngq[0] += 1
                    pg = pG[pgi]
                    for t4 in range(4):
                        tl = tq * 4 + t4
                        S.op("pe", lambda e: e.matmul(pg[:, t4 * 128:(t4 + 1) * 128], lhsT=Qb[pbuf][:, tl, :], rhs=Pb[pbuf][:, tl, :],
                                                      start=True, stop=True), reads=[("Pb", pbuf), ("Qb", pbuf)], writes=[("pG", pgi)])
                    tg = tb * NTK + tq * 4
                    S.op("act", lambda e: e.copy(out=Gs[gb][:, tg:tg + 4, :].rearrange("p a b -> p (a b)"), in_=pg[:]),
                         reads=[("pG", pgi)], writes=[("G", gb, tb)])

            def gbuild(blk, tb):
                gbuild_dve(blk, tb)
                gbuild_pe(blk, tb)

            def s_stage(blk, i):
                gb = blk % 2
                s_ = (blk * 128 + i) % NPF
                b2 = i % 2
                for dk in range(8):
                    S.op("pe", lambda e: e.matmul(pSx[b2][:, 0:TB], lhsT=Ub[s_][:, dk, :], rhs=u2k[gb][:, dk, :], start=(dk == 0), stop=(dk == 7)),
                         reads=[("UVt", s_), ("u2k", gb)], writes=[("pSx", b2)])
                S.op("act", lambda e: e.activation(out=hb[b2][:], in_=pSx[b2][:, 0:TB], func=AF.Gelu), reads=[("pSx", b2)], writes=[("hb", b2)])
                S.op("dve", lambda e: e.tensor_tensor(out=Wb[b2][:], in0=hb[b2][:], in1=Gs[gb][:, :, i], op=ALU.mult),
                     reads=[("hb", b2)] + ([("G", gb, tb) for tb in range(TB // NTK)] if i < 2 else []), writes=[("Wb", b2)])

            def v_stage(blk, i):
                s_ = (blk * 128 + i) % NPF
                b2 = i % 2
                for tt2 in range(TB // 128):
                    for dh in range(2):
                        S.op("pe", lambda e: e.matmul(pAcc[tt2 * 2 + dh][:], lhsT=Wb[b2][:, tt2 * 128:(tt2 + 1) * 128],
                                                      rhs=Vb[s_][:, dh * 512:(dh + 1) * 512], start=(i == 0), stop=(i == 127)),
                             reads=[("Wb", b2), ("UVt", s_)], writes=[("pAcc", tt2 * 2 + dh)])

            def epilogue(blk):
                t0 = blk * TB
                for tt2 in range(TB // 128):
                    for dh in range(2):
                        S.op("dve", lambda e: e.tensor_tensor(out=x2t[tt2][:, dh * 512:(dh + 1) * 512], in0=pAcc[tt2 * 2 + dh][:],
                                                             in1=gtf_bc[:, dh * 512:(dh + 1) * 512], op=ALU.mult),
                             reads=[("pAcc", tt2 * 2 + dh), "gtf_bc"], writes=[("x2t", tt2)])
                for tt2 in range(TB // 128):
                    tsl = slice(t0 + tt2 * 128, t0 + (tt2 + 1) * 128)
                    S.dma("sp", x1t[tt2][:], x1_s[tsl, :], writes=["x1t"])
                    if "peer" in debug:
                        if "peer" not in dbg:
                            dbg_out("peer", [T, D], force=True)
                        S.dma("sp", dbg["peer"][tsl, :], x2t[tt2][:], reads=[("x2t", tt2)], writes=[("dbg_peer", blk, tt2)])
                    S.op("pool", lambda e: e.tensor_add(out=x2t[tt2][:], in0=x2t[tt2][:], in1=x1t[tt2][:]),
                         reads=[("x2t", tt2), "x1t"], writes=[("x2t", tt2)])
                    S.op("act", lambda e: e.activation(out=junk[:], in_=x2t[tt2][:], func=AF.Square), reads=[("x2t", tt2)], writes=["ot"])
                    S.op("dve", lambda e: e.reduce_sum(out=fs[:, tt2:tt2 + 1], in_=junk[:], axis=AX.X), reads=["ot"], writes=[("fs", tt2)])
                    S.op("act", lambda e: e.activation(out=fr[:, tt2:tt2 + 1], in_=fs[:, tt2:tt2 + 1], func=AF.Sqrt, bias=1e-6, scale=1.0 / D),
                         reads=[("fs", tt2)], writes=[("fr", tt2)])
                    S.op("dve", lambda e: e.reciprocal(out=fr[:, tt2:tt2 + 1], in_=fr[:, tt2:tt2 + 1]), reads=[("fr", tt2)], writes=[("fr", tt2)])
                    S.op("dve", lambda e: e.scalar_tensor_tensor(out=ot[tt2][:], in0=x2t[tt2][:], scalar=fr[:, tt2:tt2 + 1], in1=fing_bc[:],
                                                                op0=ALU.mult, op1=ALU.mult),
                         reads=[("x2t", tt2), ("fr", tt2), "fing"], writes=["ot"])
                    S.dma("sp", out_d[tsl, :], ot[tt2][:], reads=["ot"], writes=[("out", blk, tt2)])

            load_block(0)
            for i in range(NPF - 1):
                load_expert(i)
            for tb in range(TB // NTK):
                gbuild(0, tb)
            for blk in range(nblk):
                if blk + 1 < nblk:
                    load_block(blk + 1)
                s_stage(blk, 0)
                s_stage(blk, 1)
                for i in range(128):
                    nxt_e = blk * 128 + i + NPF - 1
                    if nxt_e < nblk * 128:
                        load_expert(nxt_e % 128)
                    v_stage(blk, i)
                    if i + 2 < 128:
                        s_stage(blk, i + 2)
                    if blk + 1 < nblk and i % 4 == 0:
                        if i >= 4:
                            gbuild_pe(blk + 1, i // 4 - 1)
                        gbuild_dve(blk + 1, i // 4)
                if blk + 1 < nblk:
                    gbuild_pe(blk + 1, TB // NTK - 1)
                epilogue(blk)
            S.barrier()
            S.finish([])
    print(f"[kernel] instructions={S.ninst} waits={S.nwaits}")
    return nc, dbg


def col(v, n):
    return np.ascontiguousarray(np.asarray(v, np.float32).reshape(n, 128).T)


def prep_inputs(inp):
    f = lambda k: np.asarray(inp[k], np.float32)
    cols = np.zeros((128, NCOL), np.float32)
    cols[:, G1:G1 + 8] = col(f("norm_mix_g")[0], 8)
    cols[:, G2:G2 + 8] = col(f("norm_ffn_g")[0], 8)
    cols[:, CW:CW + 124] = f("conv_dw_w")[0].reshape(31, 4, 128).transpose(2, 1, 0).reshape(128, 124)
    cols[:, CB:CB + 4] = col(f("conv_dw_b")[0], 4)
    cols[:, LW:LW + 4] = col(f("conv_ln_w")[0], 4)
    cols[:, LB:LB + 4] = col(f("conv_ln_b")[0], 4)
    cols[:, MU:MU + 14] = col(f("rwkv_mu")[0], 14)
    cols[:, W0:W0 + 4] = col(f("rwkv_w0")[0], 4)
    cols[:, A0:A0 + 4] = col(f("rwkv_a0")[0], 4)
    cols[:, KK:KK + 4] = col(f("rwkv_k_k")[0], 4)
    cols[:, KA:KA + 4] = col(f("rwkv_k_a")[0], 4)
    cols[:, RK:RK + 4] = col(f("rwkv_r_k")[0].reshape(-1), 4)
    cols[:, GW:GW + 4] = col(f("rwkv_gn_w")[0], 4)
    cols[:, GB:GB + 4] = col(f("rwkv_gn_b")[0], 4)
    shared = {
        "ada_w": np.ascontiguousarray(f("ada_w")[0]),
        "ada_b_col": col(f("ada_b")[0], 48),
        "cols": cols,
        "final_g_bc": np.ascontiguousarray(np.broadcast_to(f("final_g")[None, :], (128, D))),
        "w_in": np.ascontiguousarray(f("w_in")[0]),
        "wa2": np.ascontiguousarray(np.concatenate([f("rwkv_w2")[0], f("rwkv_a2")[0]], axis=0)),
        "g2": np.ascontiguousarray(f("rwkv_g2")[0]),
        "w_out": np.ascontiguousarray(f("w_out")[0]),
        "wqT": np.ascontiguousarray(f("peer_w_q")[0].T),
        "KT": np.ascontiguousarray(f("peer_sub_keys")[0].reshape(16, 128, 128).transpose(2, 0, 1).reshape(128, 2048)),
        "UTt": np.ascontiguousarray(f("peer_u")[0].reshape(128, 128, 8, 128).transpose(0, 3, 2, 1).reshape(128 * 128, 1024)),
        "V": np.ascontiguousarray(f("peer_v")[0]),
    }
    maps = []
    for b in range(NCORES):
        m = dict(shared)
        m["x"] = np.ascontiguousarray(f("x")[b])
        m["c_col"] = col(f("c")[b], 8)
        maps.append(m)
    return maps


_NC_CACHE = {}


def kernel(**inputs):
    maps = prep_inputs(inputs)
    if "nc" not in _NC_CACHE:
        _NC_CACHE["nc"] = build_nc()[0]
    nc = _NC_CACHE["nc"]
    res = run_bass_kernel_spmd(nc, maps, core_ids=list(range(NCORES)))
    return np.stack([np.asarray(r["out"], np.float32) for r in res.results], axis=0)
```
